# Optimizing a Trainium2 kernel written in Bass

```python
import math
import jax, jax.numpy as jnp
from jax import lax
import numpy as np

D_MODEL = 1024
BATCH = 4
SEQ = 4096
DEPTH = 2

GRID_W = 64
CTX_LEN = 256
EPS = 1e-6
CONV_W = 3

W_GROUP = D_MODEL // 4
MIX_WIDTH = 4 * W_GROUP

S5_CH = 16
S5_GROUPS = W_GROUP // S5_CH
S5_STATE = 64

HY_ORDER = 2
HY_BANDS = 16
HY_EMB = 1 + 2 * HY_BANDS
HY_HIDDEN = 64
HY_TARGET = 1e-2
HY_FAST_PCT = 0.3
HY_SLOW_PCT = 1.5

RET_HEADS = 4
RET_DK = W_GROUP // RET_HEADS
RET_DV = W_GROUP // RET_HEADS
RET_CHUNK = 128
RET_ROPE_BASE = 10000.0
RET_DECAY_MIN_EXP = 5.0
RET_DECAY_MAX_EXP = 12.0

MLA_HEADS = 4
MLA_NOPE = 64
MLA_ROPE = 32
MLA_V = W_GROUP // MLA_HEADS
MLA_Q_RANK = 192
MLA_KV_RANK = 128
MLA_QK = MLA_NOPE + MLA_ROPE
Q_BLOCK = 128
ROPE_BASE = 10000.0

D_FF = 2816

OFF_HY = W_GROUP
OFF_RET = OFF_HY + 3 * W_GROUP
OFF_MLA = OFF_RET + 4 * W_GROUP
IN_WIDTH = OFF_MLA + MLA_Q_RANK + MLA_KV_RANK + MLA_ROPE

kernel_name = 'hybrid_parallel_heads_dit_block'


def rmsnorm(x, g):
    x32 = x.astype(jnp.float32)
    y = x32 * lax.rsqrt(jnp.mean(x32 * x32, axis=-1, keepdims=True) + EPS)
    return (y * g.astype(jnp.float32)).astype(x.dtype)


def modulate(h, shift, scale):
    return h * (1 + scale) + shift


def dwconv(x, w, b):
    ch = x.shape[-1]
    y = lax.conv_general_dilated(x, w[:, None, :].astype(x.dtype), window_strides=(1,),
                                 padding=[(CONV_W // 2, CONV_W // 2)],
                                 dimension_numbers=('NWC', 'WIO', 'NWC'), feature_group_count=ch)
    return y + b.astype(x.dtype)


def apply_rope(x, cos, sin):
    half = x.shape[-1] // 2
    x1, x2 = x[..., :half], x[..., half:]
    return jnp.concatenate([x1 * cos - x2 * sin, x1 * sin + x2 * cos], axis=-1)


def axial_rope_tables(n_tokens, rot_dim):
    rows = n_tokens // GRID_W
    row = jnp.repeat(jnp.arange(rows), GRID_W).astype(jnp.float32)
    col = jnp.tile(jnp.arange(GRID_W), rows).astype(jnp.float32)
    n_freq = rot_dim // 4
    inv = ROPE_BASE ** (-jnp.arange(n_freq, dtype=jnp.float32) / n_freq)
    ang = jnp.concatenate([row[:, None] * inv, col[:, None] * inv], axis=-1)
    return jnp.cos(ang), jnp.sin(ang)


def conv_ffn(h, w_up, conv_w, conv_b, w_down):
    u = dwconv(h @ w_up, conv_w, conv_b)
    a, v = jnp.split(u, 2, axis=-1)
    return (jax.nn.silu(a) * v) @ w_down


def _linear_combine(e1, e2):
    a1, b1 = e1
    a2, b2 = e2
    return a1 * a2, a2 * b1 + b2


def s5_scan(u, abar, bbar, h0):
    bu = jnp.einsum('blgc,gpc->blgp', u.astype(jnp.complex64), bbar)
    a = jnp.broadcast_to(abar, bu.shape)
    a_cum, h = lax.associative_scan(_linear_combine, (a, bu), axis=1)
    if h0 is not None:
        h = h + a_cum * h0[:, None]
    return h


def s5_mixer(u_lat, u_ctx, lam_re, lam_im, log_dt, b_re, b_im, c_re, c_im, d_skip, glu_w, glu_b, need_ctx):
    f32 = jnp.float32
    lam = lax.complex(lam_re.astype(f32), lam_im.astype(f32))
    lam_dt = lam * jnp.exp(log_dt.astype(f32))[..., None]
    abar = jnp.exp(lam_dt)
    bbar = ((abar - 1.0) / lam)[..., None] * lax.complex(b_re.astype(f32), b_im.astype(f32))
    cmat = lax.complex(c_re.astype(f32), c_im.astype(f32))

    def groups(u):
        return u.astype(f32).reshape(u.shape[0], u.shape[1], S5_GROUPS, S5_CH)

    def readout(u, h_f, h_b_rev):
        bsz, n = u.shape[0], u.shape[1]
        y = (jnp.einsum('blgp,gcp->blgc', h_f, cmat[0])
             + jnp.einsum('blgp,gcp->blgc', h_b_rev[:, ::-1], cmat[1])).real
        y = y.reshape(bsz, n, W_GROUP) + d_skip.astype(f32) * u.astype(f32)
        y = jax.nn.gelu(y)
        y = y * jax.nn.sigmoid(y @ glu_w.astype(f32) + glu_b.astype(f32))
        return y.astype(u.dtype)

    uc, ul = groups(u_ctx), groups(u_lat)
    hc_f = s5_scan(uc, abar[0], bbar[0], None)
    hc_b = s5_scan(uc[:, ::-1], abar[1], bbar[1], None)
    hl_f = s5_scan(ul, abar[0], bbar[0], hc_f[:, -1])
    hl_b = s5_scan(ul[:, ::-1], abar[1], bbar[1], hc_b[:, -1])
    y_lat = readout(u_lat, hl_f, hl_b)
    y_ctx = readout(u_ctx, hc_f, hc_b) if need_ctx else None
    return y_lat, y_ctx


def hyena_spectra(n_tokens, w1, b1, w2, b2, w3, freq, deltas):
    f32 = jnp.float32
    pos = jnp.arange(n_tokens, dtype=f32)
    t01 = pos / (n_tokens - 1)
    bands = jnp.linspace(1e-4, HY_BANDS - 1, HY_BANDS, dtype=f32)
    ang = (2.0 * math.pi / n_tokens) * pos[:, None] * bands[None, :]
    z = jnp.concatenate([t01[:, None], jnp.cos(ang), -jnp.sin(ang)], axis=-1)
    fr = freq.astype(f32)
    hid = jnp.sin(fr * (z @ w1.astype(f32) + b1.astype(f32)))
    hid = jnp.sin(fr * (hid @ w2.astype(f32) + b2.astype(f32)))
    k = (hid @ w3.astype(f32)) * jnp.exp(-t01[:, None] * jnp.abs(deltas.astype(f32)))
    k = k.reshape(n_tokens, HY_ORDER, 2, W_GROUP)
    k = k * lax.rsqrt(jnp.sum(k * k, axis=(0, 2), keepdims=True) + EPS)
    k_fwd, k_bwd = k[:, :, 0], k[:, :, 1]
    k_two = jnp.concatenate([k_fwd, jnp.zeros((1, HY_ORDER, W_GROUP), f32), k_bwd[:0:-1]], axis=0)
    return jnp.fft.rfft(k_two, axis=0)


def hyena_operator(p, conv_w, conv_b, spectra, bias):
    n = p.shape[1]
    u = dwconv(p, conv_w, conv_b).astype(jnp.float32)
    x1, x2, v = jnp.split(u, 3, axis=-1)
    b = bias.astype(jnp.float32)

    def long_conv(z, o):
        zf = jnp.fft.rfft(z, n=2 * n, axis=1)
        return jnp.fft.irfft(zf * spectra[None, :, o], n=2 * n, axis=1)[:, :n] + z * b[o]

    z = x1 * long_conv(v, 0)
    z = x2 * long_conv(z, 1)
    return z.astype(p.dtype)


def hyena_mixer(p_lat, p_ctx, conv_w, conv_b, w1, b1, w2, b2, w3, freq, deltas, bias, need_ctx):
    filt = (w1, b1, w2, b2, w3, freq, deltas)
    y_lat = hyena_operator(p_lat, conv_w, conv_b, hyena_spectra(p_lat.shape[1], *filt), bias)
    y_ctx = hyena_operator(p_ctx, conv_w, conv_b, hyena_spectra(p_ctx.shape[1], *filt), bias) if need_ctx else None
    return y_lat, y_ctx


def retention_chunkwise(q, k, v, log_gamma, s0):
    bsz, n, h, dk = q.shape
    dv = v.shape[-1]
    nc = n // RET_CHUNK
    qc = q.reshape(bsz, nc, RET_CHUNK, h, dk)
    kc = k.reshape(bsz, nc, RET_CHUNK, h, dk)
    vc = v.reshape(bsz, nc, RET_CHUNK, h, dv)
    idx = jnp.arange(RET_CHUNK, dtype=jnp.float32)
    lg = log_gamma[:, None]
    diff = idx[:, None] - idx[None, :]
    dmask = jnp.where(diff >= 0, jnp.exp(jnp.maximum(diff, 0.0)[None] * log_gamma[:, None, None]), 0.0)
    zeta = jnp.exp((RET_CHUNK - 1 - idx)[None] * lg)
    xi = jnp.exp((idx + 1)[None] * lg)
    g_chunk = jnp.exp(RET_CHUNK * log_gamma)[:, None, None]
    inner = jnp.einsum('bnihd,bnjhd->bnhij', qc, kc) * dmask
    o_inner = jnp.einsum('bnhij,bnjhe->bnihe', inner, vc)
    ds = jnp.einsum('bnjhd,hj,bnjhe->nbhde', kc, zeta, vc)

    def step(s, ds_n):
        return g_chunk * s + ds_n, s

    s_final, s_prev = lax.scan(step, s0, ds)
    o_cross = jnp.einsum('bnihd,hi,nbhde->bnihe', qc, xi, s_prev)
    return (o_inner + o_cross).reshape(bsz, n, h, dv), s_final


def retention_mixer(p_lat, p_ctx, decay_exp, gn_g, need_ctx):
    f32 = jnp.float32
    log_gamma = jnp.log1p(-jnp.exp2(-decay_exp.astype(f32)))

    def project(p, rotate):
        bsz, n, _ = p.shape
        q, k, v, g = jnp.split(p.astype(f32), 4, axis=-1)
        q = q.reshape(bsz, n, RET_HEADS, RET_DK)
        k = k.reshape(bsz, n, RET_HEADS, RET_DK) * (RET_DK ** -0.5)
        v = v.reshape(bsz, n, RET_HEADS, RET_DV)
        if rotate:
            theta = RET_ROPE_BASE ** (-jnp.linspace(0.0, 1.0, RET_DK // 2, dtype=f32))
            ang = jnp.arange(n, dtype=f32)[:, None] * theta
            cos, sin = jnp.cos(ang)[:, None], jnp.sin(ang)[:, None]
            q, k = apply_rope(q, cos, sin), apply_rope(k, cos, sin)
        return q, k, v, g

    def output(o, g, dtype):
        bsz, n = o.shape[0], o.shape[1]
        o = o * lax.rsqrt(jnp.mean(o * o, axis=-1, keepdims=True) + EPS)
        o = o.reshape(bsz, n, W_GROUP) * gn_g.astype(f32)
        return (jax.nn.silu(g) * o).astype(dtype)

    qc, kc, vc, gc = project(p_ctx, False)
    ql, kl, vl, gl = project(p_lat, True)
    s_zero = jnp.zeros((p_ctx.shape[0], RET_HEADS, RET_DK, RET_DV), f32)
    oc_f, sc_f = retention_chunkwise(qc, kc, vc, log_gamma[0], s_zero)
    oc_b, sc_b = retention_chunkwise(qc[:, ::-1], kc[:, ::-1], vc[:, ::-1], log_gamma[1], s_zero)
    ol_f, _ = retention_chunkwise(ql, kl, vl, log_gamma[0], sc_f)
    ol_b, _ = retention_chunkwise(ql[:, ::-1], kl[:, ::-1], vl[:, ::-1], log_gamma[1], sc_b)
    y_lat = output(ol_f + ol_b[:, ::-1], gl, p_lat.dtype)
    y_ctx = output(oc_f + oc_b[:, ::-1], gc, p_ctx.dtype) if need_ctx else None
    return y_lat, y_ctx


def softmax_attend(q, k, v):
    s = jnp.einsum('bqhd,bkhd->bhqk', q, k).astype(jnp.float32) * (MLA_QK ** -0.5)
    p = jax.nn.softmax(s, axis=-1).astype(v.dtype)
    return jnp.einsum('bhqk,bkhd->bqhd', p, v)


def mla_mixer(p_lat, p_ctx, q_norm_g, kv_norm_g, w_uq, w_ukv, need_ctx):
    def split(p):
        return (p[..., :MLA_Q_RANK], p[..., MLA_Q_RANK:MLA_Q_RANK + MLA_KV_RANK],
                p[..., MLA_Q_RANK + MLA_KV_RANK:])

    def queries(c_q, rope):
        bsz, n, _ = c_q.shape
        q = (rmsnorm(c_q, q_norm_g) @ w_uq).reshape(bsz, n, MLA_HEADS, MLA_QK)
        if rope is not None:
            q = jnp.concatenate([q[..., :MLA_NOPE], apply_rope(q[..., MLA_NOPE:], *rope)], axis=-1)
        return q

    def keys_values(c_kv, k_rope, rope):
        bsz, n, _ = c_kv.shape
        kv = (rmsnorm(c_kv, kv_norm_g) @ w_ukv).reshape(bsz, n, MLA_HEADS, MLA_NOPE + MLA_V)
        k_r = k_rope[:, :, None, :]
        if rope is not None:
            k_r = apply_rope(k_r, *rope)
        k = jnp.concatenate([kv[..., :MLA_NOPE],
                             jnp.broadcast_to(k_r, (bsz, n, MLA_HEADS, MLA_ROPE)).astype(kv.dtype)], axis=-1)
        return k, kv[..., MLA_NOPE:]

    bsz, n = p_lat.shape[0], p_lat.shape[1]
    cq_l, ckv_l, kr_l = split(p_lat)
    cq_c, ckv_c, kr_c = split(p_ctx)
    cos, sin = axial_rope_tables(n, MLA_ROPE)
    rope = (cos[:, None], sin[:, None])
    k_c, v_c = keys_values(ckv_c, kr_c, None)
    k_l, v_l = keys_values(ckv_l, kr_l, rope)
    q_l = queries(cq_l, rope).astype(k_l.dtype)
    k_all = jnp.concatenate([k_c, k_l], axis=1)
    v_all = jnp.concatenate([v_c, v_l], axis=1)
    q_blocks = q_l.reshape(bsz, n // Q_BLOCK, Q_BLOCK, MLA_HEADS, MLA_QK).transpose(1, 0, 2, 3, 4)
    o = lax.map(lambda qb: softmax_attend(qb, k_all, v_all), q_blocks)
    y_lat = o.transpose(1, 0, 2, 3, 4).reshape(bsz, n, MLA_HEADS * MLA_V).astype(p_lat.dtype)
    y_ctx = None
    if need_ctx:
        y_ctx = softmax_attend(queries(cq_c, None), k_c, v_c).reshape(
            bsz, p_ctx.shape[1], MLA_HEADS * MLA_V).astype(p_ctx.dtype)
    return y_lat, y_ctx


def token_mixing(h, hc, w_in, w_out, s5_p, hy_p, ret_p, mla_p, need_ctx):
    pl, pc = h @ w_in, hc @ w_in
    parts = ((0, OFF_HY, s5_mixer, s5_p), (OFF_HY, OFF_RET, hyena_mixer, hy_p),
             (OFF_RET, OFF_MLA, retention_mixer, ret_p), (OFF_MLA, IN_WIDTH, mla_mixer, mla_p))
    outs = [fn(pl[..., a:b], pc[..., a:b], *prm, need_ctx=need_ctx) for a, b, fn, prm in parts]
    y = jnp.concatenate([o[0] for o in outs], axis=-1) @ w_out
    yc = jnp.concatenate([o[1] for o in outs], axis=-1) @ w_out if need_ctx else None
    return y, yc


def setup_inputs(seed: int = 0) -> dict:
    key = jax.random.key(seed)
    ks = iter(jax.random.split(key, 64))
    f32 = jnp.float32

    def nrm(shape, scale=1.0):
        return scale * jax.random.normal(next(ks), shape, f32)

    def gain(shape):
        return 1.0 + nrm(shape, 0.05)

    L, G, P = DEPTH, S5_GROUPS, S5_STATE
    n_filt = 2 * HY_ORDER * W_GROUP
    decay_lo = abs(math.log(HY_TARGET)) / HY_SLOW_PCT
    decay_hi = abs(math.log(HY_TARGET)) / HY_FAST_PCT
    return {
        'x': nrm((BATCH, SEQ, D_MODEL)),
        'c': nrm((BATCH, D_MODEL)),
        'ctx': nrm((BATCH, CTX_LEN, D_MODEL)),
        'c_ctx': nrm((D_MODEL,)),
        'ada_w': nrm((L, D_MODEL, 6 * D_MODEL), D_MODEL ** -0.5),
        'ada_b': nrm((L, 6 * D_MODEL), 0.02),
        'norm_g': gain((L, 4, D_MODEL)),
        'w_in': nrm((L, D_MODEL, IN_WIDTH), D_MODEL ** -0.5),
        'w_out': nrm((L, MIX_WIDTH, D_MODEL), MIX_WIDTH ** -0.5),
        's5_lam_re': -0.5 + nrm((L, 2, G, P), 0.01),
        's5_lam_im': math.pi * jnp.arange(P, dtype=f32) + nrm((L, 2, G, P), 0.01),
        's5_log_dt': jax.random.uniform(next(ks), (L, 2, G), f32, math.log(1e-3), math.log(1e-1)),
        's5_b_re': nrm((L, 2, G, P, S5_CH), (2 * S5_CH) ** -0.5),
        's5_b_im': nrm((L, 2, G, P, S5_CH), (2 * S5_CH) ** -0.5),
        's5_c_re': nrm((L, 2, G, S5_CH, P), (2 * P) ** -0.5),
        's5_c_im': nrm((L, 2, G, S5_CH, P), (2 * P) ** -0.5),
        's5_d': nrm((L, W_GROUP)),
        's5_glu_w': nrm((L, W_GROUP, W_GROUP), W_GROUP ** -0.5),
        's5_glu_b': nrm((L, W_GROUP), 0.02),
        'hy_conv_w': nrm((L, CONV_W, 3 * W_GROUP), CONV_W ** -0.5),
        'hy_conv_b': nrm((L, 3 * W_GROUP), 0.02),
        'hy_w1': nrm((L, HY_EMB, HY_HIDDEN), HY_EMB ** -0.5),
        'hy_b1': nrm((L, HY_HIDDEN), 0.02),
        'hy_w2': nrm((L, HY_HIDDEN, HY_HIDDEN), HY_HIDDEN ** -0.5),
        'hy_b2': nrm((L, HY_HIDDEN), 0.02),
        'hy_w3': nrm((L, HY_HIDDEN, n_filt), HY_HIDDEN ** -0.5),
        'hy_freq': gain((L, HY_HIDDEN)),
        'hy_deltas': jnp.linspace(decay_lo, decay_hi, n_filt, dtype=f32) * gain((L, n_filt)),
        'hy_bias': nrm((L, HY_ORDER, W_GROUP)),
        'ret_decay_exp': jnp.linspace(RET_DECAY_MIN_EXP, RET_DECAY_MAX_EXP, RET_HEADS, dtype=f32) + nrm((L, 2, RET_HEADS), 0.1),
        'ret_gn_g': gain((L, W_GROUP)),
        'mla_q_norm_g': gain((L, MLA_Q_RANK)),
        'mla_kv_norm_g': gain((L, MLA_KV_RANK)),
        'mla_w_uq': nrm((L, MLA_Q_RANK, MLA_HEADS * MLA_QK), MLA_Q_RANK ** -0.5),
        'mla_w_ukv': nrm((L, MLA_KV_RANK, MLA_HEADS * (MLA_NOPE + MLA_V)), MLA_KV_RANK ** -0.5),
        'ffn_w_up': nrm((L, D_MODEL, 2 * D_FF), D_MODEL ** -0.5),
        'ffn_conv_w': nrm((L, CONV_W, 2 * D_FF), CONV_W ** -0.5),
        'ffn_conv_b': nrm((L, 2 * D_FF), 0.02),
        'ffn_w_down': nrm((L, D_FF, D_MODEL), D_FF ** -0.5),
    }


def reference(x, c, ctx, c_ctx, ada_w, ada_b, norm_g, w_in, w_out,
              s5_lam_re, s5_lam_im, s5_log_dt, s5_b_re, s5_b_im, s5_c_re, s5_c_im, s5_d, s5_glu_w, s5_glu_b,
              hy_conv_w, hy_conv_b, hy_w1, hy_b1, hy_w2, hy_b2, hy_w3, hy_freq, hy_deltas, hy_bias,
              ret_decay_exp, ret_gn_g, mla_q_norm_g, mla_kv_norm_g, mla_w_uq, mla_w_ukv,
              ffn_w_up, ffn_conv_w, ffn_conv_b, ffn_w_down):
    sc_lat = jax.nn.silu(c)
    sc_ctx = jax.nn.silu(c_ctx)
    for l in range(DEPTH):
        need_ctx = l < DEPTH - 1
        mod = sc_lat @ ada_w[l] + ada_b[l]
        sh1, sc1, g1, sh2, sc2, g2 = [m[:, None, :] for m in jnp.split(mod, 6, axis=-1)]
        modc = sc_ctx @ ada_w[l] + ada_b[l]
        csh1, csc1, cg1, csh2, csc2, cg2 = jnp.split(modc, 6, axis=-1)
        s5_p = (s5_lam_re[l], s5_lam_im[l], s5_log_dt[l], s5_b_re[l], s5_b_im[l],
                s5_c_re[l], s5_c_im[l], s5_d[l], s5_glu_w[l], s5_glu_b[l])
        hy_p = (hy_conv_w[l], hy_conv_b[l], hy_w1[l], hy_b1[l], hy_w2[l], hy_b2[l],
                hy_w3[l], hy_freq[l], hy_deltas[l], hy_bias[l])
        ret_p = (ret_decay_exp[l], ret_gn_g[l])
        mla_p = (mla_q_norm_g[l], mla_kv_norm_g[l], mla_w_uq[l], mla_w_ukv[l])
        ffn_p = (ffn_w_up[l], ffn_conv_w[l], ffn_conv_b[l], ffn_w_down[l])

        h = modulate(rmsnorm(x, norm_g[l, 0]), sh1, sc1)
        hc = modulate(rmsnorm(ctx, norm_g[l, 0]), csh1, csc1)
        y, yc = token_mixing(h, hc, w_in[l], w_out[l], s5_p, hy_p, ret_p, mla_p, need_ctx)
        x = x + g1 * rmsnorm(y, norm_g[l, 1])
        h = modulate(rmsnorm(x, norm_g[l, 2]), sh2, sc2)
        x = x + g2 * rmsnorm(conv_ffn(h, *ffn_p), norm_g[l, 3])
        if need_ctx:
            ctx = ctx + cg1 * rmsnorm(yc, norm_g[l, 1])
            hc = modulate(rmsnorm(ctx, norm_g[l, 2]), csh2, csc2)
            ctx = ctx + cg2 * rmsnorm(conv_ffn(hc, *ffn_p), norm_g[l, 3])
    return x
```

```python
import math
import numpy as np
import ml_dtypes
import concourse.bass as bass
import concourse.mybir as mybir
from concourse.bass_utils import run_bass_kernel_spmd
from contextlib import ExitStack
from types import SimpleNamespace

F32 = mybir.dt.float32
BF16 = mybir.dt.bfloat16
I32 = mybir.dt.int32
ALU = mybir.AluOpType
AF = mybir.ActivationFunctionType

ENGS = ("pe", "act", "dve", "pool", "sp")
NSLOT = 6

D = 1024
NB = 4
SEQ = 4096
CTX = 256
T = SEQ + CTX
DEPTH = 2
EPS = 1e-6
WG = 256
DFF = 2816
NTT = T // 128
W_PAD = T + 4
CTX0 = 1
LAT0 = 259
NFM = 2048
NTM = 1536
CH512 = [(i * 512, min(512, T - i * 512)) for i in range((T + 511) // 512)]


class Prog:
    def __init__(self, nc):
        self.nc = nc
        self.ops = []
        self.es = ExitStack()
        self.arena = None
        self.aoff = 0

    def sb(self, name, shape, dtype=F32):
        return self.es.enter_context(self.nc.sbuf_tensor(name, list(shape), dtype))

    def ps(self, name, shape, dtype=F32):
        return self.es.enter_context(self.nc.psum_tensor(name, list(shape), dtype))

    def dram(self, name, shape, dtype=F32, kind="Internal"):
        return self.nc.dram_tensor(name, list(shape), dtype, kind=kind)

    def init_arena(self, words):
        self.arena = self.sb("arena", [128, words], F32)
        self.awords = words
        self.aoff = 0

    def mark(self):
        return self.aoff

    def release(self, m):
        self.aoff = m

    def alloc(self, shape, dtype=F32):
        npart = shape[0]
        n = int(np.prod(shape[1:]))
        if dtype == F32 or dtype == I32:
            words = n
        else:
            words = (n + 1) // 2
        words = (words + 7) // 8 * 8
        off = self.aoff
        self.aoff += words
        assert self.aoff <= self.awords, "arena overflow %d > %d" % (self.aoff, self.awords)
        v = self.arena[0:npart, off:off + words]
        if dtype != F32:
            v = v.bitcast(dtype)
        v = v[:, 0:n]
        if len(shape) == 3:
            v = v.rearrange("p (a b) -> p a b", a=shape[1])
        elif len(shape) == 4:
            v = v.rearrange("p (a b c) -> p a b c", a=shape[1], b=shape[2])
        return v

    def op(self, eng, fn, reads=(), writes=(), dma=False):
        self.ops.append(dict(eng=eng, fn=fn, reads=tuple(reads), writes=tuple(writes), dma=dma, bar=False))

    def barrier(self):
        self.ops.append(dict(bar=True))

    def dma(self, out, in_, reads, writes, q="sp", **kw):
        self.op(q, lambda e: e.dma_start(out=out, in_=in_, **kw), reads, writes, dma=True)

    def mm(self, out, lhsT, rhs, start, stop, reads, writes, **kw):
        self.op("pe", lambda e: e.matmul(out, lhsT, rhs, start=start, stop=stop, **kw), reads, writes)

    def tr(self, out, in_, ident, reads, writes):
        self.op("pe", lambda e: e.transpose(out, in_, ident), reads, writes)

    def act(self, out, in_, func, reads, writes, **kw):
        self.op("act", lambda e: e.activation(out=out, in_=in_, func=func, **kw), reads, writes)

    def v(self, eng, name, reads, writes, *a, **kw):
        self.op(eng, lambda e: getattr(e, name)(*a, **kw), reads, writes)

    def emit(self):
        nc = self.nc
        es = self.es
        csem = {e: es.enter_context(nc.semaphore("c_" + e)) for e in ("pe", "act", "dve", "pool")}
        dsem = {q: [es.enter_context(nc.semaphore("d_%s%d" % (q, i))) for i in range(NSLOT)]
                for q in ("sp", "act", "pool")}
        ccount = {e: 0 for e in csem}
        dcount = {q: 0 for q in dsem}
        slotuse = {q: [0] * NSLOT for q in dsem}
        semobj = {}
        for e in csem:
            semobj[("c", e)] = csem[e]
        for q in dsem:
            for i in range(NSLOT):
                semobj[("d", q, i)] = dsem[q][i]
        know = {e: {} for e in ENGS}
        last_w = {}
        readers = {}
        streams = {e: [] for e in ENGS}
        bar_know = {}
        bar_done = {e: True for e in ENGS}

        def cur_all():
            d = {}
            for q in dsem:
                for i in range(NSLOT):
                    if slotuse[q][i] > 0:
                        d[("d", q, i)] = slotuse[q][i] * 16
            for e in csem:
                if ccount[e] > 0:
                    d[("c", e)] = ccount[e]
            return d

        for op in self.ops:
            if op["bar"]:
                bar_know = cur_all()
                bar_done = {e: False for e in ENGS}
                continue
            e = op["eng"]
            deps = []
            for r in op["reads"]:
                if r in last_w:
                    deps.append(last_w[r])
            for w in op["writes"]:
                if w in last_w:
                    deps.append(last_w[w])
                for rd in readers.get(w, {}).values():
                    deps.append(rd)
            waits = {}
            kn = know[e]
            if not bar_done[e]:
                bar_done[e] = True
                for sk, val in bar_know.items():
                    if sk == ("c", "pe") and e == "pe":
                        continue
                    if kn.get(sk, 0) < val:
                        waits[sk] = val
                        kn[sk] = val
            if op["dma"]:
                slot = dcount[e] % NSLOT
                dcount[e] += 1
                sk = ("d", e, slot)
                prev = slotuse[e][slot] * 16
                if prev > 0 and kn.get(sk, 0) < prev:
                    waits[sk] = max(waits.get(sk, 0), prev)
                    kn[sk] = prev
                slotuse[e][slot] += 1
                tok = (sk, slotuse[e][slot] * 16)
            else:
                ccount[e] += 1
                tok = (("c", e), ccount[e])
            for (dtok, dknow) in deps:
                sk, val = dtok
                if sk == ("c", "pe") and e == "pe" and not op["dma"]:
                    continue
                if kn.get(sk, 0) >= val:
                    continue
                waits[sk] = max(waits.get(sk, 0), val)
                for k2, v2 in dknow.items():
                    if kn.get(k2, 0) < v2:
                        kn[k2] = v2
                kn[sk] = max(kn.get(sk, 0), val)
            myknow = dict(kn)
            myknow[tok[0]] = max(myknow.get(tok[0], 0), tok[1])
            if (not op["dma"]) and e == "pe":
                kn[tok[0]] = tok[1]
            streams[e].append((op, waits, tok))
            entry = (tok, myknow)
            for w in op["writes"]:
                last_w[w] = entry
                readers[w] = {}
            for r in op["reads"]:
                if r not in op["writes"]:
                    readers.setdefault(r, {})[tok[0]] = entry
        fin = cur_all()
        self.stats = dict(ccount=dict(ccount), dcount=dict(dcount))

        def run_stream(ename, eng):
            for (op, waits, tok) in streams[ename]:
                for sk, val in waits.items():
                    eng.wait_ge(semobj[sk], val)
                ins = op["fn"](eng)
                ins.then_inc(semobj[tok[0]], 16 if op["dma"] else 1)
            if ename == "sp":
                for sk, val in fin.items():
                    eng.wait_ge(semobj[sk], val)

        with nc.Block() as block:
            @block.sync
            def _(eng):
                run_stream("sp", eng)

            @block.tensor
            def _(eng):
                run_stream("pe", eng)

            @block.scalar
            def _(eng):
                run_stream("act", eng)

            @block.vector
            def _(eng):
                run_stream("dve", eng)

            @block.gpsimd
            def _(eng):
                run_stream("pool", eng)
        es.close()


class Rot:
    def __init__(self, views, name):
        self.views = views
        self.name = name
        self.i = 0

    def next(self):
        i = self.i % len(self.views)
        self.i += 1
        return self.views[i], "%s%d" % (self.name, i)


def _swap_halves_cols(w, width):
    n = w.shape[1]
    idx = np.arange(n).reshape(n // width, 2, width // 2)[:, ::-1, :].reshape(-1)
    return w[:, idx]


def host_consts():
    c = {}
    c["ident"] = np.eye(128, dtype=np.float32)
    return c


def arrange_in_cols(w_in):
    o = {}
    s5u = w_in[:, 0:256]
    hy = w_in[:, 256:1024]
    rq = w_in[:, 1024:1280]
    rk = w_in[:, 1280:1536]
    rv = w_in[:, 1536:1792]
    rg = w_in[:, 1792:2048]
    cq = w_in[:, 2048:2240]
    ckv = w_in[:, 2240:2368]
    kr = w_in[:, 2368:2400]
    z64 = np.zeros((w_in.shape[0], 64), np.float32)
    fm = np.concatenate([s5u, rq, rk, rg, _swap_halves_cols(rq, 64), _swap_halves_cols(rk, 64),
                         cq, z64, ckv, kr, _swap_halves_cols(kr, 32), z64], axis=1)
    assert fm.shape[1] == NFM
    tm = np.concatenate([hy, rk, _swap_halves_cols(rk, 64), rv], axis=1)
    assert tm.shape[1] == NTM
    o["w_fm"] = np.ascontiguousarray(fm)
    o["w_tm"] = np.ascontiguousarray(tm)
    return o


def prep_layer_weights(inp, l):
    return arrange_in_cols(inp["w_in"][l])


class Builder:
    def __init__(self, cfg):
        self.cfg = cfg
        self.nc = bass.Bass("TRN2", target_bir_lowering=False)
        self.P = Prog(self.nc)
        self.inputs = {}
        self.din = {}
        self.dscr = {}
        self.ext_out = []

    def inp(self, name, shape, dtype=F32):
        t = self.P.dram(name, shape, dtype, kind="ExternalInput")
        self.din[name] = t
        return t

    def scr(self, name, shape, dtype=F32):
        cfg = self.cfg
        if name in cfg.get("dbg_in", ()):
            kind = "ExternalInput"
        elif name in cfg.get("dbg_out", ()):
            kind = "ExternalOutput"
            self.ext_out.append(name)
        else:
            kind = "Internal"
        t = self.P.dram(name, shape, dtype, kind=kind)
        self.dscr[name] = t
        return t


def build_program(cfg):
    B = Builder(cfg)
    P = B.P
    nc = B.nc
    layers = cfg.get("layers", list(range(DEPTH)))
    stages = cfg.get("stages", "ABCDEF")
    mixers = cfg.get("mixers", ("s5", "hy", "ret", "mla"))

    x_in = B.inp("x_in", [SEQ, D])
    ctx_in = B.inp("ctx_in", [CTX, D])
    cs_in = B.inp("cs_in", [128, 8, 2])
    ident_in = B.inp("ident", [128, 128])
    ada_w = B.inp("ada_w", [DEPTH, D, 6 * D])
    ada_bT = B.inp("ada_bT", [DEPTH, 128, 48])
    ada_brow = B.inp("ada_brow", [DEPTH, 1, 6 * D])
    ng_T = B.inp("ng_T", [DEPTH, 4, 128, 8])
    ng_row = B.inp("ng_row", [DEPTH, 4, 1, D])
    w_fm = B.inp("w_fm", [DEPTH, D, NFM])
    w_tm = B.inp("w_tm", [DEPTH, D, NTM])
    w_out = B.inp("w_out", [DEPTH, D, D])
    w_up = B.inp("w_up", [DEPTH, D, 2 * DFF])
    ffn_cw = B.inp("ffn_cw", [DEPTH, 128, 44, 3])
    ffn_cb = B.inp("ffn_cb", [DEPTH, 128, 44])
    w_down = B.inp("w_down", [DEPTH, DFF, D])
    y_out = P.dram("y", [SEQ, D], F32, kind="ExternalOutput")

    xs = B.scr("xs", [T, D])
    pT_d = B.scr("pT_d", [NFM, T])
    hyp_d = B.scr("hyp_d", [W_PAD, 768])
    rtok_d = B.scr("rtok_d", [T, 768])
    oT_d = B.scr("oT_d", [D, T], BF16)
    gT_d = B.scr("gT_d", [DFF, W_PAD], BF16)

    ident = P.sb("ident_sb", [128, 128])
    zero_sb = P.sb("zero_sb", [128, 768])
    scs = P.sb("scs", [128, 8, 2])
    modT = P.sb("modT", [128, 48, 2])
    AB = P.sb("ABmod", [128, 4, 8, 2])
    grow = P.sb("grow", [128, 2, 2, D])
    eps_t = P.sb("eps_t", [128, 1])
    ones_f = P.sb("ones_f", [128, 128])
    ones_b = P.sb("ones_b", [128, 128], BF16)
    pss = [P.ps("ps%d" % i, [128, 512]) for i in range(8)]
    psrot = Rot([p[:] for p in pss], "ps")
    P.init_arena(cfg.get("arena_words", 47000))

    P.dma(ident[:], ident_in.ap(), [], ["ident"])
    P.v("dve", "memset", [], ["zero_sb"], zero_sb[:], 0.0)
    P.v("dve", "memset", [], ["eps_t"], eps_t[:], EPS)
    P.v("dve", "memset", [], ["ones_f"], ones_f[:], 1.0)
    P.v("dve", "memset", [], ["ones_b"], ones_b[:], 1.0)
    cs_raw = P.sb("cs_raw", [128, 8, 2])
    P.dma(cs_raw[:], cs_in.ap(), [], ["cs_raw"])
    P.act(scs[:], cs_raw[:], AF.Silu, ["cs_raw"], ["scs"])
    for r in (0, CTX0 + CTX, CTX0 + CTX + 1, W_PAD - 1):
        P.dma(hyp_d.ap()[r:r + 1, :], zero_sb[0:1, 0:768], ["zero_sb"], ["hyp_d"], q="pool")

    def xrows(layer, t0, n):
        if layer == 0:
            if t0 < CTX:
                return ctx_in.ap()[t0:t0 + n, :]
            return x_in.ap()[t0 - CTX:t0 - CTX + n, :]
        return xs.ap()[t0:t0 + n, :]

    def stage_A(l):
        m0 = P.mark()
        sc_rep = P.alloc([128, 8, 2, 128])
        P.v("dve", "tensor_copy", ["scs"], ["sc_rep"], sc_rep, scs[:].unsqueeze(3).to_broadcast([128, 8, 2, 128]))
        wbufs = Rot([P.alloc([128, 8, 512]) for _ in range(2)], "adaw")
        abT = P.alloc([128, 48])
        P.dma(abT, ada_bT.ap()[l], [], ["abT"])
        ngT = P.alloc([128, 4, 8])
        P.dma(ngT, ada_dummy_ngT(l), [], ["ngT"])
        brow = P.alloc([128, 2, D])
        ngrow = P.alloc([128, 2, D])
        for wi, c0 in enumerate((2 * D, 5 * D)):
            P.dma(brow[:, wi, :], ada_brow.ap()[l, 0:1, c0:c0 + D].partition_broadcast(128)[:, 0, :], [], ["brow"])
            P.dma(ngrow[:, wi, :], ng_row.ap()[l, 1 + 2 * wi, 0:1, :].partition_broadcast(128)[:, 0, :], [], ["ngrow"])
        for ci in range(12):
            wb, wk = wbufs.next()
            P.dma(wb, ada_w.ap()[l].rearrange("(k p) n -> p k n", p=128)[:, :, ci * 512:(ci + 1) * 512], [], [wk])
            for mi in range(4):
                m = ci * 4 + mi
                ps, pk = psrot.next()
                for k in range(8):
                    P.mm(ps[:, 0:2], wb[:, k, mi * 128:(mi + 1) * 128], scs[:, k, :], k == 0, k == 7,
                         [wk, "scs"], [pk])
                P.v("dve", "tensor_scalar", [pk, "abT"], ["modT"], modT[:, m, :], ps[:, 0:2], abT[:, m:m + 1], None, ALU.add)
            if ci in (4, 5, 10, 11):
                wi = 0 if ci < 6 else 1
                cc = (ci - 4) if ci < 6 else (ci - 10)
                for v in range(2):
                    ps, pk = psrot.next()
                    for k in range(8):
                        P.mm(ps, sc_rep[:, k, v, :], wb[:, k, :], k == 0, k == 7, [wk, "sc_rep"], [pk])
                    P.v("dve", "tensor_tensor", [pk, "brow"], ["grow"], grow[:, wi, v, cc * 512:(cc + 1) * 512], ps,
                        brow[:, wi, cc * 512:(cc + 1) * 512], ALU.add)
        for wi in range(2):
            for v in range(2):
                P.v("pool", "tensor_tensor", ["grow", "ngrow"], ["grow"], grow[:, wi, v, :], grow[:, wi, v, :], ngrow[:, wi, :], ALU.mult)
        for wh in range(2):
            shm = 0 if wh == 0 else 24
            scm = 8 if wh == 0 else 32
            for v in range(2):
                P.v("dve", "scalar_tensor_tensor", ["modT", "ngT"], ["AB"], AB[:, 2 * wh, :, v], modT[:, scm:scm + 8, v], 1.0,
                    ngT[:, 2 * wh, :], ALU.add, ALU.mult)
                P.v("dve", "tensor_copy", ["modT"], ["AB"], AB[:, 2 * wh + 1, :, v], modT[:, shm:shm + 8, v])
        P.barrier()
        P.release(m0)

    def ada_dummy_ngT(l):
        return ng_T.ap()[l].rearrange("f p k -> p f k")

    def norm_mod_transpose(xt, xkey, hT, hkey, tt, wh, scratch):
        v = 1 if tt < 2 else 0
        junk, ss, rstd, xn = scratch
        P.act(junk, xt, AF.Square, [xkey], ["nm_junk", "nm_ss"], accum_out=ss)
        P.act(rstd, ss, AF.Sqrt, ["nm_ss", "eps_t"], ["nm_rstd"], bias=eps_t[:], scale=1.0 / D)
        P.v("dve", "reciprocal", ["nm_rstd"], ["nm_rstd"], rstd, rstd)
        P.act(xn, xt, AF.Identity, [xkey, "nm_rstd"], ["nm_xn"], scale=rstd)
        for half in range(2):
            ps, pk = psrot.next()
            for kk in range(4):
                k = half * 4 + kk
                P.tr(ps[:, kk * 128:(kk + 1) * 128], xn[:, k * 128:(k + 1) * 128], ident[:], ["nm_xn", "ident"], [pk])
            for kk in range(4):
                k = half * 4 + kk
                eng = "dve" if kk % 2 == 0 else "pool"
                if eng == "pool":
                    P.act(hT[:, k, tt * 128:(tt + 1) * 128], ps[:, kk * 128:(kk + 1) * 128], AF.Identity,
                          [pk, "AB"], [hkey], bias=AB[:, 2 * wh + 1, k, v:v + 1], scale=AB[:, 2 * wh, k, v:v + 1])
                else:
                    P.v("dve", "tensor_scalar", [pk, "AB"], [hkey], hT[:, k, tt * 128:(tt + 1) * 128],
                        ps[:, kk * 128:(kk + 1) * 128], AB[:, 2 * wh, k, v:v + 1], AB[:, 2 * wh + 1, k, v:v + 1],
                        ALU.mult, ALU.add)

    def nm_scratch():
        return (P.alloc([128, D]), P.alloc([128, 1]), P.alloc([128, 1]), P.alloc([128, D]))

    def stage_B(l, hT):
        m0 = P.mark()
        xbufs = Rot([P.alloc([128, D]) for _ in range(3)], "xt")
        scratch = nm_scratch()
        for tt in range(NTT):
            xt, xk = xbufs.next()
            P.dma(xt, xrows(l, tt * 128, 128), ["xs%d" % tt], [xk])
            norm_mod_transpose(xt, xk, hT, "hT", tt, 0, scratch)

    def load_cast(dst_bf, src_ap, shape, stage_rot, key, eng="pool"):
        st, sk = stage_rot.next()
        P.dma(st, src_ap, [], [sk])
        if eng == "act":
            P.act(dst_bf, st, AF.Copy, [sk], [key])
        else:
            P.v(eng, "tensor_copy", [sk], [key], dst_bf, st)

    def stage_C(l, hT):
        m0 = P.mark()
        wst = Rot([P.alloc([128, 8, 512]) for _ in range(2)], "wst")
        wbf = Rot([P.alloc([128, 8, 512], BF16) for _ in range(2)], "wbf")
        outb = Rot([P.alloc([128, T]) for _ in range(2)], "pout")
        wsrc = w_fm.ap()[l].rearrange("(k p) n -> p k n", p=128)
        for cg in range(NFM // 512):
            wb, wk = wbf.next()
            load_cast(wb, wsrc[:, :, cg * 512:(cg + 1) * 512], None, wst, wk, eng="pool")
            for ci in range(4):
                ct = cg * 4 + ci
                ob, ok = outb.next()
                for ch, (c0, cn) in enumerate(CH512):
                    ps, pk = psrot.next()
                    for k in range(8):
                        P.mm(ps[:, 0:cn], wb[:, k, ci * 128:(ci + 1) * 128], hT[:, k, c0:c0 + cn], k == 0, k == 7,
                             [wk, "hT"], [pk])
                    if ch % 2 == 0:
                        P.act(ob[:, c0:c0 + cn], ps[:, 0:cn], AF.Copy, [pk], [ok])
                    else:
                        P.v("dve", "tensor_copy", [pk], [ok], ob[:, c0:c0 + cn], ps[:, 0:cn])
                P.dma(pT_d.ap()[ct * 128:(ct + 1) * 128, :], ob, [ok], ["pT_d"], q="pool")
        tmo = Rot([P.alloc([128, 512]) for _ in range(3)], "tmo")
        wsrc = w_tm.ap()[l].rearrange("(k p) n -> p k n", p=128)
        for cg in range(NTM // 512):
            wb, wk = wbf.next()
            load_cast(wb, wsrc[:, :, cg * 512:(cg + 1) * 512], None, wst, wk, eng="pool")
            for tt in range(NTT):
                ps, pk = psrot.next()
                for k in range(8):
                    P.mm(ps, hT[:, k, tt * 128:(tt + 1) * 128], wb[:, k, :], k == 0, k == 7, [wk, "hT"], [pk])
                ob, ok = tmo.next()
                if tt % 2 == 0:
                    P.act(ob, ps, AF.Copy, [pk], [ok])
                else:
                    P.v("dve", "tensor_copy", [pk], [ok], ob, ps)
                row_h = (CTX0 + tt * 128) if tt < 2 else (LAT0 + (tt - 2) * 128)
                c0 = cg * 512
                if c0 + 512 <= 768:
                    P.dma(hyp_d.ap()[row_h:row_h + 128, c0:c0 + 512], ob, [ok], ["hyp_d"], q="pool")
                elif c0 >= 768:
                    P.dma(rtok_d.ap()[tt * 128:(tt + 1) * 128, c0 - 768:c0 - 768 + 512], ob, [ok], ["rtok_d"], q="pool")
                else:
                    nh = 768 - c0
                    P.dma(hyp_d.ap()[row_h:row_h + 128, c0:768], ob[:, 0:nh], [ok], ["hyp_d"], q="pool")
                    P.dma(rtok_d.ap()[tt * 128:(tt + 1) * 128, 0:512 - nh], ob[:, nh:512], [ok], ["rtok_d"], q="pool")
        P.barrier()
        P.release(m0)

    def stage_E(l, h2T, last):
        m0 = P.mark()
        wst = Rot([P.alloc([128, 8, 512]) for _ in range(2)], "wst")
        wo = P.alloc([128, 8, D], BF16)
        wsrc = w_out.ap()[l].rearrange("(k p) n -> p k n", p=128)
        for hh in range(2):
            load_cast(wo[:, :, hh * 512:(hh + 1) * 512], wsrc[:, :, hh * 512:(hh + 1) * 512], None, wst, "wo", eng="pool")
        obufs = Rot([P.alloc([128, 8, 128], BF16) for _ in range(3)], "oT")
        xbufs = Rot([P.alloc([128, D]) for _ in range(3)], "xt")
        ysb = P.alloc([128, D])
        junk = P.alloc([128, D])
        ss = P.alloc([128, 1])
        rstd = P.alloc([128, 1])
        tmp = P.alloc([128, D])
        xnb = Rot([P.alloc([128, D]) for _ in range(2)], "xnew")
        scratch = nm_scratch()
        osrc = oT_d.ap().rearrange("(k p) t -> p k t", p=128)
        for tt in range(2 if last else 0, NTT):
            v = 1 if tt < 2 else 0
            ob, ok = obufs.next()
            P.dma(ob, osrc[:, :, tt * 128:(tt + 1) * 128], ["oT_d"], [ok])
            xt, xk = xbufs.next()
            P.dma(xt, xrows(l, tt * 128, 128), ["xs%d" % tt], [xk])
            for hh in range(2):
                ps, pk = psrot.next()
                for k in range(8):
                    P.mm(ps, ob[:, k, :], wo[:, k, hh * 512:(hh + 1) * 512], k == 0, k == 7, [ok, "wo"], [pk])
                P.act(ysb[:, hh * 512:(hh + 1) * 512], ps, AF.Copy, [pk], ["ysb"])
            P.act(junk, ysb, AF.Square, ["ysb"], ["e_junk", "e_ss"], accum_out=ss)
            P.act(rstd, ss, AF.Sqrt, ["e_ss", "eps_t"], ["e_rstd"], bias=eps_t[:], scale=1.0 / D)
            P.v("dve", "reciprocal", ["e_rstd"], ["e_rstd"], rstd, rstd)
            P.v("dve", "scalar_tensor_tensor", ["ysb", "e_rstd", "grow"], ["e_tmp"], tmp, ysb, rstd, grow[:, 0, v, :], ALU.mult, ALU.mult)
            xn, xnk = xnb.next()
            P.v("pool", "tensor_tensor", ["e_tmp", xk], [xnk], xn, tmp, xt, ALU.add)
            P.dma(xs.ap()[tt * 128:(tt + 1) * 128, :], xn, [xnk], ["xs%d" % tt], q="pool")
            norm_mod_transpose(xn, xnk, h2T, "h2T", tt, 1, scratch)
        P.barrier()
        P.release(m0)

    def stage_F(l, h2T, last):
        m0 = P.mark()
        wst = Rot([P.alloc([128, 8, 256]) for _ in range(2)], "fwst")
        wbf = Rot([P.alloc([128, 8, 256], BF16) for _ in range(2)], "fwbf")
        cw = P.alloc([128, 44, 3])
        cb = P.alloc([128, 44])
        P.dma(cw, ffn_cw.ap()[l], [], ["ffn_cw"])
        P.dma(cb, ffn_cb.ap()[l], [], ["ffn_cb"])
        ua = P.alloc([128, W_PAD])
        uv = P.alloc([128, W_PAD])
        ca = P.alloc([128, W_PAD])
        cv = P.alloc([128, W_PAD])
        gb = Rot([P.alloc([128, W_PAD], BF16) for _ in range(1)], "gb")
        for u_ in (ua, uv):
            P.v("dve", "memset", [], ["ua", "uv"], u_, 0.0)
        wsrc = w_up.ap()[l].rearrange("(k p) n -> p k n", p=128)
        Wp = W_PAD
        for j in range(22):
            wb, wk = wbf.next()
            st, sk = wst.next()
            P.dma(st[:, :, 0:128], wsrc[:, :, j * 128:(j + 1) * 128], [], [sk])
            P.dma(st[:, :, 128:256], wsrc[:, :, DFF + j * 128:DFF + (j + 1) * 128], [], [sk])
            P.v("pool", "tensor_copy", [sk], [wk], wb, st)
            for half, (ub, ukey) in enumerate(((ua, "ua"), (uv, "uv"))):
                for ch, (c0, cn) in enumerate(CH512):
                    ps, pk = psrot.next()
                    for k in range(8):
                        P.mm(ps[:, 0:cn], wb[:, k, half * 128:(half + 1) * 128], h2T[:, k, c0:c0 + cn], k == 0, k == 7,
                             [wk, "h2T"], [pk])
                    segs = []
                    if c0 < CTX:
                        segs.append((0, CTX, CTX0))
                        segs.append((CTX, cn - CTX, LAT0))
                    else:
                        segs.append((0, cn, LAT0 + c0 - CTX))
                    for (s0, sn, d0) in segs:
                        if (ch + half) % 2 == 0:
                            P.act(ub[:, d0:d0 + sn], ps[:, s0:s0 + sn], AF.Copy, [pk], [ukey])
                        else:
                            P.v("dve", "tensor_copy", [pk], [ukey], ub[:, d0:d0 + sn], ps[:, s0:s0 + sn])
            for half, (ub, ukey, cbuf, ckey) in enumerate(((ua, "ua", ca, "ca"), (uv, "uv", cv, "cv"))):
                jj = j + 22 * half
                P.act(cbuf[:, 1:Wp - 1], ub[:, 1:Wp - 1], AF.Identity, [ukey, "ffn_cw", "ffn_cb"], [ckey],
                      bias=cb[:, jj:jj + 1], scale=cw[:, jj, 1:2])
                P.v("dve", "scalar_tensor_tensor", [ukey, ckey, "ffn_cw"], [ckey], cbuf[:, 1:Wp - 1], ub[:, 0:Wp - 2],
                    cw[:, jj, 0:1], cbuf[:, 1:Wp - 1], ALU.mult, ALU.add)
                P.v("dve", "scalar_tensor_tensor", [ukey, ckey, "ffn_cw"], [ckey], cbuf[:, 1:Wp - 1],
                    ub[:, 2:Wp], cw[:, jj, 2:3], cbuf[:, 1:Wp - 1], ALU.mult, ALU.add)
            P.act(ca[:, 1:Wp - 1], ca[:, 1:Wp - 1], AF.Silu, ["ca"], ["ca"])
            g, gk = gb.next()
            P.v("pool", "tensor_tensor", ["ca", "cv"], [gk], g[:, 1:Wp - 1], ca[:, 1:Wp - 1], cv[:, 1:Wp - 1], ALU.mult)
            P.dma(gT_d.ap()[j * 128:(j + 1) * 128, 1:Wp - 1], g[:, 1:Wp - 1], [gk], ["gT_d"], q="pool")
        P.barrier()
        P.release(m0)

    def stage_F2(l, last):
        m0 = P.mark()
        wst = Rot([P.alloc([128, 2, D]) for _ in range(2)], "dwst")
        wd = P.alloc([128, 22, D], BF16)
        wsrc = w_down.ap()[l].rearrange("(j p) n -> p j n", p=128)
        for j2 in range(11):
            load_cast(wd[:, 2 * j2:2 * j2 + 2, :], wsrc[:, 2 * j2:2 * j2 + 2, :], None, wst, "wd", eng="pool")
        gbufs = Rot([P.alloc([128, 22, 128], BF16) for _ in range(3)], "gt")
        xbufs = Rot([P.alloc([128, D]) for _ in range(3)], "xt")
        ysb = P.alloc([128, D])
        junk = P.alloc([128, D])
        ss = P.alloc([128, 1])
        rstd = P.alloc([128, 1])
        tmp = P.alloc([128, D])
        xnb = Rot([P.alloc([128, D]) for _ in range(2)], "xnew")
        gsrc = gT_d.ap().rearrange("(j p) t -> p j t", p=128)
        for tt in range(2 if last else 0, NTT):
            v = 1 if tt < 2 else 0
            col = (CTX0 + tt * 128) if tt < 2 else (LAT0 + (tt - 2) * 128)
            gt, gk = gbufs.next()
            P.dma(gt, gsrc[:, :, col:col + 128], ["gT_d"], [gk])
            xt, xk = xbufs.next()
            P.dma(xt, xs.ap()[tt * 128:(tt + 1) * 128, :], ["xs%d" % tt], [xk])
            for hh in range(2):
                ps, pk = psrot.next()
                for j in range(22):
                    P.mm(ps, gt[:, j, :], wd[:, j, hh * 512:(hh + 1) * 512], j == 0, j == 21, [gk, "wd"], [pk])
                P.act(ysb[:, hh * 512:(hh + 1) * 512], ps, AF.Copy, [pk], ["ysb"])
            P.act(junk, ysb, AF.Square, ["ysb"], ["e_junk", "e_ss"], accum_out=ss)
            P.act(rstd, ss, AF.Sqrt, ["e_ss", "eps_t"], ["e_rstd"], bias=eps_t[:], scale=1.0 / D)
            P.v("dve", "reciprocal", ["e_rstd"], ["e_rstd"], rstd, rstd)
            P.v("dve", "scalar_tensor_tensor", ["ysb", "e_rstd", "grow"], ["e_tmp"], tmp, ysb, rstd, grow[:, 1, v, :], ALU.mult, ALU.mult)
            xn, xnk = xnb.next()
            P.v("pool", "tensor_tensor", ["e_tmp", xk], [xnk], xn, tmp, xt, ALU.add)
            if last:
                P.dma(y_out.ap()[(tt - 2) * 128:(tt - 1) * 128, :], xn, [xnk], ["y"], q="pool")
            else:
                P.dma(xs.ap()[tt * 128:(tt + 1) * 128, :], xn, [xnk], ["xs%d" % tt], q="pool")
        P.barrier()
        P.release(m0)

    for l in layers:
        last = (l == DEPTH - 1)
        if "A" in stages:
            stage_A(l)
        mh = P.mark()
        hT = P.alloc([128, 8, T], BF16)
        if "B" in stages:
            stage_B(l, hT)
        if "C" in stages:
            stage_C(l, hT)
        P.barrier()
        P.release(mh)
        if "D" in stages:
            C = SimpleNamespace(B=B, P=P, nc=nc, psrot=psrot, pss=pss, pT_d=pT_d, hyp_d=hyp_d, rtok_d=rtok_d, oT_d=oT_d,
                                ident=ident, zero_sb=zero_sb, eps_t=eps_t, ones_f=ones_f, ones_b=ones_b, cfg=cfg)
            for mx in mixers:
                MIXERS[mx](C, l, last)
                P.barrier()
        mh = P.mark()
        h2T = P.alloc([128, 8, T], BF16)
        if "E" in stages:
            stage_E(l, h2T, last)
        if "F" in stages:
            stage_F(l, h2T, last)
        P.barrier()
        P.release(mh)
        if "F" in stages:
            stage_F2(l, last)
    P.emit()
    return B


MIXERS = {}
HOSTPREP = {}


def make_inputs(inp, cfg=None):
    sh = {}
    sh.update(host_consts())
    L = DEPTH
    sh["ada_w"] = np.ascontiguousarray(inp["ada_w"])
    sh["ada_bT"] = np.ascontiguousarray(inp["ada_b"].reshape(L, 48, 128).transpose(0, 2, 1))
    sh["ada_brow"] = np.ascontiguousarray(inp["ada_b"].reshape(L, 1, 6 * D))
    sh["ng_T"] = np.ascontiguousarray(inp["norm_g"].reshape(L, 4, 8, 128).transpose(0, 1, 3, 2))
    sh["ng_row"] = np.ascontiguousarray(inp["norm_g"].reshape(L, 4, 1, D))
    lw = [prep_layer_weights(inp, l) for l in range(L)]
    sh["w_fm"] = np.stack([w["w_fm"] for w in lw])
    sh["w_tm"] = np.stack([w["w_tm"] for w in lw])
    sh["w_out"] = np.ascontiguousarray(inp["w_out"])
    sh["w_up"] = np.ascontiguousarray(inp["ffn_w_up"])
    sh["ffn_cw"] = np.ascontiguousarray(inp["ffn_conv_w"].reshape(L, 3, 44, 128).transpose(0, 3, 2, 1))
    sh["ffn_cb"] = np.ascontiguousarray(inp["ffn_conv_b"].reshape(L, 44, 128).transpose(0, 2, 1))
    sh["w_down"] = np.ascontiguousarray(inp["ffn_w_down"])
    for fn in HOSTPREP.values():
        sh.update(fn(inp))
    per = []
    for core in range(8):
        b = core % NB
        d = {}
        d["x_in"] = np.ascontiguousarray(inp["x"][b])
        d["ctx_in"] = np.ascontiguousarray(inp["ctx"][b])
        cs = np.stack([inp["c"][b], inp["c_ctx"]], axis=-1)
        d["cs_in"] = np.ascontiguousarray(cs.reshape(8, 128, 2).transpose(1, 0, 2))
        per.append(d)
    return sh, per


_CACHE = {}


def kernel(**inputs):
    inp = {k: np.asarray(v) for k, v in inputs.items()}
    cfg = {}
    Bd = build_program(cfg)
    sh, per = make_inputs(inp)
    in_maps = []
    for core in range(8):
        m = dict(sh)
        m.update(per[core])
        in_maps.append(m)
    res = run_bass_kernel_spmd(Bd.nc, in_maps, core_ids=list(range(8)))
    out = np.stack([res.results[b]["y"] for b in range(NB)], axis=0)
    return out.astype(np.float32)


MLA_H = 4
QK = 96
PT_CQA, PT_CQB, PT_CKV, PT_KR, PT_KRS = 1536, 1664, 1792, 1920, 1952
OT_MLA = 768


def host_mla(inp):
    L = DEPTH
    o = {}
    wq = inp["mla_w_uq"].reshape(L, 192, MLA_H, QK)
    wqB = wq.copy()
    rope = wq[..., 64:96].reshape(L, 192, MLA_H, 2, 16)[..., ::-1, :].reshape(L, 192, MLA_H, 32)
    wqB[..., 64:96] = rope
    o["mla_wq"] = np.ascontiguousarray(np.stack([wq, wqB], axis=2).reshape(L, 192, 2 * MLA_H * QK))
    wkv = inp["mla_w_ukv"].reshape(L, 128, MLA_H, 128)
    o["mla_wkn"] = np.ascontiguousarray(wkv[..., 0:64].reshape(L, 128, 256))
    o["mla_wv"] = np.ascontiguousarray(wkv[..., 64:128].reshape(L, 128, 256))
    qg = np.zeros((L, 256), np.float32)
    qg[:, :192] = inp["mla_q_norm_g"]
    o["mla_qg"] = np.ascontiguousarray(qg.reshape(L, 2, 128).transpose(0, 2, 1))
    o["mla_kvg"] = np.ascontiguousarray(inp["mla_kv_norm_g"].reshape(L, 128, 1))
    n = SEQ
    rows = n // 64
    row = np.repeat(np.arange(rows), 64).astype(np.float32)
    col = np.tile(np.arange(64), rows).astype(np.float32)
    nf = 8
    inv = (np.float32(10000.0) ** (-np.arange(nf, dtype=np.float32) / nf)).astype(np.float32)
    ang = np.concatenate([row[:, None] * inv, col[:, None] * inv], axis=-1).astype(np.float32)
    cos, sin = np.cos(ang).astype(np.float32), np.sin(ang).astype(np.float32)
    Cf = np.ones((96, T), np.float32)
    Sf = np.zeros((96, T), np.float32)
    Cf[64:80, CTX:] = cos.T
    Cf[80:96, CTX:] = cos.T
    Sf[64:80, CTX:] = -sin.T
    Sf[80:96, CTX:] = sin.T
    o["mla_C"] = Cf
    o["mla_S"] = Sf
    return o


def mix_mla(C, l, last):
    P, B = C.P, C.B
    psrot = C.psrot
    if "mla_wq" not in B.din:
        B.inp("mla_wq", [DEPTH, 192, 768])
        B.inp("mla_wkn", [DEPTH, 128, 256])
        B.inp("mla_wv", [DEPTH, 128, 256])
        B.inp("mla_qg", [DEPTH, 128, 2])
        B.inp("mla_kvg", [DEPTH, 128, 1])
        B.inp("mla_C", [96, T])
        B.inp("mla_S", [96, T])
    din = B.din
    pT = C.pT_d.ap()
    m0 = P.mark()
    wq_st = P.alloc([128, 2, 768])
    wq = P.alloc([128, 2, 768], BF16)
    P.dma(wq_st[:, 0, :], din["mla_wq"].ap()[l, 0:128, :], [], ["mla_wq_st"])
    P.dma(wq_st[0:64, 1, :], din["mla_wq"].ap()[l, 128:192, :], [], ["mla_wq_st"])
    P.v("pool", "tensor_copy", ["mla_wq_st"], ["mla_wq"], wq[:, 0, :], wq_st[:, 0, :])
    P.v("pool", "tensor_copy", ["mla_wq_st"], ["mla_wq"], wq[0:64, 1, :], wq_st[0:64, 1, :])
    wk_st = P.alloc([128, 512])
    wkv = P.alloc([128, 512], BF16)
    P.dma(wk_st[:, 0:256], din["mla_wkn"].ap()[l], [], ["mla_wk_st"])
    P.dma(wk_st[:, 256:512], din["mla_wv"].ap()[l], [], ["mla_wk_st"])
    P.v("pool", "tensor_copy", ["mla_wk_st"], ["mla_wkv"], wkv, wk_st)
    qg = P.alloc([128, 2])
    kvg = P.alloc([128, 1])
    P.dma(qg, din["mla_qg"].ap()[l], [], ["mla_qg"])
    P.dma(kvg, din["mla_kvg"].ap()[l], [], ["mla_kvg"])
    qT = [P.alloc([96, T], BF16) for _ in range(MLA_H)]
    kT = [P.alloc([96, T], BF16) for _ in range(MLA_H)]
    vtok = P.alloc([128, NTT, 256], BF16)
    m1 = P.mark()
    NB_ = 2
    cqa = Rot([P.alloc([128, 512]) for _ in range(NB_)], "m_cqa")
    cqb = Rot([P.alloc([64, 512]) for _ in range(NB_)], "m_cqb")
    ckv = Rot([P.alloc([128, 512]) for _ in range(NB_)], "m_ckv")
    krb = Rot([P.alloc([96, 2, 512]) for _ in range(NB_)], "m_kr")
    tabs = Rot([P.alloc([96, 2, 512]) for _ in range(NB_)], "m_tab")
    sq = P.alloc([128, 3, 512])
    rstd = P.alloc([128, 2, 512])
    cqn_a = P.alloc([128, 512], BF16)
    cqn_b = P.alloc([64, 512], BF16)
    ckvn = P.alloc([128, 512], BF16)
    t1 = P.alloc([96, 512])
    t2 = P.alloc([96, 512])
    for ch, (c0, cn) in enumerate(CH512):
        a, ak = cqa.next()
        b, bk = cqb.next()
        kv, kvk = ckv.next()
        kr, krk = krb.next()
        tb, tbk = tabs.next()
        P.dma(a[:, 0:cn], pT[PT_CQA:PT_CQA + 128, c0:c0 + cn], ["pT_d"], [ak])
        P.dma(b[:, 0:cn], pT[PT_CQB:PT_CQB + 64, c0:c0 + cn], ["pT_d"], [bk])
        P.dma(kv[:, 0:cn], pT[PT_CKV:PT_CKV + 128, c0:c0 + cn], ["pT_d"], [kvk])
        P.dma(kr[64:96, 0, 0:cn], pT[PT_KR:PT_KR + 32, c0:c0 + cn], ["pT_d"], [krk])
        P.dma(kr[64:96, 1, 0:cn], pT[PT_KRS:PT_KRS + 32, c0:c0 + cn], ["pT_d"], [krk])
        P.dma(tb[:, 0, 0:cn], din["mla_C"].ap()[:, c0:c0 + cn], [], [tbk])
        P.dma(tb[:, 1, 0:cn], din["mla_S"].ap()[:, c0:c0 + cn], [], [tbk])
        P.act(sq[:, 0, 0:cn], a[:, 0:cn], AF.Square, [ak], ["m_sq"])
        P.act(sq[0:64, 1, 0:cn], b[:, 0:cn], AF.Square, [bk], ["m_sq"])
        P.act(sq[:, 2, 0:cn], kv[:, 0:cn], AF.Square, [kvk], ["m_sq"])
        ps, pk = psrot.next()
        P.mm(ps[:, 0:cn], C.ones_f[:, :], sq[:, 0, 0:cn], True, False, ["m_sq", "ones_f"], [pk])
        P.mm(ps[:, 0:cn], C.ones_f[0:64, :], sq[0:64, 1, 0:cn], False, True, ["m_sq", "ones_f"], [pk])
        P.act(rstd[:, 0, 0:cn], ps[:, 0:cn], AF.Sqrt, [pk, "eps_t"], ["m_rstd"], bias=C.eps_t[:], scale=1.0 / 192)
        ps2, pk2 = psrot.next()
        P.mm(ps2[:, 0:cn], C.ones_f[:, :], sq[:, 2, 0:cn], True, True, ["m_sq", "ones_f"], [pk2])
        P.act(rstd[:, 1, 0:cn], ps2[:, 0:cn], AF.Sqrt, [pk2, "eps_t"], ["m_rstd"], bias=C.eps_t[:], scale=1.0 / 128)
        P.v("dve", "reciprocal", ["m_rstd"], ["m_rstd"], rstd[:, :, 0:cn], rstd[:, :, 0:cn])
        P.v("dve", "scalar_tensor_tensor", [ak, "mla_qg", "m_rstd"], ["m_cqn_a"], cqn_a[:, 0:cn], a[:, 0:cn], qg[:, 0:1],
            rstd[:, 0, 0:cn], ALU.mult, ALU.mult)
        P.v("dve", "scalar_tensor_tensor", [bk, "mla_qg", "m_rstd"], ["m_cqn_b"], cqn_b[:, 0:cn], b[:, 0:cn], qg[0:64, 1:2],
            rstd[0:64, 0, 0:cn], ALU.mult, ALU.mult)
        P.v("dve", "scalar_tensor_tensor", [kvk, "mla_kvg", "m_rstd"], ["m_ckvn"], ckvn[:, 0:cn], kv[:, 0:cn], kvg[:, 0:1],
            rstd[:, 1, 0:cn], ALU.mult, ALU.mult)
        for h in range(MLA_H):
            psA, pkA = psrot.next()
            psB, pkB = psrot.next()
            for (pp, ppk, ab) in ((psA, pkA, 0), (psB, pkB, 1)):
                w0 = (ab * MLA_H + h) * QK
                P.mm(pp[0:96, 0:cn], wq[:, 0, w0:w0 + QK], cqn_a[:, 0:cn], True, False, ["mla_wq", "m_cqn_a"], [ppk])
                P.mm(pp[0:96, 0:cn], wq[0:64, 1, w0:w0 + QK], cqn_b[:, 0:cn], False, True, ["mla_wq", "m_cqn_b"], [ppk])
            P.v("dve", "tensor_tensor", [pkA, tbk], ["m_t1"], t1[:, 0:cn], psA[0:96, 0:cn], tb[:, 0, 0:cn], ALU.mult)
            P.v("dve", "tensor_tensor", [pkB, tbk], ["m_t2"], t2[:, 0:cn], psB[0:96, 0:cn], tb[:, 1, 0:cn], ALU.mult)
            P.v("pool", "tensor_tensor", ["m_t1", "m_t2"], ["m_qT%d" % h], qT[h][:, c0:c0 + cn], t1[:, 0:cn], t2[:, 0:cn], ALU.add)
        for h in range(MLA_H):
            ps, pk = psrot.next()
            P.mm(ps[0:64, 0:cn], wkv[:, h * 64:(h + 1) * 64], ckvn[:, 0:cn], True, True, ["mla_wkv", "m_ckvn"], [pk])
            P.act(kT[h][0:64, c0:c0 + cn], ps[0:64, 0:cn], AF.Copy, [pk], ["m_kT%d" % h])
        P.v("dve", "tensor_tensor", [krk, tbk], ["m_t1"], t1[64:96, 0:cn], kr[64:96, 0, 0:cn], tb[64:96, 0, 0:cn], ALU.mult)
        P.v("dve", "tensor_tensor", [krk, tbk], ["m_t2"], t2[64:96, 0:cn], kr[64:96, 1, 0:cn], tb[64:96, 1, 0:cn], ALU.mult)
        P.v("dve", "tensor_tensor", ["m_t1", "m_t2"], ["m_t1"], t1[64:96, 0:cn], t1[64:96, 0:cn], t2[64:96, 0:cn], ALU.add)
        for h in range(MLA_H):
            if h % 2 == 0:
                P.act(kT[h][64:96, c0:c0 + cn], t1[64:96, 0:cn], AF.Copy, ["m_t1"], ["m_kT%d" % h])
            else:
                P.v("pool", "tensor_copy", ["m_t1"], ["m_kT%d" % h], kT[h][64:96, c0:c0 + cn], t1[64:96, 0:cn])
        for i in range(cn // 128):
            tt = c0 // 128 + i
            ps, pk = psrot.next()
            P.mm(ps[:, 0:256], ckvn[:, i * 128:(i + 1) * 128], wkv[:, 256:512], True, True, ["mla_wkv", "m_ckvn"], [pk])
            P.act(vtok[:, tt, :], ps[:, 0:256], AF.Copy, [pk], ["m_vtok"])
    P.barrier()
    P.release(m1)
    srot = Rot([p[:] for p in C.pss[0:6]], "ps")
    ps_o, pk_o = C.pss[6][:], "ps6"
    ps_d, pk_d = C.pss[7][:], "ps7"
    pT_b = Rot([P.alloc([128, 512], BF16) for _ in range(4)], "m_pT")
    rden = P.alloc([64, 512])
    ost = Rot([P.alloc([64, 512], BF16) for _ in range(2)], "m_ost")
    scale = 1.0 / math.sqrt(QK)
    blocks = [(CTX + i * 512, 512, 0, NTT) for i in range(SEQ // 512)]
    if not last:
        blocks = [(0, CTX, 0, 2)] + blocks
    for (q0, qn, kt0, kt1) in blocks:
        for h in range(MLA_H):
            for kt in range(kt0, kt1):
                ps, pk = srot.next()
                P.mm(ps[:, 0:qn], kT[h][:, kt * 128:(kt + 1) * 128], qT[h][:, q0:q0 + qn], True, True,
                     ["m_kT%d" % h, "m_qT%d" % h], [pk])
                pb, pbk = pT_b.next()
                P.act(pb[:, 0:qn], ps[:, 0:qn], AF.Exp, [pk], [pbk], scale=scale)
                P.mm(ps_o[0:64, 0:qn], vtok[:, kt, h * 64:(h + 1) * 64], pb[:, 0:qn], kt == kt0, kt == kt1 - 1, [pbk, "m_vtok"], [pk_o])
                P.mm(ps_d[0:64, 0:qn], C.ones_b[:, 0:64], pb[:, 0:qn], kt == kt0, kt == kt1 - 1, [pbk, "ones_b"], [pk_d])
            P.v("dve", "reciprocal", [pk_d], ["m_rden"], rden[:, 0:qn], ps_d[0:64, 0:qn])
            ob, obk = ost.next()
            P.v("dve", "tensor_tensor", [pk_o, "m_rden"], [obk], ob[:, 0:qn], ps_o[0:64, 0:qn], rden[:, 0:qn], ALU.mult)
            P.dma(C.oT_d.ap()[OT_MLA + h * 64:OT_MLA + (h + 1) * 64, q0:q0 + qn], ob[:, 0:qn], [obk], ["oT_d"], q="pool")
    P.barrier()
    P.release(m0)


MIXERS["mla"] = mix_mla
HOSTPREP["mla"] = host_mla


RH = 4
PT_RQ, PT_RK, PT_RG, PT_RQS, PT_RKS = 256, 512, 768, 1024, 1280
OT_RET = 512
LN2 = math.log(2.0)


def host_ret(inp):
    L = DEPTH
    o = {}
    f32 = np.float32
    theta = (f32(10000.0) ** (-np.linspace(0.0, 1.0, 32, dtype=f32))).astype(f32)
    ang = (np.arange(SEQ, dtype=f32)[:, None] * theta).astype(f32)
    cos, sin = np.cos(ang).astype(f32), np.sin(ang).astype(f32)
    Ct = np.ones((T, 64), f32)
    St = np.zeros((T, 64), f32)
    Ct[CTX:, 0:32] = cos
    Ct[CTX:, 32:64] = cos
    St[CTX:, 0:32] = -sin
    St[CTX:, 32:64] = sin
    o["ret_Ct"] = Ct
    o["ret_St"] = St
    o["ret_C"] = np.ascontiguousarray(np.tile(Ct.T, (2, 1)))
    o["ret_S"] = np.ascontiguousarray(np.tile(St.T, (2, 1)))
    j = np.arange(128, dtype=f32)[:, None]
    i = np.arange(128, dtype=f32)[None, :]
    cst = np.zeros((128, 6, 128), f32)
    cst[:, 0, :] = np.maximum(i - j, 0)
    cst[:, 1, :] = np.maximum(j - i, 0)
    cst[:, 2, :] = (i >= j)
    cst[:, 3, :] = (j >= i)
    cst[:, 4, :] = i + 1
    cst[:, 5, :] = 128 - i
    o["ret_cst"] = cst
    col = np.zeros((128, 2), f32)
    col[:, 0] = 127 - np.arange(128)
    col[:, 1] = np.arange(128)
    o["ret_col"] = col
    hm = np.zeros((128, 2), f32)
    hm[:64, 0] = 1.0
    hm[64:, 1] = 1.0
    o["ret_hm"] = hm
    o["ret_dexp"] = np.ascontiguousarray(inp["ret_decay_exp"].reshape(L, 1, 8))
    o["ret_gnT"] = np.ascontiguousarray(inp["ret_gn_g"].reshape(L, RH, 64).transpose(0, 2, 1))
    return o


def mix_ret(C, l, last):
    P, B = C.P, C.B
    psrot = C.psrot
    if "ret_C" not in B.din:
        B.inp("ret_Ct", [T, 64]); B.inp("ret_St", [T, 64])
        B.inp("ret_C", [128, T]); B.inp("ret_S", [128, T])
        B.inp("ret_cst", [128, 6, 128]); B.inp("ret_col", [128, 2]); B.inp("ret_hm", [128, 2])
        B.inp("ret_dexp", [DEPTH, 1, 8]); B.inp("ret_gnT", [DEPTH, 64, 4])
    din = B.din
    pT = C.pT_d.ap()
    m0 = P.mark()
    cst = P.alloc([128, 6, 128])
    P.dma(cst, din["ret_cst"].ap(), [], ["r_cst"])
    colc = P.alloc([128, 2])
    P.dma(colc, din["ret_col"].ap(), [], ["r_col"])
    gnT = P.alloc([64, 4])
    P.dma(gnT, din["ret_gnT"].ap()[l], [], ["r_gnT"])
    lg8 = P.alloc([128, 8])
    P.dma(lg8, din["ret_dexp"].ap()[l, 0:1, :].partition_broadcast(128)[:, 0, :], [], ["r_lg8"])
    P.act(lg8, lg8, AF.Exp, ["r_lg8"], ["r_lg8"], scale=-LN2)
    P.act(lg8, lg8, AF.Ln, ["r_lg8"], ["r_lg8"], scale=-1.0, bias=1.0)
    lgp = P.alloc([128, 2, 2])
    for d_ in range(2):
        for pr in range(2):
            P.v("dve", "tensor_copy", ["r_lg8"], ["r_lgp"], lgp[0:64, d_, pr:pr + 1], lg8[0:64, d_ * 4 + 2 * pr:d_ * 4 + 2 * pr + 1])
            P.v("dve", "tensor_copy", ["r_lg8"], ["r_lgp"], lgp[64:128, d_, pr:pr + 1], lg8[64:128, d_ * 4 + 2 * pr + 1:d_ * 4 + 2 * pr + 2])
    Mk = P.alloc([128, 4, 128])
    mt = P.alloc([128, 128])
    for h in range(RH):
        P.act(Mk[:, h, :], cst[:, 0, :], AF.Exp, ["r_cst", "r_lg8"], ["r_Mk"], scale=lg8[:, h:h + 1])
        P.v("dve", "tensor_tensor", ["r_Mk", "r_cst"], ["r_Mk"], Mk[:, h, :], Mk[:, h, :], cst[:, 2, :], ALU.mult)
        P.act(mt, cst[:, 1, :], AF.Exp, ["r_cst", "r_lg8"], ["r_mt"], scale=lg8[:, 4 + h:5 + h])
        P.v("dve", "tensor_tensor", ["r_mt", "r_cst"], ["r_mt"], mt, mt, cst[:, 3, :], ALU.mult)
        P.v("dve", "tensor_tensor", ["r_mt", "r_Mk"], ["r_Mk"], Mk[:, h, :], Mk[:, h, :], mt, ALU.add)
    XI = P.alloc([128, 2, 2, 128])
    gcolp = P.alloc([128, 2, 2])
    for d_ in range(2):
        for pr in range(2):
            P.act(XI[:, d_, pr, :], cst[:, 4 + d_, :], AF.Exp, ["r_cst", "r_lgp"], ["r_XI"], scale=lgp[:, d_, pr:pr + 1])
            P.act(gcolp[:, d_, pr:pr + 1], lgp[:, d_, pr:pr + 1], AF.Exp, ["r_lgp"], ["r_gcol"], scale=128.0)
    Z = P.alloc([128, 2, 4])
    for d_ in range(2):
        for h in range(RH):
            P.act(Z[:, d_, h:h + 1], colc[:, d_:d_ + 1], AF.Exp, ["r_col", "r_lg8"], ["r_Z"], scale=lg8[:, d_ * 4 + h:d_ * 4 + h + 1])
    hm = P.alloc([128, 2])
    P.dma(hm, din["ret_hm"].ap(), [], ["r_hm"])
    TAB = P.alloc([128, 3, 4, 128])
    for h in range(RH):
        pr, hp = h // 2, h % 2
        P.v("dve", "tensor_copy", ["r_hm"], ["r_TAB"], TAB[:, 0, h, :], hm[:, hp:hp + 1].to_broadcast([128, 128]))
        for d_ in range(2):
            P.v("dve", "tensor_scalar", ["r_XI", "r_hm"], ["r_TAB"], TAB[:, 1 + d_, h, :], XI[:, d_, pr, :], hm[:, hp:hp + 1], None, ALU.mult)
    if C.cfg.get("ret_stop", 9) <= 0:
        P.barrier(); P.release(m0); return
    qr = [P.alloc([128, T], BF16) for _ in range(2)]
    kr = [P.alloc([128, T], BF16) for _ in range(2)]
    vt = P.alloc([128, NTT, 256], BF16)
    Sall = [[P.alloc([128, NTT, 128], BF16) for _ in range(2)] for _ in range(2)]
    m1 = P.mark()
    kz = [P.alloc([128, NTT, 256], BF16) for _ in range(2)]
    m2 = P.mark()
    ld = Rot([P.alloc([128, 4, 512]) for _ in range(2)], "r_ld")
    tb = Rot([P.alloc([128, 2, 512]) for _ in range(2)], "r_tb")
    t1 = P.alloc([128, 512]); t2 = P.alloc([128, 512])
    for ch, (c0, cn) in enumerate(CH512):
        tbv, tbk = tb.next()
        P.dma(tbv[:, 0, 0:cn], din["ret_C"].ap()[:, c0:c0 + cn], [], [tbk])
        P.dma(tbv[:, 1, 0:cn], din["ret_S"].ap()[:, c0:c0 + cn], [], [tbk])
        nq = cn // 128
        for pr in range(2):
            lv, lk = ld.next()
            for ii, r0 in enumerate((PT_RQ, PT_RQS, PT_RK, PT_RKS)):
                P.dma(lv[:, ii, 0:cn], pT[r0 + pr * 128:r0 + (pr + 1) * 128, c0:c0 + cn], ["pT_d"], [lk])
            P.v("dve", "tensor_tensor", [lk, tbk], ["r_t1"], t1[:, 0:cn], lv[:, 0, 0:cn], tbv[:, 0, 0:cn], ALU.mult)
            P.v("pool", "tensor_tensor", [lk, tbk], ["r_t2"], t2[:, 0:cn], lv[:, 1, 0:cn], tbv[:, 1, 0:cn], ALU.mult)
            P.v("dve", "tensor_tensor", ["r_t1", "r_t2"], ["r_t1"], t1[:, 0:cn], t1[:, 0:cn], t2[:, 0:cn], ALU.add)
            P.act(qr[pr][:, c0:c0 + cn], t1[:, 0:cn], AF.Copy, ["r_t1"], ["r_qr%d" % pr], scale=0.125)
            P.v("dve", "tensor_tensor", [lk, tbk], ["r_t1"], t1[:, 0:cn], lv[:, 2, 0:cn], tbv[:, 0, 0:cn], ALU.mult)
            P.v("pool", "tensor_tensor", [lk, tbk], ["r_t2"], t2[:, 0:cn], lv[:, 3, 0:cn], tbv[:, 1, 0:cn], ALU.mult)
            P.v("pool", "tensor_tensor", ["r_t1", "r_t2"], ["r_kr%d" % pr], kr[pr][:, c0:c0 + cn], t1[:, 0:cn], t2[:, 0:cn], ALU.add)
    if C.cfg.get("ret_stop", 9) <= 0.5:
        P.barrier(); P.release(m0); return
    tl = Rot([P.alloc([128, 768]) for _ in range(3)], "r_tl")
    tt_tab = Rot([P.alloc([128, 2, 64]) for _ in range(3)], "r_ttab")
    kk1 = P.alloc([128, 4, 64]); kk2 = P.alloc([128, 4, 64])
    for tt in range(NTT):
        tv, tk = tl.next()
        P.dma(tv, C.rtok_d.ap()[tt * 128:(tt + 1) * 128, :], ["rtok_d"], [tk])
        tab, tabk = tt_tab.next()
        P.dma(tab[:, 0, :], din["ret_Ct"].ap()[tt * 128:(tt + 1) * 128, :], [], [tabk])
        P.dma(tab[:, 1, :], din["ret_St"].ap()[tt * 128:(tt + 1) * 128, :], [], [tabk])
        kview = tv[:, 0:256].rearrange("p (h d) -> p h d", h=4)
        ksview = tv[:, 256:512].rearrange("p (h d) -> p h d", h=4)
        P.v("dve", "tensor_tensor", [tk, tabk], ["r_kk1"], kk1, kview, tab[:, 0, :].unsqueeze(1).to_broadcast([128, 4, 64]), ALU.mult)
        P.v("pool", "tensor_tensor", [tk, tabk], ["r_kk2"], kk2, ksview, tab[:, 1, :].unsqueeze(1).to_broadcast([128, 4, 64]), ALU.mult)
        P.v("dve", "tensor_tensor", ["r_kk1", "r_kk2"], ["r_kk1"], kk1, kk1, kk2, ALU.add)
        for d_ in range(2):
            P.v("dve" if d_ == 0 else "pool", "tensor_tensor", ["r_kk1", "r_Z"], ["r_kz%d" % d_],
                kz[d_][:, tt, :].rearrange("p (h d) -> p h d", h=4), kk1,
                Z[:, d_, :].unsqueeze(2).to_broadcast([128, 4, 64]), ALU.mult)
        P.act(vt[:, tt, :], tv[:, 512:768], AF.Copy, [tk], ["r_vt"])
    if C.cfg.get("ret_stop", 9) <= 1:
        P.barrier(); P.release(m0); return
    Srun = [[[P.alloc([128, 128]) for _ in range(2)] for _ in range(2)] for _ in range(2)]
    order = [list(range(NTT)), [1, 0] + list(range(NTT - 1, 1, -1))]
    for d_ in range(2):
        for pr in range(2):
            P.v("dve", "memset", [], ["r_S%d%d" % (d_, pr)], Sall[d_][pr][:, order[d_][0], :], 0.0)
    for s in range(NTT - 1):
        for d_ in range(2):
            for pr in range(2):
                n = order[d_][s]
                nxt = order[d_][s + 1]
                ps, pk = psrot.next()
                cur = Srun[d_][pr][s % 2]
                new = Srun[d_][pr][(s + 1) % 2]
                ck = "r_Sr%d%d%d" % (d_, pr, s % 2)
                nk = "r_Sr%d%d%d" % (d_, pr, (s + 1) % 2)
                P.mm(ps[:, 0:128], kz[d_][:, n, pr * 128:(pr + 1) * 128], vt[:, n, pr * 128:(pr + 1) * 128], True, True,
                     ["r_kz%d" % d_, "r_vt"], [pk])
                if s > 0:
                    P.v("dve", "scalar_tensor_tensor", [pk, ck, "r_gcol"], [nk], new, cur, gcolp[:, d_, pr:pr + 1], ps[:, 0:128],
                        ALU.mult, ALU.add)
                else:
                    P.v("dve", "tensor_copy", [pk], [nk], new, ps[:, 0:128])
                P.act(Sall[d_][pr][:, nxt, :], new, AF.Copy, [nk], ["r_S%d%d" % (d_, pr)])
    P.barrier()
    P.release(m1)
    if C.cfg.get("ret_stop", 9) <= 2:
        P.barrier(); P.release(m0); return
    msk = Rot([P.alloc([128, 4, 128], BF16) for _ in range(2)], "r_msk")
    gld = Rot([P.alloc([64, 4, 128]) for _ in range(2)], "r_gld")
    sqo = P.alloc([64, 512])
    rstd = P.alloc([64, 512])
    on = P.alloc([64, 4, 128])
    ost = Rot([P.alloc([64, 4, 128], BF16) for _ in range(2)], "r_ost")
    qtr = Rot([P.alloc([128, 3, 2, 128], BF16) for _ in range(4)], "r_qt")
    gsrc = pT[PT_RG:PT_RG + 256, :].rearrange("(h e) t -> e h t", h=4)
    odst = C.oT_d.ap()[OT_RET:OT_RET + 256, :].rearrange("(h e) t -> e h t", h=4)
    for n in range(2 if last else 0, NTT):
        cs = slice(n * 128, (n + 1) * 128)
        gv, gk = gld.next()
        P.dma(gv, gsrc[:, :, cs], ["pT_d"], [gk])
        P.act(gv, gv, AF.Silu, [gk], [gk])
        QT = []
        for pr in range(2):
            qv, qk = qtr.next()
            P.v("dve" if pr == 0 else "pool", "tensor_tensor", ["r_qr%d" % pr, "r_TAB"], [qk], qv,
                qr[pr][:, cs].unsqueeze(1).unsqueeze(1).to_broadcast([128, 3, 2, 128]), TAB[:, :, 2 * pr:2 * pr + 2, :], ALU.mult)
            QT.append((qv, qk))
        ps_s, pk_s = psrot.next()
        for h in range(RH):
            pr, hp = h // 2, h % 2
            P.mm(ps_s[:, h * 128:(h + 1) * 128], kr[pr][:, cs], QT[pr][0][:, 0, hp, :], True, True,
                 ["r_kr%d" % pr, QT[pr][1]], [pk_s])
        mv, mk = msk.next()
        P.v("dve", "tensor_tensor", [pk_s, "r_Mk"], [mk], mv, ps_s.rearrange("p (h i) -> p h i", h=4), Mk, ALU.mult)
        ps_o, pk_o = psrot.next()
        for h in range(RH):
            pr, hp = h // 2, h % 2
            hs = slice(hp * 64, (hp + 1) * 64)
            oo = ps_o[0:64, h * 128:(h + 1) * 128]
            P.mm(oo, vt[:, n, h * 64:(h + 1) * 64], mv[:, h, :], True, False, ["r_vt", mk], [pk_o])
            P.mm(oo, Sall[0][pr][:, n, hs], QT[pr][0][:, 1, hp, :], False, False, ["r_S0%d" % pr, QT[pr][1]], [pk_o])
            P.mm(oo, Sall[1][pr][:, n, hs], QT[pr][0][:, 2, hp, :], False, True, ["r_S1%d" % pr, QT[pr][1]], [pk_o])
        P.act(sqo, ps_o[0:64, :], AF.Square, [pk_o], ["r_sqo"])
        ps_n, pk_n = psrot.next()
        P.mm(ps_n[0:64, :], C.ones_f[0:64, 0:64], sqo, True, True, ["r_sqo", "ones_f"], [pk_n])
        P.act(rstd, ps_n[0:64, :], AF.Sqrt, [pk_n, "eps_t"], ["r_rstd"], bias=C.eps_t[0:64, :], scale=1.0 / 64)
        P.v("dve", "reciprocal", ["r_rstd"], ["r_rstd"], rstd, rstd)
        P.v("dve", "tensor_tensor", [pk_o, "r_rstd"], ["r_on"], on, ps_o[0:64, :].rearrange("p (h i) -> p h i", h=4),
            rstd.rearrange("p (h i) -> p h i", h=4), ALU.mult)
        P.v("pool", "tensor_tensor", ["r_on", "r_gnT"], ["r_on"], on, on, gnT.unsqueeze(2).to_broadcast([64, 4, 128]), ALU.mult)
        ov, ok = ost.next()
        P.v("pool", "tensor_tensor", ["r_on", gk], [ok], ov, on, gv, ALU.mult)
        P.dma(odst[:, :, cs], ov, [ok], ["oT_d"], q="pool")
    P.barrier()
    P.release(m0)


MIXERS["ret"] = mix_ret
HOSTPREP["ret"] = host_ret


TWO_PI = 2.0 * math.pi
S5_SEGS = [(0, 256)] + [(256 + 1024 * k, 1024) for k in range(4)]


def host_s5(inp):
    L = DEPTH
    f32 = np.float32
    o = {}
    lam_re = inp["s5_lam_re"]; lam_im = inp["s5_lam_im"]
    ldt = np.repeat(inp["s5_log_dt"][..., None], 64, axis=-1)
    prow = np.stack([lam_re, lam_im, ldt], axis=1)
    o["s5_prow"] = np.ascontiguousarray(prow.reshape(L, 3, 1, 2048)).astype(f32)
    pT_ = prow.reshape(L, 3, 2, 8, 2, 64).transpose(0, 1, 4, 5, 2, 3)
    o["s5_pT"] = np.ascontiguousarray(pT_.reshape(L, 3, 128, 16)).astype(f32)
    Bexp = np.zeros((L, 2, 2, 128, 8, 128), f32)
    Cexp = np.zeros((L, 2, 2, 128, 8, 128), f32)
    for ri, (bk, ck) in enumerate((("s5_b_re", "s5_c_re"), ("s5_b_im", "s5_c_im"))):
        b = inp[bk]
        c = inp[ck]
        for m in range(8):
            for a in range(2):
                g = 2 * m + a
                q0 = 32 * (m % 4) + 16 * a
                Bexp[:, ri, :, q0:q0 + 16, m, 64 * a:64 * a + 64] = b[:, :, g].transpose(0, 1, 3, 2)
                Cexp[:, ri, :, 64 * a:64 * a + 64, m, q0:q0 + 16] = c[:, :, g].transpose(0, 1, 3, 2)
    o["s5_Bexp"] = Bexp
    o["s5_Cexp"] = Cexp
    o["s5_dT"] = np.ascontiguousarray(inp["s5_d"].reshape(L, 2, 128).transpose(0, 2, 1))
    o["s5_gbT"] = np.ascontiguousarray(inp["s5_glu_b"].reshape(L, 2, 128).transpose(0, 2, 1))
    o["s5_gw"] = np.ascontiguousarray(inp["s5_glu_w"])
    o["s5_iota"] = np.ascontiguousarray(np.broadcast_to(np.arange(1024, dtype=f32), (128, 1024)))
    return o


def rev_ap(ap):
    (ps, pn), (fs, fn) = ap.ap
    return bass.AP(ap.tensor, ap.offset + (fn - 1) * fs, [[ps, pn], [-fs, fn]])


def frac_round(P, eng_c, x, ki, kf, keys):
    xk, kik, kfk = keys
    P.v(eng_c, "tensor_copy", [xk], [kik], ki, x)
    P.v(eng_c, "tensor_copy", [kik], [kfk], kf, ki)
    P.v("dve", "tensor_tensor", [xk, kfk], [xk], x, x, kf, ALU.subtract)


def mix_s5(C, l, last):
    P, B = C.P, C.B
    psrot = C.psrot
    if "s5_prow" not in B.din:
        B.inp("s5_prow", [DEPTH, 3, 1, 2048]); B.inp("s5_pT", [DEPTH, 3, 128, 16])
        B.inp("s5_Bexp", [DEPTH, 2, 2, 128, 8, 128]); B.inp("s5_Cexp", [DEPTH, 2, 2, 128, 8, 128])
        B.inp("s5_dT", [DEPTH, 128, 2]); B.inp("s5_gbT", [DEPTH, 128, 2]); B.inp("s5_gw", [DEPTH, 256, 256])
        B.inp("s5_iota", [128, 1024])
    din = B.din
    pT = C.pT_d.ap()
    m0 = P.mark()
    NW = 2048
    LB = [[P.alloc([128, 2, 8, 128], BF16) for _ in range(2)]]
    LB = LB[0]
    LC = [P.alloc([128, 2, 8, 128], BF16) for _ in range(2)]
    rP = P.alloc([128, 16])
    thP = P.alloc([128, 16])
    m1 = P.mark()
    pr_ = [P.alloc([128, NW]) for _ in range(3)]
    for i in range(3):
        P.dma(pr_[i], din["s5_prow"].ap()[l, i, 0:1, :].partition_broadcast(128)[:, 0, :], [], ["s5_pr%d" % i])
    lre, lim, dt = pr_
    P.act(dt, dt, AF.Exp, ["s5_pr2"], ["s5_pr2"])
    re = P.alloc([128, NW]); im = P.alloc([128, NW])
    P.v("dve", "tensor_tensor", ["s5_pr0", "s5_pr2"], ["s5_re"], re, lre, dt, ALU.mult)
    P.v("dve", "tensor_tensor", ["s5_pr1", "s5_pr2"], ["s5_im"], im, lim, dt, ALU.mult)
    r_ = P.alloc([128, NW])
    P.act(r_, re, AF.Exp, ["s5_re"], ["s5_r"])
    ki = P.alloc([128, NW], I32); kf = P.alloc([128, NW])
    ph = P.alloc([128, NW]); ph2 = P.alloc([128, NW])
    P.v("dve", "tensor_scalar", ["s5_im"], ["s5_ph"], ph, im, 1.0 / TWO_PI, None, ALU.mult)
    P.v("dve", "tensor_scalar", ["s5_ph"], ["s5_ph2"], ph2, ph, 0.25, None, ALU.add)
    frac_round(P, "dve", ph, ki, kf, ("s5_ph", "s5_ki", "s5_kf"))
    frac_round(P, "dve", ph2, ki, kf, ("s5_ph2", "s5_ki", "s5_kf"))
    sn = ph; cs_ = ph2
    P.act(sn, ph, AF.Sin, ["s5_ph"], ["s5_ph"], scale=TWO_PI)
    P.act(cs_, ph2, AF.Sin, ["s5_ph2"], ["s5_ph2"], scale=TWO_PI)
    nre = re; nim = im
    P.v("dve", "tensor_tensor", ["s5_r", "s5_ph2"], ["s5_re"], nre, r_, cs_, ALU.mult)
    P.v("dve", "tensor_scalar", ["s5_re"], ["s5_re"], nre, nre, -1.0, None, ALU.add)
    P.v("dve", "tensor_tensor", ["s5_r", "s5_ph"], ["s5_im"], nim, r_, sn, ALU.mult)
    den = r_; tmp = kf
    P.v("dve", "tensor_tensor", ["s5_pr0", "s5_re", "s5_im"], ["s5_r"], den, lre, lre, ALU.mult)
    P.v("dve", "tensor_tensor", ["s5_pr1", "s5_kf"], ["s5_kf"], tmp, lim, lim, ALU.mult)
    P.v("dve", "tensor_tensor", ["s5_r", "s5_kf"], ["s5_r"], den, den, tmp, ALU.add)
    P.v("dve", "reciprocal", ["s5_r"], ["s5_r"], den, den)
    cre = ph; cim = ph2
    P.v("dve", "tensor_tensor", ["s5_re", "s5_pr0", "s5_ph"], ["s5_ph"], cre, nre, lre, ALU.mult)
    P.v("dve", "tensor_tensor", ["s5_im", "s5_pr1"], ["s5_kf"], tmp, nim, lim, ALU.mult)
    P.v("dve", "tensor_tensor", ["s5_ph", "s5_kf"], ["s5_ph"], cre, cre, tmp, ALU.add)
    P.v("dve", "tensor_tensor", ["s5_ph", "s5_r"], ["s5_ph"], cre, cre, den, ALU.mult)
    P.v("dve", "tensor_tensor", ["s5_im", "s5_pr0", "s5_ph2"], ["s5_ph2"], cim, nim, lre, ALU.mult)
    P.v("dve", "tensor_tensor", ["s5_re", "s5_pr1"], ["s5_kf"], tmp, nre, lim, ALU.mult)
    P.v("dve", "tensor_tensor", ["s5_ph2", "s5_kf"], ["s5_ph2"], cim, cim, tmp, ALU.subtract)
    P.v("dve", "tensor_tensor", ["s5_ph2", "s5_r"], ["s5_ph2"], cim, cim, den, ALU.mult)
    bre = pr_[0]; bim = pr_[1]; t_a = pr_[2]; t_b = re
    P.dma(bre.rearrange("p (d x) -> p d x", d=2), din["s5_Bexp"].ap()[l, 0].rearrange("d q m s -> q d (m s)"), ["s5_ph", "s5_ph2"], ["s5_pr0"], q="pool")
    P.dma(bim.rearrange("p (d x) -> p d x", d=2), din["s5_Bexp"].ap()[l, 1].rearrange("d q m s -> q d (m s)"), ["s5_ph", "s5_ph2"], ["s5_pr1"], q="pool")
    P.v("dve", "tensor_tensor", ["s5_ph", "s5_pr0"], ["s5_pr2"], t_a, cre, bre, ALU.mult)
    P.v("dve", "tensor_tensor", ["s5_ph2", "s5_pr1"], ["s5_re"], t_b, cim, bim, ALU.mult)
    P.v("dve", "tensor_tensor", ["s5_pr2", "s5_re"], ["s5_LB"], LB[0].rearrange("p d m s -> p (d m s)"), t_a, t_b, ALU.subtract)
    P.v("dve", "tensor_tensor", ["s5_ph", "s5_pr1"], ["s5_pr2"], t_a, cre, bim, ALU.mult)
    P.v("dve", "tensor_tensor", ["s5_ph2", "s5_pr0"], ["s5_re"], t_b, cim, bre, ALU.mult)
    P.v("dve", "tensor_tensor", ["s5_pr2", "s5_re"], ["s5_LB"], LB[1].rearrange("p d m s -> p (d m s)"), t_a, t_b, ALU.add)
    cst_ = im
    P.dma(cst_.rearrange("p (d x) -> p d x", d=2), din["s5_Cexp"].ap()[l, 0].rearrange("d q m s -> q d (m s)"), ["s5_im"], ["s5_im"], q="pool")
    P.act(LC[0].rearrange("p d m s -> p (d m s)"), cst_, AF.Copy, ["s5_im"], ["s5_LC"])
    cst2 = kf
    P.dma(cst2.rearrange("p (d x) -> p d x", d=2), din["s5_Cexp"].ap()[l, 1].rearrange("d q m s -> q d (m s)"), ["s5_kf"], ["s5_kf"], q="pool")
    P.act(LC[1].rearrange("p d m s -> p (d m s)"), cst2, AF.Copy, ["s5_kf"], ["s5_LC"], scale=-1.0)
    pP = P.alloc([128, 3, 16])
    P.dma(pP, din["s5_pT"].ap()[l].rearrange("i q n -> q i n"), [], ["s5_pP"])
    P.act(pP[:, 2, :], pP[:, 2, :], AF.Exp, ["s5_pP"], ["s5_pP"])
    P.v("dve", "tensor_tensor", ["s5_pP"], ["s5_rP"], rP, pP[:, 0, :], pP[:, 2, :], ALU.mult)
    P.act(rP, rP, AF.Exp, ["s5_rP"], ["s5_rP"])
    P.v("dve", "tensor_tensor", ["s5_pP"], ["s5_thP"], thP, pP[:, 1, :], pP[:, 2, :], ALU.mult)
    P.v("dve", "tensor_scalar", ["s5_thP"], ["s5_thP"], thP, thP, 1.0 / TWO_PI, None, ALU.mult)
    kiP = P.alloc([128, 16], I32); kfP = P.alloc([128, 16])
    frac_round(P, "dve", thP, kiP, kfP, ("s5_thP", "s5_kiP", "s5_kfP"))
    P.barrier()
    P.release(m1)
    ubf = P.alloc([128, 2, T], BF16)
    yacc = P.alloc([128, 2, T])
    iota = P.alloc([128, 1024])
    P.dma(iota, din["s5_iota"].ap(), [], ["s5_iota"])
    ust = Rot([P.alloc([128, 512]) for _ in range(2)], "s5_ust")
    for ut in range(2):
        for (c0, cn) in CH512:
            sv, sk = ust.next()
            P.dma(sv[:, 0:cn], pT[ut * 128:(ut + 1) * 128, c0:c0 + cn], ["pT_d"], [sk])
            P.act(ubf[:, ut, c0:c0 + cn], sv[:, 0:cn], AF.Copy, [sk], ["s5_ubf"])
    SEG = 1024
    NBUF = 2
    m_rot = P.mark()
    COS = Rot([P.alloc([128, SEG]) for _ in range(NBUF)], "s5_cos")
    SIN = Rot([P.alloc([128, SEG]) for _ in range(NBUF)], "s5_sin")
    PH = Rot([P.alloc([128, SEG]) for _ in range(NBUF)], "s5_phs")
    KI = Rot([P.alloc([128, SEG], I32) for _ in range(NBUF)], "s5_kis")
    KF = Rot([P.alloc([128, SEG]) for _ in range(NBUF)], "s5_kfs")
    BR = Rot([P.alloc([128, SEG]) for _ in range(NBUF)], "s5_br")
    BI = Rot([P.alloc([128, SEG]) for _ in range(NBUF)], "s5_bi")
    GR = Rot([P.alloc([128, SEG]) for _ in range(NBUF)], "s5_gr")
    GI = Rot([P.alloc([128, SEG]) for _ in range(NBUF)], "s5_gi")
    T1 = Rot([P.alloc([128, SEG]) for _ in range(NBUF)], "s5_t1")
    T2 = Rot([P.alloc([128, SEG]) for _ in range(NBUF)], "s5_t2")
    HR = Rot([P.alloc([128, SEG], BF16) for _ in range(NBUF)], "s5_hr")
    HI = Rot([P.alloc([128, SEG], BF16) for _ in range(NBUF)], "s5_hi")
    offs = P.alloc([128, 4])
    offi = P.alloc([128, 4], I32)
    glast = P.alloc([128, 2])
    first_acc = [True, True]
    for d_ in range(2):
        seg_order = list(range(5)) if d_ == 0 else [0, 4, 3, 2, 1]
        for m in range(8):
            ut = m // 4
            col = d_ * 8 + m
            sbase = 0
            for si, sg in enumerate(seg_order):
                t0, n = S5_SEGS[sg]
                P.v("dve", "tensor_scalar", ["s5_thP"], ["s5_offs"], offs[:, 0:1], thP[:, col:col + 1], float(sbase), None, ALU.mult)
                P.v("dve", "tensor_copy", ["s5_offs"], ["s5_offi"], offi[:, 0:1], offs[:, 0:1])
                P.v("dve", "tensor_copy", ["s5_offi"], ["s5_offs"], offs[:, 1:2], offi[:, 0:1])
                P.v("dve", "tensor_tensor", ["s5_offs"], ["s5_offs"], offs[:, 2:3], offs[:, 0:1], offs[:, 1:2], ALU.subtract)
                P.v("dve", "tensor_scalar", ["s5_offs"], ["s5_offs"], offs[:, 3:4], offs[:, 2:3], 0.25, None, ALU.add)
                phv, phk = PH.next(); kiv, kik = KI.next(); kfv, kfk = KF.next()
                cosv, cosk = COS.next(); sinv, sink = SIN.next()
                P.act(phv[:, 0:n], iota[:, 0:n], AF.Identity, ["s5_iota", "s5_thP", "s5_offs"], [phk],
                      scale=thP[:, col:col + 1], bias=offs[:, 2:3])
                frac_round(P, "dve", phv[:, 0:n], kiv[:, 0:n], kfv[:, 0:n], (phk, kik, kfk))
                P.act(sinv[:, 0:n], phv[:, 0:n], AF.Sin, [phk], [sink], scale=TWO_PI)
                P.act(phv[:, 0:n], iota[:, 0:n], AF.Identity, ["s5_iota", "s5_thP", "s5_offs"], [phk],
                      scale=thP[:, col:col + 1], bias=offs[:, 3:4])
                frac_round(P, "dve", phv[:, 0:n], kiv[:, 0:n], kfv[:, 0:n], (phk, kik, kfk))
                P.act(cosv[:, 0:n], phv[:, 0:n], AF.Sin, [phk], [cosk], scale=TWO_PI)
                brv, brk = BR.next(); biv, bik = BI.next()
                t1v, t1k = T1.next(); t2v, t2k = T2.next()
                for c0 in range(0, n, 512):
                    cn = min(512, n - c0)
                    psr, pkr = psrot.next()
                    psi, pki = psrot.next()
                    P.mm(psr[:, 0:cn], LB[0][:, d_, m, :], ubf[:, ut, t0 + c0:t0 + c0 + cn], True, True, ["s5_LB", "s5_ubf"], [pkr])
                    P.mm(psi[:, 0:cn], LB[1][:, d_, m, :], ubf[:, ut, t0 + c0:t0 + c0 + cn], True, True, ["s5_LB", "s5_ubf"], [pki])
                    if d_ == 0:
                        j0 = c0
                        sr, si_ = psr[:, 0:cn], psi[:, 0:cn]
                    else:
                        j0 = n - c0 - cn
                        sr, si_ = rev_ap(psr[:, 0:cn]), rev_ap(psi[:, 0:cn])
                    js = slice(j0, j0 + cn)
                    P.v("dve", "tensor_tensor", [pkr, cosk], [brk], brv[:, js], sr, cosv[:, js], ALU.mult)
                    P.v("dve", "tensor_tensor", [pki, sink], [t1k], t1v[:, js], si_, sinv[:, js], ALU.mult)
                    P.v("pool", "tensor_tensor", [brk, t1k], [brk], brv[:, js], brv[:, js], t1v[:, js], ALU.add)
                    P.v("dve", "tensor_tensor", [pki, cosk], [bik], biv[:, js], si_, cosv[:, js], ALU.mult)
                    P.v("dve", "tensor_tensor", [pkr, sink], [t2k], t2v[:, js], sr, sinv[:, js], ALU.mult)
                    P.v("pool", "tensor_tensor", [bik, t2k], [bik], biv[:, js], biv[:, js], t2v[:, js], ALU.subtract)
                grv, grk = GR.next(); giv, gik = GI.next()
                ini_r = 0.0 if si == 0 else glast[:, 0:1]
                ini_i = 0.0 if si == 0 else glast[:, 1:2]
                rb = rP[:, col:col + 1].to_broadcast([128, n])
                P.v("dve", "tensor_tensor_scan", [brk, "s5_rP", "s5_glast"], [grk], grv[:, 0:n], rb, brv[:, 0:n], ini_r, ALU.mult, ALU.add)
                P.v("dve", "tensor_tensor_scan", [bik, "s5_rP", "s5_glast"], [gik], giv[:, 0:n], rb, biv[:, 0:n], ini_i, ALU.mult, ALU.add)
                P.v("dve", "tensor_copy", [grk], ["s5_glast"], glast[:, 0:1], grv[:, n - 1:n])
                P.v("dve", "tensor_copy", [gik], ["s5_glast"], glast[:, 1:2], giv[:, n - 1:n])
                hrv, hrk = HR.next(); hiv, hik = HI.next()
                t1v, t1k = T1.next(); t2v, t2k = T2.next()
                ho_r = hrv[:, 0:n] if d_ == 0 else rev_ap(hrv[:, 0:n])
                ho_i = hiv[:, 0:n] if d_ == 0 else rev_ap(hiv[:, 0:n])
                P.v("pool", "tensor_tensor", [grk, cosk], [t1k], t1v[:, 0:n], grv[:, 0:n], cosv[:, 0:n], ALU.mult)
                P.v("dve", "tensor_tensor", [gik, sink], [t2k], t2v[:, 0:n], giv[:, 0:n], sinv[:, 0:n], ALU.mult)
                P.v("pool", "tensor_tensor", [t1k, t2k], [hrk], ho_r, t1v[:, 0:n], t2v[:, 0:n], ALU.subtract)
                t1v, t1k = T1.next(); t2v, t2k = T2.next()
                P.v("pool", "tensor_tensor", [grk, sink], [t1k], t1v[:, 0:n], grv[:, 0:n], sinv[:, 0:n], ALU.mult)
                P.v("dve", "tensor_tensor", [gik, cosk], [t2k], t2v[:, 0:n], giv[:, 0:n], cosv[:, 0:n], ALU.mult)
                P.v("pool", "tensor_tensor", [t1k, t2k], [hik], ho_i, t1v[:, 0:n], t2v[:, 0:n], ALU.add)
                for c0 in range(0, n, 512):
                    cn = min(512, n - c0)
                    ps, pk = psrot.next()
                    P.mm(ps[:, 0:cn], LC[0][:, d_, m, :], hrv[:, c0:c0 + cn], True, False, ["s5_LC", hrk], [pk])
                    P.mm(ps[:, 0:cn], LC[1][:, d_, m, :], hiv[:, c0:c0 + cn], False, True, ["s5_LC", hik], [pk])
                    ya = yacc[:, ut, t0 + c0:t0 + c0 + cn]
                    if d_ == 0 and m % 4 == 0:
                        P.act(ya, ps[:, 0:cn], AF.Copy, [pk], ["s5_yacc%d" % ut])
                    else:
                        P.v("dve", "tensor_tensor", [pk, "s5_yacc%d" % ut], ["s5_yacc%d" % ut], ya, ps[:, 0:cn], ya, ALU.add)
                sbase += n
    P.barrier()
    P.release(m_rot)
    m2 = P.mark()
    gw_st = P.alloc([128, 2, 256]); gw = P.alloc([128, 2, 256], BF16)
    P.dma(gw_st, din["s5_gw"].ap()[l].rearrange("(k p) n -> p k n", p=128), [], ["s5_gwst"])
    P.v("pool", "tensor_copy", ["s5_gwst"], ["s5_gw"], gw, gw_st)
    dT = P.alloc([128, 2]); gbT = P.alloc([128, 2])
    P.dma(dT, din["s5_dT"].ap()[l], [], ["s5_dT"])
    P.dma(gbT, din["s5_gbT"].ap()[l], [], ["s5_gbT"])
    yg = Rot([P.alloc([128, 2, 512]) for _ in range(2)], "s5_yg")
    ygb = Rot([P.alloc([128, 2, 512], BF16) for _ in range(2)], "s5_ygb")
    sg_ = Rot([P.alloc([128, 512]) for _ in range(2)], "s5_sg")
    ob = Rot([P.alloc([128, 512], BF16) for _ in range(2)], "s5_ob")
    for (c0, cn) in CH512:
        if last and c0 + cn <= CTX:
            continue
        ygv, ygk = yg.next(); ybv, ybk = ygb.next()
        for ut in range(2):
            sv, sk = ust.next()
            P.dma(sv[:, 0:cn], pT[ut * 128:(ut + 1) * 128, c0:c0 + cn], ["pT_d"], [sk])
            P.v("dve", "scalar_tensor_tensor", [sk, "s5_dT", "s5_yacc%d" % ut], [ygk], ygv[:, ut, 0:cn], sv[:, 0:cn], dT[:, ut:ut + 1],
                yacc[:, ut, c0:c0 + cn], ALU.mult, ALU.add)
        P.act(ygv[:, :, 0:cn], ygv[:, :, 0:cn], AF.Gelu_apprx_tanh, [ygk], [ygk])
        P.v("pool", "tensor_copy", [ygk], [ybk], ybv[:, :, 0:cn], ygv[:, :, 0:cn])
        for uo in range(2):
            ps, pk = psrot.next()
            for k in range(2):
                P.mm(ps[:, 0:cn], gw[:, k, uo * 128:(uo + 1) * 128], ybv[:, k, 0:cn], k == 0, k == 1, ["s5_gw", ybk], [pk])
            sgv, sgk = sg_.next()
            P.act(sgv[:, 0:cn], ps[:, 0:cn], AF.Sigmoid, [pk, "s5_gbT"], [sgk], bias=gbT[:, uo:uo + 1])
            ov, ok = ob.next()
            P.v("dve", "tensor_tensor", [sgk, ygk], [ok], ov[:, 0:cn], sgv[:, 0:cn], ygv[:, uo, 0:cn], ALU.mult)
            P.dma(C.oT_d.ap()[uo * 128:(uo + 1) * 128, c0:c0 + cn], ov[:, 0:cn], [ok], ["oT_d"], q="pool")
    P.barrier()
    P.release(m0)


MIXERS["s5"] = mix_s5
HOSTPREP["s5"] = host_s5


TWO_PI = 2.0 * math.pi
NF = 8192
OT_HY = 256
K1B = [(i * 8, 8) for i in range(8)] + [(64, 1)]


def _hy_feat(pos, nt):
    f32 = np.float32
    pos = pos.astype(f32)
    t01 = pos / f32(nt - 1)
    bands = np.linspace(1e-4, 15, 16, dtype=f32)
    ang = (f32(2.0 * math.pi / nt) * pos[:, None] * bands[None, :]).astype(f32)
    z = np.concatenate([t01[:, None], np.cos(ang), -np.sin(ang)], axis=-1).astype(f32)
    return z, t01


def host_hy(inp):
    L = DEPTH
    f32 = np.float32
    bf = ml_dtypes.bfloat16
    o = {}
    n = np.arange(NF)
    zT = np.zeros((2, 33, NF), f32)
    nt01 = np.zeros((2, 128, 64), f32)
    msk = np.zeros((2, 128, 64), f32)
    for kind, nt in enumerate((SEQ, CTX)):
        pos = np.zeros(NF, np.int64)
        m = np.zeros(NF, f32)
        pos[:nt] = n[:nt]; m[:nt] = 1
        hi = n > NF - nt
        pos[hi] = NF - n[hi]; m[hi] = 1
        pos[NF // 2] = 0; m[NF // 2] = 1
        z, t01 = _hy_feat(pos, nt)
        zT[kind] = z.T
        nt01[kind] = (-t01).reshape(128, 64)
        msk[kind] = m.reshape(128, 64)
    o["hy_zT"] = zT
    o["hy_nt01"] = nt01
    o["hy_msk"] = msk
    n1 = np.arange(128)[:, None]; k1 = np.arange(65)[None, :]
    ang = 2 * np.pi * n1 * k1 / 128
    o["hy_D1"] = np.concatenate([np.cos(ang), -np.sin(ang)], axis=1).astype(bf)
    n2 = np.arange(64); k2 = np.arange(64)
    Tm = np.zeros((65, 128, 3, 128), np.float64)
    for kk in range(65):
        w = np.exp(-2j * np.pi * (n2[:, None] * kk / NF + n2[:, None] * k2[None, :] / 64))
        for c2 in range(2):
            Tm[kk, c2::2, 0, c2::2] = w.real
            Tm[kk, c2::2, 1, c2::2] = w.imag
            Tm[kk, c2::2, 2, c2::2] = -w.imag
    o["hy_T"] = Tm.astype(bf)
    t2 = np.arange(64)
    w = np.exp(2j * np.pi * k2[:, None] * t2[None, :] / 64)
    RA = np.zeros((128, 2, 64, 2)); RB = np.zeros((128, 2, 64, 2))
    for c2 in range(2):
        RA[c2::2, 0, :, c2] = w.real; RA[c2::2, 1, :, c2] = w.imag
        RB[c2::2, 0, :, c2] = -w.imag; RB[c2::2, 1, :, c2] = w.real
    o["hy_R"] = np.stack([RA.reshape(128, 256), RB.reshape(128, 256)], axis=1).astype(bf)
    k1v = np.arange(65); t1 = np.arange(64)
    wgt = np.full(65, 2.0); wgt[0] = 1; wgt[64] = 1
    V = np.zeros((65, 64, 2, 64))
    for tt in range(64):
        w = (wgt[:, None] / NF) * np.exp(2j * np.pi * (tt * k1v[:, None] / NF + t1[None, :] * k1v[:, None] / 128))
        V[:, tt, 0, :] = w.real; V[:, tt, 1, :] = -w.imag
    o["hy_V"] = V.astype(bf)
    o["hy_w1"] = np.ascontiguousarray(inp["hy_w1"])
    o["hy_w2"] = np.ascontiguousarray(inp["hy_w2"])
    o["hy_cols"] = np.ascontiguousarray(np.stack([inp["hy_b1"], inp["hy_b2"], inp["hy_freq"]], axis=-1))
    w3 = inp["hy_w3"].reshape(L, 64, 2, 2, 256)
    o["hy_w3"] = np.ascontiguousarray(w3.transpose(0, 1, 3, 2, 4))
    dl = inp["hy_deltas"].reshape(L, 2, 2, 256)
    o["hy_dl"] = np.ascontiguousarray(dl.transpose(0, 2, 1, 3).reshape(L, 2, 1, 512))
    cw = np.concatenate([inp["hy_conv_w"], inp["hy_conv_b"][:, None, :]], axis=1)
    o["hy_cw"] = np.ascontiguousarray(cw.reshape(L, 1, 4 * 768))
    o["hy_bias"] = np.ascontiguousarray(inp["hy_bias"].reshape(L, 1, 512))
    return o


def mix_hy(C, l, last):
    P, B = C.P, C.B
    psrot = C.psrot
    if "hy_zT" not in B.din:
        B.inp("hy_zT", [2, 33, NF]); B.inp("hy_nt01", [2, 128, 64]); B.inp("hy_msk", [2, 128, 64])
        B.inp("hy_D1", [128, 130], BF16); B.inp("hy_T", [65, 128, 3, 128], BF16)
        B.inp("hy_R", [128, 2, 256], BF16); B.inp("hy_V", [65, 64, 2, 64], BF16)
        B.inp("hy_w1", [DEPTH, 33, 64]); B.inp("hy_w2", [DEPTH, 64, 64]); B.inp("hy_cols", [DEPTH, 64, 3])
        B.inp("hy_w3", [DEPTH, 64, 2, 2, 256]); B.inp("hy_dl", [DEPTH, 2, 1, 512])
        B.inp("hy_cw", [DEPTH, 1, 3072]); B.inp("hy_bias", [DEPTH, 1, 512])
        B.scr("hspec_d", [2, 2, 2, 2, 128, 65 * 64])
    din = B.din
    hspec = B.dscr["hspec_d"]
    kinds = [0] if last else [0, 1]
    m0 = P.mark()
    D1 = P.alloc([128, 130], BF16)
    P.dma(D1, din["hy_D1"].ap(), [], ["hy_D1"])
    Rm = P.alloc([128, 2, 256], BF16)
    P.dma(Rm, din["hy_R"].ap(), [], ["hy_R"])
    Vm = P.alloc([65, 64, 2, 64], BF16)
    P.dma(Vm, din["hy_V"].ap(), [], ["hy_V"])
    Trot = Rot([P.alloc([128, 3, 128], BF16) for _ in range(4)], "hy_T")
    B1 = P.alloc([128, 64, 130], BF16)
    mfwd = P.mark()

    def fwd(xv, Kk, xkey, consume):
        for q0 in range(0, 64, 3):
            qn = min(3, 64 - q0)
            ps, pk = psrot.next()
            for j in range(qn):
                q = q0 + j
                P.mm(ps[:, j * 130:(j + 1) * 130], xv[:, q, :, :].rearrange("p n c -> p (n c)"), D1[0:Kk, :], True, True, [xkey, "hy_D1"], [pk])
            src = ps[:, 0:qn * 130].rearrange("p (a b) -> p a b", a=qn)
            if (q0 // 3) % 2 == 0:
                P.act(B1[:, q0:q0 + qn, :], src, AF.Copy, [pk], ["hy_B1"])
            else:
                P.v("dve", "tensor_copy", [pk], ["hy_B1"], B1[:, q0:q0 + qn, :], src)
        for bi, (k0, kn) in enumerate(K1B):
            psr, pkr = psrot.next()
            psi, pki = psrot.next()
            for j in range(kn):
                k1 = k0 + j
                tv, tk = Trot.next()
                P.dma(tv, din["hy_T"].ap()[k1], [], [tk])
                br = B1[:, :, k1]
                bi_ = B1[:, :, 65 + k1]
                P.mm(psr[:, j * 64:(j + 1) * 64], tv[:, 0, :], br, True, False, [tk, "hy_B1"], [pkr])
                P.mm(psr[:, j * 64:(j + 1) * 64], tv[:, 2, :], bi_, False, True, [tk, "hy_B1"], [pkr])
                P.mm(psi[:, j * 64:(j + 1) * 64], tv[:, 1, :], br, True, False, [tk, "hy_B1"], [pki])
                P.mm(psi[:, j * 64:(j + 1) * 64], tv[:, 0, :], bi_, False, True, [tk, "hy_B1"], [pki])
            consume(bi, k0, kn, psr, pkr, psi, pki)

    w1 = P.alloc([33, 64]); w2 = P.alloc([64, 64]); cols = P.alloc([64, 3])
    P.dma(w1, din["hy_w1"].ap()[l], [], ["hy_w1"])
    P.dma(w2, din["hy_w2"].ap()[l], [], ["hy_w2"])
    P.dma(cols, din["hy_cols"].ap()[l], [], ["hy_cols"])
    sc = P.alloc([64, 4])
    P.v("dve", "tensor_scalar", ["hy_cols"], ["hy_sc"], sc[:, 0:1], cols[:, 2:3], 1.0 / TWO_PI, None, ALU.mult)
    P.v("dve", "tensor_tensor", ["hy_cols", "hy_sc"], ["hy_sc"], sc[:, 1:2], cols[:, 0:1], sc[:, 0:1], ALU.mult)
    P.v("dve", "tensor_tensor", ["hy_cols", "hy_sc"], ["hy_sc"], sc[:, 2:3], cols[:, 1:2], sc[:, 0:1], ALU.mult)
    w3 = P.alloc([64, 2, 2, 256])
    P.dma(w3, din["hy_w3"].ap()[l], [], ["hy_w3"])
    absd = P.alloc([128, 2, 256])
    for d_ in range(2):
        P.dma(absd[d_ * 64:(d_ + 1) * 64].rearrange("p o c -> p (o c)"),
              din["hy_dl"].ap()[l, d_, 0:1, :].partition_broadcast(64)[:, 0, :], [], ["hy_absd"])
    P.act(absd, absd, AF.Abs, ["hy_absd"], ["hy_absd"])
    mflt = P.mark()
    hid2 = P.alloc([64, NF])
    zrot = Rot([P.alloc([33, 512]) for _ in range(2)], "hy_z")
    phb = P.alloc([64, 512]); kib = P.alloc([64, 512], I32); kfb = P.alloc([64, 512]); h1b = P.alloc([64, 512])
    nt01 = P.alloc([128, 64]); msk = P.alloc([128, 64])
    krot = Rot([P.alloc([128, 256]) for _ in range(3)], "hy_kr")
    xk = P.alloc([128, 128, 64, 2], BF16)
    dec = Rot([P.alloc([128, 256]) for _ in range(2)], "hy_dec")
    sqr = Rot([P.alloc([128, 512]) for _ in range(2)], "hy_sq")
    ssa = P.alloc([128, 256]); rs = P.alloc([128, 256])
    yst = Rot([P.alloc([128, 2, 512]) for _ in range(2)], "hy_yst")

    def sin_layer(ps, pk, n, bcol, outv, okey):
        P.act(phb[:, 0:n], ps[0:64, 0:n], AF.Identity, [pk, "hy_sc"], ["hy_ph"], scale=sc[:, 0:1], bias=sc[:, bcol:bcol + 1])
        P.v("dve", "tensor_copy", ["hy_ph"], ["hy_ki"], kib[:, 0:n], phb[:, 0:n])
        P.v("dve", "tensor_copy", ["hy_ki"], ["hy_kf"], kfb[:, 0:n], kib[:, 0:n])
        P.v("dve", "tensor_tensor", ["hy_ph", "hy_kf"], ["hy_ph"], phb[:, 0:n], phb[:, 0:n], kfb[:, 0:n], ALU.subtract)
        P.act(outv, phb[:, 0:n], AF.Sin, ["hy_ph"], [okey], scale=TWO_PI)

    for kind in kinds:
        P.dma(nt01, din["hy_nt01"].ap()[kind], ["hy_nt01r"], ["hy_nt01"])
        P.dma(msk, din["hy_msk"].ap()[kind], ["hy_mskr"], ["hy_msk"])
        for ch in range(NF // 512):
            zv, zk = zrot.next()
            P.dma(zv, din["hy_zT"].ap()[kind, :, ch * 512:(ch + 1) * 512], [], [zk])
            ps, pk = psrot.next()
            P.mm(ps[0:64, :], w1, zv, True, True, ["hy_w1", zk], [pk])
            sin_layer(ps, pk, 512, 1, h1b, "hy_h1")
            ps2, pk2 = psrot.next()
            P.mm(ps2[0:64, :], w2, h1b, True, True, ["hy_w2", "hy_h1"], [pk2])
            sin_layer(ps2, pk2, 512, 2, hid2[:, ch * 512:(ch + 1) * 512], "hy_hid2")
        for o_ in range(2):
            pss_, pks_ = C.pss[7][:], "ps7"
            prot7 = Rot([p[:] for p in C.pss[0:7]], "ps")
            for pas in range(2):
                for n2 in range(64):
                    psf, pkf = prot7.next()
                    psb, pkb = prot7.next()
                    lh = hid2[:, n2:NF:64]
                    P.mm(psf[:, 0:256], lh, w3[:, 0, o_, :], True, True, ["hy_hid2", "hy_w3"], [pkf])
                    P.mm(psb[:, 0:256], lh, w3[:, 1, o_, :], True, True, ["hy_hid2", "hy_w3"], [pkb])
                    dv, dk = dec.next()
                    P.act(dv, absd[:, o_, :], AF.Exp, ["hy_absd", "hy_nt01"], [dk], scale=nt01[:, n2:n2 + 1])
                    kv, kk_ = krot.next()
                    P.v("dve", "scalar_tensor_tensor", [pkf, "hy_msk", dk], [kk_], kv[0:64, :], psf[0:64, 0:256],
                        msk[0:64, n2:n2 + 1], dv[0:64, :], ALU.mult, ALU.mult)
                    P.v("dve", "scalar_tensor_tensor", [pkb, "hy_msk", dk], [kk_], kv[64:128, :], psb[64:128, 0:256],
                        msk[64:128, n2:n2 + 1], dv[64:128, :], ALU.mult, ALU.mult)
                    if pas == 0:
                        sv, sk = sqr.next()
                        P.act(sv[:, 0:256], kv, AF.Square, [kk_], [sk])
                        P.mm(pss_[:, 0:256], C.ones_f[:, :], sv[:, 0:256], n2 == 0, n2 == 63, [sk, "ones_f"], [pks_])
                    else:
                        P.v("pool", "tensor_tensor", [kk_, "hy_rs"], ["hy_xk"], xk[:, :, n2, :], kv.rearrange("p (q c) -> p q c", c=2), rs.rearrange("p (q c) -> p q c", c=2), ALU.mult)
                if pas == 0:
                    P.act(rs, pss_[:, 0:256], AF.Sqrt, [pks_, "eps_t"], ["hy_rs"], bias=C.eps_t[:], scale=1.0)
                    P.v("dve", "reciprocal", ["hy_rs"], ["hy_rs"], rs, rs)
            P.v("dve", "memset", [], ["hy_xk"], xk[64:65, :, 0, :], 0.0)
            if C.cfg.get("hy_dbg") and kind == 0 and o_ == 0:
                dbg1 = B.scr("hy_dbg_hid2", [64, NF])
                dbg2 = B.scr("hy_dbg_xk", [128, 128 * 64 * 2], BF16)
                dbg3 = B.scr("hy_dbg_rs", [128, 256])
                P.dma(dbg1.ap(), hid2, ["hy_hid2"], ["dbg1"], q="pool")
                P.dma(dbg2.ap(), xk.rearrange("p a b c -> p (a b c)"), ["hy_xk"], ["dbg2"], q="pool")
                P.dma(dbg3.ap(), rs, ["hy_rs"], ["dbg3"], q="pool")
            for hf in range(2):
                def consume(bi, k0, kn, psr, pkr, psi, pki, kind=kind, o_=o_, hf=hf):
                    yv, yk = yst.next()
                    P.act(yv[:, 0, 0:kn * 64], psr[:, 0:kn * 64], AF.Copy, [pkr], [yk])
                    P.v("dve", "tensor_copy", [pki], [yk], yv[:, 1, 0:kn * 64], psi[:, 0:kn * 64])
                    for ri in range(2):
                        P.dma(hspec.ap()[kind, o_, hf, ri, :, k0 * 64:(k0 + kn) * 64], yv[:, ri, 0:kn * 64], [yk], ["hspec_d"], q="pool")
                fwd(xk[:, hf * 64:(hf + 1) * 64, :, :], 128, "hy_xk", consume)
    P.barrier()
    P.release(mflt)
    cwr = P.alloc([64, 4, 768])
    P.dma(cwr.rearrange("p a c -> p (a c)"), din["hy_cw"].ap()[l, 0:1, :].partition_broadcast(64)[:, 0, :], [], ["hy_cw"])
    hbr = P.alloc([64, 2, 256])
    P.dma(hbr.rearrange("p a c -> p (a c)"), din["hy_bias"].ap()[l, 0:1, :].partition_broadcast(64)[:, 0, :], [], ["hy_hb"])
    xv = P.alloc([64, 64, 64, 2], BF16)
    z1 = P.alloc([64, 64, 64, 2], BF16)
    Zr = P.alloc([128, 65, 64], BF16); Zi = P.alloc([128, 65, 64], BF16)
    G = P.alloc([65, 2, 64, 128], BF16)
    oTs = P.alloc([128, SEQ], BF16)
    dwl = Rot([P.alloc([64, 3, 4, 128]) for _ in range(2)], "hy_dwl")
    dwa = Rot([P.alloc([64, 4, 128]) for _ in range(3)], "hy_dwa")
    dwt = P.alloc([64, 4, 128])
    hrot = Rot([P.alloc([128, 2, 512]) for _ in range(2)], "hy_h")
    tt1 = P.alloc([128, 512]); tt2 = P.alloc([128, 512])
    gt = P.alloc([64, 4, 128]); z2f = P.alloc([64, 4, 128])

    def blkv(x, M, b):
        return x[0:M, :, 4 * b:4 * b + 4, :].rearrange("p q n c -> p n q c")

    def dwconv_blk(kind, set_, hf, b):
        M = 64 if kind == 0 else 4
        base = LAT0 if kind == 0 else CTX0
        ntok = SEQ if kind == 0 else CTX
        c0 = set_ * 256 + hf * 128
        lv, lk = dwl.next()
        for s in range(3):
            src = C.hyp_d.ap()[base + s - 1:base + s - 1 + ntok, c0:c0 + 128].rearrange("(a b) c -> a b c", b=64)[:, 4 * b:4 * b + 4, :]
            P.dma(lv[0:M, s, :, :], src, ["hyp_d"], [lk])
        av, ak = dwa.next()
        wv = lambda tap: cwr[0:M, tap, c0:c0 + 128].unsqueeze(1).to_broadcast([M, 4, 128])
        P.v("dve", "tensor_tensor", [lk, "hy_cw"], [ak], av[0:M], lv[0:M, 1], wv(1), ALU.mult)
        P.v("pool", "tensor_tensor", [lk, "hy_cw"], ["hy_dwt"], dwt[0:M], lv[0:M, 0], wv(0), ALU.mult)
        P.v("pool", "tensor_tensor", [ak, "hy_dwt"], [ak], av[0:M], av[0:M], dwt[0:M], ALU.add)
        P.v("dve", "tensor_tensor", [lk, "hy_cw"], ["hy_dwt"], dwt[0:M], lv[0:M, 2], wv(2), ALU.mult)
        P.v("pool", "tensor_tensor", [ak, "hy_dwt"], [ak], av[0:M], av[0:M], dwt[0:M], ALU.add)
        P.v("pool", "tensor_tensor", [ak, "hy_cw"], [ak], av[0:M], av[0:M], wv(3), ALU.add)
        return av, ak

    def inverse(kind, evac):
        M = 64 if kind == 0 else 4
        for q0 in range(0, 64, 2):
            ps, pk = psrot.next()
            for j in range(2):
                q = q0 + j
                P.mm(ps[0:65, j * 256:(j + 1) * 256], Zr[:, :, q], Rm[:, 0, :], True, False, ["hy_Z", "hy_R"], [pk])
                P.mm(ps[0:65, j * 256:(j + 1) * 256], Zi[:, :, q], Rm[:, 1, :], False, True, ["hy_Z", "hy_R"], [pk])
            for j in range(2):
                q = q0 + j
                src = ps[0:65, j * 256:(j + 1) * 256].rearrange("p (r t c) -> p r t c", r=2, t=64)
                if j == 0:
                    P.act(G[:, :, :, 2 * q:2 * q + 2], src, AF.Copy, [pk], ["hy_G"])
                else:
                    P.v("dve", "tensor_copy", [pk], ["hy_G"], G[:, :, :, 2 * q:2 * q + 2], src)
        for b in range(16):
            ps, pk = psrot.next()
            for j in range(4):
                t2 = 4 * b + j
                P.mm(ps[0:M, j * 128:(j + 1) * 128], Vm[:, t2, 0, 0:M], G[:, 0, t2, :], True, False, ["hy_V", "hy_G"], [pk])
                P.mm(ps[0:M, j * 128:(j + 1) * 128], Vm[:, t2, 1, 0:M], G[:, 1, t2, :], False, True, ["hy_V", "hy_G"], [pk])
            evac(b, ps, pk, M)

    for kind in kinds:
        M = 64 if kind == 0 else 4
        ntok = SEQ if kind == 0 else CTX
        tok0 = CTX if kind == 0 else 0
        for hf in range(2):
            if kind == 1:
                P.v("dve", "memset", [], ["hy_xv"], xv, 0.0)
                P.v("pool", "memset", [], ["hy_z1"], z1, 0.0)
            for b in range(16):
                av, ak = dwconv_blk(kind, 2, hf, b)
                P.act(blkv(xv, M, b), av[0:M].rearrange("p n (q c) -> p n q c", c=2), AF.Copy, [ak], ["hy_xv"])
            for o_ in range(2):
                src_x = xv if o_ == 0 else z1
                src_k = "hy_xv" if o_ == 0 else "hy_z1"

                def consume(bi, k0, kn, psr, pkr, psi, pki, kind=kind, o_=o_, hf=hf):
                    hv, hk = hrot.next()
                    for ri in range(2):
                        P.dma(hv[:, ri, 0:kn * 64], hspec.ap()[kind, o_, hf, ri, :, k0 * 64:(k0 + kn) * 64], ["hspec_d"], [hk])
                    w = kn * 64
                    zr = Zr[:, k0:k0 + kn, :].rearrange("p a b -> p (a b)")
                    zi = Zi[:, k0:k0 + kn, :].rearrange("p a b -> p (a b)")
                    P.v("dve", "tensor_tensor", [pkr, hk], ["hy_tt1"], tt1[:, 0:w], psr[:, 0:w], hv[:, 0, 0:w], ALU.mult)
                    P.v("dve", "tensor_tensor", [pki, hk], ["hy_tt2"], tt2[:, 0:w], psi[:, 0:w], hv[:, 1, 0:w], ALU.mult)
                    P.v("pool", "tensor_tensor", ["hy_tt1", "hy_tt2"], ["hy_Z"], zr, tt1[:, 0:w], tt2[:, 0:w], ALU.subtract)
                    P.v("dve", "tensor_tensor", [pkr, hk], ["hy_tt1"], tt1[:, 0:w], psr[:, 0:w], hv[:, 1, 0:w], ALU.mult)
                    P.v("dve", "tensor_tensor", [pki, hk], ["hy_tt2"], tt2[:, 0:w], psi[:, 0:w], hv[:, 0, 0:w], ALU.mult)
                    P.v("pool", "tensor_tensor", ["hy_tt1", "hy_tt2"], ["hy_Z"], zi, tt1[:, 0:w], tt2[:, 0:w], ALU.add)
                fwd(src_x, 64, src_k, consume)

                def evac(b, ps, pk, M, kind=kind, o_=o_, hf=hf, src_x=src_x, src_k=src_k):
                    xg, xgk = dwconv_blk(kind, o_, hf, b)
                    yv = ps[0:M, :].rearrange("p (a c) -> p a c", a=4)
                    brow = hbr[0:M, o_, hf * 128:(hf + 1) * 128].unsqueeze(1).to_broadcast([M, 4, 128])
                    P.v("pool", "tensor_tensor", [src_k, "hy_hb"], ["hy_gt"], gt[0:M].rearrange("p n (q c) -> p n q c", c=2), blkv(src_x, M, b), brow.rearrange("p n (q c) -> p n q c", c=2), ALU.mult)
                    P.v("dve", "tensor_tensor", [pk, "hy_gt"], ["hy_gt"], gt[0:M], yv, gt[0:M], ALU.add)
                    if o_ == 0:
                        P.v("dve", "tensor_tensor", ["hy_gt", xgk], ["hy_z1"], blkv(z1, M, b), gt[0:M].rearrange("p n (q c) -> p n q c", c=2), xg[0:M].rearrange("p n (q c) -> p n q c", c=2), ALU.mult)
                    else:
                        P.v("dve", "tensor_tensor", ["hy_gt", xgk], ["hy_z2f"], z2f[0:M], gt[0:M], xg[0:M], ALU.mult)
                        pt, ptk = psrot.next()
                        for j in range(4):
                            P.tr(pt[:, j * 64:j * 64 + M], z2f[0:M, j, :], C.ident[0:M, 0:M], ["hy_z2f", "ident"], [ptk])
                        dst = oTs[:, 0:ntok].rearrange("p (a b) -> p b a", b=64)[:, 4 * b:4 * b + 4, :]
                        srcp = pt[:, 0:256].rearrange("p (j a) -> p j a", j=4)[:, :, 0:M]
                        P.act(dst, srcp, AF.Copy, [ptk], ["hy_oTs"])
                inverse(kind, evac)
            P.dma(C.oT_d.ap()[OT_HY + hf * 128:OT_HY + (hf + 1) * 128, tok0:tok0 + ntok], oTs[:, 0:ntok], ["hy_oTs"], ["oT_d"], q="pool")
    P.barrier()
    P.release(m0)


MIXERS["hy"] = mix_hy
HOSTPREP["hy"] = host_hy
```

```python
import math
import numpy as np
import ml_dtypes
import concourse.bass as bass
import concourse.mybir as mybir
from concourse.bass_utils import run_bass_kernel_spmd
from contextlib import ExitStack
from types import SimpleNamespace

F32 = mybir.dt.float32
BF16 = mybir.dt.bfloat16
I32 = mybir.dt.int32
ALU = mybir.AluOpType
AF = mybir.ActivationFunctionType

ENGS = ("pe", "act", "dve", "pool", "sp")
NSLOT = 6

D = 1024
NB = 4
SEQ = 4096
CTX = 256
T = SEQ + CTX
DEPTH = 2
EPS = 1e-6
WG = 256
DFF = 2816
NTT = T // 128
W_PAD = T + 4
CTX0 = 1
LAT0 = 259
NFM = 2048
NTM = 1536
CH512 = [(i * 512, min(512, T - i * 512)) for i in range((T + 511) // 512)]


class Prog:
    def __init__(self, nc):
        self.nc = nc
        self.ops = []
        self.es = ExitStack()
        self.arena = None
        self.aoff = 0

    def sb(self, name, shape, dtype=F32):
        return self.es.enter_context(self.nc.sbuf_tensor(name, list(shape), dtype))

    def ps(self, name, shape, dtype=F32):
        return self.es.enter_context(self.nc.psum_tensor(name, list(shape), dtype))

    def dram(self, name, shape, dtype=F32, kind="Internal"):
        return self.nc.dram_tensor(name, list(shape), dtype, kind=kind)

    def init_arena(self, words):
        self.arena = self.sb("arena", [128, words], F32)
        self.awords = words
        self.aoff = 0

    def mark(self):
        return self.aoff

    def release(self, m):
        self.aoff = m

    def alloc(self, shape, dtype=F32):
        npart = shape[0]
        n = int(np.prod(shape[1:]))
        if dtype == F32 or dtype == I32:
            words = n
        else:
            words = (n + 1) // 2
        words = (words + 7) // 8 * 8
        off = self.aoff
        self.aoff += words
        assert self.aoff <= self.awords, "arena overflow %d > %d" % (self.aoff, self.awords)
        v = self.arena[0:npart, off:off + words]
        if dtype != F32:
            v = v.bitcast(dtype)
        v = v[:, 0:n]
        if len(shape) == 3:
            v = v.rearrange("p (a b) -> p a b", a=shape[1])
        elif len(shape) == 4:
            v = v.rearrange("p (a b c) -> p a b c", a=shape[1], b=shape[2])
        return v

    def op(self, eng, fn, reads=(), writes=(), dma=False):
        self.ops.append(dict(eng=eng, fn=fn, reads=tuple(reads), writes=tuple(writes), dma=dma, bar=False))

    def barrier(self):
        self.ops.append(dict(bar=True))

    def dma(self, out, in_, reads, writes, q="sp", **kw):
        self.op(q, lambda e: e.dma_start(out=out, in_=in_, **kw), reads, writes, dma=True)

    def mm(self, out, lhsT, rhs, start, stop, reads, writes, **kw):
        self.op("pe", lambda e: e.matmul(out, lhsT, rhs, start=start, stop=stop, **kw), reads, writes)

    def tr(self, out, in_, ident, reads, writes):
        self.op("pe", lambda e: e.transpose(out, in_, ident), reads, writes)

    def act(self, out, in_, func, reads, writes, **kw):
        self.op("act", lambda e: e.activation(out=out, in_=in_, func=func, **kw), reads, writes)

    def v(self, eng, name, reads, writes, *a, **kw):
        self.op(eng, lambda e: getattr(e, name)(*a, **kw), reads, writes)

    def emit(self):
        nc = self.nc
        es = self.es
        csem = {e: es.enter_context(nc.semaphore("c_" + e)) for e in ("pe", "act", "dve", "pool")}
        dsem = {q: [es.enter_context(nc.semaphore("d_%s%d" % (q, i))) for i in range(NSLOT)]
                for q in ("sp", "act", "pool")}
        ccount = {e: 0 for e in csem}
        dcount = {q: 0 for q in dsem}
        slotuse = {q: [0] * NSLOT for q in dsem}
        semobj = {}
        for e in csem:
            semobj[("c", e)] = csem[e]
        for q in dsem:
            for i in range(NSLOT):
                semobj[("d", q, i)] = dsem[q][i]
        know = {e: {} for e in ENGS}
        last_w = {}
        readers = {}
        streams = {e: [] for e in ENGS}
        bar_know = {}
        bar_done = {e: True for e in ENGS}

        def cur_all():
            d = {}
            for q in dsem:
                for i in range(NSLOT):
                    if slotuse[q][i] > 0:
                        d[("d", q, i)] = slotuse[q][i] * 16
            for e in csem:
                if ccount[e] > 0:
                    d[("c", e)] = ccount[e]
            return d

        for op in self.ops:
            if op["bar"]:
                bar_know = cur_all()
                bar_done = {e: False for e in ENGS}
                continue
            e = op["eng"]
            deps = []
            for r in op["reads"]:
                if r in last_w:
                    deps.append(last_w[r])
            for w in op["writes"]:
                if w in last_w:
                    deps.append(last_w[w])
                for rd in readers.get(w, {}).values():
                    deps.append(rd)
            waits = {}
            kn = know[e]
            if not bar_done[e]:
                bar_done[e] = True
                for sk, val in bar_know.items():
                    if sk == ("c", "pe") and e == "pe":
                        continue
                    if kn.get(sk, 0) < val:
                        waits[sk] = val
                        kn[sk] = val
            if op["dma"]:
                slot = dcount[e] % NSLOT
                dcount[e] += 1
                sk = ("d", e, slot)
                prev = slotuse[e][slot] * 16
                if prev > 0 and kn.get(sk, 0) < prev:
                    waits[sk] = max(waits.get(sk, 0), prev)
                    kn[sk] = prev
                slotuse[e][slot] += 1
                tok = (sk, slotuse[e][slot] * 16)
            else:
                ccount[e] += 1
                tok = (("c", e), ccount[e])
            for (dtok, dknow) in deps:
                sk, val = dtok
                if sk == ("c", "pe") and e == "pe" and not op["dma"]:
                    continue
                if kn.get(sk, 0) >= val:
                    continue
                waits[sk] = max(waits.get(sk, 0), val)
                for k2, v2 in dknow.items():
                    if kn.get(k2, 0) < v2:
                        kn[k2] = v2
                kn[sk] = max(kn.get(sk, 0), val)
            myknow = dict(kn)
            myknow[tok[0]] = max(myknow.get(tok[0], 0), tok[1])
            if (not op["dma"]) and e == "pe":
                kn[tok[0]] = tok[1]
            streams[e].append((op, waits, tok))
            entry = (tok, myknow)
            for w in op["writes"]:
                last_w[w] = entry
                readers[w] = {}
            for r in op["reads"]:
                if r not in op["writes"]:
                    readers.setdefault(r, {})[tok[0]] = entry
        fin = cur_all()
        self.stats = dict(ccount=dict(ccount), dcount=dict(dcount))

        def run_stream(ename, eng):
            for (op, waits, tok) in streams[ename]:
                for sk, val in waits.items():
                    eng.wait_ge(semobj[sk], val)
                ins = op["fn"](eng)
                ins.then_inc(semobj[tok[0]], 16 if op["dma"] else 1)
            if ename == "sp":
                for sk, val in fin.items():
                    eng.wait_ge(semobj[sk], val)

        with nc.Block() as block:
            @block.sync
            def _(eng):
                run_stream("sp", eng)

            @block.tensor
            def _(eng):
                run_stream("pe", eng)

            @block.scalar
            def _(eng):
                run_stream("act", eng)

            @block.vector
            def _(eng):
                run_stream("dve", eng)

            @block.gpsimd
            def _(eng):
                run_stream("pool", eng)
        es.close()


class Rot:
    def __init__(self, views, name):
        self.views = views
        self.name = name
        self.i = 0

    def next(self):
        i = self.i % len(self.views)
        self.i += 1
        return self.views[i], "%s%d" % (self.name, i)


def _swap_halves_cols(w, width):
    n = w.shape[1]
    idx = np.arange(n).reshape(n // width, 2, width // 2)[:, ::-1, :].reshape(-1)
    return w[:, idx]


def host_consts():
    c = {}
    c["ident"] = np.eye(128, dtype=np.float32)
    return c


def arrange_in_cols(w_in):
    o = {}
    s5u = w_in[:, 0:256]
    hy = w_in[:, 256:1024]
    rq = w_in[:, 1024:1280]
    rk = w_in[:, 1280:1536]
    rv = w_in[:, 1536:1792]
    rg = w_in[:, 1792:2048]
    cq = w_in[:, 2048:2240]
    ckv = w_in[:, 2240:2368]
    kr = w_in[:, 2368:2400]
    z64 = np.zeros((w_in.shape[0], 64), np.float32)
    fm = np.concatenate([s5u, rq, rk, rg, _swap_halves_cols(rq, 64), _swap_halves_cols(rk, 64),
                         cq, z64, ckv, kr, _swap_halves_cols(kr, 32), z64], axis=1)
    assert fm.shape[1] == NFM
    tm = np.concatenate([hy, rk, _swap_halves_cols(rk, 64), rv], axis=1)
    assert tm.shape[1] == NTM
    o["w_fm"] = np.ascontiguousarray(fm)
    o["w_tm"] = np.ascontiguousarray(tm)
    return o


def prep_layer_weights(inp, l):
    return arrange_in_cols(inp["w_in"][l])


class Builder:
    def __init__(self, cfg):
        self.cfg = cfg
        self.nc = bass.Bass("TRN2", target_bir_lowering=False)
        self.P = Prog(self.nc)
        self.inputs = {}
        self.din = {}
        self.dscr = {}
        self.ext_out = []

    def inp(self, name, shape, dtype=F32):
        t = self.P.dram(name, shape, dtype, kind="ExternalInput")
        self.din[name] = t
        return t

    def scr(self, name, shape, dtype=F32):
        cfg = self.cfg
        if name in cfg.get("dbg_in", ()):
            kind = "ExternalInput"
        elif name in cfg.get("dbg_out", ()):
            kind = "ExternalOutput"
            self.ext_out.append(name)
        else:
            kind = "Internal"
        t = self.P.dram(name, shape, dtype, kind=kind)
        self.dscr[name] = t
        return t


def build_program(cfg):
    B = Builder(cfg)
    P = B.P
    nc = B.nc
    layers = cfg.get("layers", list(range(DEPTH)))
    stages = cfg.get("stages", "ABCDEF")
    mixers = cfg.get("mixers", ("s5", "hy", "ret", "mla"))

    x_in = B.inp("x_in", [SEQ, D])
    ctx_in = B.inp("ctx_in", [CTX, D])
    cs_in = B.inp("cs_in", [128, 8, 2])
    ident_in = B.inp("ident", [128, 128])
    ada_w = B.inp("ada_w", [DEPTH, D, 6 * D])
    ada_bT = B.inp("ada_bT", [DEPTH, 128, 48])
    ada_brow = B.inp("ada_brow", [DEPTH, 1, 6 * D])
    ng_T = B.inp("ng_T", [DEPTH, 4, 128, 8])
    ng_row = B.inp("ng_row", [DEPTH, 4, 1, D])
    w_fm = B.inp("w_fm", [DEPTH, D, NFM])
    w_tm = B.inp("w_tm", [DEPTH, D, NTM])
    w_out = B.inp("w_out", [DEPTH, D, D])
    w_up = B.inp("w_up", [DEPTH, D, 2 * DFF])
    ffn_cw = B.inp("ffn_cw", [DEPTH, 128, 44, 3])
    ffn_cb = B.inp("ffn_cb", [DEPTH, 128, 44])
    w_down = B.inp("w_down", [DEPTH, DFF, D])
    y_out = P.dram("y", [SEQ, D], F32, kind="ExternalOutput")

    xs = B.scr("xs", [T, D])
    pT_d = B.scr("pT_d", [NFM, T])
    hyp_d = B.scr("hyp_d", [W_PAD, 768])
    rtok_d = B.scr("rtok_d", [T, 768])
    oT_d = B.scr("oT_d", [D, T], BF16)
    gT_d = B.scr("gT_d", [DFF, W_PAD], BF16)

    ident = P.sb("ident_sb", [128, 128])
    zero_sb = P.sb("zero_sb", [128, 768])
    scs = P.sb("scs", [128, 8, 2])
    modT = P.sb("modT", [128, 48, 2])
    AB = P.sb("ABmod", [128, 4, 8, 2])
    grow = P.sb("grow", [128, 2, 2, D])
    eps_t = P.sb("eps_t", [128, 1])
    ones_f = P.sb("ones_f", [128, 128])
    ones_b = P.sb("ones_b", [128, 128], BF16)
    pss = [P.ps("ps%d" % i, [128, 512]) for i in range(8)]
    psrot = Rot([p[:] for p in pss], "ps")
    P.init_arena(cfg.get("arena_words", 47000))

    P.dma(ident[:], ident_in.ap(), [], ["ident"])
    P.v("dve", "memset", [], ["zero_sb"], zero_sb[:], 0.0)
    P.v("dve", "memset", [], ["eps_t"], eps_t[:], EPS)
    P.v("dve", "memset", [], ["ones_f"], ones_f[:], 1.0)
    P.v("dve", "memset", [], ["ones_b"], ones_b[:], 1.0)
    cs_raw = P.sb("cs_raw", [128, 8, 2])
    P.dma(cs_raw[:], cs_in.ap(), [], ["cs_raw"])
    P.act(scs[:], cs_raw[:], AF.Silu, ["cs_raw"], ["scs"])
    for r in (0, CTX0 + CTX, CTX0 + CTX + 1, W_PAD - 1):
        P.dma(hyp_d.ap()[r:r + 1, :], zero_sb[0:1, 0:768], ["zero_sb"], ["hyp_d"], q="pool")

    def xrows(layer, t0, n):
        if layer == 0:
            if t0 < CTX:
                return ctx_in.ap()[t0:t0 + n, :]
            return x_in.ap()[t0 - CTX:t0 - CTX + n, :]
        return xs.ap()[t0:t0 + n, :]

    def stage_A(l):
        m0 = P.mark()
        sc_rep = P.alloc([128, 8, 2, 128])
        P.v("dve", "tensor_copy", ["scs"], ["sc_rep"], sc_rep, scs[:].unsqueeze(3).to_broadcast([128, 8, 2, 128]))
        wbufs = Rot([P.alloc([128, 8, 512]) for _ in range(2)], "adaw")
        abT = P.alloc([128, 48])
        P.dma(abT, ada_bT.ap()[l], [], ["abT"])
        ngT = P.alloc([128, 4, 8])
        P.dma(ngT, ada_dummy_ngT(l), [], ["ngT"])
        brow = P.alloc([128, 2, D])
        ngrow = P.alloc([128, 2, D])
        for wi, c0 in enumerate((2 * D, 5 * D)):
            P.dma(brow[:, wi, :], ada_brow.ap()[l, 0:1, c0:c0 + D].partition_broadcast(128)[:, 0, :], [], ["brow"])
            P.dma(ngrow[:, wi, :], ng_row.ap()[l, 1 + 2 * wi, 0:1, :].partition_broadcast(128)[:, 0, :], [], ["ngrow"])
        for ci in range(12):
            wb, wk = wbufs.next()
            P.dma(wb, ada_w.ap()[l].rearrange("(k p) n -> p k n", p=128)[:, :, ci * 512:(ci + 1) * 512], [], [wk])
            for mi in range(4):
                m = ci * 4 + mi
                ps, pk = psrot.next()
                for k in range(8):
                    P.mm(ps[:, 0:2], wb[:, k, mi * 128:(mi + 1) * 128], scs[:, k, :], k == 0, k == 7,
                         [wk, "scs"], [pk])
                P.v("dve", "tensor_scalar", [pk, "abT"], ["modT"], modT[:, m, :], ps[:, 0:2], abT[:, m:m + 1], None, ALU.add)
            if ci in (4, 5, 10, 11):
                wi = 0 if ci < 6 else 1
                cc = (ci - 4) if ci < 6 else (ci - 10)
                for v in range(2):
                    ps, pk = psrot.next()
                    for k in range(8):
                        P.mm(ps, sc_rep[:, k, v, :], wb[:, k, :], k == 0, k == 7, [wk, "sc_rep"], [pk])
                    P.v("dve", "tensor_tensor", [pk, "brow"], ["grow"], grow[:, wi, v, cc * 512:(cc + 1) * 512], ps,
                        brow[:, wi, cc * 512:(cc + 1) * 512], ALU.add)
        for wi in range(2):
            for v in range(2):
                P.v("pool", "tensor_tensor", ["grow", "ngrow"], ["grow"], grow[:, wi, v, :], grow[:, wi, v, :], ngrow[:, wi, :], ALU.mult)
        for wh in range(2):
            shm = 0 if wh == 0 else 24
            scm = 8 if wh == 0 else 32
            for v in range(2):
                P.v("dve", "scalar_tensor_tensor", ["modT", "ngT"], ["AB"], AB[:, 2 * wh, :, v], modT[:, scm:scm + 8, v], 1.0,
                    ngT[:, 2 * wh, :], ALU.add, ALU.mult)
                P.v("dve", "tensor_copy", ["modT"], ["AB"], AB[:, 2 * wh + 1, :, v], modT[:, shm:shm + 8, v])
        P.barrier()
        P.release(m0)

    def ada_dummy_ngT(l):
        return ng_T.ap()[l].rearrange("f p k -> p f k")

    def norm_mod_transpose(xt, xkey, hT, hkey, tt, wh, scratch):
        v = 1 if tt < 2 else 0
        junk, ss, rstd, xn = scratch
        P.act(junk, xt, AF.Square, [xkey], ["nm_junk", "nm_ss"], accum_out=ss)
        P.act(rstd, ss, AF.Sqrt, ["nm_ss", "eps_t"], ["nm_rstd"], bias=eps_t[:], scale=1.0 / D)
        P.v("dve", "reciprocal", ["nm_rstd"], ["nm_rstd"], rstd, rstd)
        P.act(xn, xt, AF.Identity, [xkey, "nm_rstd"], ["nm_xn"], scale=rstd)
        for half in range(2):
            ps, pk = psrot.next()
            for kk in range(4):
                k = half * 4 + kk
                P.tr(ps[:, kk * 128:(kk + 1) * 128], xn[:, k * 128:(k + 1) * 128], ident[:], ["nm_xn", "ident"], [pk])
            for kk in range(4):
                k = half * 4 + kk
                eng = "dve" if kk % 2 == 0 else "pool"
                if eng == "pool":
                    P.act(hT[:, k, tt * 128:(tt + 1) * 128], ps[:, kk * 128:(kk + 1) * 128], AF.Identity,
                          [pk, "AB"], [hkey], bias=AB[:, 2 * wh + 1, k, v:v + 1], scale=AB[:, 2 * wh, k, v:v + 1])
                else:
                    P.v("dve", "tensor_scalar", [pk, "AB"], [hkey], hT[:, k, tt * 128:(tt + 1) * 128],
                        ps[:, kk * 128:(kk + 1) * 128], AB[:, 2 * wh, k, v:v + 1], AB[:, 2 * wh + 1, k, v:v + 1],
                        ALU.mult, ALU.add)

    def nm_scratch():
        return (P.alloc([128, D]), P.alloc([128, 1]), P.alloc([128, 1]), P.alloc([128, D]))

    def stage_B(l, hT):
        m0 = P.mark()
        xbufs = Rot([P.alloc([128, D]) for _ in range(3)], "xt")
        scratch = nm_scratch()
        for tt in range(NTT):
            xt, xk = xbufs.next()
            P.dma(xt, xrows(l, tt * 128, 128), ["xs%d" % tt], [xk])
            norm_mod_transpose(xt, xk, hT, "hT", tt, 0, scratch)

    def load_cast(dst_bf, src_ap, shape, stage_rot, key, eng="pool"):
        st, sk = stage_rot.next()
        P.dma(st, src_ap, [], [sk])
        if eng == "act":
            P.act(dst_bf, st, AF.Copy, [sk], [key])
        else:
            P.v(eng, "tensor_copy", [sk], [key], dst_bf, st)

    def stage_C(l, hT):
        m0 = P.mark()
        wst = Rot([P.alloc([128, 8, 512]) for _ in range(2)], "wst")
        wbf = Rot([P.alloc([128, 8, 512], BF16) for _ in range(2)], "wbf")
        outb = Rot([P.alloc([128, T]) for _ in range(2)], "pout")
        wsrc = w_fm.ap()[l].rearrange("(k p) n -> p k n", p=128)
        for cg in range(NFM // 512):
            wb, wk = wbf.next()
            load_cast(wb, wsrc[:, :, cg * 512:(cg + 1) * 512], None, wst, wk, eng="pool")
            for ci in range(4):
                ct = cg * 4 + ci
                ob, ok = outb.next()
                for ch, (c0, cn) in enumerate(CH512):
                    ps, pk = psrot.next()
                    for k in range(8):
                        P.mm(ps[:, 0:cn], wb[:, k, ci * 128:(ci + 1) * 128], hT[:, k, c0:c0 + cn], k == 0, k == 7,
                             [wk, "hT"], [pk])
                    if ch % 2 == 0:
                        P.act(ob[:, c0:c0 + cn], ps[:, 0:cn], AF.Copy, [pk], [ok])
                    else:
                        P.v("dve", "tensor_copy", [pk], [ok], ob[:, c0:c0 + cn], ps[:, 0:cn])
                P.dma(pT_d.ap()[ct * 128:(ct + 1) * 128, :], ob, [ok], ["pT_d"], q="pool")
        tmo = Rot([P.alloc([128, 512]) for _ in range(3)], "tmo")
        wsrc = w_tm.ap()[l].rearrange("(k p) n -> p k n", p=128)
        for cg in range(NTM // 512):
            wb, wk = wbf.next()
            load_cast(wb, wsrc[:, :, cg * 512:(cg + 1) * 512], None, wst, wk, eng="pool")
            for tt in range(NTT):
                ps, pk = psrot.next()
                for k in range(8):
                    P.mm(ps, hT[:, k, tt * 128:(tt + 1) * 128], wb[:, k, :], k == 0, k == 7, [wk, "hT"], [pk])
                ob, ok = tmo.next()
                if tt % 2 == 0:
                    P.act(ob, ps, AF.Copy, [pk], [ok])
                else:
                    P.v("dve", "tensor_copy", [pk], [ok], ob, ps)
                row_h = (CTX0 + tt * 128) if tt < 2 else (LAT0 + (tt - 2) * 128)
                c0 = cg * 512
                if c0 + 512 <= 768:
                    P.dma(hyp_d.ap()[row_h:row_h + 128, c0:c0 + 512], ob, [ok], ["hyp_d"], q="pool")
                elif c0 >= 768:
                    P.dma(rtok_d.ap()[tt * 128:(tt + 1) * 128, c0 - 768:c0 - 768 + 512], ob, [ok], ["rtok_d"], q="pool")
                else:
                    nh = 768 - c0
                    P.dma(hyp_d.ap()[row_h:row_h + 128, c0:768], ob[:, 0:nh], [ok], ["hyp_d"], q="pool")
                    P.dma(rtok_d.ap()[tt * 128:(tt + 1) * 128, 0:512 - nh], ob[:, nh:512], [ok], ["rtok_d"], q="pool")
        P.barrier()
        P.release(m0)

    def stage_E(l, h2T, last):
        m0 = P.mark()
        wst = Rot([P.alloc([128, 8, 512]) for _ in range(2)], "wst")
        wo = P.alloc([128, 8, D], BF16)
        wsrc = w_out.ap()[l].rearrange("(k p) n -> p k n", p=128)
        for hh in range(2):
            load_cast(wo[:, :, hh * 512:(hh + 1) * 512], wsrc[:, :, hh * 512:(hh + 1) * 512], None, wst, "wo", eng="pool")
        obufs = Rot([P.alloc([128, 8, 128], BF16) for _ in range(3)], "oT")
        xbufs = Rot([P.alloc([128, D]) for _ in range(3)], "xt")
        ysb = P.alloc([128, D])
        junk = P.alloc([128, D])
        ss = P.alloc([128, 1])
        rstd = P.alloc([128, 1])
        tmp = P.alloc([128, D])
        xnb = Rot([P.alloc([128, D]) for _ in range(2)], "xnew")
        scratch = nm_scratch()
        osrc = oT_d.ap().rearrange("(k p) t -> p k t", p=128)
        for tt in range(2 if last else 0, NTT):
            v = 1 if tt < 2 else 0
            ob, ok = obufs.next()
            P.dma(ob, osrc[:, :, tt * 128:(tt + 1) * 128], ["oT_d"], [ok])
            xt, xk = xbufs.next()
            P.dma(xt, xrows(l, tt * 128, 128), ["xs%d" % tt], [xk])
            for hh in range(2):
                ps, pk = psrot.next()
                for k in range(8):
                    P.mm(ps, ob[:, k, :], wo[:, k, hh * 512:(hh + 1) * 512], k == 0, k == 7, [ok, "wo"], [pk])
                P.act(ysb[:, hh * 512:(hh + 1) * 512], ps, AF.Copy, [pk], ["ysb"])
            P.act(junk, ysb, AF.Square, ["ysb"], ["e_junk", "e_ss"], accum_out=ss)
            P.act(rstd, ss, AF.Sqrt, ["e_ss", "eps_t"], ["e_rstd"], bias=eps_t[:], scale=1.0 / D)
            P.v("dve", "reciprocal", ["e_rstd"], ["e_rstd"], rstd, rstd)
            P.v("dve", "scalar_tensor_tensor", ["ysb", "e_rstd", "grow"], ["e_tmp"], tmp, ysb, rstd, grow[:, 0, v, :], ALU.mult, ALU.mult)
            xn, xnk = xnb.next()
            P.v("pool", "tensor_tensor", ["e_tmp", xk], [xnk], xn, tmp, xt, ALU.add)
            P.dma(xs.ap()[tt * 128:(tt + 1) * 128, :], xn, [xnk], ["xs%d" % tt], q="pool")
            norm_mod_transpose(xn, xnk, h2T, "h2T", tt, 1, scratch)
        P.barrier()
        P.release(m0)

    def stage_F(l, h2T, last):
        m0 = P.mark()
        wst = Rot([P.alloc([128, 8, 256]) for _ in range(2)], "fwst")
        wbf = Rot([P.alloc([128, 8, 256], BF16) for _ in range(2)], "fwbf")
        cw = P.alloc([128, 44, 3])
        cb = P.alloc([128, 44])
        P.dma(cw, ffn_cw.ap()[l], [], ["ffn_cw"])
        P.dma(cb, ffn_cb.ap()[l], [], ["ffn_cb"])
        ua = P.alloc([128, W_PAD])
        uv = P.alloc([128, W_PAD])
        ca = P.alloc([128, W_PAD])
        cv = P.alloc([128, W_PAD])
        gb = Rot([P.alloc([128, W_PAD], BF16) for _ in range(1)], "gb")
        for u_ in (ua, uv):
            P.v("dve", "memset", [], ["ua", "uv"], u_, 0.0)
        wsrc = w_up.ap()[l].rearrange("(k p) n -> p k n", p=128)
        Wp = W_PAD
        for j in range(22):
            wb, wk = wbf.next()
            st, sk = wst.next()
            P.dma(st[:, :, 0:128], wsrc[:, :, j * 128:(j + 1) * 128], [], [sk])
            P.dma(st[:, :, 128:256], wsrc[:, :, DFF + j * 128:DFF + (j + 1) * 128], [], [sk])
            P.v("pool", "tensor_copy", [sk], [wk], wb, st)
            for half, (ub, ukey) in enumerate(((ua, "ua"), (uv, "uv"))):
                for ch, (c0, cn) in enumerate(CH512):
                    ps, pk = psrot.next()
                    for k in range(8):
                        P.mm(ps[:, 0:cn], wb[:, k, half * 128:(half + 1) * 128], h2T[:, k, c0:c0 + cn], k == 0, k == 7,
                             [wk, "h2T"], [pk])
                    segs = []
                    if c0 < CTX:
                        segs.append((0, CTX, CTX0))
                        segs.append((CTX, cn - CTX, LAT0))
                    else:
                        segs.append((0, cn, LAT0 + c0 - CTX))
                    for (s0, sn, d0) in segs:
                        if (ch + half) % 2 == 0:
                            P.act(ub[:, d0:d0 + sn], ps[:, s0:s0 + sn], AF.Copy, [pk], [ukey])
                        else:
                            P.v("dve", "tensor_copy", [pk], [ukey], ub[:, d0:d0 + sn], ps[:, s0:s0 + sn])
            for half, (ub, ukey, cbuf, ckey) in enumerate(((ua, "ua", ca, "ca"), (uv, "uv", cv, "cv"))):
                jj = j + 22 * half
                P.act(cbuf[:, 1:Wp - 1], ub[:, 1:Wp - 1], AF.Identity, [ukey, "ffn_cw", "ffn_cb"], [ckey],
                      bias=cb[:, jj:jj + 1], scale=cw[:, jj, 1:2])
                P.v("dve", "scalar_tensor_tensor", [ukey, ckey, "ffn_cw"], [ckey], cbuf[:, 1:Wp - 1], ub[:, 0:Wp - 2],
                    cw[:, jj, 0:1], cbuf[:, 1:Wp - 1], ALU.mult, ALU.add)
                P.v("dve", "scalar_tensor_tensor", [ukey, ckey, "ffn_cw"], [ckey], cbuf[:, 1:Wp - 1],
                    ub[:, 2:Wp], cw[:, jj, 2:3], cbuf[:, 1:Wp - 1], ALU.mult, ALU.add)
            P.act(ca[:, 1:Wp - 1], ca[:, 1:Wp - 1], AF.Silu, ["ca"], ["ca"])
            g, gk = gb.next()
            P.v("pool", "tensor_tensor", ["ca", "cv"], [gk], g[:, 1:Wp - 1], ca[:, 1:Wp - 1], cv[:, 1:Wp - 1], ALU.mult)
            P.dma(gT_d.ap()[j * 128:(j + 1) * 128, 1:Wp - 1], g[:, 1:Wp - 1], [gk], ["gT_d"], q="pool")
        P.barrier()
        P.release(m0)

    def stage_F2(l, last):
        m0 = P.mark()
        wst = Rot([P.alloc([128, 2, D]) for _ in range(2)], "dwst")
        wd = P.alloc([128, 22, D], BF16)
        wsrc = w_down.ap()[l].rearrange("(j p) n -> p j n", p=128)
        for j2 in range(11):
            load_cast(wd[:, 2 * j2:2 * j2 + 2, :], wsrc[:, 2 * j2:2 * j2 + 2, :], None, wst, "wd", eng="pool")
        gbufs = Rot([P.alloc([128, 22, 128], BF16) for _ in range(3)], "gt")
        xbufs = Rot([P.alloc([128, D]) for _ in range(3)], "xt")
        ysb = P.alloc([128, D])
        junk = P.alloc([128, D])
        ss = P.alloc([128, 1])
        rstd = P.alloc([128, 1])
        tmp = P.alloc([128, D])
        xnb = Rot([P.alloc([128, D]) for _ in range(2)], "xnew")
        gsrc = gT_d.ap().rearrange("(j p) t -> p j t", p=128)
        for tt in range(2 if last else 0, NTT):
            v = 1 if tt < 2 else 0
            col = (CTX0 + tt * 128) if tt < 2 else (LAT0 + (tt - 2) * 128)
            gt, gk = gbufs.next()
            P.dma(gt, gsrc[:, :, col:col + 128], ["gT_d"], [gk])
            xt, xk = xbufs.next()
            P.dma(xt, xs.ap()[tt * 128:(tt + 1) * 128, :], ["xs%d" % tt], [xk])
            for hh in range(2):
                ps, pk = psrot.next()
                for j in range(22):
                    P.mm(ps, gt[:, j, :], wd[:, j, hh * 512:(hh + 1) * 512], j == 0, j == 21, [gk, "wd"], [pk])
                P.act(ysb[:, hh * 512:(hh + 1) * 512], ps, AF.Copy, [pk], ["ysb"])
            P.act(junk, ysb, AF.Square, ["ysb"], ["e_junk", "e_ss"], accum_out=ss)
            P.act(rstd, ss, AF.Sqrt, ["e_ss", "eps_t"], ["e_rstd"], bias=eps_t[:], scale=1.0 / D)
            P.v("dve", "reciprocal", ["e_rstd"], ["e_rstd"], rstd, rstd)
            P.v("dve", "scalar_tensor_tensor", ["ysb", "e_rstd", "grow"], ["e_tmp"], tmp, ysb, rstd, grow[:, 1, v, :], ALU.mult, ALU.mult)
            xn, xnk = xnb.next()
            P.v("pool", "tensor_tensor", ["e_tmp", xk], [xnk], xn, tmp, xt, ALU.add)
            if last:
                P.dma(y_out.ap()[(tt - 2) * 128:(tt - 1) * 128, :], xn, [xnk], ["y"], q="pool")
            else:
                P.dma(xs.ap()[tt * 128:(tt + 1) * 128, :], xn, [xnk], ["xs%d" % tt], q="pool")
        P.barrier()
        P.release(m0)

    for l in layers:
        last = (l == DEPTH - 1)
        if "A" in stages:
            stage_A(l)
        mh = P.mark()
        hT = P.alloc([128, 8, T], BF16)
        if "B" in stages:
            stage_B(l, hT)
        if "C" in stages:
            stage_C(l, hT)
        P.barrier()
        P.release(mh)
        if "D" in stages:
            C = SimpleNamespace(B=B, P=P, nc=nc, psrot=psrot, pss=pss, pT_d=pT_d, hyp_d=hyp_d, rtok_d=rtok_d, oT_d=oT_d,
                                ident=ident, zero_sb=zero_sb, eps_t=eps_t, ones_f=ones_f, ones_b=ones_b, cfg=cfg)
            for mx in mixers:
                MIXERS[mx](C, l, last)
                P.barrier()
        mh = P.mark()
        h2T = P.alloc([128, 8, T], BF16)
        if "E" in stages:
            stage_E(l, h2T, last)
        if "F" in stages:
            stage_F(l, h2T, last)
        P.barrier()
        P.release(mh)
        if "F" in stages:
            stage_F2(l, last)
    P.emit()
    return B


MIXERS = {}
HOSTPREP = {}


def make_inputs(inp, cfg=None):
    sh = {}
    sh.update(host_consts())
    L = DEPTH
    sh["ada_w"] = np.ascontiguousarray(inp["ada_w"])
    sh["ada_bT"] = np.ascontiguousarray(inp["ada_b"].reshape(L, 48, 128).transpose(0, 2, 1))
    sh["ada_brow"] = np.ascontiguousarray(inp["ada_b"].reshape(L, 1, 6 * D))
    sh["ng_T"] = np.ascontiguousarray(inp["norm_g"].reshape(L, 4, 8, 128).transpose(0, 1, 3, 2))
    sh["ng_row"] = np.ascontiguousarray(inp["norm_g"].reshape(L, 4, 1, D))
    lw = [prep_layer_weights(inp, l) for l in range(L)]
    sh["w_fm"] = np.stack([w["w_fm"] for w in lw])
    sh["w_tm"] = np.stack([w["w_tm"] for w in lw])
    sh["w_out"] = np.ascontiguousarray(inp["w_out"])
    sh["w_up"] = np.ascontiguousarray(inp["ffn_w_up"])
    sh["ffn_cw"] = np.ascontiguousarray(inp["ffn_conv_w"].reshape(L, 3, 44, 128).transpose(0, 3, 2, 1))
    sh["ffn_cb"] = np.ascontiguousarray(inp["ffn_conv_b"].reshape(L, 44, 128).transpose(0, 2, 1))
    sh["w_down"] = np.ascontiguousarray(inp["ffn_w_down"])
    for fn in HOSTPREP.values():
        sh.update(fn(inp))
    per = []
    for core in range(8):
        b = core % NB
        d = {}
        d["x_in"] = np.ascontiguousarray(inp["x"][b])
        d["ctx_in"] = np.ascontiguousarray(inp["ctx"][b])
        cs = np.stack([inp["c"][b], inp["c_ctx"]], axis=-1)
        d["cs_in"] = np.ascontiguousarray(cs.reshape(8, 128, 2).transpose(1, 0, 2))
        per.append(d)
    return sh, per


_CACHE = {}


def kernel(**inputs):
    inp = {k: np.asarray(v) for k, v in inputs.items()}
    cfg = {}
    Bd = build_program(cfg)
    sh, per = make_inputs(inp)
    in_maps = []
    for core in range(8):
        m = dict(sh)
        m.update(per[core])
        in_maps.append(m)
    res = run_bass_kernel_spmd(Bd.nc, in_maps, core_ids=list(range(8)))
    out = np.stack([res.results[b]["y"] for b in range(NB)], axis=0)
    return out.astype(np.float32)


MLA_H = 4
QK = 96
PT_CQA, PT_CQB, PT_CKV, PT_KR, PT_KRS = 1536, 1664, 1792, 1920, 1952
OT_MLA = 768


def host_mla(inp):
    L = DEPTH
    o = {}
    wq = inp["mla_w_uq"].reshape(L, 192, MLA_H, QK)
    wqB = wq.copy()
    rope = wq[..., 64:96].reshape(L, 192, MLA_H, 2, 16)[..., ::-1, :].reshape(L, 192, MLA_H, 32)
    wqB[..., 64:96] = rope
    o["mla_wq"] = np.ascontiguousarray(np.stack([wq, wqB], axis=2).reshape(L, 192, 2 * MLA_H * QK))
    wkv = inp["mla_w_ukv"].reshape(L, 128, MLA_H, 128)
    o["mla_wkn"] = np.ascontiguousarray(wkv[..., 0:64].reshape(L, 128, 256))
    o["mla_wv"] = np.ascontiguousarray(wkv[..., 64:128].reshape(L, 128, 256))
    qg = np.zeros((L, 256), np.float32)
    qg[:, :192] = inp["mla_q_norm_g"]
    o["mla_qg"] = np.ascontiguousarray(qg.reshape(L, 2, 128).transpose(0, 2, 1))
    o["mla_kvg"] = np.ascontiguousarray(inp["mla_kv_norm_g"].reshape(L, 128, 1))
    n = SEQ
    rows = n // 64
    row = np.repeat(np.arange(rows), 64).astype(np.float32)
    col = np.tile(np.arange(64), rows).astype(np.float32)
    nf = 8
    inv = (np.float32(10000.0) ** (-np.arange(nf, dtype=np.float32) / nf)).astype(np.float32)
    ang = np.concatenate([row[:, None] * inv, col[:, None] * inv], axis=-1).astype(np.float32)
    cos, sin = np.cos(ang).astype(np.float32), np.sin(ang).astype(np.float32)
    Cf = np.ones((96, T), np.float32)
    Sf = np.zeros((96, T), np.float32)
    Cf[64:80, CTX:] = cos.T
    Cf[80:96, CTX:] = cos.T
    Sf[64:80, CTX:] = -sin.T
    Sf[80:96, CTX:] = sin.T
    o["mla_C"] = Cf
    o["mla_S"] = Sf
    return o


def mix_mla(C, l, last):
    P, B = C.P, C.B
    psrot = C.psrot
    if "mla_wq" not in B.din:
        B.inp("mla_wq", [DEPTH, 192, 768])
        B.inp("mla_wkn", [DEPTH, 128, 256])
        B.inp("mla_wv", [DEPTH, 128, 256])
        B.inp("mla_qg", [DEPTH, 128, 2])
        B.inp("mla_kvg", [DEPTH, 128, 1])
        B.inp("mla_C", [96, T])
        B.inp("mla_S", [96, T])
    din = B.din
    pT = C.pT_d.ap()
    m0 = P.mark()
    wq_st = P.alloc([128, 2, 768])
    wq = P.alloc([128, 2, 768], BF16)
    P.dma(wq_st[:, 0, :], din["mla_wq"].ap()[l, 0:128, :], [], ["mla_wq_st"])
    P.dma(wq_st[0:64, 1, :], din["mla_wq"].ap()[l, 128:192, :], [], ["mla_wq_st"])
    P.v("pool", "tensor_copy", ["mla_wq_st"], ["mla_wq"], wq[:, 0, :], wq_st[:, 0, :])
    P.v("pool", "tensor_copy", ["mla_wq_st"], ["mla_wq"], wq[0:64, 1, :], wq_st[0:64, 1, :])
    wk_st = P.alloc([128, 512])
    wkv = P.alloc([128, 512], BF16)
    P.dma(wk_st[:, 0:256], din["mla_wkn"].ap()[l], [], ["mla_wk_st"])
    P.dma(wk_st[:, 256:512], din["mla_wv"].ap()[l], [], ["mla_wk_st"])
    P.v("pool", "tensor_copy", ["mla_wk_st"], ["mla_wkv"], wkv, wk_st)
    qg = P.alloc([128, 2])
    kvg = P.alloc([128, 1])
    P.dma(qg, din["mla_qg"].ap()[l], [], ["mla_qg"])
    P.dma(kvg, din["mla_kvg"].ap()[l], [], ["mla_kvg"])
    qT = [P.alloc([96, T], BF16) for _ in range(MLA_H)]
    kT = [P.alloc([96, T], BF16) for _ in range(MLA_H)]
    vtok = P.alloc([128, NTT, 256], BF16)
    m1 = P.mark()
    NB_ = 2
    cqa = Rot([P.alloc([128, 512]) for _ in range(NB_)], "m_cqa")
    cqb = Rot([P.alloc([64, 512]) for _ in range(NB_)], "m_cqb")
    ckv = Rot([P.alloc([128, 512]) for _ in range(NB_)], "m_ckv")
    krb = Rot([P.alloc([96, 2, 512]) for _ in range(NB_)], "m_kr")
    tabs = Rot([P.alloc([96, 2, 512]) for _ in range(NB_)], "m_tab")
    sq = P.alloc([128, 3, 512])
    rstd = P.alloc([128, 2, 512])
    cqn_a = P.alloc([128, 512], BF16)
    cqn_b = P.alloc([64, 512], BF16)
    ckvn = P.alloc([128, 512], BF16)
    t1 = P.alloc([96, 512])
    t2 = P.alloc([96, 512])
    for ch, (c0, cn) in enumerate(CH512):
        a, ak = cqa.next()
        b, bk = cqb.next()
        kv, kvk = ckv.next()
        kr, krk = krb.next()
        tb, tbk = tabs.next()
        P.dma(a[:, 0:cn], pT[PT_CQA:PT_CQA + 128, c0:c0 + cn], ["pT_d"], [ak])
        P.dma(b[:, 0:cn], pT[PT_CQB:PT_CQB + 64, c0:c0 + cn], ["pT_d"], [bk])
        P.dma(kv[:, 0:cn], pT[PT_CKV:PT_CKV + 128, c0:c0 + cn], ["pT_d"], [kvk])
        P.dma(kr[64:96, 0, 0:cn], pT[PT_KR:PT_KR + 32, c0:c0 + cn], ["pT_d"], [krk])
        P.dma(kr[64:96, 1, 0:cn], pT[PT_KRS:PT_KRS + 32, c0:c0 + cn], ["pT_d"], [krk])
        P.dma(tb[:, 0, 0:cn], din["mla_C"].ap()[:, c0:c0 + cn], [], [tbk])
        P.dma(tb[:, 1, 0:cn], din["mla_S"].ap()[:, c0:c0 + cn], [], [tbk])
        P.act(sq[:, 0, 0:cn], a[:, 0:cn], AF.Square, [ak], ["m_sq"])
        P.act(sq[0:64, 1, 0:cn], b[:, 0:cn], AF.Square, [bk], ["m_sq"])
        P.act(sq[:, 2, 0:cn], kv[:, 0:cn], AF.Square, [kvk], ["m_sq"])
        ps, pk = psrot.next()
        P.mm(ps[:, 0:cn], C.ones_f[:, :], sq[:, 0, 0:cn], True, False, ["m_sq", "ones_f"], [pk])
        P.mm(ps[:, 0:cn], C.ones_f[0:64, :], sq[0:64, 1, 0:cn], False, True, ["m_sq", "ones_f"], [pk])
        P.act(rstd[:, 0, 0:cn], ps[:, 0:cn], AF.Sqrt, [pk, "eps_t"], ["m_rstd"], bias=C.eps_t[:], scale=1.0 / 192)
        ps2, pk2 = psrot.next()
        P.mm(ps2[:, 0:cn], C.ones_f[:, :], sq[:, 2, 0:cn], True, True, ["m_sq", "ones_f"], [pk2])
        P.act(rstd[:, 1, 0:cn], ps2[:, 0:cn], AF.Sqrt, [pk2, "eps_t"], ["m_rstd"], bias=C.eps_t[:], scale=1.0 / 128)
        P.v("dve", "reciprocal", ["m_rstd"], ["m_rstd"], rstd[:, :, 0:cn], rstd[:, :, 0:cn])
        P.v("dve", "scalar_tensor_tensor", [ak, "mla_qg", "m_rstd"], ["m_cqn_a"], cqn_a[:, 0:cn], a[:, 0:cn], qg[:, 0:1],
            rstd[:, 0, 0:cn], ALU.mult, ALU.mult)
        P.v("dve", "scalar_tensor_tensor", [bk, "mla_qg", "m_rstd"], ["m_cqn_b"], cqn_b[:, 0:cn], b[:, 0:cn], qg[0:64, 1:2],
            rstd[0:64, 0, 0:cn], ALU.mult, ALU.mult)
        P.v("dve", "scalar_tensor_tensor", [kvk, "mla_kvg", "m_rstd"], ["m_ckvn"], ckvn[:, 0:cn], kv[:, 0:cn], kvg[:, 0:1],
            rstd[:, 1, 0:cn], ALU.mult, ALU.mult)
        for h in range(MLA_H):
            psA, pkA = psrot.next()
            psB, pkB = psrot.next()
            for (pp, ppk, ab) in ((psA, pkA, 0), (psB, pkB, 1)):
                w0 = (ab * MLA_H + h) * QK
                P.mm(pp[0:96, 0:cn], wq[:, 0, w0:w0 + QK], cqn_a[:, 0:cn], True, False, ["mla_wq", "m_cqn_a"], [ppk])
                P.mm(pp[0:96, 0:cn], wq[0:64, 1, w0:w0 + QK], cqn_b[:, 0:cn], False, True, ["mla_wq", "m_cqn_b"], [ppk])
            P.v("dve", "tensor_tensor", [pkA, tbk], ["m_t1"], t1[:, 0:cn], psA[0:96, 0:cn], tb[:, 0, 0:cn], ALU.mult)
            P.v("dve", "tensor_tensor", [pkB, tbk], ["m_t2"], t2[:, 0:cn], psB[0:96, 0:cn], tb[:, 1, 0:cn], ALU.mult)
            P.v("pool", "tensor_tensor", ["m_t1", "m_t2"], ["m_qT%d" % h], qT[h][:, c0:c0 + cn], t1[:, 0:cn], t2[:, 0:cn], ALU.add)
        for h in range(MLA_H):
            ps, pk = psrot.next()
            P.mm(ps[0:64, 0:cn], wkv[:, h * 64:(h + 1) * 64], ckvn[:, 0:cn], True, True, ["mla_wkv", "m_ckvn"], [pk])
            P.act(kT[h][0:64, c0:c0 + cn], ps[0:64, 0:cn], AF.Copy, [pk], ["m_kT%d" % h])
        P.v("dve", "tensor_tensor", [krk, tbk], ["m_t1"], t1[64:96, 0:cn], kr[64:96, 0, 0:cn], tb[64:96, 0, 0:cn], ALU.mult)
        P.v("dve", "tensor_tensor", [krk, tbk], ["m_t2"], t2[64:96, 0:cn], kr[64:96, 1, 0:cn], tb[64:96, 1, 0:cn], ALU.mult)
        P.v("dve", "tensor_tensor", ["m_t1", "m_t2"], ["m_t1"], t1[64:96, 0:cn], t1[64:96, 0:cn], t2[64:96, 0:cn], ALU.add)
        for h in range(MLA_H):
            if h % 2 == 0:
                P.act(kT[h][64:96, c0:c0 + cn], t1[64:96, 0:cn], AF.Copy, ["m_t1"], ["m_kT%d" % h])
            else:
                P.v("pool", "tensor_copy", ["m_t1"], ["m_kT%d" % h], kT[h][64:96, c0:c0 + cn], t1[64:96, 0:cn])
        for i in range(cn // 128):
            tt = c0 // 128 + i
            ps, pk = psrot.next()
            P.mm(ps[:, 0:256], ckvn[:, i * 128:(i + 1) * 128], wkv[:, 256:512], True, True, ["mla_wkv", "m_ckvn"], [pk])
            P.act(vtok[:, tt, :], ps[:, 0:256], AF.Copy, [pk], ["m_vtok"])
    P.barrier()
    P.release(m1)
    srot = Rot([p[:] for p in C.pss[0:6]], "ps")
    ps_o, pk_o = C.pss[6][:], "ps6"
    ps_d, pk_d = C.pss[7][:], "ps7"
    pT_b = Rot([P.alloc([128, 512], BF16) for _ in range(6)], "m_pT")
    rden = P.alloc([64, 512])
    ost = Rot([P.alloc([64, 512], BF16) for _ in range(2)], "m_ost")
    scale = 1.0 / math.sqrt(QK)
    blocks = [(CTX + i * 512, 512, 0, NTT) for i in range(SEQ // 512)]
    if not last:
        blocks = [(0, CTX, 0, 2)] + blocks
    steps = []
    for (q0, qn, kt0, kt1) in blocks:
        for h in range(MLA_H):
            for kt in range(kt0, kt1):
                steps.append((q0, qn, kt0, kt1, h, kt))
    LOOK = 2
    pend = {}

    def issue_scores(i):
        q0, qn, kt0, kt1, h, kt = steps[i]
        ps, pk = srot.next()
        P.mm(ps[:, 0:qn], kT[h][:, kt * 128:(kt + 1) * 128], qT[h][:, q0:q0 + qn], True, True,
             ["m_kT%d" % h, "m_qT%d" % h], [pk])
        pb, pbk = pT_b.next()
        P.act(pb[:, 0:qn], ps[:, 0:qn], AF.Exp, [pk], [pbk], scale=scale)
        pend[i] = (pb, pbk)

    for i in range(min(LOOK, len(steps))):
        issue_scores(i)
    for i in range(len(steps)):
        if i + LOOK < len(steps):
            issue_scores(i + LOOK)
        q0, qn, kt0, kt1, h, kt = steps[i]
        pb, pbk = pend.pop(i)
        P.mm(ps_o[0:64, 0:qn], vtok[:, kt, h * 64:(h + 1) * 64], pb[:, 0:qn], kt == kt0, kt == kt1 - 1, [pbk, "m_vtok"], [pk_o])
        P.mm(ps_d[0:64, 0:qn], C.ones_b[:, 0:64], pb[:, 0:qn], kt == kt0, kt == kt1 - 1, [pbk, "ones_b"], [pk_d])
        if kt == kt1 - 1:
            P.v("dve", "reciprocal", [pk_d], ["m_rden"], rden[:, 0:qn], ps_d[0:64, 0:qn])
            ob, obk = ost.next()
            P.v("dve", "tensor_tensor", [pk_o, "m_rden"], [obk], ob[:, 0:qn], ps_o[0:64, 0:qn], rden[:, 0:qn], ALU.mult)
            P.dma(C.oT_d.ap()[OT_MLA + h * 64:OT_MLA + (h + 1) * 64, q0:q0 + qn], ob[:, 0:qn], [obk], ["oT_d"], q="pool")
    P.barrier()
    P.release(m0)


MIXERS["mla"] = mix_mla
HOSTPREP["mla"] = host_mla


RH = 4
PT_RQ, PT_RK, PT_RG, PT_RQS, PT_RKS = 256, 512, 768, 1024, 1280
OT_RET = 512
LN2 = math.log(2.0)


def host_ret(inp):
    L = DEPTH
    o = {}
    f32 = np.float32
    theta = (f32(10000.0) ** (-np.linspace(0.0, 1.0, 32, dtype=f32))).astype(f32)
    ang = (np.arange(SEQ, dtype=f32)[:, None] * theta).astype(f32)
    cos, sin = np.cos(ang).astype(f32), np.sin(ang).astype(f32)
    Ct = np.ones((T, 64), f32)
    St = np.zeros((T, 64), f32)
    Ct[CTX:, 0:32] = cos
    Ct[CTX:, 32:64] = cos
    St[CTX:, 0:32] = -sin
    St[CTX:, 32:64] = sin
    o["ret_Ct"] = Ct
    o["ret_St"] = St
    o["ret_C"] = np.ascontiguousarray(np.tile(Ct.T, (2, 1)))
    o["ret_S"] = np.ascontiguousarray(np.tile(St.T, (2, 1)))
    j = np.arange(128, dtype=f32)[:, None]
    i = np.arange(128, dtype=f32)[None, :]
    cst = np.zeros((128, 6, 128), f32)
    cst[:, 0, :] = np.maximum(i - j, 0)
    cst[:, 1, :] = np.maximum(j - i, 0)
    cst[:, 2, :] = (i >= j)
    cst[:, 3, :] = (j >= i)
    cst[:, 4, :] = i + 1
    cst[:, 5, :] = 128 - i
    o["ret_cst"] = cst
    col = np.zeros((128, 2), f32)
    col[:, 0] = 127 - np.arange(128)
    col[:, 1] = np.arange(128)
    o["ret_col"] = col
    hm = np.zeros((128, 2), f32)
    hm[:64, 0] = 1.0
    hm[64:, 1] = 1.0
    o["ret_hm"] = hm
    o["ret_dexp"] = np.ascontiguousarray(inp["ret_decay_exp"].reshape(L, 1, 8))
    o["ret_gnT"] = np.ascontiguousarray(inp["ret_gn_g"].reshape(L, RH, 64).transpose(0, 2, 1))
    return o


def mix_ret(C, l, last):
    P, B = C.P, C.B
    psrot = C.psrot
    if "ret_C" not in B.din:
        B.inp("ret_Ct", [T, 64]); B.inp("ret_St", [T, 64])
        B.inp("ret_C", [128, T]); B.inp("ret_S", [128, T])
        B.inp("ret_cst", [128, 6, 128]); B.inp("ret_col", [128, 2]); B.inp("ret_hm", [128, 2])
        B.inp("ret_dexp", [DEPTH, 1, 8]); B.inp("ret_gnT", [DEPTH, 64, 4])
    din = B.din
    pT = C.pT_d.ap()
    m0 = P.mark()
    cst = P.alloc([128, 6, 128])
    P.dma(cst, din["ret_cst"].ap(), [], ["r_cst"])
    colc = P.alloc([128, 2])
    P.dma(colc, din["ret_col"].ap(), [], ["r_col"])
    gnT = P.alloc([64, 4])
    P.dma(gnT, din["ret_gnT"].ap()[l], [], ["r_gnT"])
    lg8 = P.alloc([128, 8])
    P.dma(lg8, din["ret_dexp"].ap()[l, 0:1, :].partition_broadcast(128)[:, 0, :], [], ["r_lg8"])
    P.act(lg8, lg8, AF.Exp, ["r_lg8"], ["r_lg8"], scale=-LN2)
    P.act(lg8, lg8, AF.Ln, ["r_lg8"], ["r_lg8"], scale=-1.0, bias=1.0)
    lgp = P.alloc([128, 2, 2])
    for d_ in range(2):
        for pr in range(2):
            P.v("dve", "tensor_copy", ["r_lg8"], ["r_lgp"], lgp[0:64, d_, pr:pr + 1], lg8[0:64, d_ * 4 + 2 * pr:d_ * 4 + 2 * pr + 1])
            P.v("dve", "tensor_copy", ["r_lg8"], ["r_lgp"], lgp[64:128, d_, pr:pr + 1], lg8[64:128, d_ * 4 + 2 * pr + 1:d_ * 4 + 2 * pr + 2])
    Mk = P.alloc([128, 4, 128])
    mt = P.alloc([128, 128])
    for h in range(RH):
        P.act(Mk[:, h, :], cst[:, 0, :], AF.Exp, ["r_cst", "r_lg8"], ["r_Mk"], scale=lg8[:, h:h + 1])
        P.v("dve", "tensor_tensor", ["r_Mk", "r_cst"], ["r_Mk"], Mk[:, h, :], Mk[:, h, :], cst[:, 2, :], ALU.mult)
        P.act(mt, cst[:, 1, :], AF.Exp, ["r_cst", "r_lg8"], ["r_mt"], scale=lg8[:, 4 + h:5 + h])
        P.v("dve", "tensor_tensor", ["r_mt", "r_cst"], ["r_mt"], mt, mt, cst[:, 3, :], ALU.mult)
        P.v("dve", "tensor_tensor", ["r_mt", "r_Mk"], ["r_Mk"], Mk[:, h, :], Mk[:, h, :], mt, ALU.add)
    XI = P.alloc([128, 2, 2, 128])
    gcolp = P.alloc([128, 2, 2])
    for d_ in range(2):
        for pr in range(2):
            P.act(XI[:, d_, pr, :], cst[:, 4 + d_, :], AF.Exp, ["r_cst", "r_lgp"], ["r_XI"], scale=lgp[:, d_, pr:pr + 1])
            P.act(gcolp[:, d_, pr:pr + 1], lgp[:, d_, pr:pr + 1], AF.Exp, ["r_lgp"], ["r_gcol"], scale=128.0)
    Z = P.alloc([128, 2, 4])
    for d_ in range(2):
        for h in range(RH):
            P.act(Z[:, d_, h:h + 1], colc[:, d_:d_ + 1], AF.Exp, ["r_col", "r_lg8"], ["r_Z"], scale=lg8[:, d_ * 4 + h:d_ * 4 + h + 1])
    hm = P.alloc([128, 2])
    P.dma(hm, din["ret_hm"].ap(), [], ["r_hm"])
    TAB = P.alloc([128, 3, 4, 128])
    for h in range(RH):
        pr, hp = h // 2, h % 2
        P.v("dve", "tensor_copy", ["r_hm"], ["r_TAB"], TAB[:, 0, h, :], hm[:, hp:hp + 1].to_broadcast([128, 128]))
        for d_ in range(2):
            P.v("dve", "tensor_scalar", ["r_XI", "r_hm"], ["r_TAB"], TAB[:, 1 + d_, h, :], XI[:, d_, pr, :], hm[:, hp:hp + 1], None, ALU.mult)
    if C.cfg.get("ret_stop", 9) <= 0:
        P.barrier(); P.release(m0); return
    qr = [P.alloc([128, T], BF16) for _ in range(2)]
    kr = [P.alloc([128, T], BF16) for _ in range(2)]
    vt = P.alloc([128, NTT, 256], BF16)
    Sall = [[P.alloc([128, NTT, 128], BF16) for _ in range(2)] for _ in range(2)]
    m1 = P.mark()
    kz = [P.alloc([128, NTT, 256], BF16) for _ in range(2)]
    m2 = P.mark()
    ld = Rot([P.alloc([128, 4, 512]) for _ in range(2)], "r_ld")
    tb = Rot([P.alloc([128, 2, 512]) for _ in range(2)], "r_tb")
    t1 = P.alloc([128, 512]); t2 = P.alloc([128, 512])
    for ch, (c0, cn) in enumerate(CH512):
        tbv, tbk = tb.next()
        P.dma(tbv[:, 0, 0:cn], din["ret_C"].ap()[:, c0:c0 + cn], [], [tbk])
        P.dma(tbv[:, 1, 0:cn], din["ret_S"].ap()[:, c0:c0 + cn], [], [tbk])
        nq = cn // 128
        for pr in range(2):
            lv, lk = ld.next()
            for ii, r0 in enumerate((PT_RQ, PT_RQS, PT_RK, PT_RKS)):
                P.dma(lv[:, ii, 0:cn], pT[r0 + pr * 128:r0 + (pr + 1) * 128, c0:c0 + cn], ["pT_d"], [lk])
            P.v("dve", "tensor_tensor", [lk, tbk], ["r_t1"], t1[:, 0:cn], lv[:, 0, 0:cn], tbv[:, 0, 0:cn], ALU.mult)
            P.v("pool", "tensor_tensor", [lk, tbk], ["r_t2"], t2[:, 0:cn], lv[:, 1, 0:cn], tbv[:, 1, 0:cn], ALU.mult)
            P.v("dve", "tensor_tensor", ["r_t1", "r_t2"], ["r_t1"], t1[:, 0:cn], t1[:, 0:cn], t2[:, 0:cn], ALU.add)
            P.act(qr[pr][:, c0:c0 + cn], t1[:, 0:cn], AF.Copy, ["r_t1"], ["r_qr%d" % pr], scale=0.125)
            P.v("dve", "tensor_tensor", [lk, tbk], ["r_t1"], t1[:, 0:cn], lv[:, 2, 0:cn], tbv[:, 0, 0:cn], ALU.mult)
            P.v("pool", "tensor_tensor", [lk, tbk], ["r_t2"], t2[:, 0:cn], lv[:, 3, 0:cn], tbv[:, 1, 0:cn], ALU.mult)
            P.v("pool", "tensor_tensor", ["r_t1", "r_t2"], ["r_kr%d" % pr], kr[pr][:, c0:c0 + cn], t1[:, 0:cn], t2[:, 0:cn], ALU.add)
    if C.cfg.get("ret_stop", 9) <= 0.5:
        P.barrier(); P.release(m0); return
    tl = Rot([P.alloc([128, 768]) for _ in range(3)], "r_tl")
    tt_tab = Rot([P.alloc([128, 2, 64]) for _ in range(3)], "r_ttab")
    kk1 = P.alloc([128, 4, 64]); kk2 = P.alloc([128, 4, 64])
    for tt in range(NTT):
        tv, tk = tl.next()
        P.dma(tv, C.rtok_d.ap()[tt * 128:(tt + 1) * 128, :], ["rtok_d"], [tk])
        tab, tabk = tt_tab.next()
        P.dma(tab[:, 0, :], din["ret_Ct"].ap()[tt * 128:(tt + 1) * 128, :], [], [tabk])
        P.dma(tab[:, 1, :], din["ret_St"].ap()[tt * 128:(tt + 1) * 128, :], [], [tabk])
        kview = tv[:, 0:256].rearrange("p (h d) -> p h d", h=4)
        ksview = tv[:, 256:512].rearrange("p (h d) -> p h d", h=4)
        P.v("dve", "tensor_tensor", [tk, tabk], ["r_kk1"], kk1, kview, tab[:, 0, :].unsqueeze(1).to_broadcast([128, 4, 64]), ALU.mult)
        P.v("pool", "tensor_tensor", [tk, tabk], ["r_kk2"], kk2, ksview, tab[:, 1, :].unsqueeze(1).to_broadcast([128, 4, 64]), ALU.mult)
        P.v("dve", "tensor_tensor", ["r_kk1", "r_kk2"], ["r_kk1"], kk1, kk1, kk2, ALU.add)
        for d_ in range(2):
            P.v("dve" if d_ == 0 else "pool", "tensor_tensor", ["r_kk1", "r_Z"], ["r_kz%d" % d_],
                kz[d_][:, tt, :].rearrange("p (h d) -> p h d", h=4), kk1,
                Z[:, d_, :].unsqueeze(2).to_broadcast([128, 4, 64]), ALU.mult)
        P.act(vt[:, tt, :], tv[:, 512:768], AF.Copy, [tk], ["r_vt"])
    if C.cfg.get("ret_stop", 9) <= 1:
        P.barrier(); P.release(m0); return
    Srun = [[[P.alloc([128, 128]) for _ in range(2)] for _ in range(2)] for _ in range(2)]
    order = [list(range(NTT)), [1, 0] + list(range(NTT - 1, 1, -1))]
    for d_ in range(2):
        for pr in range(2):
            P.v("dve", "memset", [], ["r_S%d%d" % (d_, pr)], Sall[d_][pr][:, order[d_][0], :], 0.0)
    for s in range(NTT - 1):
        for d_ in range(2):
            for pr in range(2):
                n = order[d_][s]
                nxt = order[d_][s + 1]
                ps, pk = psrot.next()
                cur = Srun[d_][pr][s % 2]
                new = Srun[d_][pr][(s + 1) % 2]
                ck = "r_Sr%d%d%d" % (d_, pr, s % 2)
                nk = "r_Sr%d%d%d" % (d_, pr, (s + 1) % 2)
                P.mm(ps[:, 0:128], kz[d_][:, n, pr * 128:(pr + 1) * 128], vt[:, n, pr * 128:(pr + 1) * 128], True, True,
                     ["r_kz%d" % d_, "r_vt"], [pk])
                if s > 0:
                    P.v("dve", "scalar_tensor_tensor", [pk, ck, "r_gcol"], [nk], new, cur, gcolp[:, d_, pr:pr + 1], ps[:, 0:128],
                        ALU.mult, ALU.add)
                else:
                    P.v("dve", "tensor_copy", [pk], [nk], new, ps[:, 0:128])
                P.act(Sall[d_][pr][:, nxt, :], new, AF.Copy, [nk], ["r_S%d%d" % (d_, pr)])
    P.barrier()
    P.release(m1)
    if C.cfg.get("ret_stop", 9) <= 2:
        P.barrier(); P.release(m0); return
    msk = Rot([P.alloc([128, 4, 128], BF16) for _ in range(2)], "r_msk")
    gld = Rot([P.alloc([64, 4, 128]) for _ in range(2)], "r_gld")
    sqo = P.alloc([64, 512])
    rstd = P.alloc([64, 512])
    on = P.alloc([64, 4, 128])
    ost = Rot([P.alloc([64, 4, 128], BF16) for _ in range(2)], "r_ost")
    qtr = Rot([P.alloc([128, 3, 2, 128], BF16) for _ in range(4)], "r_qt")
    gsrc = pT[PT_RG:PT_RG + 256, :].rearrange("(h e) t -> e h t", h=4)
    odst = C.oT_d.ap()[OT_RET:OT_RET + 256, :].rearrange("(h e) t -> e h t", h=4)
    for n in range(2 if last else 0, NTT):
        cs = slice(n * 128, (n + 1) * 128)
        gv, gk = gld.next()
        P.dma(gv, gsrc[:, :, cs], ["pT_d"], [gk])
        P.act(gv, gv, AF.Silu, [gk], [gk])
        QT = []
        for pr in range(2):
            qv, qk = qtr.next()
            P.v("dve" if pr == 0 else "pool", "tensor_tensor", ["r_qr%d" % pr, "r_TAB"], [qk], qv,
                qr[pr][:, cs].unsqueeze(1).unsqueeze(1).to_broadcast([128, 3, 2, 128]), TAB[:, :, 2 * pr:2 * pr + 2, :], ALU.mult)
            QT.append((qv, qk))
        ps_s, pk_s = psrot.next()
        for h in range(RH):
            pr, hp = h // 2, h % 2
            P.mm(ps_s[:, h * 128:(h + 1) * 128], kr[pr][:, cs], QT[pr][0][:, 0, hp, :], True, True,
                 ["r_kr%d" % pr, QT[pr][1]], [pk_s])
        mv, mk = msk.next()
        P.v("dve", "tensor_tensor", [pk_s, "r_Mk"], [mk], mv, ps_s.rearrange("p (h i) -> p h i", h=4), Mk, ALU.mult)
        ps_o, pk_o = psrot.next()
        for h in range(RH):
            pr, hp = h // 2, h % 2
            hs = slice(hp * 64, (hp + 1) * 64)
            oo = ps_o[0:64, h * 128:(h + 1) * 128]
            P.mm(oo, vt[:, n, h * 64:(h + 1) * 64], mv[:, h, :], True, False, ["r_vt", mk], [pk_o])
            P.mm(oo, Sall[0][pr][:, n, hs], QT[pr][0][:, 1, hp, :], False, False, ["r_S0%d" % pr, QT[pr][1]], [pk_o])
            P.mm(oo, Sall[1][pr][:, n, hs], QT[pr][0][:, 2, hp, :], False, True, ["r_S1%d" % pr, QT[pr][1]], [pk_o])
        P.act(sqo, ps_o[0:64, :], AF.Square, [pk_o], ["r_sqo"])
        ps_n, pk_n = psrot.next()
        P.mm(ps_n[0:64, :], C.ones_f[0:64, 0:64], sqo, True, True, ["r_sqo", "ones_f"], [pk_n])
        P.act(rstd, ps_n[0:64, :], AF.Sqrt, [pk_n, "eps_t"], ["r_rstd"], bias=C.eps_t[0:64, :], scale=1.0 / 64)
        P.v("dve", "reciprocal", ["r_rstd"], ["r_rstd"], rstd, rstd)
        P.v("dve", "tensor_tensor", [pk_o, "r_rstd"], ["r_on"], on, ps_o[0:64, :].rearrange("p (h i) -> p h i", h=4),
            rstd.rearrange("p (h i) -> p h i", h=4), ALU.mult)
        P.v("pool", "tensor_tensor", ["r_on", "r_gnT"], ["r_on"], on, on, gnT.unsqueeze(2).to_broadcast([64, 4, 128]), ALU.mult)
        ov, ok = ost.next()
        P.v("pool", "tensor_tensor", ["r_on", gk], [ok], ov, on, gv, ALU.mult)
        P.dma(odst[:, :, cs], ov, [ok], ["oT_d"], q="pool")
    P.barrier()
    P.release(m0)


MIXERS["ret"] = mix_ret
HOSTPREP["ret"] = host_ret


TWO_PI = 2.0 * math.pi
S5_SEGS = [(0, 256)] + [(256 + 1024 * k, 1024) for k in range(4)]


def host_s5(inp):
    L = DEPTH
    f32 = np.float32
    o = {}
    lam_re = inp["s5_lam_re"]; lam_im = inp["s5_lam_im"]
    ldt = np.repeat(inp["s5_log_dt"][..., None], 64, axis=-1)
    prow = np.stack([lam_re, lam_im, ldt], axis=1)
    o["s5_prow"] = np.ascontiguousarray(prow.reshape(L, 3, 1, 2048)).astype(f32)
    pT_ = prow.reshape(L, 3, 2, 8, 2, 64).transpose(0, 1, 4, 5, 2, 3)
    o["s5_pT"] = np.ascontiguousarray(pT_.reshape(L, 3, 128, 16)).astype(f32)
    Bexp = np.zeros((L, 2, 2, 128, 8, 128), f32)
    Cexp = np.zeros((L, 2, 2, 128, 8, 128), f32)
    for ri, (bk, ck) in enumerate((("s5_b_re", "s5_c_re"), ("s5_b_im", "s5_c_im"))):
        b = inp[bk]
        c = inp[ck]
        for m in range(8):
            for a in range(2):
                g = 2 * m + a
                q0 = 32 * (m % 4) + 16 * a
                Bexp[:, ri, :, q0:q0 + 16, m, 64 * a:64 * a + 64] = b[:, :, g].transpose(0, 1, 3, 2)
                Cexp[:, ri, :, 64 * a:64 * a + 64, m, q0:q0 + 16] = c[:, :, g].transpose(0, 1, 3, 2)
    o["s5_Bexp"] = Bexp
    o["s5_Cexp"] = Cexp
    o["s5_dT"] = np.ascontiguousarray(inp["s5_d"].reshape(L, 2, 128).transpose(0, 2, 1))
    o["s5_gbT"] = np.ascontiguousarray(inp["s5_glu_b"].reshape(L, 2, 128).transpose(0, 2, 1))
    o["s5_gw"] = np.ascontiguousarray(inp["s5_glu_w"])
    o["s5_iota"] = np.ascontiguousarray(np.broadcast_to(np.arange(1024, dtype=f32), (128, 1024)))
    return o


def rev_ap(ap):
    (ps, pn), (fs, fn) = ap.ap
    return bass.AP(ap.tensor, ap.offset + (fn - 1) * fs, [[ps, pn], [-fs, fn]])


def frac_round(P, eng_c, x, ki, kf, keys):
    xk, kik, kfk = keys
    P.v(eng_c, "tensor_copy", [xk], [kik], ki, x)
    P.v(eng_c, "tensor_copy", [kik], [kfk], kf, ki)
    P.v("dve", "tensor_tensor", [xk, kfk], [xk], x, x, kf, ALU.subtract)


def mix_s5(C, l, last):
    P, B = C.P, C.B
    psrot = C.psrot
    if "s5_prow" not in B.din:
        B.inp("s5_prow", [DEPTH, 3, 1, 2048]); B.inp("s5_pT", [DEPTH, 3, 128, 16])
        B.inp("s5_Bexp", [DEPTH, 2, 2, 128, 8, 128]); B.inp("s5_Cexp", [DEPTH, 2, 2, 128, 8, 128])
        B.inp("s5_dT", [DEPTH, 128, 2]); B.inp("s5_gbT", [DEPTH, 128, 2]); B.inp("s5_gw", [DEPTH, 256, 256])
        B.inp("s5_iota", [128, 1024])
    din = B.din
    pT = C.pT_d.ap()
    m0 = P.mark()
    NW = 2048
    LB = [[P.alloc([128, 2, 8, 128], BF16) for _ in range(2)]]
    LB = LB[0]
    LC = [P.alloc([128, 2, 8, 128], BF16) for _ in range(2)]
    rP = P.alloc([128, 16])
    thP = P.alloc([128, 16])
    m1 = P.mark()
    pr_ = [P.alloc([128, NW]) for _ in range(3)]
    for i in range(3):
        P.dma(pr_[i], din["s5_prow"].ap()[l, i, 0:1, :].partition_broadcast(128)[:, 0, :], [], ["s5_pr%d" % i])
    lre, lim, dt = pr_
    P.act(dt, dt, AF.Exp, ["s5_pr2"], ["s5_pr2"])
    re = P.alloc([128, NW]); im = P.alloc([128, NW])
    P.v("dve", "tensor_tensor", ["s5_pr0", "s5_pr2"], ["s5_re"], re, lre, dt, ALU.mult)
    P.v("dve", "tensor_tensor", ["s5_pr1", "s5_pr2"], ["s5_im"], im, lim, dt, ALU.mult)
    r_ = P.alloc([128, NW])
    P.act(r_, re, AF.Exp, ["s5_re"], ["s5_r"])
    ki = P.alloc([128, NW], I32); kf = P.alloc([128, NW])
    ph = P.alloc([128, NW]); ph2 = P.alloc([128, NW])
    P.v("dve", "tensor_scalar", ["s5_im"], ["s5_ph"], ph, im, 1.0 / TWO_PI, None, ALU.mult)
    P.v("dve", "tensor_scalar", ["s5_ph"], ["s5_ph2"], ph2, ph, 0.25, None, ALU.add)
    frac_round(P, "dve", ph, ki, kf, ("s5_ph", "s5_ki", "s5_kf"))
    frac_round(P, "dve", ph2, ki, kf, ("s5_ph2", "s5_ki", "s5_kf"))
    sn = ph; cs_ = ph2
    P.act(sn, ph, AF.Sin, ["s5_ph"], ["s5_ph"], scale=TWO_PI)
    P.act(cs_, ph2, AF.Sin, ["s5_ph2"], ["s5_ph2"], scale=TWO_PI)
    nre = re; nim = im
    P.v("dve", "tensor_tensor", ["s5_r", "s5_ph2"], ["s5_re"], nre, r_, cs_, ALU.mult)
    P.v("dve", "tensor_scalar", ["s5_re"], ["s5_re"], nre, nre, -1.0, None, ALU.add)
    P.v("dve", "tensor_tensor", ["s5_r", "s5_ph"], ["s5_im"], nim, r_, sn, ALU.mult)
    den = r_; tmp = kf
    P.v("dve", "tensor_tensor", ["s5_pr0", "s5_re", "s5_im"], ["s5_r"], den, lre, lre, ALU.mult)
    P.v("dve", "tensor_tensor", ["s5_pr1", "s5_kf"], ["s5_kf"], tmp, lim, lim, ALU.mult)
    P.v("dve", "tensor_tensor", ["s5_r", "s5_kf"], ["s5_r"], den, den, tmp, ALU.add)
    P.v("dve", "reciprocal", ["s5_r"], ["s5_r"], den, den)
    cre = ph; cim = ph2
    P.v("dve", "tensor_tensor", ["s5_re", "s5_pr0", "s5_ph"], ["s5_ph"], cre, nre, lre, ALU.mult)
    P.v("dve", "tensor_tensor", ["s5_im", "s5_pr1"], ["s5_kf"], tmp, nim, lim, ALU.mult)
    P.v("dve", "tensor_tensor", ["s5_ph", "s5_kf"], ["s5_ph"], cre, cre, tmp, ALU.add)
    P.v("dve", "tensor_tensor", ["s5_ph", "s5_r"], ["s5_ph"], cre, cre, den, ALU.mult)
    P.v("dve", "tensor_tensor", ["s5_im", "s5_pr0", "s5_ph2"], ["s5_ph2"], cim, nim, lre, ALU.mult)
    P.v("dve", "tensor_tensor", ["s5_re", "s5_pr1"], ["s5_kf"], tmp, nre, lim, ALU.mult)
    P.v("dve", "tensor_tensor", ["s5_ph2", "s5_kf"], ["s5_ph2"], cim, cim, tmp, ALU.subtract)
    P.v("dve", "tensor_tensor", ["s5_ph2", "s5_r"], ["s5_ph2"], cim, cim, den, ALU.mult)
    bre = pr_[0]; bim = pr_[1]; t_a = pr_[2]; t_b = re
    P.dma(bre.rearrange("p (d x) -> p d x", d=2), din["s5_Bexp"].ap()[l, 0].rearrange("d q m s -> q d (m s)"), ["s5_ph", "s5_ph2"], ["s5_pr0"], q="pool")
    P.dma(bim.rearrange("p (d x) -> p d x", d=2), din["s5_Bexp"].ap()[l, 1].rearrange("d q m s -> q d (m s)"), ["s5_ph", "s5_ph2"], ["s5_pr1"], q="pool")
    P.v("dve", "tensor_tensor", ["s5_ph", "s5_pr0"], ["s5_pr2"], t_a, cre, bre, ALU.mult)
    P.v("dve", "tensor_tensor", ["s5_ph2", "s5_pr1"], ["s5_re"], t_b, cim, bim, ALU.mult)
    P.v("dve", "tensor_tensor", ["s5_pr2", "s5_re"], ["s5_LB"], LB[0].rearrange("p d m s -> p (d m s)"), t_a, t_b, ALU.subtract)
    P.v("dve", "tensor_tensor", ["s5_ph", "s5_pr1"], ["s5_pr2"], t_a, cre, bim, ALU.mult)
    P.v("dve", "tensor_tensor", ["s5_ph2", "s5_pr0"], ["s5_re"], t_b, cim, bre, ALU.mult)
    P.v("dve", "tensor_tensor", ["s5_pr2", "s5_re"], ["s5_LB"], LB[1].rearrange("p d m s -> p (d m s)"), t_a, t_b, ALU.add)
    cst_ = im
    P.dma(cst_.rearrange("p (d x) -> p d x", d=2), din["s5_Cexp"].ap()[l, 0].rearrange("d q m s -> q d (m s)"), ["s5_im"], ["s5_im"], q="pool")
    P.act(LC[0].rearrange("p d m s -> p (d m s)"), cst_, AF.Copy, ["s5_im"], ["s5_LC"])
    cst2 = kf
    P.dma(cst2.rearrange("p (d x) -> p d x", d=2), din["s5_Cexp"].ap()[l, 1].rearrange("d q m s -> q d (m s)"), ["s5_kf"], ["s5_kf"], q="pool")
    P.act(LC[1].rearrange("p d m s -> p (d m s)"), cst2, AF.Copy, ["s5_kf"], ["s5_LC"], scale=-1.0)
    pP = P.alloc([128, 3, 16])
    P.dma(pP, din["s5_pT"].ap()[l].rearrange("i q n -> q i n"), [], ["s5_pP"])
    P.act(pP[:, 2, :], pP[:, 2, :], AF.Exp, ["s5_pP"], ["s5_pP"])
    P.v("dve", "tensor_tensor", ["s5_pP"], ["s5_rP"], rP, pP[:, 0, :], pP[:, 2, :], ALU.mult)
    P.act(rP, rP, AF.Exp, ["s5_rP"], ["s5_rP"])
    P.v("dve", "tensor_tensor", ["s5_pP"], ["s5_thP"], thP, pP[:, 1, :], pP[:, 2, :], ALU.mult)
    P.v("dve", "tensor_scalar", ["s5_thP"], ["s5_thP"], thP, thP, 1.0 / TWO_PI, None, ALU.mult)
    kiP = P.alloc([128, 16], I32); kfP = P.alloc([128, 16])
    frac_round(P, "dve", thP, kiP, kfP, ("s5_thP", "s5_kiP", "s5_kfP"))
    P.barrier()
    P.release(m1)
    ubf = P.alloc([128, 2, T], BF16)
    yacc = P.alloc([128, 2, T])
    iota = P.alloc([128, 1024])
    P.dma(iota, din["s5_iota"].ap(), [], ["s5_iota"])
    ust = Rot([P.alloc([128, 512]) for _ in range(2)], "s5_ust")
    for ut in range(2):
        for (c0, cn) in CH512:
            sv, sk = ust.next()
            P.dma(sv[:, 0:cn], pT[ut * 128:(ut + 1) * 128, c0:c0 + cn], ["pT_d"], [sk])
            P.act(ubf[:, ut, c0:c0 + cn], sv[:, 0:cn], AF.Copy, [sk], ["s5_ubf"])
    SEG = 1024
    NBUF = 2
    m_rot = P.mark()
    COS = Rot([P.alloc([128, SEG]) for _ in range(NBUF)], "s5_cos")
    SIN = Rot([P.alloc([128, SEG]) for _ in range(NBUF)], "s5_sin")
    PH = Rot([P.alloc([128, SEG]) for _ in range(NBUF)], "s5_phs")
    KI = Rot([P.alloc([128, SEG], I32) for _ in range(NBUF)], "s5_kis")
    KF = Rot([P.alloc([128, SEG]) for _ in range(NBUF)], "s5_kfs")
    BR = Rot([P.alloc([128, SEG]) for _ in range(NBUF)], "s5_br")
    BI = Rot([P.alloc([128, SEG]) for _ in range(NBUF)], "s5_bi")
    GR = Rot([P.alloc([128, SEG]) for _ in range(NBUF)], "s5_gr")
    GI = Rot([P.alloc([128, SEG]) for _ in range(NBUF)], "s5_gi")
    T1 = Rot([P.alloc([128, SEG]) for _ in range(NBUF)], "s5_t1")
    T2 = Rot([P.alloc([128, SEG]) for _ in range(NBUF)], "s5_t2")
    HR = Rot([P.alloc([128, SEG], BF16) for _ in range(NBUF)], "s5_hr")
    HI = Rot([P.alloc([128, SEG], BF16) for _ in range(NBUF)], "s5_hi")
    offs = P.alloc([128, 4])
    offi = P.alloc([128, 4], I32)
    glast = P.alloc([128, 2])
    first_acc = [True, True]
    for d_ in range(2):
        seg_order = list(range(5)) if d_ == 0 else [0, 4, 3, 2, 1]
        for m in range(8):
            ut = m // 4
            col = d_ * 8 + m
            sbase = 0
            for si, sg in enumerate(seg_order):
                t0, n = S5_SEGS[sg]
                P.v("dve", "tensor_scalar", ["s5_thP"], ["s5_offs"], offs[:, 0:1], thP[:, col:col + 1], float(sbase), None, ALU.mult)
                P.v("dve", "tensor_copy", ["s5_offs"], ["s5_offi"], offi[:, 0:1], offs[:, 0:1])
                P.v("dve", "tensor_copy", ["s5_offi"], ["s5_offs"], offs[:, 1:2], offi[:, 0:1])
                P.v("dve", "tensor_tensor", ["s5_offs"], ["s5_offs"], offs[:, 2:3], offs[:, 0:1], offs[:, 1:2], ALU.subtract)
                P.v("dve", "tensor_scalar", ["s5_offs"], ["s5_offs"], offs[:, 3:4], offs[:, 2:3], 0.25, None, ALU.add)
                phv, phk = PH.next(); kiv, kik = KI.next(); kfv, kfk = KF.next()
                cosv, cosk = COS.next(); sinv, sink = SIN.next()
                P.act(phv[:, 0:n], iota[:, 0:n], AF.Identity, ["s5_iota", "s5_thP", "s5_offs"], [phk],
                      scale=thP[:, col:col + 1], bias=offs[:, 2:3])
                frac_round(P, "dve", phv[:, 0:n], kiv[:, 0:n], kfv[:, 0:n], (phk, kik, kfk))
                P.act(sinv[:, 0:n], phv[:, 0:n], AF.Sin, [phk], [sink], scale=TWO_PI)
                P.act(phv[:, 0:n], iota[:, 0:n], AF.Identity, ["s5_iota", "s5_thP", "s5_offs"], [phk],
                      scale=thP[:, col:col + 1], bias=offs[:, 3:4])
                frac_round(P, "dve", phv[:, 0:n], kiv[:, 0:n], kfv[:, 0:n], (phk, kik, kfk))
                P.act(cosv[:, 0:n], phv[:, 0:n], AF.Sin, [phk], [cosk], scale=TWO_PI)
                brv, brk = BR.next(); biv, bik = BI.next()
                t1v, t1k = T1.next(); t2v, t2k = T2.next()
                for c0 in range(0, n, 512):
                    cn = min(512, n - c0)
                    psr, pkr = psrot.next()
                    psi, pki = psrot.next()
                    P.mm(psr[:, 0:cn], LB[0][:, d_, m, :], ubf[:, ut, t0 + c0:t0 + c0 + cn], True, True, ["s5_LB", "s5_ubf"], [pkr])
                    P.mm(psi[:, 0:cn], LB[1][:, d_, m, :], ubf[:, ut, t0 + c0:t0 + c0 + cn], True, True, ["s5_LB", "s5_ubf"], [pki])
                    if d_ == 0:
                        j0 = c0
                        sr, si_ = psr[:, 0:cn], psi[:, 0:cn]
                    else:
                        j0 = n - c0 - cn
                        sr, si_ = rev_ap(psr[:, 0:cn]), rev_ap(psi[:, 0:cn])
                    js = slice(j0, j0 + cn)
                    P.v("dve", "tensor_tensor", [pkr, cosk], [brk], brv[:, js], sr, cosv[:, js], ALU.mult)
                    P.v("dve", "tensor_tensor", [pki, sink], [t1k], t1v[:, js], si_, sinv[:, js], ALU.mult)
                    P.v("pool", "tensor_tensor", [brk, t1k], [brk], brv[:, js], brv[:, js], t1v[:, js], ALU.add)
                    P.v("dve", "tensor_tensor", [pki, cosk], [bik], biv[:, js], si_, cosv[:, js], ALU.mult)
                    P.v("dve", "tensor_tensor", [pkr, sink], [t2k], t2v[:, js], sr, sinv[:, js], ALU.mult)
                    P.v("pool", "tensor_tensor", [bik, t2k], [bik], biv[:, js], biv[:, js], t2v[:, js], ALU.subtract)
                grv, grk = GR.next(); giv, gik = GI.next()
                ini_r = 0.0 if si == 0 else glast[:, 0:1]
                ini_i = 0.0 if si == 0 else glast[:, 1:2]
                rb = rP[:, col:col + 1].to_broadcast([128, n])
                P.v("dve", "tensor_tensor_scan", [brk, "s5_rP", "s5_glast"], [grk], grv[:, 0:n], rb, brv[:, 0:n], ini_r, ALU.mult, ALU.add)
                P.v("dve", "tensor_tensor_scan", [bik, "s5_rP", "s5_glast"], [gik], giv[:, 0:n], rb, biv[:, 0:n], ini_i, ALU.mult, ALU.add)
                P.v("dve", "tensor_copy", [grk], ["s5_glast"], glast[:, 0:1], grv[:, n - 1:n])
                P.v("dve", "tensor_copy", [gik], ["s5_glast"], glast[:, 1:2], giv[:, n - 1:n])
                hrv, hrk = HR.next(); hiv, hik = HI.next()
                t1v, t1k = T1.next(); t2v, t2k = T2.next()
                ho_r = hrv[:, 0:n] if d_ == 0 else rev_ap(hrv[:, 0:n])
                ho_i = hiv[:, 0:n] if d_ == 0 else rev_ap(hiv[:, 0:n])
                P.v("pool", "tensor_tensor", [grk, cosk], [t1k], t1v[:, 0:n], grv[:, 0:n], cosv[:, 0:n], ALU.mult)
                P.v("dve", "tensor_tensor", [gik, sink], [t2k], t2v[:, 0:n], giv[:, 0:n], sinv[:, 0:n], ALU.mult)
                P.v("pool", "tensor_tensor", [t1k, t2k], [hrk], ho_r, t1v[:, 0:n], t2v[:, 0:n], ALU.subtract)
                t1v, t1k = T1.next(); t2v, t2k = T2.next()
                P.v("pool", "tensor_tensor", [grk, sink], [t1k], t1v[:, 0:n], grv[:, 0:n], sinv[:, 0:n], ALU.mult)
                P.v("dve", "tensor_tensor", [gik, cosk], [t2k], t2v[:, 0:n], giv[:, 0:n], cosv[:, 0:n], ALU.mult)
                P.v("pool", "tensor_tensor", [t1k, t2k], [hik], ho_i, t1v[:, 0:n], t2v[:, 0:n], ALU.add)
                for c0 in range(0, n, 512):
                    cn = min(512, n - c0)
                    ps, pk = psrot.next()
                    P.mm(ps[:, 0:cn], LC[0][:, d_, m, :], hrv[:, c0:c0 + cn], True, False, ["s5_LC", hrk], [pk])
                    P.mm(ps[:, 0:cn], LC[1][:, d_, m, :], hiv[:, c0:c0 + cn], False, True, ["s5_LC", hik], [pk])
                    ya = yacc[:, ut, t0 + c0:t0 + c0 + cn]
                    if d_ == 0 and m % 4 == 0:
                        P.act(ya, ps[:, 0:cn], AF.Copy, [pk], ["s5_yacc%d" % ut])
                    else:
                        P.v("dve", "tensor_tensor", [pk, "s5_yacc%d" % ut], ["s5_yacc%d" % ut], ya, ps[:, 0:cn], ya, ALU.add)
                sbase += n
    P.barrier()
    P.release(m_rot)
    m2 = P.mark()
    gw_st = P.alloc([128, 2, 256]); gw = P.alloc([128, 2, 256], BF16)
    P.dma(gw_st, din["s5_gw"].ap()[l].rearrange("(k p) n -> p k n", p=128), [], ["s5_gwst"])
    P.v("pool", "tensor_copy", ["s5_gwst"], ["s5_gw"], gw, gw_st)
    dT = P.alloc([128, 2]); gbT = P.alloc([128, 2])
    P.dma(dT, din["s5_dT"].ap()[l], [], ["s5_dT"])
    P.dma(gbT, din["s5_gbT"].ap()[l], [], ["s5_gbT"])
    yg = Rot([P.alloc([128, 2, 512]) for _ in range(2)], "s5_yg")
    ygb = Rot([P.alloc([128, 2, 512], BF16) for _ in range(2)], "s5_ygb")
    sg_ = Rot([P.alloc([128, 512]) for _ in range(2)], "s5_sg")
    ob = Rot([P.alloc([128, 512], BF16) for _ in range(2)], "s5_ob")
    for (c0, cn) in CH512:
        if last and c0 + cn <= CTX:
            continue
        ygv, ygk = yg.next(); ybv, ybk = ygb.next()
        for ut in range(2):
            sv, sk = ust.next()
            P.dma(sv[:, 0:cn], pT[ut * 128:(ut + 1) * 128, c0:c0 + cn], ["pT_d"], [sk])
            P.v("dve", "scalar_tensor_tensor", [sk, "s5_dT", "s5_yacc%d" % ut], [ygk], ygv[:, ut, 0:cn], sv[:, 0:cn], dT[:, ut:ut + 1],
                yacc[:, ut, c0:c0 + cn], ALU.mult, ALU.add)
        P.act(ygv[:, :, 0:cn], ygv[:, :, 0:cn], AF.Gelu_apprx_tanh, [ygk], [ygk])
        P.v("pool", "tensor_copy", [ygk], [ybk], ybv[:, :, 0:cn], ygv[:, :, 0:cn])
        for uo in range(2):
            ps, pk = psrot.next()
            for k in range(2):
                P.mm(ps[:, 0:cn], gw[:, k, uo * 128:(uo + 1) * 128], ybv[:, k, 0:cn], k == 0, k == 1, ["s5_gw", ybk], [pk])
            sgv, sgk = sg_.next()
            P.act(sgv[:, 0:cn], ps[:, 0:cn], AF.Sigmoid, [pk, "s5_gbT"], [sgk], bias=gbT[:, uo:uo + 1])
            ov, ok = ob.next()
            P.v("dve", "tensor_tensor", [sgk, ygk], [ok], ov[:, 0:cn], sgv[:, 0:cn], ygv[:, uo, 0:cn], ALU.mult)
            P.dma(C.oT_d.ap()[uo * 128:(uo + 1) * 128, c0:c0 + cn], ov[:, 0:cn], [ok], ["oT_d"], q="pool")
    P.barrier()
    P.release(m0)


MIXERS["s5"] = mix_s5
HOSTPREP["s5"] = host_s5


TWO_PI = 2.0 * math.pi
NF = 8192
OT_HY = 256
K1B = [(i * 8, 8) for i in range(8)] + [(64, 1)]
CTX_K1 = [0, 16, 32, 48, 64]
K1LIST = [list(range(65)), CTX_K1]


def _hy_feat(pos, nt):
    f32 = np.float32
    pos = pos.astype(f32)
    t01 = pos / f32(nt - 1)
    bands = np.linspace(1e-4, 15, 16, dtype=f32)
    ang = (f32(2.0 * math.pi / nt) * pos[:, None] * bands[None, :]).astype(f32)
    z = np.concatenate([t01[:, None], np.cos(ang), -np.sin(ang)], axis=-1).astype(f32)
    return z, t01


def host_hy(inp):
    L = DEPTH
    f32 = np.float32
    bf = ml_dtypes.bfloat16
    o = {}
    n = np.arange(NF)
    zT = np.zeros((2, 33, NF), f32)
    nt01 = np.zeros((2, 128, 64), f32)
    msk = np.zeros((2, 128, 64), f32)
    for kind, nt in enumerate((SEQ, CTX)):
        pos = np.zeros(NF, np.int64)
        m = np.zeros(NF, f32)
        pos[:nt] = n[:nt]; m[:nt] = 1
        hi = n > NF - nt
        pos[hi] = NF - n[hi]; m[hi] = 1
        pos[NF // 2] = 0; m[NF // 2] = 1
        z, t01 = _hy_feat(pos, nt)
        zT[kind] = z.T
        nt01[kind] = (-t01).reshape(128, 64)
        msk[kind] = m.reshape(128, 64)
    o["hy_zT"] = zT
    o["hy_nt01"] = nt01
    o["hy_msk"] = msk
    n1 = np.arange(128)[:, None]; k1 = np.arange(65)[None, :]
    ang = 2 * np.pi * n1 * k1 / 128
    o["hy_D1"] = np.concatenate([np.cos(ang), -np.sin(ang)], axis=1).astype(bf)
    n2 = np.arange(64); k2 = np.arange(64)
    Tm = np.zeros((65, 128, 3, 128), np.float64)
    for kk in range(65):
        w = np.exp(-2j * np.pi * (n2[:, None] * kk / NF + n2[:, None] * k2[None, :] / 64))
        for c2 in range(2):
            Tm[kk, c2::2, 0, c2::2] = w.real
            Tm[kk, c2::2, 1, c2::2] = w.imag
            Tm[kk, c2::2, 2, c2::2] = -w.imag
    o["hy_T"] = Tm.astype(bf)
    t2 = np.arange(64)
    w = np.exp(2j * np.pi * k2[:, None] * t2[None, :] / 64)
    RA = np.zeros((128, 2, 64, 2)); RB = np.zeros((128, 2, 64, 2))
    for c2 in range(2):
        RA[c2::2, 0, :, c2] = w.real; RA[c2::2, 1, :, c2] = w.imag
        RB[c2::2, 0, :, c2] = -w.imag; RB[c2::2, 1, :, c2] = w.real
    o["hy_R"] = np.stack([RA.reshape(128, 256), RB.reshape(128, 256)], axis=1).astype(bf)
    k1v = np.arange(65); t1 = np.arange(64)
    wgt = np.full(65, 2.0); wgt[0] = 1; wgt[64] = 1
    V = np.zeros((65, 64, 2, 64))
    for tt in range(64):
        w = (wgt[:, None] / NF) * np.exp(2j * np.pi * (tt * k1v[:, None] / NF + t1[None, :] * k1v[:, None] / 128))
        V[:, tt, 0, :] = w.real; V[:, tt, 1, :] = -w.imag
    o["hy_V"] = V.astype(bf)
    o["hy_Vc"] = np.ascontiguousarray(V[CTX_K1][:, :, :, 0:4]).astype(bf)
    o["hy_w1"] = np.ascontiguousarray(inp["hy_w1"])
    o["hy_w2"] = np.ascontiguousarray(inp["hy_w2"])
    o["hy_cols"] = np.ascontiguousarray(np.stack([inp["hy_b1"], inp["hy_b2"], inp["hy_freq"]], axis=-1))
    w3 = inp["hy_w3"].reshape(L, 64, 2, 2, 256)
    o["hy_w3"] = np.ascontiguousarray(w3.transpose(0, 1, 3, 2, 4))
    dl = inp["hy_deltas"].reshape(L, 2, 2, 256)
    o["hy_dl"] = np.ascontiguousarray(dl.transpose(0, 2, 1, 3).reshape(L, 2, 1, 512))
    cw = np.concatenate([inp["hy_conv_w"], inp["hy_conv_b"][:, None, :]], axis=1)
    o["hy_cw"] = np.ascontiguousarray(cw.reshape(L, 1, 4 * 768))
    o["hy_bias"] = np.ascontiguousarray(inp["hy_bias"].reshape(L, 1, 512))
    return o


def mix_hy(C, l, last):
    P, B = C.P, C.B
    psrot = C.psrot
    if "hy_zT" not in B.din:
        B.inp("hy_zT", [2, 33, NF]); B.inp("hy_nt01", [2, 128, 64]); B.inp("hy_msk", [2, 128, 64])
        B.inp("hy_D1", [128, 130], BF16); B.inp("hy_T", [65, 128, 3, 128], BF16)
        B.inp("hy_R", [128, 2, 256], BF16); B.inp("hy_V", [65, 64, 2, 64], BF16); B.inp("hy_Vc", [5, 64, 2, 4], BF16)
        B.inp("hy_w1", [DEPTH, 33, 64]); B.inp("hy_w2", [DEPTH, 64, 64]); B.inp("hy_cols", [DEPTH, 64, 3])
        B.inp("hy_w3", [DEPTH, 64, 2, 2, 256]); B.inp("hy_dl", [DEPTH, 2, 1, 512])
        B.inp("hy_cw", [DEPTH, 1, 3072]); B.inp("hy_bias", [DEPTH, 1, 512])
        B.scr("hspec_d", [2, 2, 2, 2, 128, 65 * 64])
    din = B.din
    hspec = B.dscr["hspec_d"]
    kinds = [0] if last else [0, 1]
    m0 = P.mark()
    D1 = P.alloc([128, 130], BF16)
    P.dma(D1, din["hy_D1"].ap(), [], ["hy_D1"])
    Rm = P.alloc([128, 2, 256], BF16)
    P.dma(Rm, din["hy_R"].ap(), [], ["hy_R"])
    Vm = P.alloc([65, 64, 2, 64], BF16)
    P.dma(Vm, din["hy_V"].ap(), [], ["hy_V"])
    Vmc = P.alloc([5, 64, 2, 4], BF16)
    P.dma(Vmc, din["hy_Vc"].ap(), [], ["hy_V"])
    Trot = Rot([P.alloc([128, 3, 128], BF16) for _ in range(4)], "hy_T")
    B1 = P.alloc([128, 64, 130], BF16)
    mfwd = P.mark()

    def fwd(xv, Kk, xkey, consume, k1list):
        for q0 in range(0, 64, 3):
            qn = min(3, 64 - q0)
            ps, pk = psrot.next()
            for j in range(qn):
                q = q0 + j
                P.mm(ps[:, j * 130:(j + 1) * 130], xv[:, q, :, :].rearrange("p n c -> p (n c)"), D1[0:Kk, :], True, True, [xkey, "hy_D1"], [pk])
            src = ps[:, 0:qn * 130].rearrange("p (a b) -> p a b", a=qn)
            if (q0 // 3) % 2 == 0:
                P.act(B1[:, q0:q0 + qn, :], src, AF.Copy, [pk], ["hy_B1"])
            else:
                P.v("dve", "tensor_copy", [pk], ["hy_B1"], B1[:, q0:q0 + qn, :], src)
        for bi, k0 in enumerate(range(0, len(k1list), 8)):
            kn = min(8, len(k1list) - k0)
            psr, pkr = psrot.next()
            psi, pki = psrot.next()
            for j in range(kn):
                k1 = k1list[k0 + j]
                tv, tk = Trot.next()
                P.dma(tv, din["hy_T"].ap()[k1], [], [tk])
                br = B1[:, :, k1]
                bi_ = B1[:, :, 65 + k1]
                P.mm(psr[:, j * 64:(j + 1) * 64], tv[:, 0, :], br, True, False, [tk, "hy_B1"], [pkr])
                P.mm(psr[:, j * 64:(j + 1) * 64], tv[:, 2, :], bi_, False, True, [tk, "hy_B1"], [pkr])
                P.mm(psi[:, j * 64:(j + 1) * 64], tv[:, 1, :], br, True, False, [tk, "hy_B1"], [pki])
                P.mm(psi[:, j * 64:(j + 1) * 64], tv[:, 0, :], bi_, False, True, [tk, "hy_B1"], [pki])
            consume(bi, k0, kn, psr, pkr, psi, pki)

    w1 = P.alloc([33, 64]); w2 = P.alloc([64, 64]); cols = P.alloc([64, 3])
    P.dma(w1, din["hy_w1"].ap()[l], [], ["hy_w1"])
    P.dma(w2, din["hy_w2"].ap()[l], [], ["hy_w2"])
    P.dma(cols, din["hy_cols"].ap()[l], [], ["hy_cols"])
    sc = P.alloc([64, 4])
    P.v("dve", "tensor_scalar", ["hy_cols"], ["hy_sc"], sc[:, 0:1], cols[:, 2:3], 1.0 / TWO_PI, None, ALU.mult)
    P.v("dve", "tensor_tensor", ["hy_cols", "hy_sc"], ["hy_sc"], sc[:, 1:2], cols[:, 0:1], sc[:, 0:1], ALU.mult)
    P.v("dve", "tensor_tensor", ["hy_cols", "hy_sc"], ["hy_sc"], sc[:, 2:3], cols[:, 1:2], sc[:, 0:1], ALU.mult)
    w3 = P.alloc([64, 2, 2, 256])
    P.dma(w3, din["hy_w3"].ap()[l], [], ["hy_w3"])
    absd = P.alloc([128, 2, 256])
    for d_ in range(2):
        P.dma(absd[d_ * 64:(d_ + 1) * 64].rearrange("p o c -> p (o c)"),
              din["hy_dl"].ap()[l, d_, 0:1, :].partition_broadcast(64)[:, 0, :], [], ["hy_absd"])
    P.act(absd, absd, AF.Abs, ["hy_absd"], ["hy_absd"])
    mflt = P.mark()
    hid2 = P.alloc([64, NF])
    zrot = Rot([P.alloc([33, 512]) for _ in range(2)], "hy_z")
    phb = P.alloc([64, 512]); kib = P.alloc([64, 512], I32); kfb = P.alloc([64, 512]); h1b = P.alloc([64, 512])
    nt01 = P.alloc([128, 64]); msk = P.alloc([128, 64])
    krot = Rot([P.alloc([128, 256]) for _ in range(4)], "hy_kr")
    xk = P.alloc([128, 128, 64, 2], BF16)
    dec = Rot([P.alloc([128, 256]) for _ in range(3)], "hy_dec")
    sqr = Rot([P.alloc([128, 256]) for _ in range(4)], "hy_sq")
    ssa = P.alloc([128, 256]); rs = P.alloc([128, 256])
    yst = Rot([P.alloc([128, 2, 512]) for _ in range(2)], "hy_yst")

    def sin_layer(ps, pk, n, bcol, outv, okey):
        P.act(phb[:, 0:n], ps[0:64, 0:n], AF.Identity, [pk, "hy_sc"], ["hy_ph"], scale=sc[:, 0:1], bias=sc[:, bcol:bcol + 1])
        P.v("dve", "tensor_copy", ["hy_ph"], ["hy_ki"], kib[:, 0:n], phb[:, 0:n])
        P.v("dve", "tensor_copy", ["hy_ki"], ["hy_kf"], kfb[:, 0:n], kib[:, 0:n])
        P.v("dve", "tensor_tensor", ["hy_ph", "hy_kf"], ["hy_ph"], phb[:, 0:n], phb[:, 0:n], kfb[:, 0:n], ALU.subtract)
        P.act(outv, phb[:, 0:n], AF.Sin, ["hy_ph"], [okey], scale=TWO_PI)

    for kind in kinds:
        P.dma(nt01, din["hy_nt01"].ap()[kind], ["hy_nt01r"], ["hy_nt01"])
        P.dma(msk, din["hy_msk"].ap()[kind], ["hy_mskr"], ["hy_msk"])
        for ch in (range(NF // 512) if kind == 0 else (0, 8, 15)):
            zv, zk = zrot.next()
            P.dma(zv, din["hy_zT"].ap()[kind, :, ch * 512:(ch + 1) * 512], [], [zk])
            ps, pk = psrot.next()
            P.mm(ps[0:64, :], w1, zv, True, True, ["hy_w1", zk], [pk])
            sin_layer(ps, pk, 512, 1, h1b, "hy_h1")
            ps2, pk2 = psrot.next()
            P.mm(ps2[0:64, :], w2, h1b, True, True, ["hy_w2", "hy_h1"], [pk2])
            sin_layer(ps2, pk2, 512, 2, hid2[:, ch * 512:(ch + 1) * 512], "hy_hid2")
        for o_ in range(2):
            pss_, pks_ = C.pss[7][:], "ps7"
            prot7 = Rot([p[:] for p in C.pss[0:7]], "ps")
            LOOKF = 2
            pendf = {}

            def issue_k(n2, o_=o_, kind=kind):
                psf, pkf = prot7.next()
                psb, pkb = prot7.next()
                lh = hid2[:, n2:NF:64]
                P.mm(psf[:, 0:256], lh, w3[:, 0, o_, :], True, True, ["hy_hid2", "hy_w3"], [pkf])
                P.mm(psb[:, 0:256], lh, w3[:, 1, o_, :], True, True, ["hy_hid2", "hy_w3"], [pkb])
                dv, dk = dec.next()
                P.act(dv, absd[:, o_, :], AF.Exp, ["hy_absd", "hy_nt01"], [dk], scale=nt01[:, n2:n2 + 1])
                kv, kk_ = krot.next()
                P.v("dve", "scalar_tensor_tensor", [pkf, "hy_msk", dk], [kk_], kv[0:64, :], psf[0:64, 0:256],
                    msk[0:64, n2:n2 + 1], dv[0:64, :], ALU.mult, ALU.mult)
                P.v("dve", "scalar_tensor_tensor", [pkb, "hy_msk", dk], [kk_], kv[64:128, :], psb[64:128, 0:256],
                    msk[64:128, n2:n2 + 1], dv[64:128, :], ALU.mult, ALU.mult)
                sv, sk = sqr.next()
                P.act(sv[:, 0:256], kv, AF.Square, [kk_], [sk])
                P.v("pool", "tensor_copy", [kk_], ["hy_xk"], xk[:, :, n2, :], kv.rearrange("p (q c) -> p q c", c=2))
                pendf[n2] = (sv, sk)

            for n2 in range(LOOKF):
                issue_k(n2)
            for n2 in range(64):
                if n2 + LOOKF < 64:
                    issue_k(n2 + LOOKF)
                sv, sk = pendf.pop(n2)
                P.mm(pss_[:, 0:256], C.ones_f[:, :], sv[:, 0:256], n2 == 0, n2 == 63, [sk, "ones_f"], [pks_])
            P.act(rs, pss_[:, 0:256], AF.Sqrt, [pks_, "eps_t"], ["hy_rs"], bias=C.eps_t[:], scale=1.0)
            P.v("dve", "reciprocal", ["hy_rs"], ["hy_rs"], rs, rs)
            rsb = rs.rearrange("p (q c) -> p q c", c=2).unsqueeze(2).to_broadcast([128, 128, 16, 2])
            for g4 in range(4):
                xs_ = xk[:, :, g4 * 16:(g4 + 1) * 16, :]
                P.v("dve" if g4 % 2 == 0 else "pool", "tensor_tensor", ["hy_xk", "hy_rs"], ["hy_xk"], xs_, xs_, rsb, ALU.mult)
            P.v("dve", "memset", [], ["hy_xk"], xk[64:65, :, 0, :], 0.0)
            if C.cfg.get("hy_dbg") and kind == 0 and o_ == 0:
                dbg1 = B.scr("hy_dbg_hid2", [64, NF])
                dbg2 = B.scr("hy_dbg_xk", [128, 128 * 64 * 2], BF16)
                dbg3 = B.scr("hy_dbg_rs", [128, 256])
                P.dma(dbg1.ap(), hid2, ["hy_hid2"], ["dbg1"], q="pool")
                P.dma(dbg2.ap(), xk.rearrange("p a b c -> p (a b c)"), ["hy_xk"], ["dbg2"], q="pool")
                P.dma(dbg3.ap(), rs, ["hy_rs"], ["dbg3"], q="pool")
            for hf in range(2):
                def consume(bi, k0, kn, psr, pkr, psi, pki, kind=kind, o_=o_, hf=hf):
                    yv, yk = yst.next()
                    fs = 1.0 if kind == 0 else 16.0
                    P.act(yv[:, 0, 0:kn * 64], psr[:, 0:kn * 64], AF.Copy, [pkr], [yk], scale=fs)
                    P.v("dve", "tensor_scalar", [pki], [yk], yv[:, 1, 0:kn * 64], psi[:, 0:kn * 64], fs, None, ALU.mult)
                    for ri in range(2):
                        P.dma(hspec.ap()[kind, o_, hf, ri, :, k0 * 64:(k0 + kn) * 64], yv[:, ri, 0:kn * 64], [yk], ["hspec_d"], q="pool")
                fwd(xk[:, hf * 64:(hf + 1) * 64, :, :], 128, "hy_xk", consume, K1LIST[kind])
    P.barrier()
    P.release(mflt)
    if C.cfg.get("hy_stop", 9) <= 1:
        P.release(m0); return
    cwr = P.alloc([64, 4, 768])
    P.dma(cwr.rearrange("p a c -> p (a c)"), din["hy_cw"].ap()[l, 0:1, :].partition_broadcast(64)[:, 0, :], [], ["hy_cw"])
    hbr = P.alloc([64, 2, 256])
    P.dma(hbr.rearrange("p a c -> p (a c)"), din["hy_bias"].ap()[l, 0:1, :].partition_broadcast(64)[:, 0, :], [], ["hy_hb"])
    xv = P.alloc([64, 64, 64, 2], BF16)
    z1 = P.alloc([64, 64, 64, 2], BF16)
    Zr = P.alloc([128, 65, 64], BF16); Zi = P.alloc([128, 65, 64], BF16)
    G = P.alloc([65, 2, 64, 128], BF16)
    oTs = P.alloc([128, SEQ], BF16)
    dwl = Rot([P.alloc([64, 3, 4, 128]) for _ in range(2)], "hy_dwl")
    dwa = Rot([P.alloc([64, 4, 128]) for _ in range(3)], "hy_dwa")
    dwt = P.alloc([64, 4, 128])
    hrot = Rot([P.alloc([128, 2, 512]) for _ in range(2)], "hy_h")
    tt1 = P.alloc([128, 512]); tt2 = P.alloc([128, 512])
    gt = P.alloc([64, 4, 128]); z2f = P.alloc([64, 4, 128])

    def blkv(x, M, b):
        return x[0:M, :, 4 * b:4 * b + 4, :].rearrange("p q n c -> p n q c")

    def dwconv_blk(kind, set_, hf, b):
        M = 64 if kind == 0 else 4
        base = LAT0 if kind == 0 else CTX0
        ntok = SEQ if kind == 0 else CTX
        c0 = set_ * 256 + hf * 128
        lv, lk = dwl.next()
        for s in range(3):
            src = C.hyp_d.ap()[base + s - 1:base + s - 1 + ntok, c0:c0 + 128].rearrange("(a b) c -> a b c", b=64)[:, 4 * b:4 * b + 4, :]
            P.dma(lv[0:M, s, :, :], src, ["hyp_d"], [lk])
        av, ak = dwa.next()
        wv = lambda tap: cwr[0:M, tap, c0:c0 + 128].unsqueeze(1).to_broadcast([M, 4, 128])
        P.v("dve", "tensor_tensor", [lk, "hy_cw"], [ak], av[0:M], lv[0:M, 1], wv(1), ALU.mult)
        P.v("pool", "tensor_tensor", [lk, "hy_cw"], ["hy_dwt"], dwt[0:M], lv[0:M, 0], wv(0), ALU.mult)
        P.v("pool", "tensor_tensor", [ak, "hy_dwt"], [ak], av[0:M], av[0:M], dwt[0:M], ALU.add)
        P.v("dve", "tensor_tensor", [lk, "hy_cw"], ["hy_dwt"], dwt[0:M], lv[0:M, 2], wv(2), ALU.mult)
        P.v("pool", "tensor_tensor", [ak, "hy_dwt"], [ak], av[0:M], av[0:M], dwt[0:M], ALU.add)
        P.v("pool", "tensor_tensor", [ak, "hy_cw"], [ak], av[0:M], av[0:M], wv(3), ALU.add)
        return av, ak

    def inverse(kind, evac):
        M = 64 if kind == 0 else 4
        nk = len(K1LIST[kind])
        Vsel = Vm if kind == 0 else Vmc
        for q0 in range(0, 64, 2):
            ps, pk = psrot.next()
            for j in range(2):
                q = q0 + j
                P.mm(ps[0:nk, j * 256:(j + 1) * 256], Zr[:, 0:nk, q], Rm[:, 0, :], True, False, ["hy_Z", "hy_R"], [pk])
                P.mm(ps[0:nk, j * 256:(j + 1) * 256], Zi[:, 0:nk, q], Rm[:, 1, :], False, True, ["hy_Z", "hy_R"], [pk])
            for j in range(2):
                q = q0 + j
                src = ps[0:nk, j * 256:(j + 1) * 256].rearrange("p (r t c) -> p r t c", r=2, t=64)
                if j == 0:
                    P.act(G[0:nk, :, :, 2 * q:2 * q + 2], src, AF.Copy, [pk], ["hy_G"])
                else:
                    P.v("dve", "tensor_copy", [pk], ["hy_G"], G[0:nk, :, :, 2 * q:2 * q + 2], src)
        for b in range(16):
            ps, pk = psrot.next()
            for j in range(4):
                t2 = 4 * b + j
                P.mm(ps[0:M, j * 128:(j + 1) * 128], Vsel[0:nk, t2, 0, 0:M], G[0:nk, 0, t2, :], True, False, ["hy_V", "hy_G"], [pk])
                P.mm(ps[0:M, j * 128:(j + 1) * 128], Vsel[0:nk, t2, 1, 0:M], G[0:nk, 1, t2, :], False, True, ["hy_V", "hy_G"], [pk])
            evac(b, ps, pk, M)

    for kind in kinds:
        M = 64 if kind == 0 else 4
        ntok = SEQ if kind == 0 else CTX
        tok0 = CTX if kind == 0 else 0
        for hf in range(2):
            if kind == 1:
                P.v("dve", "memset", [], ["hy_xv"], xv, 0.0)
                P.v("pool", "memset", [], ["hy_z1"], z1, 0.0)
            for b in range(16):
                av, ak = dwconv_blk(kind, 2, hf, b)
                P.act(blkv(xv, M, b), av[0:M].rearrange("p n (q c) -> p n q c", c=2), AF.Copy, [ak], ["hy_xv"])
            for o_ in range(2):
                src_x = xv if o_ == 0 else z1
                src_k = "hy_xv" if o_ == 0 else "hy_z1"

                def consume(bi, k0, kn, psr, pkr, psi, pki, kind=kind, o_=o_, hf=hf):
                    hv, hk = hrot.next()
                    for ri in range(2):
                        P.dma(hv[:, ri, 0:kn * 64], hspec.ap()[kind, o_, hf, ri, :, k0 * 64:(k0 + kn) * 64], ["hspec_d"], [hk])
                    w = kn * 64
                    zr = Zr[:, k0:k0 + kn, :].rearrange("p a b -> p (a b)")
                    zi = Zi[:, k0:k0 + kn, :].rearrange("p a b -> p (a b)")
                    P.v("dve", "tensor_tensor", [pkr, hk], ["hy_tt1"], tt1[:, 0:w], psr[:, 0:w], hv[:, 0, 0:w], ALU.mult)
                    P.v("dve", "tensor_tensor", [pki, hk], ["hy_tt2"], tt2[:, 0:w], psi[:, 0:w], hv[:, 1, 0:w], ALU.mult)
                    P.v("pool", "tensor_tensor", ["hy_tt1", "hy_tt2"], ["hy_Z"], zr, tt1[:, 0:w], tt2[:, 0:w], ALU.subtract)
                    P.v("dve", "tensor_tensor", [pkr, hk], ["hy_tt1"], tt1[:, 0:w], psr[:, 0:w], hv[:, 1, 0:w], ALU.mult)
                    P.v("dve", "tensor_tensor", [pki, hk], ["hy_tt2"], tt2[:, 0:w], psi[:, 0:w], hv[:, 0, 0:w], ALU.mult)
                    P.v("pool", "tensor_tensor", ["hy_tt1", "hy_tt2"], ["hy_Z"], zi, tt1[:, 0:w], tt2[:, 0:w], ALU.add)
                fwd(src_x, 64, src_k, consume, K1LIST[kind])

                def evac(b, ps, pk, M, kind=kind, o_=o_, hf=hf, src_x=src_x, src_k=src_k):
                    xg, xgk = dwconv_blk(kind, o_, hf, b)
                    yv = ps[0:M, :].rearrange("p (a c) -> p a c", a=4)
                    brow = hbr[0:M, o_, hf * 128:(hf + 1) * 128].unsqueeze(1).to_broadcast([M, 4, 128])
                    P.v("pool", "tensor_tensor", [src_k, "hy_hb"], ["hy_gt"], gt[0:M].rearrange("p n (q c) -> p n q c", c=2), blkv(src_x, M, b), brow.rearrange("p n (q c) -> p n q c", c=2), ALU.mult)
                    P.v("dve", "tensor_tensor", [pk, "hy_gt"], ["hy_gt"], gt[0:M], yv, gt[0:M], ALU.add)
                    if o_ == 0:
                        P.v("dve", "tensor_tensor", ["hy_gt", xgk], ["hy_z1"], blkv(z1, M, b), gt[0:M].rearrange("p n (q c) -> p n q c", c=2), xg[0:M].rearrange("p n (q c) -> p n q c", c=2), ALU.mult)
                    else:
                        P.v("dve", "tensor_tensor", ["hy_gt", xgk], ["hy_z2f"], z2f[0:M], gt[0:M], xg[0:M], ALU.mult)
                        pt, ptk = psrot.next()
                        for j in range(4):
                            P.tr(pt[:, j * 64:j * 64 + M], z2f[0:M, j, :], C.ident[0:M, 0:M], ["hy_z2f", "ident"], [ptk])
                        dst = oTs[:, 0:ntok].rearrange("p (a b) -> p b a", b=64)[:, 4 * b:4 * b + 4, :]
                        srcp = pt[:, 0:256].rearrange("p (j a) -> p j a", j=4)[:, :, 0:M]
                        P.act(dst, srcp, AF.Copy, [ptk], ["hy_oTs"])
                inverse(kind, evac)
            P.dma(C.oT_d.ap()[OT_HY + hf * 128:OT_HY + (hf + 1) * 128, tok0:tok0 + ntok], oTs[:, 0:ntok], ["hy_oTs"], ["oT_d"], q="pool")
    P.barrier()
    P.release(m0)


MIXERS["hy"] = mix_hy
HOSTPREP["hy"] = host_hy
```

```python
import math
import numpy as np
import ml_dtypes
import concourse.bass as bass
import concourse.mybir as mybir
from concourse.bass_utils import run_bass_kernel_spmd
from contextlib import ExitStack
from types import SimpleNamespace

F32 = mybir.dt.float32
BF16 = mybir.dt.bfloat16
I32 = mybir.dt.int32
ALU = mybir.AluOpType
AF = mybir.ActivationFunctionType

ENGS = ("pe", "act", "dve", "pool", "sp")
import os as _os
NO_SELF_SYNC = set(_os.environ.get("NO_SELF_SYNC", "").split(",")) - {""}
NSLOT = 6

D = 1024
NB = 4
SEQ = 4096
CTX = 256
T = SEQ + CTX
DEPTH = 2
EPS = 1e-6
WG = 256
DFF = 2816
NTT = T // 128
W_PAD = T + 4
CTX0 = 1
LAT0 = 259
NFM = 2048
NTM = 1536
CH512 = [(i * 512, min(512, T - i * 512)) for i in range((T + 511) // 512)]


class Prog:
    def __init__(self, nc):
        self.nc = nc
        self.ops = []
        self.es = ExitStack()
        self.arena = None
        self.aoff = 0

    def sb(self, name, shape, dtype=F32):
        return self.es.enter_context(self.nc.sbuf_tensor(name, list(shape), dtype))

    def ps(self, name, shape, dtype=F32):
        return self.es.enter_context(self.nc.psum_tensor(name, list(shape), dtype))

    def dram(self, name, shape, dtype=F32, kind="Internal"):
        return self.nc.dram_tensor(name, list(shape), dtype, kind=kind)

    def init_arena(self, words):
        self.arena = self.sb("arena", [128, words], F32)
        self.awords = words
        self.aoff = 0

    def mark(self):
        return self.aoff

    def release(self, m):
        self.aoff = m

    def alloc(self, shape, dtype=F32):
        npart = shape[0]
        n = int(np.prod(shape[1:]))
        if dtype == F32 or dtype == I32:
            words = n
        else:
            words = (n + 1) // 2
        words = (words + 7) // 8 * 8
        off = self.aoff
        self.aoff += words
        assert self.aoff <= self.awords, "arena overflow %d > %d" % (self.aoff, self.awords)
        v = self.arena[0:npart, off:off + words]
        if dtype != F32:
            v = v.bitcast(dtype)
        v = v[:, 0:n]
        if len(shape) == 3:
            v = v.rearrange("p (a b) -> p a b", a=shape[1])
        elif len(shape) == 4:
            v = v.rearrange("p (a b c) -> p a b c", a=shape[1], b=shape[2])
        return v

    def op(self, eng, fn, reads=(), writes=(), dma=False):
        self.ops.append(dict(eng=eng, fn=fn, reads=tuple(reads), writes=tuple(writes), dma=dma, bar=False))

    def barrier(self):
        self.ops.append(dict(bar=True))

    def dma(self, out, in_, reads, writes, q="sp", **kw):
        self.op(q, lambda e: e.dma_start(out=out, in_=in_, **kw), reads, writes, dma=True)

    def mm(self, out, lhsT, rhs, start, stop, reads, writes, **kw):
        self.op("pe", lambda e: e.matmul(out, lhsT, rhs, start=start, stop=stop, **kw), reads, writes)

    def tr(self, out, in_, ident, reads, writes):
        self.op("pe", lambda e: e.transpose(out, in_, ident), reads, writes)

    def act(self, out, in_, func, reads, writes, **kw):
        self.op("act", lambda e: e.activation(out=out, in_=in_, func=func, **kw), reads, writes)

    def v(self, eng, name, reads, writes, *a, **kw):
        self.op(eng, lambda e: getattr(e, name)(*a, **kw), reads, writes)

    def emit(self):
        nc = self.nc
        es = self.es
        csem = {e: es.enter_context(nc.semaphore("c_" + e)) for e in ("pe", "act", "dve", "pool")}
        dsem = {q: [es.enter_context(nc.semaphore("d_%s%d" % (q, i))) for i in range(NSLOT)]
                for q in ("sp", "act", "pool")}
        ccount = {e: 0 for e in csem}
        dcount = {q: 0 for q in dsem}
        slotuse = {q: [0] * NSLOT for q in dsem}
        semobj = {}
        for e in csem:
            semobj[("c", e)] = csem[e]
        for q in dsem:
            for i in range(NSLOT):
                semobj[("d", q, i)] = dsem[q][i]
        know = {e: {} for e in ENGS}
        last_w = {}
        readers = {}
        streams = {e: [] for e in ENGS}
        bar_know = {}
        bar_done = {e: True for e in ENGS}

        def cur_all():
            d = {}
            for q in dsem:
                for i in range(NSLOT):
                    if slotuse[q][i] > 0:
                        d[("d", q, i)] = slotuse[q][i] * 16
            for e in csem:
                if ccount[e] > 0:
                    d[("c", e)] = ccount[e]
            return d

        for op in self.ops:
            if op["bar"]:
                bar_know = cur_all()
                bar_done = {e: False for e in ENGS}
                continue
            e = op["eng"]
            deps = []
            for r in op["reads"]:
                if r in last_w:
                    deps.append(last_w[r])
            for w in op["writes"]:
                if w in last_w:
                    deps.append(last_w[w])
                for rd in readers.get(w, {}).values():
                    deps.append(rd)
            waits = {}
            kn = know[e]
            if not bar_done[e]:
                bar_done[e] = True
                for sk, val in bar_know.items():
                    if sk == ("c", "pe") and e == "pe":
                        continue
                    if kn.get(sk, 0) < val:
                        waits[sk] = val
                        kn[sk] = val
            if op["dma"]:
                slot = dcount[e] % NSLOT
                dcount[e] += 1
                sk = ("d", e, slot)
                prev = slotuse[e][slot] * 16
                if prev > 0 and kn.get(sk, 0) < prev:
                    waits[sk] = max(waits.get(sk, 0), prev)
                    kn[sk] = prev
                slotuse[e][slot] += 1
                tok = (sk, slotuse[e][slot] * 16)
            else:
                ccount[e] += 1
                tok = (("c", e), ccount[e])
            for (dtok, dknow) in deps:
                sk, val = dtok
                if sk == ("c", "pe") and e == "pe" and not op["dma"]:
                    continue
                if (not op["dma"]) and sk == ("c", e) and e in NO_SELF_SYNC:
                    continue
                if kn.get(sk, 0) >= val:
                    continue
                waits[sk] = max(waits.get(sk, 0), val)
                for k2, v2 in dknow.items():
                    if kn.get(k2, 0) < v2:
                        kn[k2] = v2
                kn[sk] = max(kn.get(sk, 0), val)
            myknow = dict(kn)
            myknow[tok[0]] = max(myknow.get(tok[0], 0), tok[1])
            if (not op["dma"]) and e == "pe":
                kn[tok[0]] = tok[1]
            streams[e].append((op, waits, tok))
            entry = (tok, myknow)
            for w in op["writes"]:
                last_w[w] = entry
                readers[w] = {}
            for r in op["reads"]:
                if r not in op["writes"]:
                    readers.setdefault(r, {})[tok[0]] = entry
        fin = cur_all()
        self.stats = dict(ccount=dict(ccount), dcount=dict(dcount))

        def run_stream(ename, eng):
            for (op, waits, tok) in streams[ename]:
                for sk, val in waits.items():
                    eng.wait_ge(semobj[sk], val)
                ins = op["fn"](eng)
                ins.then_inc(semobj[tok[0]], 16 if op["dma"] else 1)
            if ename == "sp":
                for sk, val in fin.items():
                    eng.wait_ge(semobj[sk], val)

        with nc.Block() as block:
            @block.sync
            def _(eng):
                run_stream("sp", eng)

            @block.tensor
            def _(eng):
                run_stream("pe", eng)

            @block.scalar
            def _(eng):
                run_stream("act", eng)

            @block.vector
            def _(eng):
                run_stream("dve", eng)

            @block.gpsimd
            def _(eng):
                run_stream("pool", eng)
        es.close()


class Rot:
    def __init__(self, views, name):
        self.views = views
        self.name = name
        self.i = 0

    def next(self):
        i = self.i % len(self.views)
        self.i += 1
        return self.views[i], "%s%d" % (self.name, i)


def _swap_halves_cols(w, width):
    n = w.shape[1]
    idx = np.arange(n).reshape(n // width, 2, width // 2)[:, ::-1, :].reshape(-1)
    return w[:, idx]


def host_consts():
    c = {}
    c["ident"] = np.eye(128, dtype=np.float32)
    return c


def arrange_in_cols(w_in):
    o = {}
    s5u = w_in[:, 0:256]
    hy = w_in[:, 256:1024]
    rq = w_in[:, 1024:1280]
    rk = w_in[:, 1280:1536]
    rv = w_in[:, 1536:1792]
    rg = w_in[:, 1792:2048]
    cq = w_in[:, 2048:2240]
    ckv = w_in[:, 2240:2368]
    kr = w_in[:, 2368:2400]
    z64 = np.zeros((w_in.shape[0], 64), np.float32)
    fm = np.concatenate([s5u, rq, rk, rg, _swap_halves_cols(rq, 64), _swap_halves_cols(rk, 64),
                         cq, z64, ckv, kr, _swap_halves_cols(kr, 32), z64], axis=1)
    assert fm.shape[1] == NFM
    tm = np.concatenate([hy, rk, _swap_halves_cols(rk, 64), rv], axis=1)
    assert tm.shape[1] == NTM
    o["w_fm"] = np.ascontiguousarray(fm)
    o["w_tm"] = np.ascontiguousarray(tm)
    return o


def prep_layer_weights(inp, l):
    return arrange_in_cols(inp["w_in"][l])


class Builder:
    def __init__(self, cfg):
        self.cfg = cfg
        self.nc = bass.Bass("TRN2", target_bir_lowering=False)
        self.P = Prog(self.nc)
        self.inputs = {}
        self.din = {}
        self.dscr = {}
        self.ext_out = []

    def inp(self, name, shape, dtype=F32):
        t = self.P.dram(name, shape, dtype, kind="ExternalInput")
        self.din[name] = t
        return t

    def scr(self, name, shape, dtype=F32):
        cfg = self.cfg
        if name in cfg.get("dbg_in", ()):
            kind = "ExternalInput"
        elif name in cfg.get("dbg_out", ()):
            kind = "ExternalOutput"
            self.ext_out.append(name)
        else:
            kind = "Internal"
        t = self.P.dram(name, shape, dtype, kind=kind)
        self.dscr[name] = t
        return t


def build_program(cfg):
    B = Builder(cfg)
    P = B.P
    nc = B.nc
    layers = cfg.get("layers", list(range(DEPTH)))
    stages = cfg.get("stages", "ABCDEF")
    mixers = cfg.get("mixers", ("s5", "hy", "ret", "mla"))

    x_in = B.inp("x_in", [SEQ, D])
    ctx_in = B.inp("ctx_in", [CTX, D])
    cs_in = B.inp("cs_in", [128, 8, 2])
    ident_in = B.inp("ident", [128, 128])
    ada_w = B.inp("ada_w", [DEPTH, D, 6 * D])
    ada_bT = B.inp("ada_bT", [DEPTH, 128, 48])
    ada_brow = B.inp("ada_brow", [DEPTH, 1, 6 * D])
    ng_T = B.inp("ng_T", [DEPTH, 4, 128, 8])
    ng_row = B.inp("ng_row", [DEPTH, 4, 1, D])
    w_fm = B.inp("w_fm", [DEPTH, D, NFM])
    w_tm = B.inp("w_tm", [DEPTH, D, NTM])
    w_out = B.inp("w_out", [DEPTH, D, D])
    w_up = B.inp("w_up", [DEPTH, D, 2 * DFF])
    ffn_cw = B.inp("ffn_cw", [DEPTH, 128, 44, 3])
    ffn_cb = B.inp("ffn_cb", [DEPTH, 128, 44])
    w_down = B.inp("w_down", [DEPTH, DFF, D])
    y_out = P.dram("y", [SEQ, D], F32, kind="ExternalOutput")

    xs = B.scr("xs", [T, D])
    pT_d = B.scr("pT_d", [NFM, T])
    hyp_d = B.scr("hyp_d", [W_PAD, 768])
    rtok_d = B.scr("rtok_d", [T, 768])
    oT_d = B.scr("oT_d", [D, T], BF16)
    gT_d = B.scr("gT_d", [DFF, W_PAD], BF16)

    ident = P.sb("ident_sb", [128, 128])
    zero_sb = P.sb("zero_sb", [128, 768])
    scs = P.sb("scs", [128, 8, 2])
    modT = P.sb("modT", [128, 48, 2])
    AB = P.sb("ABmod", [128, 4, 8, 2])
    grow = P.sb("grow", [128, 2, 2, D])
    eps_t = P.sb("eps_t", [128, 1])
    ones_f = P.sb("ones_f", [128, 128])
    ones_b = P.sb("ones_b", [128, 128], BF16)
    pss = [P.ps("ps%d" % i, [128, 512]) for i in range(8)]
    psrot = Rot([p[:] for p in pss], "ps")
    P.init_arena(cfg.get("arena_words", 47000))

    P.dma(ident[:], ident_in.ap(), [], ["ident"])
    P.v("dve", "memset", [], ["zero_sb"], zero_sb[:], 0.0)
    P.v("dve", "memset", [], ["eps_t"], eps_t[:], EPS)
    P.v("dve", "memset", [], ["ones_f"], ones_f[:], 1.0)
    P.v("dve", "memset", [], ["ones_b"], ones_b[:], 1.0)
    cs_raw = P.sb("cs_raw", [128, 8, 2])
    P.dma(cs_raw[:], cs_in.ap(), [], ["cs_raw"])
    P.act(scs[:], cs_raw[:], AF.Silu, ["cs_raw"], ["scs"])
    for r in (0, CTX0 + CTX, CTX0 + CTX + 1, W_PAD - 1):
        P.dma(hyp_d.ap()[r:r + 1, :], zero_sb[0:1, 0:768], ["zero_sb"], ["hyp_d"], q="pool")

    def xrows(layer, t0, n):
        if layer == 0:
            if t0 < CTX:
                return ctx_in.ap()[t0:t0 + n, :]
            return x_in.ap()[t0 - CTX:t0 - CTX + n, :]
        return xs.ap()[t0:t0 + n, :]

    def stage_A(l):
        m0 = P.mark()
        sc_rep = P.alloc([128, 8, 2, 128])
        P.v("dve", "tensor_copy", ["scs"], ["sc_rep"], sc_rep, scs[:].unsqueeze(3).to_broadcast([128, 8, 2, 128]))
        wbufs = Rot([P.alloc([128, 8, 512]) for _ in range(2)], "adaw")
        abT = P.alloc([128, 48])
        P.dma(abT, ada_bT.ap()[l], [], ["abT"])
        ngT = P.alloc([128, 4, 8])
        P.dma(ngT, ada_dummy_ngT(l), [], ["ngT"])
        brow = P.alloc([128, 2, D])
        ngrow = P.alloc([128, 2, D])
        for wi, c0 in enumerate((2 * D, 5 * D)):
            P.dma(brow[:, wi, :], ada_brow.ap()[l, 0:1, c0:c0 + D].partition_broadcast(128)[:, 0, :], [], ["brow"])
            P.dma(ngrow[:, wi, :], ng_row.ap()[l, 1 + 2 * wi, 0:1, :].partition_broadcast(128)[:, 0, :], [], ["ngrow"])
        for ci in range(12):
            wb, wk = wbufs.next()
            P.dma(wb, ada_w.ap()[l].rearrange("(k p) n -> p k n", p=128)[:, :, ci * 512:(ci + 1) * 512], [], [wk])
            for mi in range(4):
                m = ci * 4 + mi
                ps, pk = psrot.next()
                for k in range(8):
                    P.mm(ps[:, 0:2], wb[:, k, mi * 128:(mi + 1) * 128], scs[:, k, :], k == 0, k == 7,
                         [wk, "scs"], [pk])
                P.v("dve", "tensor_scalar", [pk, "abT"], ["modT"], modT[:, m, :], ps[:, 0:2], abT[:, m:m + 1], None, ALU.add)
            if ci in (4, 5, 10, 11):
                wi = 0 if ci < 6 else 1
                cc = (ci - 4) if ci < 6 else (ci - 10)
                for v in range(2):
                    ps, pk = psrot.next()
                    for k in range(8):
                        P.mm(ps, sc_rep[:, k, v, :], wb[:, k, :], k == 0, k == 7, [wk, "sc_rep"], [pk])
                    P.v("dve", "tensor_tensor", [pk, "brow"], ["grow"], grow[:, wi, v, cc * 512:(cc + 1) * 512], ps,
                        brow[:, wi, cc * 512:(cc + 1) * 512], ALU.add)
        for wi in range(2):
            for v in range(2):
                P.v("pool", "tensor_tensor", ["grow", "ngrow"], ["grow"], grow[:, wi, v, :], grow[:, wi, v, :], ngrow[:, wi, :], ALU.mult)
        for wh in range(2):
            shm = 0 if wh == 0 else 24
            scm = 8 if wh == 0 else 32
            for v in range(2):
                P.v("dve", "scalar_tensor_tensor", ["modT", "ngT"], ["AB"], AB[:, 2 * wh, :, v], modT[:, scm:scm + 8, v], 1.0,
                    ngT[:, 2 * wh, :], ALU.add, ALU.mult)
                P.v("dve", "tensor_copy", ["modT"], ["AB"], AB[:, 2 * wh + 1, :, v], modT[:, shm:shm + 8, v])
        P.barrier()
        P.release(m0)

    def ada_dummy_ngT(l):
        return ng_T.ap()[l].rearrange("f p k -> p f k")

    def norm_mod_transpose(xt, xkey, hT, hkey, tt, wh, scratch):
        v = 1 if tt < 2 else 0
        junk, ss, rstd, xn = scratch
        P.act(junk, xt, AF.Square, [xkey], ["nm_junk", "nm_ss"], accum_out=ss)
        P.act(rstd, ss, AF.Sqrt, ["nm_ss", "eps_t"], ["nm_rstd"], bias=eps_t[:], scale=1.0 / D)
        P.v("dve", "reciprocal", ["nm_rstd"], ["nm_rstd"], rstd, rstd)
        P.act(xn, xt, AF.Identity, [xkey, "nm_rstd"], ["nm_xn"], scale=rstd)
        for half in range(2):
            ps, pk = psrot.next()
            for kk in range(4):
                k = half * 4 + kk
                P.tr(ps[:, kk * 128:(kk + 1) * 128], xn[:, k * 128:(k + 1) * 128], ident[:], ["nm_xn", "ident"], [pk])
            for kk in range(4):
                k = half * 4 + kk
                eng = "dve" if kk % 2 == 0 else "pool"
                if eng == "pool":
                    P.act(hT[:, k, tt * 128:(tt + 1) * 128], ps[:, kk * 128:(kk + 1) * 128], AF.Identity,
                          [pk, "AB"], [hkey], bias=AB[:, 2 * wh + 1, k, v:v + 1], scale=AB[:, 2 * wh, k, v:v + 1])
                else:
                    P.v("dve", "tensor_scalar", [pk, "AB"], [hkey], hT[:, k, tt * 128:(tt + 1) * 128],
                        ps[:, kk * 128:(kk + 1) * 128], AB[:, 2 * wh, k, v:v + 1], AB[:, 2 * wh + 1, k, v:v + 1],
                        ALU.mult, ALU.add)

    def nm_scratch():
        return (P.alloc([128, D]), P.alloc([128, 1]), P.alloc([128, 1]), P.alloc([128, D]))

    def stage_B(l, hT):
        m0 = P.mark()
        xbufs = Rot([P.alloc([128, D]) for _ in range(3)], "xt")
        scratch = nm_scratch()
        for tt in range(NTT):
            xt, xk = xbufs.next()
            P.dma(xt, xrows(l, tt * 128, 128), ["xs%d" % tt], [xk])
            norm_mod_transpose(xt, xk, hT, "hT", tt, 0, scratch)

    def load_cast(dst_bf, src_ap, shape, stage_rot, key, eng="pool"):
        st, sk = stage_rot.next()
        P.dma(st, src_ap, [], [sk])
        if eng == "act":
            P.act(dst_bf, st, AF.Copy, [sk], [key])
        else:
            P.v(eng, "tensor_copy", [sk], [key], dst_bf, st)

    def stage_C(l, hT):
        m0 = P.mark()
        wst = Rot([P.alloc([128, 8, 512]) for _ in range(2)], "wst")
        wbf = Rot([P.alloc([128, 8, 512], BF16) for _ in range(2)], "wbf")
        outb = Rot([P.alloc([128, T]) for _ in range(2)], "pout")
        wsrc = w_fm.ap()[l].rearrange("(k p) n -> p k n", p=128)
        for cg in range(NFM // 512):
            wb, wk = wbf.next()
            load_cast(wb, wsrc[:, :, cg * 512:(cg + 1) * 512], None, wst, wk, eng="pool")
            for ci in range(4):
                ct = cg * 4 + ci
                ob, ok = outb.next()
                for ch, (c0, cn) in enumerate(CH512):
                    ps, pk = psrot.next()
                    for k in range(8):
                        P.mm(ps[:, 0:cn], wb[:, k, ci * 128:(ci + 1) * 128], hT[:, k, c0:c0 + cn], k == 0, k == 7,
                             [wk, "hT"], [pk])
                    if ch % 2 == 0:
                        P.act(ob[:, c0:c0 + cn], ps[:, 0:cn], AF.Copy, [pk], [ok])
                    else:
                        P.v("dve", "tensor_copy", [pk], [ok], ob[:, c0:c0 + cn], ps[:, 0:cn])
                P.dma(pT_d.ap()[ct * 128:(ct + 1) * 128, :], ob, [ok], ["pT_d"], q="pool")
        tmo = Rot([P.alloc([128, 512]) for _ in range(3)], "tmo")
        wsrc = w_tm.ap()[l].rearrange("(k p) n -> p k n", p=128)
        for cg in range(NTM // 512):
            wb, wk = wbf.next()
            load_cast(wb, wsrc[:, :, cg * 512:(cg + 1) * 512], None, wst, wk, eng="pool")
            for tt in range(NTT):
                ps, pk = psrot.next()
                for k in range(8):
                    P.mm(ps, hT[:, k, tt * 128:(tt + 1) * 128], wb[:, k, :], k == 0, k == 7, [wk, "hT"], [pk])
                ob, ok = tmo.next()
                if tt % 2 == 0:
                    P.act(ob, ps, AF.Copy, [pk], [ok])
                else:
                    P.v("dve", "tensor_copy", [pk], [ok], ob, ps)
                row_h = (CTX0 + tt * 128) if tt < 2 else (LAT0 + (tt - 2) * 128)
                c0 = cg * 512
                if c0 + 512 <= 768:
                    P.dma(hyp_d.ap()[row_h:row_h + 128, c0:c0 + 512], ob, [ok], ["hyp_d"], q="pool")
                elif c0 >= 768:
                    P.dma(rtok_d.ap()[tt * 128:(tt + 1) * 128, c0 - 768:c0 - 768 + 512], ob, [ok], ["rtok_d"], q="pool")
                else:
                    nh = 768 - c0
                    P.dma(hyp_d.ap()[row_h:row_h + 128, c0:768], ob[:, 0:nh], [ok], ["hyp_d"], q="pool")
                    P.dma(rtok_d.ap()[tt * 128:(tt + 1) * 128, 0:512 - nh], ob[:, nh:512], [ok], ["rtok_d"], q="pool")
        P.barrier()
        P.release(m0)

    def stage_E(l, h2T, last):
        m0 = P.mark()
        wst = Rot([P.alloc([128, 8, 512]) for _ in range(2)], "wst")
        wo = P.alloc([128, 8, D], BF16)
        wsrc = w_out.ap()[l].rearrange("(k p) n -> p k n", p=128)
        for hh in range(2):
            load_cast(wo[:, :, hh * 512:(hh + 1) * 512], wsrc[:, :, hh * 512:(hh + 1) * 512], None, wst, "wo", eng="pool")
        obufs = Rot([P.alloc([128, 8, 128], BF16) for _ in range(3)], "oT")
        xbufs = Rot([P.alloc([128, D]) for _ in range(3)], "xt")
        ysb = P.alloc([128, D])
        junk = P.alloc([128, D])
        ss = P.alloc([128, 1])
        rstd = P.alloc([128, 1])
        tmp = P.alloc([128, D])
        xnb = Rot([P.alloc([128, D]) for _ in range(2)], "xnew")
        scratch = nm_scratch()
        osrc = oT_d.ap().rearrange("(k p) t -> p k t", p=128)
        for tt in range(2 if last else 0, NTT):
            v = 1 if tt < 2 else 0
            ob, ok = obufs.next()
            P.dma(ob, osrc[:, :, tt * 128:(tt + 1) * 128], ["oT_d"], [ok])
            xt, xk = xbufs.next()
            P.dma(xt, xrows(l, tt * 128, 128), ["xs%d" % tt], [xk])
            for hh in range(2):
                ps, pk = psrot.next()
                for k in range(8):
                    P.mm(ps, ob[:, k, :], wo[:, k, hh * 512:(hh + 1) * 512], k == 0, k == 7, [ok, "wo"], [pk])
                P.act(ysb[:, hh * 512:(hh + 1) * 512], ps, AF.Copy, [pk], ["ysb"])
            P.act(junk, ysb, AF.Square, ["ysb"], ["e_junk", "e_ss"], accum_out=ss)
            P.act(rstd, ss, AF.Sqrt, ["e_ss", "eps_t"], ["e_rstd"], bias=eps_t[:], scale=1.0 / D)
            P.v("dve", "reciprocal", ["e_rstd"], ["e_rstd"], rstd, rstd)
            P.v("dve", "scalar_tensor_tensor", ["ysb", "e_rstd", "grow"], ["e_tmp"], tmp, ysb, rstd, grow[:, 0, v, :], ALU.mult, ALU.mult)
            xn, xnk = xnb.next()
            P.v("pool", "tensor_tensor", ["e_tmp", xk], [xnk], xn, tmp, xt, ALU.add)
            P.dma(xs.ap()[tt * 128:(tt + 1) * 128, :], xn, [xnk], ["xs%d" % tt], q="pool")
            norm_mod_transpose(xn, xnk, h2T, "h2T", tt, 1, scratch)
        P.barrier()
        P.release(m0)

    def stage_F(l, h2T, last):
        m0 = P.mark()
        wst = Rot([P.alloc([128, 8, 256]) for _ in range(2)], "fwst")
        wbf = Rot([P.alloc([128, 8, 256], BF16) for _ in range(2)], "fwbf")
        cw = P.alloc([128, 44, 3])
        cb = P.alloc([128, 44])
        P.dma(cw, ffn_cw.ap()[l], [], ["ffn_cw"])
        P.dma(cb, ffn_cb.ap()[l], [], ["ffn_cb"])
        uab = [P.alloc([128, W_PAD], BF16) for _ in range(2)]
        uvb = [P.alloc([128, W_PAD], BF16) for _ in range(2)]
        ca = P.alloc([128, W_PAD])
        cv = P.alloc([128, W_PAD])
        gb = Rot([P.alloc([128, W_PAD], BF16) for _ in range(1)], "gb")
        for i_ in range(2):
            P.v("dve", "memset", [], ["ua%d" % i_], uab[i_], 0.0)
            P.v("pool", "memset", [], ["uv%d" % i_], uvb[i_], 0.0)
        wsrc = w_up.ap()[l].rearrange("(k p) n -> p k n", p=128)
        Wp = W_PAD

        def f_mm(j):
            wb, wk = wbf.next()
            st, sk = wst.next()
            P.dma(st[:, :, 0:128], wsrc[:, :, j * 128:(j + 1) * 128], [], [sk])
            P.dma(st[:, :, 128:256], wsrc[:, :, DFF + j * 128:DFF + (j + 1) * 128], [], [sk])
            P.v("pool", "tensor_copy", [sk], [wk], wb, st)
            for half, (ub, ukey) in enumerate(((uab[j % 2], "ua%d" % (j % 2)), (uvb[j % 2], "uv%d" % (j % 2)))):
                for ch, (c0, cn) in enumerate(CH512):
                    ps, pk = psrot.next()
                    for k in range(8):
                        P.mm(ps[:, 0:cn], wb[:, k, half * 128:(half + 1) * 128], h2T[:, k, c0:c0 + cn], k == 0, k == 7,
                             [wk, "h2T"], [pk])
                    segs = []
                    if c0 < CTX:
                        segs.append((0, CTX, CTX0))
                        segs.append((CTX, cn - CTX, LAT0))
                    else:
                        segs.append((0, cn, LAT0 + c0 - CTX))
                    for (s0, sn, d0) in segs:
                        if (ch + half) % 2 == 0:
                            P.act(ub[:, d0:d0 + sn], ps[:, s0:s0 + sn], AF.Copy, [pk], [ukey])
                        else:
                            P.v("dve", "tensor_copy", [pk], [ukey], ub[:, d0:d0 + sn], ps[:, s0:s0 + sn])

        def f_conv(j):
            for half, (ub, ukey, cbuf, ckey) in enumerate(((uab[j % 2], "ua%d" % (j % 2), ca, "ca"), (uvb[j % 2], "uv%d" % (j % 2), cv, "cv"))):
                jj = j + 22 * half
                P.act(cbuf[:, 1:Wp - 1], ub[:, 1:Wp - 1], AF.Identity, [ukey, "ffn_cw", "ffn_cb"], [ckey],
                      bias=cb[:, jj:jj + 1], scale=cw[:, jj, 1:2])
                P.v("dve", "scalar_tensor_tensor", [ukey, ckey, "ffn_cw"], [ckey], cbuf[:, 1:Wp - 1], ub[:, 0:Wp - 2],
                    cw[:, jj, 0:1], cbuf[:, 1:Wp - 1], ALU.mult, ALU.add)
                P.v("dve", "scalar_tensor_tensor", [ukey, ckey, "ffn_cw"], [ckey], cbuf[:, 1:Wp - 1],
                    ub[:, 2:Wp], cw[:, jj, 2:3], cbuf[:, 1:Wp - 1], ALU.mult, ALU.add)
            P.act(ca[:, 1:Wp - 1], ca[:, 1:Wp - 1], AF.Silu, ["ca"], ["ca"])
            g, gk = gb.next()
            P.v("pool", "tensor_tensor", ["ca", "cv"], [gk], g[:, 1:Wp - 1], ca[:, 1:Wp - 1], cv[:, 1:Wp - 1], ALU.mult)
            P.dma(gT_d.ap()[j * 128:(j + 1) * 128, 1:Wp - 1], g[:, 1:Wp - 1], [gk], ["gT_d"], q="pool")

        for j in range(22):
            f_mm(j)
            if j >= 1:
                f_conv(j - 1)
        f_conv(21)
        P.barrier()
        P.release(m0)

    def stage_F2(l, last):
        m0 = P.mark()
        wst = Rot([P.alloc([128, 2, D]) for _ in range(2)], "dwst")
        wd = P.alloc([128, 22, D], BF16)
        wsrc = w_down.ap()[l].rearrange("(j p) n -> p j n", p=128)
        for j2 in range(11):
            load_cast(wd[:, 2 * j2:2 * j2 + 2, :], wsrc[:, 2 * j2:2 * j2 + 2, :], None, wst, "wd", eng="pool")
        gbufs = Rot([P.alloc([128, 22, 128], BF16) for _ in range(3)], "gt")
        xbufs = Rot([P.alloc([128, D]) for _ in range(3)], "xt")
        ysb = P.alloc([128, D])
        junk = P.alloc([128, D])
        ss = P.alloc([128, 1])
        rstd = P.alloc([128, 1])
        tmp = P.alloc([128, D])
        xnb = Rot([P.alloc([128, D]) for _ in range(2)], "xnew")
        gsrc = gT_d.ap().rearrange("(j p) t -> p j t", p=128)
        for tt in range(2 if last else 0, NTT):
            v = 1 if tt < 2 else 0
            col = (CTX0 + tt * 128) if tt < 2 else (LAT0 + (tt - 2) * 128)
            gt, gk = gbufs.next()
            P.dma(gt, gsrc[:, :, col:col + 128], ["gT_d"], [gk])
            xt, xk = xbufs.next()
            P.dma(xt, xs.ap()[tt * 128:(tt + 1) * 128, :], ["xs%d" % tt], [xk])
            for hh in range(2):
                ps, pk = psrot.next()
                for j in range(22):
                    P.mm(ps, gt[:, j, :], wd[:, j, hh * 512:(hh + 1) * 512], j == 0, j == 21, [gk, "wd"], [pk])
                P.act(ysb[:, hh * 512:(hh + 1) * 512], ps, AF.Copy, [pk], ["ysb"])
            P.act(junk, ysb, AF.Square, ["ysb"], ["e_junk", "e_ss"], accum_out=ss)
            P.act(rstd, ss, AF.Sqrt, ["e_ss", "eps_t"], ["e_rstd"], bias=eps_t[:], scale=1.0 / D)
            P.v("dve", "reciprocal", ["e_rstd"], ["e_rstd"], rstd, rstd)
            P.v("dve", "scalar_tensor_tensor", ["ysb", "e_rstd", "grow"], ["e_tmp"], tmp, ysb, rstd, grow[:, 1, v, :], ALU.mult, ALU.mult)
            xn, xnk = xnb.next()
            P.v("pool", "tensor_tensor", ["e_tmp", xk], [xnk], xn, tmp, xt, ALU.add)
            if last:
                P.dma(y_out.ap()[(tt - 2) * 128:(tt - 1) * 128, :], xn, [xnk], ["y"], q="pool")
            else:
                P.dma(xs.ap()[tt * 128:(tt + 1) * 128, :], xn, [xnk], ["xs%d" % tt], q="pool")
        P.barrier()
        P.release(m0)

    for l in layers:
        last = (l == DEPTH - 1)
        if "A" in stages:
            stage_A(l)
        mh = P.mark()
        hT = P.alloc([128, 8, T], BF16)
        if "B" in stages:
            stage_B(l, hT)
        if "C" in stages:
            stage_C(l, hT)
        P.barrier()
        P.release(mh)
        if "D" in stages:
            C = SimpleNamespace(B=B, P=P, nc=nc, psrot=psrot, pss=pss, pT_d=pT_d, hyp_d=hyp_d, rtok_d=rtok_d, oT_d=oT_d,
                                ident=ident, zero_sb=zero_sb, eps_t=eps_t, ones_f=ones_f, ones_b=ones_b, cfg=cfg)
            for mx in mixers:
                MIXERS[mx](C, l, last)
                P.barrier()
        mh = P.mark()
        h2T = P.alloc([128, 8, T], BF16)
        if "E" in stages:
            stage_E(l, h2T, last)
        if "F" in stages:
            stage_F(l, h2T, last)
        P.barrier()
        P.release(mh)
        if "F" in stages:
            stage_F2(l, last)
    P.emit()
    return B


MIXERS = {}
HOSTPREP = {}


def make_inputs(inp, cfg=None):
    sh = {}
    sh.update(host_consts())
    L = DEPTH
    sh["ada_w"] = np.ascontiguousarray(inp["ada_w"])
    sh["ada_bT"] = np.ascontiguousarray(inp["ada_b"].reshape(L, 48, 128).transpose(0, 2, 1))
    sh["ada_brow"] = np.ascontiguousarray(inp["ada_b"].reshape(L, 1, 6 * D))
    sh["ng_T"] = np.ascontiguousarray(inp["norm_g"].reshape(L, 4, 8, 128).transpose(0, 1, 3, 2))
    sh["ng_row"] = np.ascontiguousarray(inp["norm_g"].reshape(L, 4, 1, D))
    lw = [prep_layer_weights(inp, l) for l in range(L)]
    sh["w_fm"] = np.stack([w["w_fm"] for w in lw])
    sh["w_tm"] = np.stack([w["w_tm"] for w in lw])
    sh["w_out"] = np.ascontiguousarray(inp["w_out"])
    sh["w_up"] = np.ascontiguousarray(inp["ffn_w_up"])
    sh["ffn_cw"] = np.ascontiguousarray(inp["ffn_conv_w"].reshape(L, 3, 44, 128).transpose(0, 3, 2, 1))
    sh["ffn_cb"] = np.ascontiguousarray(inp["ffn_conv_b"].reshape(L, 44, 128).transpose(0, 2, 1))
    sh["w_down"] = np.ascontiguousarray(inp["ffn_w_down"])
    for fn in HOSTPREP.values():
        sh.update(fn(inp))
    per = []
    for core in range(8):
        b = core % NB
        d = {}
        d["x_in"] = np.ascontiguousarray(inp["x"][b])
        d["ctx_in"] = np.ascontiguousarray(inp["ctx"][b])
        cs = np.stack([inp["c"][b], inp["c_ctx"]], axis=-1)
        d["cs_in"] = np.ascontiguousarray(cs.reshape(8, 128, 2).transpose(1, 0, 2))
        per.append(d)
    return sh, per


_CACHE = {}
ACTIVE_CORES = [0, 1, 4, 5]


def kernel(**inputs):
    inp = {k: np.asarray(v) for k, v in inputs.items()}
    cfg = {}
    Bd = build_program(cfg)
    sh, per = make_inputs(inp)
    zero_keys = ("x_in", "ctx_in", "w_fm", "w_tm", "w_out", "w_up", "w_down", "ada_w")
    zeros = {k: np.zeros_like(sh[k] if k in sh else per[0][k]) for k in zero_keys}
    in_maps = []
    for core in range(8):
        m = dict(sh)
        if core in ACTIVE_CORES:
            m.update(per[ACTIVE_CORES.index(core)])
        else:
            m.update(per[0])
            m.update(zeros)
        in_maps.append(m)
    res = run_bass_kernel_spmd(Bd.nc, in_maps, core_ids=list(range(8)))
    out = np.stack([res.results[ACTIVE_CORES[b]]["y"] for b in range(NB)], axis=0)
    return out.astype(np.float32)


MLA_H = 4
QK = 96
PT_CQA, PT_CQB, PT_CKV, PT_KR, PT_KRS = 1536, 1664, 1792, 1920, 1952
OT_MLA = 768


def host_mla(inp):
    L = DEPTH
    o = {}
    wq = inp["mla_w_uq"].reshape(L, 192, MLA_H, QK)
    wqB = wq.copy()
    rope = wq[..., 64:96].reshape(L, 192, MLA_H, 2, 16)[..., ::-1, :].reshape(L, 192, MLA_H, 32)
    wqB[..., 64:96] = rope
    o["mla_wq"] = np.ascontiguousarray(np.stack([wq, wqB], axis=2).reshape(L, 192, 2 * MLA_H * QK))
    wkv = inp["mla_w_ukv"].reshape(L, 128, MLA_H, 128)
    o["mla_wkn"] = np.ascontiguousarray(wkv[..., 0:64].reshape(L, 128, 256))
    o["mla_wv"] = np.ascontiguousarray(wkv[..., 64:128].reshape(L, 128, 256))
    qg = np.zeros((L, 256), np.float32)
    qg[:, :192] = inp["mla_q_norm_g"]
    o["mla_qg"] = np.ascontiguousarray(qg.reshape(L, 2, 128).transpose(0, 2, 1))
    o["mla_kvg"] = np.ascontiguousarray(inp["mla_kv_norm_g"].reshape(L, 128, 1))
    n = SEQ
    rows = n // 64
    row = np.repeat(np.arange(rows), 64).astype(np.float32)
    col = np.tile(np.arange(64), rows).astype(np.float32)
    nf = 8
    inv = (np.float32(10000.0) ** (-np.arange(nf, dtype=np.float32) / nf)).astype(np.float32)
    ang = np.concatenate([row[:, None] * inv, col[:, None] * inv], axis=-1).astype(np.float32)
    cos, sin = np.cos(ang).astype(np.float32), np.sin(ang).astype(np.float32)
    Cf = np.ones((96, T), np.float32)
    Sf = np.zeros((96, T), np.float32)
    Cf[64:80, CTX:] = cos.T
    Cf[80:96, CTX:] = cos.T
    Sf[64:80, CTX:] = -sin.T
    Sf[80:96, CTX:] = sin.T
    o["mla_C"] = Cf
    o["mla_S"] = Sf
    return o


def mix_mla(C, l, last):
    P, B = C.P, C.B
    psrot = C.psrot
    if "mla_wq" not in B.din:
        B.inp("mla_wq", [DEPTH, 192, 768])
        B.inp("mla_wkn", [DEPTH, 128, 256])
        B.inp("mla_wv", [DEPTH, 128, 256])
        B.inp("mla_qg", [DEPTH, 128, 2])
        B.inp("mla_kvg", [DEPTH, 128, 1])
        B.inp("mla_C", [96, T])
        B.inp("mla_S", [96, T])
    din = B.din
    pT = C.pT_d.ap()
    m0 = P.mark()
    wq_st = P.alloc([128, 2, 768])
    wq = P.alloc([128, 2, 768], BF16)
    P.dma(wq_st[:, 0, :], din["mla_wq"].ap()[l, 0:128, :], [], ["mla_wq_st"])
    P.dma(wq_st[0:64, 1, :], din["mla_wq"].ap()[l, 128:192, :], [], ["mla_wq_st"])
    P.v("pool", "tensor_copy", ["mla_wq_st"], ["mla_wq"], wq[:, 0, :], wq_st[:, 0, :])
    P.v("pool", "tensor_copy", ["mla_wq_st"], ["mla_wq"], wq[0:64, 1, :], wq_st[0:64, 1, :])
    wk_st = P.alloc([128, 512])
    wkv = P.alloc([128, 512], BF16)
    P.dma(wk_st[:, 0:256], din["mla_wkn"].ap()[l], [], ["mla_wk_st"])
    P.dma(wk_st[:, 256:512], din["mla_wv"].ap()[l], [], ["mla_wk_st"])
    P.v("pool", "tensor_copy", ["mla_wk_st"], ["mla_wkv"], wkv, wk_st)
    qg = P.alloc([128, 2])
    kvg = P.alloc([128, 1])
    P.dma(qg, din["mla_qg"].ap()[l], [], ["mla_qg"])
    P.dma(kvg, din["mla_kvg"].ap()[l], [], ["mla_kvg"])
    qT = [P.alloc([96, T], BF16) for _ in range(MLA_H)]
    kT = [P.alloc([96, T], BF16) for _ in range(MLA_H)]
    vtok = P.alloc([128, NTT, 256], BF16)
    m1 = P.mark()
    NB_ = 2
    cqa = Rot([P.alloc([128, 512]) for _ in range(NB_)], "m_cqa")
    cqb = Rot([P.alloc([64, 512]) for _ in range(NB_)], "m_cqb")
    ckv = Rot([P.alloc([128, 512]) for _ in range(NB_)], "m_ckv")
    krb = Rot([P.alloc([96, 2, 512]) for _ in range(NB_)], "m_kr")
    tabs = Rot([P.alloc([96, 2, 512]) for _ in range(NB_)], "m_tab")
    sq = P.alloc([128, 3, 512])
    rstd = P.alloc([128, 2, 512])
    cqn_a = P.alloc([128, 512], BF16)
    cqn_b = P.alloc([64, 512], BF16)
    ckvn = P.alloc([128, 512], BF16)
    t1 = P.alloc([96, 512])
    t2 = P.alloc([96, 512])
    for ch, (c0, cn) in enumerate(CH512):
        a, ak = cqa.next()
        b, bk = cqb.next()
        kv, kvk = ckv.next()
        kr, krk = krb.next()
        tb, tbk = tabs.next()
        P.dma(a[:, 0:cn], pT[PT_CQA:PT_CQA + 128, c0:c0 + cn], ["pT_d"], [ak])
        P.dma(b[:, 0:cn], pT[PT_CQB:PT_CQB + 64, c0:c0 + cn], ["pT_d"], [bk])
        P.dma(kv[:, 0:cn], pT[PT_CKV:PT_CKV + 128, c0:c0 + cn], ["pT_d"], [kvk])
        P.dma(kr[64:96, 0, 0:cn], pT[PT_KR:PT_KR + 32, c0:c0 + cn], ["pT_d"], [krk])
        P.dma(kr[64:96, 1, 0:cn], pT[PT_KRS:PT_KRS + 32, c0:c0 + cn], ["pT_d"], [krk])
        P.dma(tb[:, 0, 0:cn], din["mla_C"].ap()[:, c0:c0 + cn], [], [tbk])
        P.dma(tb[:, 1, 0:cn], din["mla_S"].ap()[:, c0:c0 + cn], [], [tbk])
        P.act(sq[:, 0, 0:cn], a[:, 0:cn], AF.Square, [ak], ["m_sq"])
        P.act(sq[0:64, 1, 0:cn], b[:, 0:cn], AF.Square, [bk], ["m_sq"])
        P.act(sq[:, 2, 0:cn], kv[:, 0:cn], AF.Square, [kvk], ["m_sq"])
        ps, pk = psrot.next()
        P.mm(ps[:, 0:cn], C.ones_f[:, :], sq[:, 0, 0:cn], True, False, ["m_sq", "ones_f"], [pk])
        P.mm(ps[:, 0:cn], C.ones_f[0:64, :], sq[0:64, 1, 0:cn], False, True, ["m_sq", "ones_f"], [pk])
        P.act(rstd[:, 0, 0:cn], ps[:, 0:cn], AF.Sqrt, [pk, "eps_t"], ["m_rstd"], bias=C.eps_t[:], scale=1.0 / 192)
        ps2, pk2 = psrot.next()
        P.mm(ps2[:, 0:cn], C.ones_f[:, :], sq[:, 2, 0:cn], True, True, ["m_sq", "ones_f"], [pk2])
        P.act(rstd[:, 1, 0:cn], ps2[:, 0:cn], AF.Sqrt, [pk2, "eps_t"], ["m_rstd"], bias=C.eps_t[:], scale=1.0 / 128)
        P.v("dve", "reciprocal", ["m_rstd"], ["m_rstd"], rstd[:, :, 0:cn], rstd[:, :, 0:cn])
        P.v("dve", "scalar_tensor_tensor", [ak, "mla_qg", "m_rstd"], ["m_cqn_a"], cqn_a[:, 0:cn], a[:, 0:cn], qg[:, 0:1],
            rstd[:, 0, 0:cn], ALU.mult, ALU.mult)
        P.v("dve", "scalar_tensor_tensor", [bk, "mla_qg", "m_rstd"], ["m_cqn_b"], cqn_b[:, 0:cn], b[:, 0:cn], qg[0:64, 1:2],
            rstd[0:64, 0, 0:cn], ALU.mult, ALU.mult)
        P.v("dve", "scalar_tensor_tensor", [kvk, "mla_kvg", "m_rstd"], ["m_ckvn"], ckvn[:, 0:cn], kv[:, 0:cn], kvg[:, 0:1],
            rstd[:, 1, 0:cn], ALU.mult, ALU.mult)
        for h in range(MLA_H):
            psA, pkA = psrot.next()
            psB, pkB = psrot.next()
            for (pp, ppk, ab) in ((psA, pkA, 0), (psB, pkB, 1)):
                w0 = (ab * MLA_H + h) * QK
                P.mm(pp[0:96, 0:cn], wq[:, 0, w0:w0 + QK], cqn_a[:, 0:cn], True, False, ["mla_wq", "m_cqn_a"], [ppk])
                P.mm(pp[0:96, 0:cn], wq[0:64, 1, w0:w0 + QK], cqn_b[:, 0:cn], False, True, ["mla_wq", "m_cqn_b"], [ppk])
            P.v("dve", "tensor_tensor", [pkA, tbk], ["m_t1"], t1[:, 0:cn], psA[0:96, 0:cn], tb[:, 0, 0:cn], ALU.mult)
            P.v("dve", "tensor_tensor", [pkB, tbk], ["m_t2"], t2[:, 0:cn], psB[0:96, 0:cn], tb[:, 1, 0:cn], ALU.mult)
            P.v("pool", "tensor_tensor", ["m_t1", "m_t2"], ["m_qT%d" % h], qT[h][:, c0:c0 + cn], t1[:, 0:cn], t2[:, 0:cn], ALU.add)
        for h in range(MLA_H):
            ps, pk = psrot.next()
            P.mm(ps[0:64, 0:cn], wkv[:, h * 64:(h + 1) * 64], ckvn[:, 0:cn], True, True, ["mla_wkv", "m_ckvn"], [pk])
            P.act(kT[h][0:64, c0:c0 + cn], ps[0:64, 0:cn], AF.Copy, [pk], ["m_kT%d" % h])
        P.v("dve", "tensor_tensor", [krk, tbk], ["m_t1"], t1[64:96, 0:cn], kr[64:96, 0, 0:cn], tb[64:96, 0, 0:cn], ALU.mult)
        P.v("dve", "tensor_tensor", [krk, tbk], ["m_t2"], t2[64:96, 0:cn], kr[64:96, 1, 0:cn], tb[64:96, 1, 0:cn], ALU.mult)
        P.v("dve", "tensor_tensor", ["m_t1", "m_t2"], ["m_t1"], t1[64:96, 0:cn], t1[64:96, 0:cn], t2[64:96, 0:cn], ALU.add)
        for h in range(MLA_H):
            if h % 2 == 0:
                P.act(kT[h][64:96, c0:c0 + cn], t1[64:96, 0:cn], AF.Copy, ["m_t1"], ["m_kT%d" % h])
            else:
                P.v("pool", "tensor_copy", ["m_t1"], ["m_kT%d" % h], kT[h][64:96, c0:c0 + cn], t1[64:96, 0:cn])
        for i in range(cn // 128):
            tt = c0 // 128 + i
            ps, pk = psrot.next()
            P.mm(ps[:, 0:256], ckvn[:, i * 128:(i + 1) * 128], wkv[:, 256:512], True, True, ["mla_wkv", "m_ckvn"], [pk])
            P.act(vtok[:, tt, :], ps[:, 0:256], AF.Copy, [pk], ["m_vtok"])
    P.barrier()
    P.release(m1)
    srot = Rot([p[:] for p in C.pss[0:6]], "ps")
    ps_o, pk_o = C.pss[6][:], "ps6"
    ps_d, pk_d = C.pss[7][:], "ps7"
    pT_b = Rot([P.alloc([128, 512], BF16) for _ in range(6)], "m_pT")
    rden = P.alloc([64, 512])
    ost = Rot([P.alloc([64, 512], BF16) for _ in range(2)], "m_ost")
    scale = 1.0 / math.sqrt(QK)
    blocks = [(CTX + i * 512, 512, 0, NTT) for i in range(SEQ // 512)]
    if not last:
        blocks = [(0, CTX, 0, 2)] + blocks
    steps = []
    for (q0, qn, kt0, kt1) in blocks:
        for h in range(MLA_H):
            for kt in range(kt0, kt1):
                steps.append((q0, qn, kt0, kt1, h, kt))
    LOOK = 2
    pend = {}

    def issue_scores(i):
        q0, qn, kt0, kt1, h, kt = steps[i]
        ps, pk = srot.next()
        P.mm(ps[:, 0:qn], kT[h][:, kt * 128:(kt + 1) * 128], qT[h][:, q0:q0 + qn], True, True,
             ["m_kT%d" % h, "m_qT%d" % h], [pk])
        pb, pbk = pT_b.next()
        P.act(pb[:, 0:qn], ps[:, 0:qn], AF.Exp, [pk], [pbk], scale=scale)
        pend[i] = (pb, pbk)

    for i in range(min(LOOK, len(steps))):
        issue_scores(i)
    for i in range(len(steps)):
        if i + LOOK < len(steps):
            issue_scores(i + LOOK)
        q0, qn, kt0, kt1, h, kt = steps[i]
        pb, pbk = pend.pop(i)
        P.mm(ps_o[0:64, 0:qn], vtok[:, kt, h * 64:(h + 1) * 64], pb[:, 0:qn], kt == kt0, kt == kt1 - 1, [pbk, "m_vtok"], [pk_o])
        P.mm(ps_d[0:64, 0:qn], C.ones_b[:, 0:64], pb[:, 0:qn], kt == kt0, kt == kt1 - 1, [pbk, "ones_b"], [pk_d])
        if kt == kt1 - 1:
            P.v("dve", "reciprocal", [pk_d], ["m_rden"], rden[:, 0:qn], ps_d[0:64, 0:qn])
            ob, obk = ost.next()
            P.v("dve", "tensor_tensor", [pk_o, "m_rden"], [obk], ob[:, 0:qn], ps_o[0:64, 0:qn], rden[:, 0:qn], ALU.mult)
            P.dma(C.oT_d.ap()[OT_MLA + h * 64:OT_MLA + (h + 1) * 64, q0:q0 + qn], ob[:, 0:qn], [obk], ["oT_d"], q="pool")
    P.barrier()
    P.release(m0)


MIXERS["mla"] = mix_mla
HOSTPREP["mla"] = host_mla


RH = 4
PT_RQ, PT_RK, PT_RG, PT_RQS, PT_RKS = 256, 512, 768, 1024, 1280
OT_RET = 512
LN2 = math.log(2.0)


def host_ret(inp):
    L = DEPTH
    o = {}
    f32 = np.float32
    theta = (f32(10000.0) ** (-np.linspace(0.0, 1.0, 32, dtype=f32))).astype(f32)
    ang = (np.arange(SEQ, dtype=f32)[:, None] * theta).astype(f32)
    cos, sin = np.cos(ang).astype(f32), np.sin(ang).astype(f32)
    Ct = np.ones((T, 64), f32)
    St = np.zeros((T, 64), f32)
    Ct[CTX:, 0:32] = cos
    Ct[CTX:, 32:64] = cos
    St[CTX:, 0:32] = -sin
    St[CTX:, 32:64] = sin
    o["ret_Ct"] = Ct
    o["ret_St"] = St
    o["ret_C"] = np.ascontiguousarray(np.tile(Ct.T, (2, 1)))
    o["ret_S"] = np.ascontiguousarray(np.tile(St.T, (2, 1)))
    j = np.arange(128, dtype=f32)[:, None]
    i = np.arange(128, dtype=f32)[None, :]
    cst = np.zeros((128, 6, 128), f32)
    cst[:, 0, :] = np.maximum(i - j, 0)
    cst[:, 1, :] = np.maximum(j - i, 0)
    cst[:, 2, :] = (i >= j)
    cst[:, 3, :] = (j >= i)
    cst[:, 4, :] = i + 1
    cst[:, 5, :] = 128 - i
    o["ret_cst"] = cst
    col = np.zeros((128, 2), f32)
    col[:, 0] = 127 - np.arange(128)
    col[:, 1] = np.arange(128)
    o["ret_col"] = col
    hm = np.zeros((128, 2), f32)
    hm[:64, 0] = 1.0
    hm[64:, 1] = 1.0
    o["ret_hm"] = hm
    o["ret_dexp"] = np.ascontiguousarray(inp["ret_decay_exp"].reshape(L, 1, 8))
    o["ret_gnT"] = np.ascontiguousarray(inp["ret_gn_g"].reshape(L, RH, 64).transpose(0, 2, 1))
    return o


def mix_ret(C, l, last):
    P, B = C.P, C.B
    psrot = C.psrot
    if "ret_C" not in B.din:
        B.inp("ret_Ct", [T, 64]); B.inp("ret_St", [T, 64])
        B.inp("ret_C", [128, T]); B.inp("ret_S", [128, T])
        B.inp("ret_cst", [128, 6, 128]); B.inp("ret_col", [128, 2]); B.inp("ret_hm", [128, 2])
        B.inp("ret_dexp", [DEPTH, 1, 8]); B.inp("ret_gnT", [DEPTH, 64, 4])
    din = B.din
    pT = C.pT_d.ap()
    m0 = P.mark()
    cst = P.alloc([128, 6, 128])
    P.dma(cst, din["ret_cst"].ap(), [], ["r_cst"])
    colc = P.alloc([128, 2])
    P.dma(colc, din["ret_col"].ap(), [], ["r_col"])
    gnT = P.alloc([64, 4])
    P.dma(gnT, din["ret_gnT"].ap()[l], [], ["r_gnT"])
    lg8 = P.alloc([128, 8])
    P.dma(lg8, din["ret_dexp"].ap()[l, 0:1, :].partition_broadcast(128)[:, 0, :], [], ["r_lg8"])
    P.act(lg8, lg8, AF.Exp, ["r_lg8"], ["r_lg8"], scale=-LN2)
    P.act(lg8, lg8, AF.Ln, ["r_lg8"], ["r_lg8"], scale=-1.0, bias=1.0)
    lgp = P.alloc([128, 2, 2])
    for d_ in range(2):
        for pr in range(2):
            P.v("dve", "tensor_copy", ["r_lg8"], ["r_lgp"], lgp[0:64, d_, pr:pr + 1], lg8[0:64, d_ * 4 + 2 * pr:d_ * 4 + 2 * pr + 1])
            P.v("dve", "tensor_copy", ["r_lg8"], ["r_lgp"], lgp[64:128, d_, pr:pr + 1], lg8[64:128, d_ * 4 + 2 * pr + 1:d_ * 4 + 2 * pr + 2])
    Mk = P.alloc([128, 4, 128])
    mt = P.alloc([128, 128])
    for h in range(RH):
        P.act(Mk[:, h, :], cst[:, 0, :], AF.Exp, ["r_cst", "r_lg8"], ["r_Mk"], scale=lg8[:, h:h + 1])
        P.v("dve", "tensor_tensor", ["r_Mk", "r_cst"], ["r_Mk"], Mk[:, h, :], Mk[:, h, :], cst[:, 2, :], ALU.mult)
        P.act(mt, cst[:, 1, :], AF.Exp, ["r_cst", "r_lg8"], ["r_mt"], scale=lg8[:, 4 + h:5 + h])
        P.v("dve", "tensor_tensor", ["r_mt", "r_cst"], ["r_mt"], mt, mt, cst[:, 3, :], ALU.mult)
        P.v("dve", "tensor_tensor", ["r_mt", "r_Mk"], ["r_Mk"], Mk[:, h, :], Mk[:, h, :], mt, ALU.add)
    XI = P.alloc([128, 2, 2, 128])
    gcolp = P.alloc([128, 2, 2])
    for d_ in range(2):
        for pr in range(2):
            P.act(XI[:, d_, pr, :], cst[:, 4 + d_, :], AF.Exp, ["r_cst", "r_lgp"], ["r_XI"], scale=lgp[:, d_, pr:pr + 1])
            P.act(gcolp[:, d_, pr:pr + 1], lgp[:, d_, pr:pr + 1], AF.Exp, ["r_lgp"], ["r_gcol"], scale=128.0)
    Z = P.alloc([128, 2, 4])
    for d_ in range(2):
        for h in range(RH):
            P.act(Z[:, d_, h:h + 1], colc[:, d_:d_ + 1], AF.Exp, ["r_col", "r_lg8"], ["r_Z"], scale=lg8[:, d_ * 4 + h:d_ * 4 + h + 1])
    hm = P.alloc([128, 2])
    P.dma(hm, din["ret_hm"].ap(), [], ["r_hm"])
    TAB = P.alloc([128, 3, 4, 128])
    for h in range(RH):
        pr, hp = h // 2, h % 2
        P.v("dve", "tensor_copy", ["r_hm"], ["r_TAB"], TAB[:, 0, h, :], hm[:, hp:hp + 1].to_broadcast([128, 128]))
        for d_ in range(2):
            P.v("dve", "tensor_scalar", ["r_XI", "r_hm"], ["r_TAB"], TAB[:, 1 + d_, h, :], XI[:, d_, pr, :], hm[:, hp:hp + 1], None, ALU.mult)
    if C.cfg.get("ret_stop", 9) <= 0:
        P.barrier(); P.release(m0); return
    qr = [P.alloc([128, T], BF16) for _ in range(2)]
    kr = [P.alloc([128, T], BF16) for _ in range(2)]
    vt = P.alloc([128, NTT, 256], BF16)
    Sall = [[P.alloc([128, NTT, 128], BF16) for _ in range(2)] for _ in range(2)]
    m1 = P.mark()
    kz = [P.alloc([128, NTT, 256], BF16) for _ in range(2)]
    m2 = P.mark()
    ld = Rot([P.alloc([128, 4, 512]) for _ in range(2)], "r_ld")
    tb = Rot([P.alloc([128, 2, 512]) for _ in range(2)], "r_tb")
    t1 = P.alloc([128, 512]); t2 = P.alloc([128, 512])
    for ch, (c0, cn) in enumerate(CH512):
        tbv, tbk = tb.next()
        P.dma(tbv[:, 0, 0:cn], din["ret_C"].ap()[:, c0:c0 + cn], [], [tbk])
        P.dma(tbv[:, 1, 0:cn], din["ret_S"].ap()[:, c0:c0 + cn], [], [tbk])
        nq = cn // 128
        for pr in range(2):
            lv, lk = ld.next()
            for ii, r0 in enumerate((PT_RQ, PT_RQS, PT_RK, PT_RKS)):
                P.dma(lv[:, ii, 0:cn], pT[r0 + pr * 128:r0 + (pr + 1) * 128, c0:c0 + cn], ["pT_d"], [lk])
            P.v("dve", "tensor_tensor", [lk, tbk], ["r_t1"], t1[:, 0:cn], lv[:, 0, 0:cn], tbv[:, 0, 0:cn], ALU.mult)
            P.v("pool", "tensor_tensor", [lk, tbk], ["r_t2"], t2[:, 0:cn], lv[:, 1, 0:cn], tbv[:, 1, 0:cn], ALU.mult)
            P.v("dve", "tensor_tensor", ["r_t1", "r_t2"], ["r_t1"], t1[:, 0:cn], t1[:, 0:cn], t2[:, 0:cn], ALU.add)
            P.act(qr[pr][:, c0:c0 + cn], t1[:, 0:cn], AF.Copy, ["r_t1"], ["r_qr%d" % pr], scale=0.125)
            P.v("dve", "tensor_tensor", [lk, tbk], ["r_t1"], t1[:, 0:cn], lv[:, 2, 0:cn], tbv[:, 0, 0:cn], ALU.mult)
            P.v("pool", "tensor_tensor", [lk, tbk], ["r_t2"], t2[:, 0:cn], lv[:, 3, 0:cn], tbv[:, 1, 0:cn], ALU.mult)
            P.v("pool", "tensor_tensor", ["r_t1", "r_t2"], ["r_kr%d" % pr], kr[pr][:, c0:c0 + cn], t1[:, 0:cn], t2[:, 0:cn], ALU.add)
    if C.cfg.get("ret_stop", 9) <= 0.5:
        P.barrier(); P.release(m0); return
    tl = Rot([P.alloc([128, 768]) for _ in range(3)], "r_tl")
    tt_tab = Rot([P.alloc([128, 2, 64]) for _ in range(3)], "r_ttab")
    kk1 = P.alloc([128, 4, 64]); kk2 = P.alloc([128, 4, 64])
    for tt in range(NTT):
        tv, tk = tl.next()
        P.dma(tv, C.rtok_d.ap()[tt * 128:(tt + 1) * 128, :], ["rtok_d"], [tk])
        tab, tabk = tt_tab.next()
        P.dma(tab[:, 0, :], din["ret_Ct"].ap()[tt * 128:(tt + 1) * 128, :], [], [tabk])
        P.dma(tab[:, 1, :], din["ret_St"].ap()[tt * 128:(tt + 1) * 128, :], [], [tabk])
        kview = tv[:, 0:256].rearrange("p (h d) -> p h d", h=4)
        ksview = tv[:, 256:512].rearrange("p (h d) -> p h d", h=4)
        P.v("dve", "tensor_tensor", [tk, tabk], ["r_kk1"], kk1, kview, tab[:, 0, :].unsqueeze(1).to_broadcast([128, 4, 64]), ALU.mult)
        P.v("pool", "tensor_tensor", [tk, tabk], ["r_kk2"], kk2, ksview, tab[:, 1, :].unsqueeze(1).to_broadcast([128, 4, 64]), ALU.mult)
        P.v("dve", "tensor_tensor", ["r_kk1", "r_kk2"], ["r_kk1"], kk1, kk1, kk2, ALU.add)
        for d_ in range(2):
            P.v("dve" if d_ == 0 else "pool", "tensor_tensor", ["r_kk1", "r_Z"], ["r_kz%d" % d_],
                kz[d_][:, tt, :].rearrange("p (h d) -> p h d", h=4), kk1,
                Z[:, d_, :].unsqueeze(2).to_broadcast([128, 4, 64]), ALU.mult)
        P.act(vt[:, tt, :], tv[:, 512:768], AF.Copy, [tk], ["r_vt"])
    if C.cfg.get("ret_stop", 9) <= 1:
        P.barrier(); P.release(m0); return
    Srun = [[[P.alloc([128, 128]) for _ in range(2)] for _ in range(2)] for _ in range(2)]
    order = [list(range(NTT)), [1, 0] + list(range(NTT - 1, 1, -1))]
    for d_ in range(2):
        for pr in range(2):
            P.v("dve", "memset", [], ["r_S%d%d" % (d_, pr)], Sall[d_][pr][:, order[d_][0], :], 0.0)
    for s in range(NTT - 1):
        for d_ in range(2):
            for pr in range(2):
                n = order[d_][s]
                nxt = order[d_][s + 1]
                ps, pk = psrot.next()
                cur = Srun[d_][pr][s % 2]
                new = Srun[d_][pr][(s + 1) % 2]
                ck = "r_Sr%d%d%d" % (d_, pr, s % 2)
                nk = "r_Sr%d%d%d" % (d_, pr, (s + 1) % 2)
                P.mm(ps[:, 0:128], kz[d_][:, n, pr * 128:(pr + 1) * 128], vt[:, n, pr * 128:(pr + 1) * 128], True, True,
                     ["r_kz%d" % d_, "r_vt"], [pk])
                if s > 0:
                    P.v("dve", "scalar_tensor_tensor", [pk, ck, "r_gcol"], [nk], new, cur, gcolp[:, d_, pr:pr + 1], ps[:, 0:128],
                        ALU.mult, ALU.add)
                else:
                    P.v("dve", "tensor_copy", [pk], [nk], new, ps[:, 0:128])
                P.act(Sall[d_][pr][:, nxt, :], new, AF.Copy, [nk], ["r_S%d%d" % (d_, pr)])
    P.barrier()
    P.release(m1)
    if C.cfg.get("ret_stop", 9) <= 2:
        P.barrier(); P.release(m0); return
    msk = Rot([P.alloc([128, 4, 128], BF16) for _ in range(2)], "r_msk")
    gld = Rot([P.alloc([64, 4, 128]) for _ in range(2)], "r_gld")
    sqo = P.alloc([64, 512])
    rstd = P.alloc([64, 512])
    on = P.alloc([64, 4, 128])
    ost = Rot([P.alloc([64, 4, 128], BF16) for _ in range(2)], "r_ost")
    qtr = Rot([P.alloc([128, 3, 2, 128], BF16) for _ in range(4)], "r_qt")
    gsrc = pT[PT_RG:PT_RG + 256, :].rearrange("(h e) t -> e h t", h=4)
    odst = C.oT_d.ap()[OT_RET:OT_RET + 256, :].rearrange("(h e) t -> e h t", h=4)
    for n in range(2 if last else 0, NTT):
        cs = slice(n * 128, (n + 1) * 128)
        gv, gk = gld.next()
        P.dma(gv, gsrc[:, :, cs], ["pT_d"], [gk])
        P.act(gv, gv, AF.Silu, [gk], [gk])
        QT = []
        for pr in range(2):
            qv, qk = qtr.next()
            P.v("dve" if pr == 0 else "pool", "tensor_tensor", ["r_qr%d" % pr, "r_TAB"], [qk], qv,
                qr[pr][:, cs].unsqueeze(1).unsqueeze(1).to_broadcast([128, 3, 2, 128]), TAB[:, :, 2 * pr:2 * pr + 2, :], ALU.mult)
            QT.append((qv, qk))
        ps_s, pk_s = psrot.next()
        for h in range(RH):
            pr, hp = h // 2, h % 2
            P.mm(ps_s[:, h * 128:(h + 1) * 128], kr[pr][:, cs], QT[pr][0][:, 0, hp, :], True, True,
                 ["r_kr%d" % pr, QT[pr][1]], [pk_s])
        mv, mk = msk.next()
        P.v("dve", "tensor_tensor", [pk_s, "r_Mk"], [mk], mv, ps_s.rearrange("p (h i) -> p h i", h=4), Mk, ALU.mult)
        ps_o, pk_o = psrot.next()
        for h in range(RH):
            pr, hp = h // 2, h % 2
            hs = slice(hp * 64, (hp + 1) * 64)
            oo = ps_o[0:64, h * 128:(h + 1) * 128]
            P.mm(oo, vt[:, n, h * 64:(h + 1) * 64], mv[:, h, :], True, False, ["r_vt", mk], [pk_o])
            P.mm(oo, Sall[0][pr][:, n, hs], QT[pr][0][:, 1, hp, :], False, False, ["r_S0%d" % pr, QT[pr][1]], [pk_o])
            P.mm(oo, Sall[1][pr][:, n, hs], QT[pr][0][:, 2, hp, :], False, True, ["r_S1%d" % pr, QT[pr][1]], [pk_o])
        P.act(sqo, ps_o[0:64, :], AF.Square, [pk_o], ["r_sqo"])
        ps_n, pk_n = psrot.next()
        P.mm(ps_n[0:64, :], C.ones_f[0:64, 0:64], sqo, True, True, ["r_sqo", "ones_f"], [pk_n])
        P.act(rstd, ps_n[0:64, :], AF.Sqrt, [pk_n, "eps_t"], ["r_rstd"], bias=C.eps_t[0:64, :], scale=1.0 / 64)
        P.v("dve", "reciprocal", ["r_rstd"], ["r_rstd"], rstd, rstd)
        P.v("dve", "tensor_tensor", [pk_o, "r_rstd"], ["r_on"], on, ps_o[0:64, :].rearrange("p (h i) -> p h i", h=4),
            rstd.rearrange("p (h i) -> p h i", h=4), ALU.mult)
        P.v("pool", "tensor_tensor", ["r_on", "r_gnT"], ["r_on"], on, on, gnT.unsqueeze(2).to_broadcast([64, 4, 128]), ALU.mult)
        ov, ok = ost.next()
        P.v("pool", "tensor_tensor", ["r_on", gk], [ok], ov, on, gv, ALU.mult)
        P.dma(odst[:, :, cs], ov, [ok], ["oT_d"], q="pool")
    P.barrier()
    P.release(m0)


MIXERS["ret"] = mix_ret
HOSTPREP["ret"] = host_ret


TWO_PI = 2.0 * math.pi
S5_SEGS = [(0, 256)] + [(256 + 1024 * k, 1024) for k in range(4)]


def host_s5(inp):
    L = DEPTH
    f32 = np.float32
    o = {}
    lam_re = inp["s5_lam_re"]; lam_im = inp["s5_lam_im"]
    ldt = np.repeat(inp["s5_log_dt"][..., None], 64, axis=-1)
    prow = np.stack([lam_re, lam_im, ldt], axis=1)
    o["s5_prow"] = np.ascontiguousarray(prow.reshape(L, 3, 1, 2048)).astype(f32)
    pT_ = prow.reshape(L, 3, 2, 8, 2, 64).transpose(0, 1, 4, 5, 2, 3)
    o["s5_pT"] = np.ascontiguousarray(pT_.reshape(L, 3, 128, 16)).astype(f32)
    Bexp = np.zeros((L, 2, 2, 128, 8, 128), f32)
    Cexp = np.zeros((L, 2, 2, 128, 8, 128), f32)
    for ri, (bk, ck) in enumerate((("s5_b_re", "s5_c_re"), ("s5_b_im", "s5_c_im"))):
        b = inp[bk]
        c = inp[ck]
        for m in range(8):
            for a in range(2):
                g = 2 * m + a
                q0 = 32 * (m % 4) + 16 * a
                Bexp[:, ri, :, q0:q0 + 16, m, 64 * a:64 * a + 64] = b[:, :, g].transpose(0, 1, 3, 2)
                Cexp[:, ri, :, 64 * a:64 * a + 64, m, q0:q0 + 16] = c[:, :, g].transpose(0, 1, 3, 2)
    o["s5_Bexp"] = Bexp
    o["s5_Cexp"] = Cexp
    o["s5_dT"] = np.ascontiguousarray(inp["s5_d"].reshape(L, 2, 128).transpose(0, 2, 1))
    o["s5_gbT"] = np.ascontiguousarray(inp["s5_glu_b"].reshape(L, 2, 128).transpose(0, 2, 1))
    o["s5_gw"] = np.ascontiguousarray(inp["s5_glu_w"])
    o["s5_iota"] = np.ascontiguousarray(np.broadcast_to(np.arange(1024, dtype=f32), (128, 1024)))
    return o


def rev_ap(ap):
    (ps, pn), (fs, fn) = ap.ap
    return bass.AP(ap.tensor, ap.offset + (fn - 1) * fs, [[ps, pn], [-fs, fn]])


def frac_round(P, eng_c, x, ki, kf, keys):
    xk, kik, kfk = keys
    P.v(eng_c, "tensor_copy", [xk], [kik], ki, x)
    P.v(eng_c, "tensor_copy", [kik], [kfk], kf, ki)
    P.v("dve", "tensor_tensor", [xk, kfk], [xk], x, x, kf, ALU.subtract)


def mix_s5(C, l, last):
    P, B = C.P, C.B
    psrot = C.psrot
    if "s5_prow" not in B.din:
        B.inp("s5_prow", [DEPTH, 3, 1, 2048]); B.inp("s5_pT", [DEPTH, 3, 128, 16])
        B.inp("s5_Bexp", [DEPTH, 2, 2, 128, 8, 128]); B.inp("s5_Cexp", [DEPTH, 2, 2, 128, 8, 128])
        B.inp("s5_dT", [DEPTH, 128, 2]); B.inp("s5_gbT", [DEPTH, 128, 2]); B.inp("s5_gw", [DEPTH, 256, 256])
        B.inp("s5_iota", [128, 1024])
    din = B.din
    pT = C.pT_d.ap()
    m0 = P.mark()
    NW = 2048
    LB = [[P.alloc([128, 2, 8, 128], BF16) for _ in range(2)]]
    LB = LB[0]
    LC = [P.alloc([128, 2, 8, 128], BF16) for _ in range(2)]
    rP = P.alloc([128, 16])
    thP = P.alloc([128, 16])
    m1 = P.mark()
    pr_ = [P.alloc([128, NW]) for _ in range(3)]
    for i in range(3):
        P.dma(pr_[i], din["s5_prow"].ap()[l, i, 0:1, :].partition_broadcast(128)[:, 0, :], [], ["s5_pr%d" % i])
    lre, lim, dt = pr_
    P.act(dt, dt, AF.Exp, ["s5_pr2"], ["s5_pr2"])
    re = P.alloc([128, NW]); im = P.alloc([128, NW])
    P.v("dve", "tensor_tensor", ["s5_pr0", "s5_pr2"], ["s5_re"], re, lre, dt, ALU.mult)
    P.v("dve", "tensor_tensor", ["s5_pr1", "s5_pr2"], ["s5_im"], im, lim, dt, ALU.mult)
    r_ = P.alloc([128, NW])
    P.act(r_, re, AF.Exp, ["s5_re"], ["s5_r"])
    ki = P.alloc([128, NW], I32); kf = P.alloc([128, NW])
    ph = P.alloc([128, NW]); ph2 = P.alloc([128, NW])
    P.v("dve", "tensor_scalar", ["s5_im"], ["s5_ph"], ph, im, 1.0 / TWO_PI, None, ALU.mult)
    P.v("dve", "tensor_scalar", ["s5_ph"], ["s5_ph2"], ph2, ph, 0.25, None, ALU.add)
    frac_round(P, "dve", ph, ki, kf, ("s5_ph", "s5_ki", "s5_kf"))
    frac_round(P, "dve", ph2, ki, kf, ("s5_ph2", "s5_ki", "s5_kf"))
    sn = ph; cs_ = ph2
    P.act(sn, ph, AF.Sin, ["s5_ph"], ["s5_ph"], scale=TWO_PI)
    P.act(cs_, ph2, AF.Sin, ["s5_ph2"], ["s5_ph2"], scale=TWO_PI)
    nre = re; nim = im
    P.v("dve", "tensor_tensor", ["s5_r", "s5_ph2"], ["s5_re"], nre, r_, cs_, ALU.mult)
    P.v("dve", "tensor_scalar", ["s5_re"], ["s5_re"], nre, nre, -1.0, None, ALU.add)
    P.v("dve", "tensor_tensor", ["s5_r", "s5_ph"], ["s5_im"], nim, r_, sn, ALU.mult)
    den = r_; tmp = kf
    P.v("dve", "tensor_tensor", ["s5_pr0", "s5_re", "s5_im"], ["s5_r"], den, lre, lre, ALU.mult)
    P.v("dve", "tensor_tensor", ["s5_pr1", "s5_kf"], ["s5_kf"], tmp, lim, lim, ALU.mult)
    P.v("dve", "tensor_tensor", ["s5_r", "s5_kf"], ["s5_r"], den, den, tmp, ALU.add)
    P.v("dve", "reciprocal", ["s5_r"], ["s5_r"], den, den)
    cre = ph; cim = ph2
    P.v("dve", "tensor_tensor", ["s5_re", "s5_pr0", "s5_ph"], ["s5_ph"], cre, nre, lre, ALU.mult)
    P.v("dve", "tensor_tensor", ["s5_im", "s5_pr1"], ["s5_kf"], tmp, nim, lim, ALU.mult)
    P.v("dve", "tensor_tensor", ["s5_ph", "s5_kf"], ["s5_ph"], cre, cre, tmp, ALU.add)
    P.v("dve", "tensor_tensor", ["s5_ph", "s5_r"], ["s5_ph"], cre, cre, den, ALU.mult)
    P.v("dve", "tensor_tensor", ["s5_im", "s5_pr0", "s5_ph2"], ["s5_ph2"], cim, nim, lre, ALU.mult)
    P.v("dve", "tensor_tensor", ["s5_re", "s5_pr1"], ["s5_kf"], tmp, nre, lim, ALU.mult)
    P.v("dve", "tensor_tensor", ["s5_ph2", "s5_kf"], ["s5_ph2"], cim, cim, tmp, ALU.subtract)
    P.v("dve", "tensor_tensor", ["s5_ph2", "s5_r"], ["s5_ph2"], cim, cim, den, ALU.mult)
    bre = pr_[0]; bim = pr_[1]; t_a = pr_[2]; t_b = re
    P.dma(bre.rearrange("p (d x) -> p d x", d=2), din["s5_Bexp"].ap()[l, 0].rearrange("d q m s -> q d (m s)"), ["s5_ph", "s5_ph2"], ["s5_pr0"], q="pool")
    P.dma(bim.rearrange("p (d x) -> p d x", d=2), din["s5_Bexp"].ap()[l, 1].rearrange("d q m s -> q d (m s)"), ["s5_ph", "s5_ph2"], ["s5_pr1"], q="pool")
    P.v("dve", "tensor_tensor", ["s5_ph", "s5_pr0"], ["s5_pr2"], t_a, cre, bre, ALU.mult)
    P.v("dve", "tensor_tensor", ["s5_ph2", "s5_pr1"], ["s5_re"], t_b, cim, bim, ALU.mult)
    P.v("dve", "tensor_tensor", ["s5_pr2", "s5_re"], ["s5_LB"], LB[0].rearrange("p d m s -> p (d m s)"), t_a, t_b, ALU.subtract)
    P.v("dve", "tensor_tensor", ["s5_ph", "s5_pr1"], ["s5_pr2"], t_a, cre, bim, ALU.mult)
    P.v("dve", "tensor_tensor", ["s5_ph2", "s5_pr0"], ["s5_re"], t_b, cim, bre, ALU.mult)
    P.v("dve", "tensor_tensor", ["s5_pr2", "s5_re"], ["s5_LB"], LB[1].rearrange("p d m s -> p (d m s)"), t_a, t_b, ALU.add)
    cst_ = im
    P.dma(cst_.rearrange("p (d x) -> p d x", d=2), din["s5_Cexp"].ap()[l, 0].rearrange("d q m s -> q d (m s)"), ["s5_im"], ["s5_im"], q="pool")
    P.act(LC[0].rearrange("p d m s -> p (d m s)"), cst_, AF.Copy, ["s5_im"], ["s5_LC"])
    cst2 = kf
    P.dma(cst2.rearrange("p (d x) -> p d x", d=2), din["s5_Cexp"].ap()[l, 1].rearrange("d q m s -> q d (m s)"), ["s5_kf"], ["s5_kf"], q="pool")
    P.act(LC[1].rearrange("p d m s -> p (d m s)"), cst2, AF.Copy, ["s5_kf"], ["s5_LC"], scale=-1.0)
    pP = P.alloc([128, 3, 16])
    P.dma(pP, din["s5_pT"].ap()[l].rearrange("i q n -> q i n"), [], ["s5_pP"])
    P.act(pP[:, 2, :], pP[:, 2, :], AF.Exp, ["s5_pP"], ["s5_pP"])
    P.v("dve", "tensor_tensor", ["s5_pP"], ["s5_rP"], rP, pP[:, 0, :], pP[:, 2, :], ALU.mult)
    P.act(rP, rP, AF.Exp, ["s5_rP"], ["s5_rP"])
    P.v("dve", "tensor_tensor", ["s5_pP"], ["s5_thP"], thP, pP[:, 1, :], pP[:, 2, :], ALU.mult)
    P.v("dve", "tensor_scalar", ["s5_thP"], ["s5_thP"], thP, thP, 1.0 / TWO_PI, None, ALU.mult)
    kiP = P.alloc([128, 16], I32); kfP = P.alloc([128, 16])
    frac_round(P, "dve", thP, kiP, kfP, ("s5_thP", "s5_kiP", "s5_kfP"))
    P.barrier()
    P.release(m1)
    ubf = P.alloc([128, 2, T], BF16)
    yacc = P.alloc([128, 2, T])
    iota = P.alloc([128, 1024])
    P.dma(iota, din["s5_iota"].ap(), [], ["s5_iota"])
    ust = Rot([P.alloc([128, 512]) for _ in range(2)], "s5_ust")
    for ut in range(2):
        for (c0, cn) in CH512:
            sv, sk = ust.next()
            P.dma(sv[:, 0:cn], pT[ut * 128:(ut + 1) * 128, c0:c0 + cn], ["pT_d"], [sk])
            P.act(ubf[:, ut, c0:c0 + cn], sv[:, 0:cn], AF.Copy, [sk], ["s5_ubf"])
    SEG = 1024
    NBUF = 2
    m_rot = P.mark()
    COS = Rot([P.alloc([128, SEG]) for _ in range(2)], "s5_cos")
    SIN = Rot([P.alloc([128, SEG]) for _ in range(2)], "s5_sin")
    phv = P.alloc([128, SEG]); kiv = P.alloc([128, SEG], I32); kfv = P.alloc([128, SEG])
    BUR = Rot([P.alloc([128, SEG]) for _ in range(NBUF)], "s5_bur")
    BUI = Rot([P.alloc([128, SEG]) for _ in range(NBUF)], "s5_bui")
    GR = Rot([P.alloc([128, SEG]) for _ in range(NBUF)], "s5_gr")
    GI = Rot([P.alloc([128, SEG]) for _ in range(NBUF)], "s5_gi")
    T1 = Rot([P.alloc([128, SEG]) for _ in range(NBUF)], "s5_t1")
    T1b = Rot([P.alloc([128, SEG]) for _ in range(1)], "s5_t1b")
    T2 = Rot([P.alloc([128, SEG]) for _ in range(NBUF)], "s5_t2")
    T2b = Rot([P.alloc([128, SEG]) for _ in range(1)], "s5_t2b")
    T1o = Rot([P.alloc([128, SEG]) for _ in range(1)], "s5_t1o")
    T2o = Rot([P.alloc([128, SEG]) for _ in range(1)], "s5_t2o")
    HR = Rot([P.alloc([128, SEG], BF16) for _ in range(NBUF)], "s5_hr")
    HI = Rot([P.alloc([128, SEG], BF16) for _ in range(NBUF)], "s5_hi")
    cn_ = P.alloc([128, 8])
    cni = P.alloc([128, 8], I32)
    cnf = P.alloc([128, 8])
    glast = P.alloc([128, 4])
    for d_ in range(2):
        seg_order = list(range(5)) if d_ == 0 else [0, 4, 3, 2, 1]
        for m in range(8):
            ut = m // 4
            col = d_ * 8 + m
            th = thP[:, col:col + 1]
            cosv, cosk = COS.next(); sinv, sink = SIN.next()
            P.act(phv, iota, AF.Identity, ["s5_iota", "s5_thP"], ["s5_phl"], scale=th)
            frac_round(P, "dve", phv, kiv, kfv, ("s5_phl", "s5_kil", "s5_kfl"))
            P.act(sinv, phv, AF.Sin, ["s5_phl"], [sink], scale=TWO_PI)
            P.act(phv, iota, AF.Identity, ["s5_iota", "s5_thP"], ["s5_phl"], scale=th, bias=0.25)
            frac_round(P, "dve", phv, kiv, kfv, ("s5_phl", "s5_kil", "s5_kfl"))
            P.act(cosv, phv, AF.Sin, ["s5_phl"], [cosk], scale=TWO_PI)
            P.v("dve", "tensor_scalar", ["s5_thP"], ["s5_cn"], cn_[:, 0:1], th, 256.0, None, ALU.mult)
            P.v("dve", "tensor_scalar", ["s5_thP"], ["s5_cn"], cn_[:, 2:3], th, 1024.0, None, ALU.mult)
            P.v("dve", "tensor_scalar", ["s5_cn"], ["s5_cn"], cn_[:, 1:2], cn_[:, 0:1], 0.25, None, ALU.add)
            P.v("dve", "tensor_scalar", ["s5_cn"], ["s5_cn"], cn_[:, 3:4], cn_[:, 2:3], 0.25, None, ALU.add)
            P.v("dve", "tensor_copy", ["s5_cn"], ["s5_cni"], cni[:, 0:4], cn_[:, 0:4])
            P.v("dve", "tensor_copy", ["s5_cni"], ["s5_cnf"], cnf[:, 0:4], cni[:, 0:4])
            P.v("dve", "tensor_tensor", ["s5_cn", "s5_cnf"], ["s5_cn"], cn_[:, 0:4], cn_[:, 0:4], cnf[:, 0:4], ALU.subtract)
            P.act(cn_[:, 4:8], cn_[:, 0:4], AF.Sin, ["s5_cn"], ["s5_cn"], scale=TWO_PI)
            def phaseA(si, sg, d_=d_, m=m, ut=ut, cosv=cosv, cosk=cosk, sinv=sinv, sink=sink):
                t0, n = S5_SEGS[sg]
                burv, burk = BUR.next(); buiv, buik = BUI.next()
                for c0 in range(0, n, 512):
                    cn = min(512, n - c0)
                    psr, pkr = psrot.next()
                    psi, pki = psrot.next()
                    P.mm(psr[:, 0:cn], LB[0][:, d_, m, :], ubf[:, ut, t0 + c0:t0 + c0 + cn], True, True, ["s5_LB", "s5_ubf"], [pkr])
                    P.mm(psi[:, 0:cn], LB[1][:, d_, m, :], ubf[:, ut, t0 + c0:t0 + c0 + cn], True, True, ["s5_LB", "s5_ubf"], [pki])
                    if d_ == 0:
                        j0 = c0
                        sr, si_ = psr[:, 0:cn], psi[:, 0:cn]
                    else:
                        j0 = n - c0 - cn
                        sr, si_ = rev_ap(psr[:, 0:cn]), rev_ap(psi[:, 0:cn])
                    P.act(burv[:, j0:j0 + cn], sr, AF.Copy, [pkr], [burk])
                    P.act(buiv[:, j0:j0 + cn], si_, AF.Copy, [pki], [buik])
                t1v, t1k = T1.next(); t1bv, t1bk = T1b.next(); t2v, t2k = T2.next(); t2bv, t2bk = T2b.next()
                ns = slice(0, n)
                P.v("dve", "tensor_tensor", [burk, cosk], [t1k], t1v[:, ns], burv[:, ns], cosv[:, ns], ALU.mult)
                P.v("dve", "tensor_tensor", [buik, sink], [t1bk], t1bv[:, ns], buiv[:, ns], sinv[:, ns], ALU.mult)
                P.v("dve", "tensor_tensor", [t1k, t1bk], [t1k], t1v[:, ns], t1v[:, ns], t1bv[:, ns], ALU.add)
                P.v("dve", "tensor_tensor", [buik, cosk], [t2k], t2v[:, ns], buiv[:, ns], cosv[:, ns], ALU.mult)
                P.v("dve", "tensor_tensor", [burk, sink], [t2bk], t2bv[:, ns], burv[:, ns], sinv[:, ns], ALU.mult)
                P.v("dve", "tensor_tensor", [t2k, t2bk], [t2k], t2v[:, ns], t2v[:, ns], t2bv[:, ns], ALU.subtract)
                return (t1v, t1k, t2v, t2k)

            def phaseB(si, sg, nprev, ares, d_=d_, m=m, ut=ut, col=col, cosv=cosv, cosk=cosk, sinv=sinv, sink=sink):
                t0, n = S5_SEGS[sg]
                ns = slice(0, n)
                a_t1v, a_t1k, a_t2v, a_t2k = ares
                if si == 0:
                    ini_r, ini_i = 0.0, 0.0
                else:
                    sc_, cc_ = (cn_[:, 4:5], cn_[:, 5:6]) if nprev == 256 else (cn_[:, 6:7], cn_[:, 7:8])
                    P.v("dve", "tensor_scalar", ["s5_glast", "s5_cn"], ["s5_glr"], glast[:, 2:3], glast[:, 1:2], sc_, None, ALU.mult)
                    P.v("dve", "tensor_scalar", ["s5_glast", "s5_cn"], ["s5_glr"], glast[:, 3:4], glast[:, 0:1], sc_, None, ALU.mult)
                    P.v("dve", "scalar_tensor_tensor", ["s5_glast", "s5_cn", "s5_glr"], ["s5_glr"], glast[:, 2:3], glast[:, 0:1], cc_, glast[:, 2:3],
                        ALU.mult, ALU.subtract)
                    P.v("dve", "scalar_tensor_tensor", ["s5_glast", "s5_cn", "s5_glr"], ["s5_glr"], glast[:, 3:4], glast[:, 1:2], cc_, glast[:, 3:4],
                        ALU.mult, ALU.add)
                    ini_r, ini_i = glast[:, 2:3], glast[:, 3:4]
                grv, grk = GR.next(); giv, gik = GI.next()
                rb = rP[:, col:col + 1].to_broadcast([128, n])
                P.v("dve", "tensor_tensor_scan", [a_t1k, "s5_rP", "s5_glr"], [grk], grv[:, ns], rb, a_t1v[:, ns], ini_r, ALU.mult, ALU.add)
                P.v("dve", "tensor_tensor_scan", [a_t2k, "s5_rP", "s5_glr"], [gik], giv[:, ns], rb, a_t2v[:, ns], ini_i, ALU.mult, ALU.add)
                P.v("dve", "tensor_copy", [grk], ["s5_glast"], glast[:, 0:1], grv[:, n - 1:n])
                P.v("dve", "tensor_copy", [gik], ["s5_glast"], glast[:, 1:2], giv[:, n - 1:n])
                hrv, hrk = HR.next(); hiv, hik = HI.next()
                t1v, t1k = T1o.next(); t1bv, t1bk = T1b.next(); t2v, t2k = T2o.next(); t2bv, t2bk = T2b.next()
                ho_r = hrv[:, ns] if d_ == 0 else rev_ap(hrv[:, ns])
                ho_i = hiv[:, ns] if d_ == 0 else rev_ap(hiv[:, ns])
                P.v("dve", "tensor_tensor", [grk, cosk], [t1k], t1v[:, ns], grv[:, ns], cosv[:, ns], ALU.mult)
                P.v("dve", "tensor_tensor", [gik, sink], [t1bk], t1bv[:, ns], giv[:, ns], sinv[:, ns], ALU.mult)
                P.v("dve", "tensor_tensor", [t1k, t1bk], [hrk], ho_r, t1v[:, ns], t1bv[:, ns], ALU.subtract)
                P.v("dve", "tensor_tensor", [grk, sink], [t2k], t2v[:, ns], grv[:, ns], sinv[:, ns], ALU.mult)
                P.v("dve", "tensor_tensor", [gik, cosk], [t2bk], t2bv[:, ns], giv[:, ns], cosv[:, ns], ALU.mult)
                P.v("dve", "tensor_tensor", [t2k, t2bk], [hik], ho_i, t2v[:, ns], t2bv[:, ns], ALU.add)
                for c0 in range(0, n, 512):
                    cn = min(512, n - c0)
                    ps, pk = psrot.next()
                    P.mm(ps[:, 0:cn], LC[0][:, d_, m, :], hrv[:, c0:c0 + cn], True, False, ["s5_LC", hrk], [pk])
                    P.mm(ps[:, 0:cn], LC[1][:, d_, m, :], hiv[:, c0:c0 + cn], False, True, ["s5_LC", hik], [pk])
                    ya = yacc[:, ut, t0 + c0:t0 + c0 + cn]
                    if d_ == 0 and m % 4 == 0:
                        P.act(ya, ps[:, 0:cn], AF.Copy, [pk], ["s5_yacc%d" % ut])
                    else:
                        P.v("dve", "tensor_tensor", [pk, "s5_yacc%d" % ut], ["s5_yacc%d" % ut], ya, ps[:, 0:cn], ya, ALU.add)

            ares = {0: phaseA(0, seg_order[0])}
            nprev = 0
            for si, sg in enumerate(seg_order):
                if si + 1 < len(seg_order):
                    ares[si + 1] = phaseA(si + 1, seg_order[si + 1])
                phaseB(si, sg, nprev, ares.pop(si))
                nprev = S5_SEGS[sg][1]
    P.barrier()
    P.release(m_rot)
    m2 = P.mark()
    gw_st = P.alloc([128, 2, 256]); gw = P.alloc([128, 2, 256], BF16)
    P.dma(gw_st, din["s5_gw"].ap()[l].rearrange("(k p) n -> p k n", p=128), [], ["s5_gwst"])
    P.v("pool", "tensor_copy", ["s5_gwst"], ["s5_gw"], gw, gw_st)
    dT = P.alloc([128, 2]); gbT = P.alloc([128, 2])
    P.dma(dT, din["s5_dT"].ap()[l], [], ["s5_dT"])
    P.dma(gbT, din["s5_gbT"].ap()[l], [], ["s5_gbT"])
    yg = Rot([P.alloc([128, 2, 512]) for _ in range(2)], "s5_yg")
    ygb = Rot([P.alloc([128, 2, 512], BF16) for _ in range(2)], "s5_ygb")
    sg_ = Rot([P.alloc([128, 512]) for _ in range(2)], "s5_sg")
    ob = Rot([P.alloc([128, 512], BF16) for _ in range(2)], "s5_ob")
    for (c0, cn) in CH512:
        if last and c0 + cn <= CTX:
            continue
        ygv, ygk = yg.next(); ybv, ybk = ygb.next()
        for ut in range(2):
            sv, sk = ust.next()
            P.dma(sv[:, 0:cn], pT[ut * 128:(ut + 1) * 128, c0:c0 + cn], ["pT_d"], [sk])
            P.v("dve", "scalar_tensor_tensor", [sk, "s5_dT", "s5_yacc%d" % ut], [ygk], ygv[:, ut, 0:cn], sv[:, 0:cn], dT[:, ut:ut + 1],
                yacc[:, ut, c0:c0 + cn], ALU.mult, ALU.add)
        P.act(ygv[:, :, 0:cn], ygv[:, :, 0:cn], AF.Gelu_apprx_tanh, [ygk], [ygk])
        P.v("pool", "tensor_copy", [ygk], [ybk], ybv[:, :, 0:cn], ygv[:, :, 0:cn])
        for uo in range(2):
            ps, pk = psrot.next()
            for k in range(2):
                P.mm(ps[:, 0:cn], gw[:, k, uo * 128:(uo + 1) * 128], ybv[:, k, 0:cn], k == 0, k == 1, ["s5_gw", ybk], [pk])
            sgv, sgk = sg_.next()
            P.act(sgv[:, 0:cn], ps[:, 0:cn], AF.Sigmoid, [pk, "s5_gbT"], [sgk], bias=gbT[:, uo:uo + 1])
            ov, ok = ob.next()
            P.v("dve", "tensor_tensor", [sgk, ygk], [ok], ov[:, 0:cn], sgv[:, 0:cn], ygv[:, uo, 0:cn], ALU.mult)
            P.dma(C.oT_d.ap()[uo * 128:(uo + 1) * 128, c0:c0 + cn], ov[:, 0:cn], [ok], ["oT_d"], q="pool")
    P.barrier()
    P.release(m0)


MIXERS["s5"] = mix_s5
HOSTPREP["s5"] = host_s5


TWO_PI = 2.0 * math.pi
NF = 8192
OT_HY = 256
K1B = [(i * 8, 8) for i in range(8)] + [(64, 1)]
CTX_K1 = [0, 16, 32, 48, 64]
K1LIST = [list(range(65)), CTX_K1]


def _hy_feat(pos, nt):
    f32 = np.float32
    pos = pos.astype(f32)
    t01 = pos / f32(nt - 1)
    bands = np.linspace(1e-4, 15, 16, dtype=f32)
    ang = (f32(2.0 * math.pi / nt) * pos[:, None] * bands[None, :]).astype(f32)
    z = np.concatenate([t01[:, None], np.cos(ang), -np.sin(ang)], axis=-1).astype(f32)
    return z, t01


def host_hy(inp):
    L = DEPTH
    f32 = np.float32
    bf = ml_dtypes.bfloat16
    o = {}
    n = np.arange(NF)
    zT = np.zeros((2, 33, NF), f32)
    nt01 = np.zeros((2, 128, 64), f32)
    msk = np.zeros((2, 128, 64), f32)
    for kind, nt in enumerate((SEQ, CTX)):
        pos = np.zeros(NF, np.int64)
        m = np.zeros(NF, f32)
        pos[:nt] = n[:nt]; m[:nt] = 1
        hi = n > NF - nt
        pos[hi] = NF - n[hi]; m[hi] = 1
        pos[NF // 2] = 0; m[NF // 2] = 1
        z, t01 = _hy_feat(pos, nt)
        zT[kind] = z.T
        nt01[kind] = (-t01).reshape(128, 64)
        msk[kind] = m.reshape(128, 64)
    o["hy_zT"] = zT
    o["hy_nt01"] = nt01
    o["hy_msk"] = msk
    n1 = np.arange(128)[:, None]; k1 = np.arange(65)[None, :]
    ang = 2 * np.pi * n1 * k1 / 128
    o["hy_D1"] = np.concatenate([np.cos(ang), -np.sin(ang)], axis=1).astype(bf)
    n2 = np.arange(64); k2 = np.arange(64)
    Tm = np.zeros((65, 128, 3, 128), np.float64)
    for kk in range(65):
        w = np.exp(-2j * np.pi * (n2[:, None] * kk / NF + n2[:, None] * k2[None, :] / 64))
        for c2 in range(2):
            Tm[kk, c2::2, 0, c2::2] = w.real
            Tm[kk, c2::2, 1, c2::2] = w.imag
            Tm[kk, c2::2, 2, c2::2] = -w.imag
    o["hy_T"] = Tm.astype(bf)
    t2 = np.arange(64)
    w = np.exp(2j * np.pi * k2[:, None] * t2[None, :] / 64)
    RA = np.zeros((128, 2, 64, 2)); RB = np.zeros((128, 2, 64, 2))
    for c2 in range(2):
        RA[c2::2, 0, :, c2] = w.real; RA[c2::2, 1, :, c2] = w.imag
        RB[c2::2, 0, :, c2] = -w.imag; RB[c2::2, 1, :, c2] = w.real
    o["hy_R"] = np.stack([RA.reshape(128, 256), RB.reshape(128, 256)], axis=1).astype(bf)
    k1v = np.arange(65); t1 = np.arange(64)
    wgt = np.full(65, 2.0); wgt[0] = 1; wgt[64] = 1
    V = np.zeros((65, 64, 2, 64))
    for tt in range(64):
        w = (wgt[:, None] / NF) * np.exp(2j * np.pi * (tt * k1v[:, None] / NF + t1[None, :] * k1v[:, None] / 128))
        V[:, tt, 0, :] = w.real; V[:, tt, 1, :] = -w.imag
    o["hy_V"] = V.astype(bf)
    o["hy_Vc"] = np.ascontiguousarray(V[CTX_K1][:, :, :, 0:4]).astype(bf)
    o["hy_w1"] = np.ascontiguousarray(inp["hy_w1"])
    o["hy_w2"] = np.ascontiguousarray(inp["hy_w2"])
    o["hy_cols"] = np.ascontiguousarray(np.stack([inp["hy_b1"], inp["hy_b2"], inp["hy_freq"]], axis=-1))
    w3 = inp["hy_w3"].reshape(L, 64, 2, 2, 256)
    o["hy_w3"] = np.ascontiguousarray(w3.transpose(0, 1, 3, 2, 4))
    dl = inp["hy_deltas"].reshape(L, 2, 2, 256)
    o["hy_dl"] = np.ascontiguousarray(dl.transpose(0, 2, 1, 3).reshape(L, 2, 1, 512))
    cw = np.concatenate([inp["hy_conv_w"], inp["hy_conv_b"][:, None, :]], axis=1)
    o["hy_cw"] = np.ascontiguousarray(cw.reshape(L, 1, 4 * 768))
    o["hy_bias"] = np.ascontiguousarray(inp["hy_bias"].reshape(L, 1, 512))
    return o


def mix_hy(C, l, last):
    P, B = C.P, C.B
    psrot = C.psrot
    if "hy_zT" not in B.din:
        B.inp("hy_zT", [2, 33, NF]); B.inp("hy_nt01", [2, 128, 64]); B.inp("hy_msk", [2, 128, 64])
        B.inp("hy_D1", [128, 130], BF16); B.inp("hy_T", [65, 128, 3, 128], BF16)
        B.inp("hy_R", [128, 2, 256], BF16); B.inp("hy_V", [65, 64, 2, 64], BF16); B.inp("hy_Vc", [5, 64, 2, 4], BF16)
        B.inp("hy_w1", [DEPTH, 33, 64]); B.inp("hy_w2", [DEPTH, 64, 64]); B.inp("hy_cols", [DEPTH, 64, 3])
        B.inp("hy_w3", [DEPTH, 64, 2, 2, 256]); B.inp("hy_dl", [DEPTH, 2, 1, 512])
        B.inp("hy_cw", [DEPTH, 1, 3072]); B.inp("hy_bias", [DEPTH, 1, 512])
        B.scr("hspec_d", [2, 2, 2, 2, 128, 65 * 64])
    din = B.din
    hspec = B.dscr["hspec_d"]
    kinds = [0] if last else [0, 1]
    m0 = P.mark()
    D1 = P.alloc([128, 130], BF16)
    P.dma(D1, din["hy_D1"].ap(), [], ["hy_D1"])
    Rm = P.alloc([128, 2, 256], BF16)
    P.dma(Rm, din["hy_R"].ap(), [], ["hy_R"])
    Vm = P.alloc([65, 64, 2, 64], BF16)
    P.dma(Vm, din["hy_V"].ap(), [], ["hy_V"])
    Vmc = P.alloc([5, 64, 2, 4], BF16)
    P.dma(Vmc, din["hy_Vc"].ap(), [], ["hy_V"])
    Trot = Rot([P.alloc([128, 3, 128], BF16) for _ in range(4)], "hy_T")
    B1 = P.alloc([128, 64, 130], BF16)
    mfwd = P.mark()

    def fwd(xv, Kk, xkey, consume, k1list):
        for q0 in range(0, 64, 3):
            qn = min(3, 64 - q0)
            ps, pk = psrot.next()
            for j in range(qn):
                q = q0 + j
                P.mm(ps[:, j * 130:(j + 1) * 130], xv[:, q, :, :].rearrange("p n c -> p (n c)"), D1[0:Kk, :], True, True, [xkey, "hy_D1"], [pk])
            src = ps[:, 0:qn * 130].rearrange("p (a b) -> p a b", a=qn)
            if (q0 // 3) % 2 == 0:
                P.act(B1[:, q0:q0 + qn, :], src, AF.Copy, [pk], ["hy_B1"])
            else:
                P.v("dve", "tensor_copy", [pk], ["hy_B1"], B1[:, q0:q0 + qn, :], src)
        for bi, k0 in enumerate(range(0, len(k1list), 8)):
            kn = min(8, len(k1list) - k0)
            psr, pkr = psrot.next()
            psi, pki = psrot.next()
            for j in range(kn):
                k1 = k1list[k0 + j]
                tv, tk = Trot.next()
                P.dma(tv, din["hy_T"].ap()[k1], [], [tk])
                br = B1[:, :, k1]
                bi_ = B1[:, :, 65 + k1]
                P.mm(psr[:, j * 64:(j + 1) * 64], tv[:, 0, :], br, True, False, [tk, "hy_B1"], [pkr])
                P.mm(psr[:, j * 64:(j + 1) * 64], tv[:, 2, :], bi_, False, True, [tk, "hy_B1"], [pkr])
                P.mm(psi[:, j * 64:(j + 1) * 64], tv[:, 1, :], br, True, False, [tk, "hy_B1"], [pki])
                P.mm(psi[:, j * 64:(j + 1) * 64], tv[:, 0, :], bi_, False, True, [tk, "hy_B1"], [pki])
            consume(bi, k0, kn, psr, pkr, psi, pki)

    w1 = P.alloc([33, 64]); w2 = P.alloc([64, 64]); cols = P.alloc([64, 3])
    P.dma(w1, din["hy_w1"].ap()[l], [], ["hy_w1"])
    P.dma(w2, din["hy_w2"].ap()[l], [], ["hy_w2"])
    P.dma(cols, din["hy_cols"].ap()[l], [], ["hy_cols"])
    sc = P.alloc([64, 4])
    P.v("dve", "tensor_scalar", ["hy_cols"], ["hy_sc"], sc[:, 0:1], cols[:, 2:3], 1.0 / TWO_PI, None, ALU.mult)
    P.v("dve", "tensor_tensor", ["hy_cols", "hy_sc"], ["hy_sc"], sc[:, 1:2], cols[:, 0:1], sc[:, 0:1], ALU.mult)
    P.v("dve", "tensor_tensor", ["hy_cols", "hy_sc"], ["hy_sc"], sc[:, 2:3], cols[:, 1:2], sc[:, 0:1], ALU.mult)
    w3 = P.alloc([64, 2, 2, 256])
    P.dma(w3, din["hy_w3"].ap()[l], [], ["hy_w3"])
    absd = P.alloc([128, 2, 256])
    for d_ in range(2):
        P.dma(absd[d_ * 64:(d_ + 1) * 64].rearrange("p o c -> p (o c)"),
              din["hy_dl"].ap()[l, d_, 0:1, :].partition_broadcast(64)[:, 0, :], [], ["hy_absd"])
    P.act(absd, absd, AF.Abs, ["hy_absd"], ["hy_absd"])
    mflt = P.mark()
    hid2 = P.alloc([64, NF])
    zrot = Rot([P.alloc([33, 512]) for _ in range(2)], "hy_z")
    phb = P.alloc([64, 512]); kib = P.alloc([64, 512], I32); kfb = P.alloc([64, 512]); h1b = P.alloc([64, 512])
    nt01 = P.alloc([128, 64]); msk = P.alloc([128, 64])
    krot = Rot([P.alloc([128, 256]) for _ in range(4)], "hy_kr")
    xk = P.alloc([128, 128, 64, 2], BF16)
    dec = Rot([P.alloc([128, 256]) for _ in range(3)], "hy_dec")
    sqr = Rot([P.alloc([128, 256]) for _ in range(4)], "hy_sq")
    ssa = P.alloc([128, 256]); rs = P.alloc([128, 256])
    yst = Rot([P.alloc([128, 2, 512]) for _ in range(2)], "hy_yst")

    def sin_layer(ps, pk, n, bcol, outv, okey):
        P.act(phb[:, 0:n], ps[0:64, 0:n], AF.Identity, [pk, "hy_sc"], ["hy_ph"], scale=sc[:, 0:1], bias=sc[:, bcol:bcol + 1])
        P.v("dve", "tensor_copy", ["hy_ph"], ["hy_ki"], kib[:, 0:n], phb[:, 0:n])
        P.v("dve", "tensor_copy", ["hy_ki"], ["hy_kf"], kfb[:, 0:n], kib[:, 0:n])
        P.v("dve", "tensor_tensor", ["hy_ph", "hy_kf"], ["hy_ph"], phb[:, 0:n], phb[:, 0:n], kfb[:, 0:n], ALU.subtract)
        P.act(outv, phb[:, 0:n], AF.Sin, ["hy_ph"], [okey], scale=TWO_PI)

    for kind in kinds:
        P.dma(nt01, din["hy_nt01"].ap()[kind], ["hy_nt01r"], ["hy_nt01"])
        P.dma(msk, din["hy_msk"].ap()[kind], ["hy_mskr"], ["hy_msk"])
        for ch in (range(NF // 512) if kind == 0 else (0, 8, 15)):
            zv, zk = zrot.next()
            P.dma(zv, din["hy_zT"].ap()[kind, :, ch * 512:(ch + 1) * 512], [], [zk])
            ps, pk = psrot.next()
            P.mm(ps[0:64, :], w1, zv, True, True, ["hy_w1", zk], [pk])
            sin_layer(ps, pk, 512, 1, h1b, "hy_h1")
            ps2, pk2 = psrot.next()
            P.mm(ps2[0:64, :], w2, h1b, True, True, ["hy_w2", "hy_h1"], [pk2])
            sin_layer(ps2, pk2, 512, 2, hid2[:, ch * 512:(ch + 1) * 512], "hy_hid2")
        for o_ in range(2):
            pss_, pks_ = C.pss[7][:], "ps7"
            prot7 = Rot([p[:] for p in C.pss[0:7]], "ps")
            LOOKF = 2
            pendf = {}

            def issue_k(n2, o_=o_, kind=kind):
                psf, pkf = prot7.next()
                psb, pkb = prot7.next()
                lh = hid2[:, n2:NF:64]
                P.mm(psf[:, 0:256], lh, w3[:, 0, o_, :], True, True, ["hy_hid2", "hy_w3"], [pkf])
                P.mm(psb[:, 0:256], lh, w3[:, 1, o_, :], True, True, ["hy_hid2", "hy_w3"], [pkb])
                dv, dk = dec.next()
                P.act(dv, absd[:, o_, :], AF.Exp, ["hy_absd", "hy_nt01"], [dk], scale=nt01[:, n2:n2 + 1])
                kv, kk_ = krot.next()
                P.v("dve", "scalar_tensor_tensor", [pkf, "hy_msk", dk], [kk_], kv[0:64, :], psf[0:64, 0:256],
                    msk[0:64, n2:n2 + 1], dv[0:64, :], ALU.mult, ALU.mult)
                P.v("dve", "scalar_tensor_tensor", [pkb, "hy_msk", dk], [kk_], kv[64:128, :], psb[64:128, 0:256],
                    msk[64:128, n2:n2 + 1], dv[64:128, :], ALU.mult, ALU.mult)
                sv, sk = sqr.next()
                P.act(sv[:, 0:256], kv, AF.Square, [kk_], [sk])
                P.v("pool", "tensor_copy", [kk_], ["hy_xk"], xk[:, :, n2, :], kv.rearrange("p (q c) -> p q c", c=2))
                pendf[n2] = (sv, sk)

            for n2 in range(LOOKF):
                issue_k(n2)
            for n2 in range(64):
                if n2 + LOOKF < 64:
                    issue_k(n2 + LOOKF)
                sv, sk = pendf.pop(n2)
                P.mm(pss_[:, 0:256], C.ones_f[:, :], sv[:, 0:256], n2 == 0, n2 == 63, [sk, "ones_f"], [pks_])
            P.act(rs, pss_[:, 0:256], AF.Sqrt, [pks_, "eps_t"], ["hy_rs"], bias=C.eps_t[:], scale=1.0)
            P.v("dve", "reciprocal", ["hy_rs"], ["hy_rs"], rs, rs)
            rsb = rs.rearrange("p (q c) -> p q c", c=2).unsqueeze(2).to_broadcast([128, 128, 16, 2])
            for g4 in range(4):
                xs_ = xk[:, :, g4 * 16:(g4 + 1) * 16, :]
                P.v("dve" if g4 % 2 == 0 else "pool", "tensor_tensor", ["hy_xk", "hy_rs"], ["hy_xk"], xs_, xs_, rsb, ALU.mult)
            P.v("dve", "memset", [], ["hy_xk"], xk[64:65, :, 0, :], 0.0)
            if C.cfg.get("hy_dbg") and kind == 0 and o_ == 0:
                dbg1 = B.scr("hy_dbg_hid2", [64, NF])
                dbg2 = B.scr("hy_dbg_xk", [128, 128 * 64 * 2], BF16)
                dbg3 = B.scr("hy_dbg_rs", [128, 256])
                P.dma(dbg1.ap(), hid2, ["hy_hid2"], ["dbg1"], q="pool")
                P.dma(dbg2.ap(), xk.rearrange("p a b c -> p (a b c)"), ["hy_xk"], ["dbg2"], q="pool")
                P.dma(dbg3.ap(), rs, ["hy_rs"], ["dbg3"], q="pool")
            for hf in range(2):
                def consume(bi, k0, kn, psr, pkr, psi, pki, kind=kind, o_=o_, hf=hf):
                    yv, yk = yst.next()
                    fs = 1.0 if kind == 0 else 16.0
                    P.act(yv[:, 0, 0:kn * 64], psr[:, 0:kn * 64], AF.Copy, [pkr], [yk], scale=fs)
                    P.v("dve", "tensor_scalar", [pki], [yk], yv[:, 1, 0:kn * 64], psi[:, 0:kn * 64], fs, None, ALU.mult)
                    for ri in range(2):
                        P.dma(hspec.ap()[kind, o_, hf, ri, :, k0 * 64:(k0 + kn) * 64], yv[:, ri, 0:kn * 64], [yk], ["hspec_d"], q="pool")
                fwd(xk[:, hf * 64:(hf + 1) * 64, :, :], 128, "hy_xk", consume, K1LIST[kind])
    P.barrier()
    P.release(mflt)
    if C.cfg.get("hy_stop", 9) <= 1:
        P.release(m0); return
    cwr = P.alloc([64, 4, 768])
    P.dma(cwr.rearrange("p a c -> p (a c)"), din["hy_cw"].ap()[l, 0:1, :].partition_broadcast(64)[:, 0, :], [], ["hy_cw"])
    hbr = P.alloc([64, 2, 256])
    P.dma(hbr.rearrange("p a c -> p (a c)"), din["hy_bias"].ap()[l, 0:1, :].partition_broadcast(64)[:, 0, :], [], ["hy_hb"])
    xv = P.alloc([64, 64, 64, 2], BF16)
    z1 = P.alloc([64, 64, 64, 2], BF16)
    Zr = P.alloc([128, 65, 64], BF16); Zi = P.alloc([128, 65, 64], BF16)
    G = P.alloc([65, 2, 64, 128], BF16)
    oTs = P.alloc([128, SEQ], BF16)
    dwl = Rot([P.alloc([64, 3, 4, 128]) for _ in range(2)], "hy_dwl")
    dwa = Rot([P.alloc([64, 4, 128]) for _ in range(3)], "hy_dwa")
    dwt = P.alloc([64, 4, 128])
    hrot = Rot([P.alloc([128, 2, 512]) for _ in range(2)], "hy_h")
    tt1 = P.alloc([128, 512]); tt2 = P.alloc([128, 512])
    gt = P.alloc([64, 4, 128]); z2f = P.alloc([64, 4, 128])

    def blkv(x, M, b):
        return x[0:M, :, 4 * b:4 * b + 4, :].rearrange("p q n c -> p n q c")

    def dwconv_blk(kind, set_, hf, b):
        M = 64 if kind == 0 else 4
        base = LAT0 if kind == 0 else CTX0
        ntok = SEQ if kind == 0 else CTX
        c0 = set_ * 256 + hf * 128
        lv, lk = dwl.next()
        for s in range(3):
            src = C.hyp_d.ap()[base + s - 1:base + s - 1 + ntok, c0:c0 + 128].rearrange("(a b) c -> a b c", b=64)[:, 4 * b:4 * b + 4, :]
            P.dma(lv[0:M, s, :, :], src, ["hyp_d"], [lk])
        av, ak = dwa.next()
        wv = lambda tap: cwr[0:M, tap, c0:c0 + 128].unsqueeze(1).to_broadcast([M, 4, 128])
        P.v("dve", "tensor_tensor", [lk, "hy_cw"], [ak], av[0:M], lv[0:M, 1], wv(1), ALU.mult)
        P.v("pool", "tensor_tensor", [lk, "hy_cw"], ["hy_dwt"], dwt[0:M], lv[0:M, 0], wv(0), ALU.mult)
        P.v("pool", "tensor_tensor", [ak, "hy_dwt"], [ak], av[0:M], av[0:M], dwt[0:M], ALU.add)
        P.v("dve", "tensor_tensor", [lk, "hy_cw"], ["hy_dwt"], dwt[0:M], lv[0:M, 2], wv(2), ALU.mult)
        P.v("pool", "tensor_tensor", [ak, "hy_dwt"], [ak], av[0:M], av[0:M], dwt[0:M], ALU.add)
        P.v("pool", "tensor_tensor", [ak, "hy_cw"], [ak], av[0:M], av[0:M], wv(3), ALU.add)
        return av, ak

    def inverse(kind, evac):
        M = 64 if kind == 0 else 4
        nk = len(K1LIST[kind])
        Vsel = Vm if kind == 0 else Vmc
        for q0 in range(0, 64, 2):
            ps, pk = psrot.next()
            for j in range(2):
                q = q0 + j
                P.mm(ps[0:nk, j * 256:(j + 1) * 256], Zr[:, 0:nk, q], Rm[:, 0, :], True, False, ["hy_Z", "hy_R"], [pk])
                P.mm(ps[0:nk, j * 256:(j + 1) * 256], Zi[:, 0:nk, q], Rm[:, 1, :], False, True, ["hy_Z", "hy_R"], [pk])
            for j in range(2):
                q = q0 + j
                src = ps[0:nk, j * 256:(j + 1) * 256].rearrange("p (r t c) -> p r t c", r=2, t=64)
                if j == 0:
                    P.act(G[0:nk, :, :, 2 * q:2 * q + 2], src, AF.Copy, [pk], ["hy_G"])
                else:
                    P.v("dve", "tensor_copy", [pk], ["hy_G"], G[0:nk, :, :, 2 * q:2 * q + 2], src)
        for b in range(16):
            ps, pk = psrot.next()
            for j in range(4):
                t2 = 4 * b + j
                P.mm(ps[0:M, j * 128:(j + 1) * 128], Vsel[0:nk, t2, 0, 0:M], G[0:nk, 0, t2, :], True, False, ["hy_V", "hy_G"], [pk])
                P.mm(ps[0:M, j * 128:(j + 1) * 128], Vsel[0:nk, t2, 1, 0:M], G[0:nk, 1, t2, :], False, True, ["hy_V", "hy_G"], [pk])
            evac(b, ps, pk, M)

    for kind in kinds:
        M = 64 if kind == 0 else 4
        ntok = SEQ if kind == 0 else CTX
        tok0 = CTX if kind == 0 else 0
        for hf in range(2):
            if kind == 1:
                P.v("dve", "memset", [], ["hy_xv"], xv, 0.0)
                P.v("pool", "memset", [], ["hy_z1"], z1, 0.0)
            for b in range(16):
                av, ak = dwconv_blk(kind, 2, hf, b)
                P.act(blkv(xv, M, b), av[0:M].rearrange("p n (q c) -> p n q c", c=2), AF.Copy, [ak], ["hy_xv"])
            for o_ in range(2):
                src_x = xv if o_ == 0 else z1
                src_k = "hy_xv" if o_ == 0 else "hy_z1"

                def consume(bi, k0, kn, psr, pkr, psi, pki, kind=kind, o_=o_, hf=hf):
                    hv, hk = hrot.next()
                    for ri in range(2):
                        P.dma(hv[:, ri, 0:kn * 64], hspec.ap()[kind, o_, hf, ri, :, k0 * 64:(k0 + kn) * 64], ["hspec_d"], [hk])
                    w = kn * 64
                    zr = Zr[:, k0:k0 + kn, :].rearrange("p a b -> p (a b)")
                    zi = Zi[:, k0:k0 + kn, :].rearrange("p a b -> p (a b)")
                    P.v("dve", "tensor_tensor", [pkr, hk], ["hy_tt1"], tt1[:, 0:w], psr[:, 0:w], hv[:, 0, 0:w], ALU.mult)
                    P.v("dve", "tensor_tensor", [pki, hk], ["hy_tt2"], tt2[:, 0:w], psi[:, 0:w], hv[:, 1, 0:w], ALU.mult)
                    P.v("pool", "tensor_tensor", ["hy_tt1", "hy_tt2"], ["hy_Z"], zr, tt1[:, 0:w], tt2[:, 0:w], ALU.subtract)
                    P.v("dve", "tensor_tensor", [pkr, hk], ["hy_tt1"], tt1[:, 0:w], psr[:, 0:w], hv[:, 1, 0:w], ALU.mult)
                    P.v("dve", "tensor_tensor", [pki, hk], ["hy_tt2"], tt2[:, 0:w], psi[:, 0:w], hv[:, 0, 0:w], ALU.mult)
                    P.v("pool", "tensor_tensor", ["hy_tt1", "hy_tt2"], ["hy_Z"], zi, tt1[:, 0:w], tt2[:, 0:w], ALU.add)
                fwd(src_x, 64, src_k, consume, K1LIST[kind])

                def evac(b, ps, pk, M, kind=kind, o_=o_, hf=hf, src_x=src_x, src_k=src_k):
                    xg, xgk = dwconv_blk(kind, o_, hf, b)
                    yv = ps[0:M, :].rearrange("p (a c) -> p a c", a=4)
                    brow = hbr[0:M, o_, hf * 128:(hf + 1) * 128].unsqueeze(1).to_broadcast([M, 4, 128])
                    P.v("pool", "tensor_tensor", [src_k, "hy_hb"], ["hy_gt"], gt[0:M].rearrange("p n (q c) -> p n q c", c=2), blkv(src_x, M, b), brow.rearrange("p n (q c) -> p n q c", c=2), ALU.mult)
                    P.v("dve", "tensor_tensor", [pk, "hy_gt"], ["hy_gt"], gt[0:M], yv, gt[0:M], ALU.add)
                    if o_ == 0:
                        P.v("dve", "tensor_tensor", ["hy_gt", xgk], ["hy_z1"], blkv(z1, M, b), gt[0:M].rearrange("p n (q c) -> p n q c", c=2), xg[0:M].rearrange("p n (q c) -> p n q c", c=2), ALU.mult)
                    else:
                        P.v("dve", "tensor_tensor", ["hy_gt", xgk], ["hy_z2f"], z2f[0:M], gt[0:M], xg[0:M], ALU.mult)
                        pt, ptk = psrot.next()
                        for j in range(4):
                            P.tr(pt[:, j * 64:j * 64 + M], z2f[0:M, j, :], C.ident[0:M, 0:M], ["hy_z2f", "ident"], [ptk])
                        dst = oTs[:, 0:ntok].rearrange("p (a b) -> p b a", b=64)[:, 4 * b:4 * b + 4, :]
                        srcp = pt[:, 0:256].rearrange("p (j a) -> p j a", j=4)[:, :, 0:M]
                        P.act(dst, srcp, AF.Copy, [ptk], ["hy_oTs"])
                inverse(kind, evac)
            P.dma(C.oT_d.ap()[OT_HY + hf * 128:OT_HY + (hf + 1) * 128, tok0:tok0 + ntok], oTs[:, 0:ntok], ["hy_oTs"], ["oT_d"], q="pool")
    P.barrier()
    P.release(m0)


MIXERS["hy"] = mix_hy
HOSTPREP["hy"] = host_hy
```

```python
import math
import numpy as np
import ml_dtypes
import concourse.bass as bass
import concourse.mybir as mybir
from concourse.bass_utils import run_bass_kernel_spmd
from contextlib import ExitStack
from types import SimpleNamespace

F32 = mybir.dt.float32
BF16 = mybir.dt.bfloat16
I32 = mybir.dt.int32
ALU = mybir.AluOpType
AF = mybir.ActivationFunctionType

ENGS = ("pe", "act", "dve", "pool", "sp")
import os as _os
NO_SELF_SYNC = set(_os.environ.get("NO_SELF_SYNC", "").split(",")) - {""}
NSLOT = 6

D = 1024
NB = 4
SEQ = 4096
CTX = 256
T = SEQ + CTX
DEPTH = 2
EPS = 1e-6
WG = 256
DFF = 2816
NTT = T // 128
W_PAD = T + 4
CTX0 = 1
LAT0 = 259
NFM = 2048
NTM = 1536
CH512 = [(i * 512, min(512, T - i * 512)) for i in range((T + 511) // 512)]


class Prog:
    def __init__(self, nc):
        self.nc = nc
        self.ops = []
        self.es = ExitStack()
        self.arena = None
        self.aoff = 0

    def sb(self, name, shape, dtype=F32):
        return self.es.enter_context(self.nc.sbuf_tensor(name, list(shape), dtype))

    def ps(self, name, shape, dtype=F32):
        return self.es.enter_context(self.nc.psum_tensor(name, list(shape), dtype))

    def dram(self, name, shape, dtype=F32, kind="Internal"):
        return self.nc.dram_tensor(name, list(shape), dtype, kind=kind)

    def init_arena(self, words):
        self.arena = self.sb("arena", [128, words], F32)
        self.awords = words
        self.aoff = 0

    def mark(self):
        return self.aoff

    def release(self, m):
        self.aoff = m

    def alloc(self, shape, dtype=F32):
        npart = shape[0]
        n = int(np.prod(shape[1:]))
        if dtype == F32 or dtype == I32:
            words = n
        else:
            words = (n + 1) // 2
        words = (words + 7) // 8 * 8
        off = self.aoff
        self.aoff += words
        assert self.aoff <= self.awords, "arena overflow %d > %d" % (self.aoff, self.awords)
        v = self.arena[0:npart, off:off + words]
        if dtype != F32:
            v = v.bitcast(dtype)
        v = v[:, 0:n]
        if len(shape) == 3:
            v = v.rearrange("p (a b) -> p a b", a=shape[1])
        elif len(shape) == 4:
            v = v.rearrange("p (a b c) -> p a b c", a=shape[1], b=shape[2])
        return v

    def op(self, eng, fn, reads=(), writes=(), dma=False):
        self.ops.append(dict(eng=eng, fn=fn, reads=tuple(reads), writes=tuple(writes), dma=dma, bar=False))

    def barrier(self):
        self.ops.append(dict(bar=True))

    def dma(self, out, in_, reads, writes, q="sp", **kw):
        self.op(q, lambda e: e.dma_start(out=out, in_=in_, **kw), reads, writes, dma=True)

    def mm(self, out, lhsT, rhs, start, stop, reads, writes, **kw):
        self.op("pe", lambda e: e.matmul(out, lhsT, rhs, start=start, stop=stop, **kw), reads, writes)

    def tr(self, out, in_, ident, reads, writes):
        self.op("pe", lambda e: e.transpose(out, in_, ident), reads, writes)

    def act(self, out, in_, func, reads, writes, **kw):
        self.op("act", lambda e: e.activation(out=out, in_=in_, func=func, **kw), reads, writes)

    def v(self, eng, name, reads, writes, *a, **kw):
        self.op(eng, lambda e: getattr(e, name)(*a, **kw), reads, writes)

    def emit(self):
        nc = self.nc
        es = self.es
        csem = {e: es.enter_context(nc.semaphore("c_" + e)) for e in ("pe", "act", "dve", "pool")}
        dsem = {q: [es.enter_context(nc.semaphore("d_%s%d" % (q, i))) for i in range(NSLOT)]
                for q in ("sp", "act", "pool")}
        ccount = {e: 0 for e in csem}
        dcount = {q: 0 for q in dsem}
        slotuse = {q: [0] * NSLOT for q in dsem}
        semobj = {}
        for e in csem:
            semobj[("c", e)] = csem[e]
        for q in dsem:
            for i in range(NSLOT):
                semobj[("d", q, i)] = dsem[q][i]
        know = {e: {} for e in ENGS}
        last_w = {}
        readers = {}
        streams = {e: [] for e in ENGS}
        bar_know = {}
        bar_done = {e: True for e in ENGS}

        def cur_all():
            d = {}
            for q in dsem:
                for i in range(NSLOT):
                    if slotuse[q][i] > 0:
                        d[("d", q, i)] = slotuse[q][i] * 16
            for e in csem:
                if ccount[e] > 0:
                    d[("c", e)] = ccount[e]
            return d

        for op in self.ops:
            if op["bar"]:
                bar_know = cur_all()
                bar_done = {e: False for e in ENGS}
                continue
            e = op["eng"]
            deps = []
            for r in op["reads"]:
                if r in last_w:
                    deps.append(last_w[r])
            for w in op["writes"]:
                if w in last_w:
                    deps.append(last_w[w])
                for rd in readers.get(w, {}).values():
                    deps.append(rd)
            waits = {}
            kn = know[e]
            if not bar_done[e]:
                bar_done[e] = True
                for sk, val in bar_know.items():
                    if sk == ("c", "pe") and e == "pe":
                        continue
                    if kn.get(sk, 0) < val:
                        waits[sk] = val
                        kn[sk] = val
            if op["dma"]:
                slot = dcount[e] % NSLOT
                dcount[e] += 1
                sk = ("d", e, slot)
                prev = slotuse[e][slot] * 16
                if prev > 0 and kn.get(sk, 0) < prev:
                    waits[sk] = max(waits.get(sk, 0), prev)
                    kn[sk] = prev
                slotuse[e][slot] += 1
                tok = (sk, slotuse[e][slot] * 16)
            else:
                ccount[e] += 1
                tok = (("c", e), ccount[e])
            for (dtok, dknow) in deps:
                sk, val = dtok
                if sk == ("c", "pe") and e == "pe" and not op["dma"]:
                    continue
                if (not op["dma"]) and sk == ("c", e) and e in NO_SELF_SYNC:
                    continue
                if kn.get(sk, 0) >= val:
                    continue
                waits[sk] = max(waits.get(sk, 0), val)
                for k2, v2 in dknow.items():
                    if kn.get(k2, 0) < v2:
                        kn[k2] = v2
                kn[sk] = max(kn.get(sk, 0), val)
            myknow = dict(kn)
            myknow[tok[0]] = max(myknow.get(tok[0], 0), tok[1])
            if (not op["dma"]) and e == "pe":
                kn[tok[0]] = tok[1]
            streams[e].append((op, waits, tok))
            entry = (tok, myknow)
            for w in op["writes"]:
                last_w[w] = entry
                readers[w] = {}
            for r in op["reads"]:
                if r not in op["writes"]:
                    readers.setdefault(r, {})[tok[0]] = entry
        fin = cur_all()
        self.stats = dict(ccount=dict(ccount), dcount=dict(dcount))

        def run_stream(ename, eng):
            for (op, waits, tok) in streams[ename]:
                for sk, val in waits.items():
                    eng.wait_ge(semobj[sk], val)
                ins = op["fn"](eng)
                ins.then_inc(semobj[tok[0]], 16 if op["dma"] else 1)
            if ename == "sp":
                for sk, val in fin.items():
                    eng.wait_ge(semobj[sk], val)

        with nc.Block() as block:
            @block.sync
            def _(eng):
                run_stream("sp", eng)

            @block.tensor
            def _(eng):
                run_stream("pe", eng)

            @block.scalar
            def _(eng):
                run_stream("act", eng)

            @block.vector
            def _(eng):
                run_stream("dve", eng)

            @block.gpsimd
            def _(eng):
                run_stream("pool", eng)
        es.close()


class Rot:
    def __init__(self, views, name):
        self.views = views
        self.name = name
        self.i = 0

    def next(self):
        i = self.i % len(self.views)
        self.i += 1
        return self.views[i], "%s%d" % (self.name, i)


def _swap_halves_cols(w, width):
    n = w.shape[1]
    idx = np.arange(n).reshape(n // width, 2, width // 2)[:, ::-1, :].reshape(-1)
    return w[:, idx]


def host_consts():
    c = {}
    c["ident"] = np.eye(128, dtype=np.float32)
    return c


def arrange_in_cols(w_in):
    o = {}
    s5u = w_in[:, 0:256]
    hy = w_in[:, 256:1024]
    rq = w_in[:, 1024:1280]
    rk = w_in[:, 1280:1536]
    rv = w_in[:, 1536:1792]
    rg = w_in[:, 1792:2048]
    cq = w_in[:, 2048:2240]
    ckv = w_in[:, 2240:2368]
    kr = w_in[:, 2368:2400]
    z64 = np.zeros((w_in.shape[0], 64), np.float32)
    fm = np.concatenate([s5u, rq, rk, rg, _swap_halves_cols(rq, 64), _swap_halves_cols(rk, 64),
                         cq, z64, ckv, kr, _swap_halves_cols(kr, 32), z64], axis=1)
    assert fm.shape[1] == NFM
    tm = np.concatenate([hy, rk, _swap_halves_cols(rk, 64), rv], axis=1)
    assert tm.shape[1] == NTM
    o["w_fm"] = np.ascontiguousarray(fm)
    o["w_tm"] = np.ascontiguousarray(tm)
    return o


def prep_layer_weights(inp, l):
    return arrange_in_cols(inp["w_in"][l])


class Builder:
    def __init__(self, cfg):
        self.cfg = cfg
        self.nc = bass.Bass("TRN2", target_bir_lowering=False)
        self.P = Prog(self.nc)
        self.inputs = {}
        self.din = {}
        self.dscr = {}
        self.ext_out = []

    def inp(self, name, shape, dtype=F32):
        t = self.P.dram(name, shape, dtype, kind="ExternalInput")
        self.din[name] = t
        return t

    def scr(self, name, shape, dtype=F32):
        cfg = self.cfg
        if name in cfg.get("dbg_in", ()):
            kind = "ExternalInput"
        elif name in cfg.get("dbg_out", ()):
            kind = "ExternalOutput"
            self.ext_out.append(name)
        else:
            kind = "Internal"
        t = self.P.dram(name, shape, dtype, kind=kind)
        self.dscr[name] = t
        return t


def build_program(cfg):
    B = Builder(cfg)
    P = B.P
    nc = B.nc
    layers = cfg.get("layers", list(range(DEPTH)))
    stages = cfg.get("stages", "ABCDEF")
    mixers = cfg.get("mixers", ("s5", "hy", "ret", "mla"))

    x_in = B.inp("x_in", [SEQ, D])
    ctx_in = B.inp("ctx_in", [CTX, D])
    cs_in = B.inp("cs_in", [128, 8, 2])
    ident_in = B.inp("ident", [128, 128])
    ada_w = B.inp("ada_w", [DEPTH, D, 6 * D])
    ada_bT = B.inp("ada_bT", [DEPTH, 128, 48])
    ada_brow = B.inp("ada_brow", [DEPTH, 1, 6 * D])
    ng_T = B.inp("ng_T", [DEPTH, 4, 128, 8])
    ng_row = B.inp("ng_row", [DEPTH, 4, 1, D])
    w_fm = B.inp("w_fm", [DEPTH, D, NFM])
    w_tm = B.inp("w_tm", [DEPTH, D, NTM])
    w_out = B.inp("w_out", [DEPTH, D, D])
    w_up = B.inp("w_up", [DEPTH, D, 2 * DFF])
    ffn_cw = B.inp("ffn_cw", [DEPTH, 128, 44, 3])
    ffn_cb = B.inp("ffn_cb", [DEPTH, 128, 44])
    w_down = B.inp("w_down", [DEPTH, DFF, D])
    y_out = P.dram("y", [SEQ, D], F32, kind="ExternalOutput")

    xs = B.scr("xs", [T, D])
    pT_d = B.scr("pT_d", [NFM, T])
    hyp_d = B.scr("hyp_d", [W_PAD, 768])
    rtok_d = B.scr("rtok_d", [T, 768])
    oT_d = B.scr("oT_d", [D, T], BF16)
    gT_d = B.scr("gT_d", [DFF, W_PAD], BF16)

    ident = P.sb("ident_sb", [128, 128])
    zero_sb = P.sb("zero_sb", [128, 768])
    scs = P.sb("scs", [128, 8, 2])
    modT = P.sb("modT", [128, 48, 2])
    AB = P.sb("ABmod", [128, 4, 8, 2])
    grow = P.sb("grow", [128, 2, 2, D])
    eps_t = P.sb("eps_t", [128, 1])
    ones_f = P.sb("ones_f", [128, 128])
    ones_b = P.sb("ones_b", [128, 128], BF16)
    pss = [P.ps("ps%d" % i, [128, 512]) for i in range(8)]
    psrot = Rot([p[:] for p in pss], "ps")
    P.init_arena(cfg.get("arena_words", 47000))

    P.dma(ident[:], ident_in.ap(), [], ["ident"])
    P.v("dve", "memset", [], ["zero_sb"], zero_sb[:], 0.0)
    P.v("dve", "memset", [], ["eps_t"], eps_t[:], EPS)
    P.v("dve", "memset", [], ["ones_f"], ones_f[:], 1.0)
    P.v("dve", "memset", [], ["ones_b"], ones_b[:], 1.0)
    cs_raw = P.sb("cs_raw", [128, 8, 2])
    P.dma(cs_raw[:], cs_in.ap(), [], ["cs_raw"])
    P.act(scs[:], cs_raw[:], AF.Silu, ["cs_raw"], ["scs"])
    for r in (0, CTX0 + CTX, CTX0 + CTX + 1, W_PAD - 1):
        P.dma(hyp_d.ap()[r:r + 1, :], zero_sb[0:1, 0:768], ["zero_sb"], ["hyp_d"], q="pool")

    def xrows(layer, t0, n):
        if layer == 0:
            if t0 < CTX:
                return ctx_in.ap()[t0:t0 + n, :]
            return x_in.ap()[t0 - CTX:t0 - CTX + n, :]
        return xs.ap()[t0:t0 + n, :]

    def stage_A(l):
        m0 = P.mark()
        sc_rep = P.alloc([128, 8, 2, 128])
        P.v("dve", "tensor_copy", ["scs"], ["sc_rep"], sc_rep, scs[:].unsqueeze(3).to_broadcast([128, 8, 2, 128]))
        wbufs = Rot([P.alloc([128, 8, 512]) for _ in range(2)], "adaw")
        abT = P.alloc([128, 48])
        P.dma(abT, ada_bT.ap()[l], [], ["abT"])
        ngT = P.alloc([128, 4, 8])
        P.dma(ngT, ada_dummy_ngT(l), [], ["ngT"])
        brow = P.alloc([128, 2, D])
        ngrow = P.alloc([128, 2, D])
        for wi, c0 in enumerate((2 * D, 5 * D)):
            P.dma(brow[:, wi, :], ada_brow.ap()[l, 0:1, c0:c0 + D].partition_broadcast(128)[:, 0, :], [], ["brow"])
            P.dma(ngrow[:, wi, :], ng_row.ap()[l, 1 + 2 * wi, 0:1, :].partition_broadcast(128)[:, 0, :], [], ["ngrow"])
        for ci in range(12):
            wb, wk = wbufs.next()
            P.dma(wb, ada_w.ap()[l].rearrange("(k p) n -> p k n", p=128)[:, :, ci * 512:(ci + 1) * 512], [], [wk])
            for mi in range(4):
                m = ci * 4 + mi
                ps, pk = psrot.next()
                for k in range(8):
                    P.mm(ps[:, 0:2], wb[:, k, mi * 128:(mi + 1) * 128], scs[:, k, :], k == 0, k == 7,
                         [wk, "scs"], [pk])
                P.v("dve", "tensor_scalar", [pk, "abT"], ["modT"], modT[:, m, :], ps[:, 0:2], abT[:, m:m + 1], None, ALU.add)
            if ci in (4, 5, 10, 11):
                wi = 0 if ci < 6 else 1
                cc = (ci - 4) if ci < 6 else (ci - 10)
                for v in range(2):
                    ps, pk = psrot.next()
                    for k in range(8):
                        P.mm(ps, sc_rep[:, k, v, :], wb[:, k, :], k == 0, k == 7, [wk, "sc_rep"], [pk])
                    P.v("dve", "tensor_tensor", [pk, "brow"], ["grow"], grow[:, wi, v, cc * 512:(cc + 1) * 512], ps,
                        brow[:, wi, cc * 512:(cc + 1) * 512], ALU.add)
        for wi in range(2):
            for v in range(2):
                P.v("pool", "tensor_tensor", ["grow", "ngrow"], ["grow"], grow[:, wi, v, :], grow[:, wi, v, :], ngrow[:, wi, :], ALU.mult)
        for wh in range(2):
            shm = 0 if wh == 0 else 24
            scm = 8 if wh == 0 else 32
            for v in range(2):
                P.v("dve", "scalar_tensor_tensor", ["modT", "ngT"], ["AB"], AB[:, 2 * wh, :, v], modT[:, scm:scm + 8, v], 1.0,
                    ngT[:, 2 * wh, :], ALU.add, ALU.mult)
                P.v("dve", "tensor_copy", ["modT"], ["AB"], AB[:, 2 * wh + 1, :, v], modT[:, shm:shm + 8, v])
        P.barrier()
        P.release(m0)

    def ada_dummy_ngT(l):
        return ng_T.ap()[l].rearrange("f p k -> p f k")

    def norm_mod_transpose(xt, xkey, hT, hkey, tt, wh, scratch):
        v = 1 if tt < 2 else 0
        junk, ss, rstd, xn = scratch
        P.act(junk, xt, AF.Square, [xkey], ["nm_junk", "nm_ss"], accum_out=ss)
        P.act(rstd, ss, AF.Sqrt, ["nm_ss", "eps_t"], ["nm_rstd"], bias=eps_t[:], scale=1.0 / D)
        P.v("dve", "reciprocal", ["nm_rstd"], ["nm_rstd"], rstd, rstd)
        P.act(xn, xt, AF.Identity, [xkey, "nm_rstd"], ["nm_xn"], scale=rstd)
        for half in range(2):
            ps, pk = psrot.next()
            for kk in range(4):
                k = half * 4 + kk
                P.tr(ps[:, kk * 128:(kk + 1) * 128], xn[:, k * 128:(k + 1) * 128], ident[:], ["nm_xn", "ident"], [pk])
            for kk in range(4):
                k = half * 4 + kk
                eng = "dve" if kk % 2 == 0 else "pool"
                if eng == "pool":
                    P.act(hT[:, k, tt * 128:(tt + 1) * 128], ps[:, kk * 128:(kk + 1) * 128], AF.Identity,
                          [pk, "AB"], [hkey], bias=AB[:, 2 * wh + 1, k, v:v + 1], scale=AB[:, 2 * wh, k, v:v + 1])
                else:
                    P.v("dve", "tensor_scalar", [pk, "AB"], [hkey], hT[:, k, tt * 128:(tt + 1) * 128],
                        ps[:, kk * 128:(kk + 1) * 128], AB[:, 2 * wh, k, v:v + 1], AB[:, 2 * wh + 1, k, v:v + 1],
                        ALU.mult, ALU.add)

    def nm_scratch():
        return (P.alloc([128, D]), P.alloc([128, 1]), P.alloc([128, 1]), P.alloc([128, D]))

    def stage_B(l, hT):
        m0 = P.mark()
        xbufs = Rot([P.alloc([128, D]) for _ in range(3)], "xt")
        scratch = nm_scratch()
        for tt in range(NTT):
            xt, xk = xbufs.next()
            P.dma(xt, xrows(l, tt * 128, 128), ["xs%d" % tt], [xk])
            norm_mod_transpose(xt, xk, hT, "hT", tt, 0, scratch)

    def load_cast(dst_bf, src_ap, shape, stage_rot, key, eng="pool"):
        st, sk = stage_rot.next()
        P.dma(st, src_ap, [], [sk])
        if eng == "act":
            P.act(dst_bf, st, AF.Copy, [sk], [key])
        else:
            P.v(eng, "tensor_copy", [sk], [key], dst_bf, st)

    def stage_C(l, hT):
        m0 = P.mark()
        wst = Rot([P.alloc([128, 8, 512]) for _ in range(2)], "wst")
        wbf = Rot([P.alloc([128, 8, 512], BF16) for _ in range(2)], "wbf")
        outb = Rot([P.alloc([128, T]) for _ in range(2)], "pout")
        wsrc = w_fm.ap()[l].rearrange("(k p) n -> p k n", p=128)
        for cg in range(NFM // 512):
            wb, wk = wbf.next()
            load_cast(wb, wsrc[:, :, cg * 512:(cg + 1) * 512], None, wst, wk, eng="act")
            for ci in range(4):
                ct = cg * 4 + ci
                ob, ok = outb.next()
                for ch, (c0, cn) in enumerate(CH512):
                    ps, pk = psrot.next()
                    for k in range(8):
                        P.mm(ps[:, 0:cn], wb[:, k, ci * 128:(ci + 1) * 128], hT[:, k, c0:c0 + cn], k == 0, k == 7,
                             [wk, "hT"], [pk])
                    if ch % 2 == 0:
                        P.act(ob[:, c0:c0 + cn], ps[:, 0:cn], AF.Copy, [pk], [ok])
                    else:
                        P.v("dve", "tensor_copy", [pk], [ok], ob[:, c0:c0 + cn], ps[:, 0:cn])
                P.dma(pT_d.ap()[ct * 128:(ct + 1) * 128, :], ob, [ok], ["pT_d"], q="pool")
        tmo = Rot([P.alloc([128, 512]) for _ in range(3)], "tmo")
        wsrc = w_tm.ap()[l].rearrange("(k p) n -> p k n", p=128)
        for cg in range(NTM // 512):
            wb, wk = wbf.next()
            load_cast(wb, wsrc[:, :, cg * 512:(cg + 1) * 512], None, wst, wk, eng="act")
            for tt in range(NTT):
                ps, pk = psrot.next()
                for k in range(8):
                    P.mm(ps, hT[:, k, tt * 128:(tt + 1) * 128], wb[:, k, :], k == 0, k == 7, [wk, "hT"], [pk])
                ob, ok = tmo.next()
                if tt % 2 == 0:
                    P.act(ob, ps, AF.Copy, [pk], [ok])
                else:
                    P.v("dve", "tensor_copy", [pk], [ok], ob, ps)
                row_h = (CTX0 + tt * 128) if tt < 2 else (LAT0 + (tt - 2) * 128)
                c0 = cg * 512
                if c0 + 512 <= 768:
                    P.dma(hyp_d.ap()[row_h:row_h + 128, c0:c0 + 512], ob, [ok], ["hyp_d"], q="pool")
                elif c0 >= 768:
                    P.dma(rtok_d.ap()[tt * 128:(tt + 1) * 128, c0 - 768:c0 - 768 + 512], ob, [ok], ["rtok_d"], q="pool")
                else:
                    nh = 768 - c0
                    P.dma(hyp_d.ap()[row_h:row_h + 128, c0:768], ob[:, 0:nh], [ok], ["hyp_d"], q="pool")
                    P.dma(rtok_d.ap()[tt * 128:(tt + 1) * 128, 0:512 - nh], ob[:, nh:512], [ok], ["rtok_d"], q="pool")
        P.barrier()
        P.release(m0)

    def stage_E(l, h2T, last):
        m0 = P.mark()
        wst = Rot([P.alloc([128, 8, 512]) for _ in range(2)], "wst")
        wo = P.alloc([128, 8, D], BF16)
        wsrc = w_out.ap()[l].rearrange("(k p) n -> p k n", p=128)
        for hh in range(2):
            load_cast(wo[:, :, hh * 512:(hh + 1) * 512], wsrc[:, :, hh * 512:(hh + 1) * 512], None, wst, "wo", eng="pool")
        obufs = Rot([P.alloc([128, 8, 128], BF16) for _ in range(3)], "oT")
        xbufs = Rot([P.alloc([128, D]) for _ in range(3)], "xt")
        ysb = P.alloc([128, D])
        junk = P.alloc([128, D])
        ss = P.alloc([128, 1])
        rstd = P.alloc([128, 1])
        tmp = P.alloc([128, D])
        xnb = Rot([P.alloc([128, D]) for _ in range(2)], "xnew")
        scratch = nm_scratch()
        osrc = oT_d.ap().rearrange("(k p) t -> p k t", p=128)
        ysbs = Rot([ysb, P.alloc([128, D])], "ysb")

        def e_part1(tt):
            ob, ok = obufs.next()
            P.dma(ob, osrc[:, :, tt * 128:(tt + 1) * 128], ["oT_d"], [ok])
            xt, xk = xbufs.next()
            P.dma(xt, xrows(l, tt * 128, 128), ["xs%d" % tt], [xk])
            yv, yk = ysbs.next()
            for hh in range(2):
                ps, pk = psrot.next()
                for k in range(8):
                    P.mm(ps, ob[:, k, :], wo[:, k, hh * 512:(hh + 1) * 512], k == 0, k == 7, [ok, "wo"], [pk])
                P.act(yv[:, hh * 512:(hh + 1) * 512], ps, AF.Copy, [pk], [yk])
            return (xt, xk, yv, yk)

        tiles = list(range(2 if last else 0, NTT))
        pend = {tiles[0]: e_part1(tiles[0])}
        for ti, tt in enumerate(tiles):
            if ti + 1 < len(tiles):
                pend[tiles[ti + 1]] = e_part1(tiles[ti + 1])
            xt, xk, yv, yk = pend.pop(tt)
            v = 1 if tt < 2 else 0
            P.act(junk, yv, AF.Square, [yk], ["e_junk", "e_ss"], accum_out=ss)
            P.act(rstd, ss, AF.Sqrt, ["e_ss", "eps_t"], ["e_rstd"], bias=eps_t[:], scale=1.0 / D)
            P.v("dve", "reciprocal", ["e_rstd"], ["e_rstd"], rstd, rstd)
            P.v("dve", "scalar_tensor_tensor", [yk, "e_rstd", "grow"], ["e_tmp"], tmp, yv, rstd, grow[:, 0, v, :], ALU.mult, ALU.mult)
            xn, xnk = xnb.next()
            P.v("dve", "tensor_tensor", ["e_tmp", xk], [xnk], xn, tmp, xt, ALU.add)
            P.dma(xs.ap()[tt * 128:(tt + 1) * 128, :], xn, [xnk], ["xs%d" % tt], q="pool")
            norm_mod_transpose(xn, xnk, h2T, "h2T", tt, 1, scratch)
        P.barrier()
        P.release(m0)

    def stage_F(l, h2T, last):
        m0 = P.mark()
        wst = Rot([P.alloc([128, 8, 256]) for _ in range(2)], "fwst")
        wbf = Rot([P.alloc([128, 8, 256], BF16) for _ in range(2)], "fwbf")
        cw = P.alloc([128, 44, 3])
        cb = P.alloc([128, 44])
        P.dma(cw, ffn_cw.ap()[l], [], ["ffn_cw"])
        P.dma(cb, ffn_cb.ap()[l], [], ["ffn_cb"])
        uab = [P.alloc([128, W_PAD], BF16) for _ in range(2)]
        uvb = [P.alloc([128, W_PAD], BF16) for _ in range(2)]
        ca = P.alloc([128, W_PAD])
        cv = P.alloc([128, W_PAD])
        gb = Rot([P.alloc([128, W_PAD], BF16) for _ in range(1)], "gb")
        for i_ in range(2):
            P.v("dve", "memset", [], ["ua%d" % i_], uab[i_], 0.0)
            P.v("pool", "memset", [], ["uv%d" % i_], uvb[i_], 0.0)
        wsrc = w_up.ap()[l].rearrange("(k p) n -> p k n", p=128)
        Wp = W_PAD

        def f_mm(j):
            wb, wk = wbf.next()
            st, sk = wst.next()
            P.dma(st[:, :, 0:128], wsrc[:, :, j * 128:(j + 1) * 128], [], [sk])
            P.dma(st[:, :, 128:256], wsrc[:, :, DFF + j * 128:DFF + (j + 1) * 128], [], [sk])
            P.v("pool", "tensor_copy", [sk], [wk], wb, st)
            for half, (ub, ukey) in enumerate(((uab[j % 2], "ua%d" % (j % 2)), (uvb[j % 2], "uv%d" % (j % 2)))):
                for ch, (c0, cn) in enumerate(CH512):
                    ps, pk = psrot.next()
                    for k in range(8):
                        P.mm(ps[:, 0:cn], wb[:, k, half * 128:(half + 1) * 128], h2T[:, k, c0:c0 + cn], k == 0, k == 7,
                             [wk, "h2T"], [pk])
                    segs = []
                    if c0 < CTX:
                        segs.append((0, CTX, CTX0))
                        segs.append((CTX, cn - CTX, LAT0))
                    else:
                        segs.append((0, cn, LAT0 + c0 - CTX))
                    for (s0, sn, d0) in segs:
                        if (ch + half) % 2 == 0:
                            P.act(ub[:, d0:d0 + sn], ps[:, s0:s0 + sn], AF.Copy, [pk], [ukey])
                        else:
                            P.v("dve", "tensor_copy", [pk], [ukey], ub[:, d0:d0 + sn], ps[:, s0:s0 + sn])

        def f_conv(j):
            for half, (ub, ukey, cbuf, ckey) in enumerate(((uab[j % 2], "ua%d" % (j % 2), ca, "ca"), (uvb[j % 2], "uv%d" % (j % 2), cv, "cv"))):
                jj = j + 22 * half
                P.act(cbuf[:, 1:Wp - 1], ub[:, 1:Wp - 1], AF.Identity, [ukey, "ffn_cw", "ffn_cb"], [ckey],
                      bias=cb[:, jj:jj + 1], scale=cw[:, jj, 1:2])
                P.v("dve", "scalar_tensor_tensor", [ukey, ckey, "ffn_cw"], [ckey], cbuf[:, 1:Wp - 1], ub[:, 0:Wp - 2],
                    cw[:, jj, 0:1], cbuf[:, 1:Wp - 1], ALU.mult, ALU.add)
                P.v("dve", "scalar_tensor_tensor", [ukey, ckey, "ffn_cw"], [ckey], cbuf[:, 1:Wp - 1],
                    ub[:, 2:Wp], cw[:, jj, 2:3], cbuf[:, 1:Wp - 1], ALU.mult, ALU.add)
            P.act(ca[:, 1:Wp - 1], ca[:, 1:Wp - 1], AF.Silu, ["ca"], ["ca"])
            g, gk = gb.next()
            P.v("dve", "tensor_tensor", ["ca", "cv"], [gk], g[:, 1:Wp - 1], ca[:, 1:Wp - 1], cv[:, 1:Wp - 1], ALU.mult)
            P.dma(gT_d.ap()[j * 128:(j + 1) * 128, 1:Wp - 1], g[:, 1:Wp - 1], [gk], ["gT_d"], q="pool")

        for j in range(22):
            f_mm(j)
            if j >= 1:
                f_conv(j - 1)
        f_conv(21)
        P.barrier()
        P.release(m0)

    def stage_F2(l, last):
        m0 = P.mark()
        wst = Rot([P.alloc([128, 2, D]) for _ in range(2)], "dwst")
        wd = P.alloc([128, 22, D], BF16)
        wsrc = w_down.ap()[l].rearrange("(j p) n -> p j n", p=128)
        for j2 in range(11):
            load_cast(wd[:, 2 * j2:2 * j2 + 2, :], wsrc[:, 2 * j2:2 * j2 + 2, :], None, wst, "wd", eng="pool")
        gbufs = Rot([P.alloc([128, 22, 128], BF16) for _ in range(3)], "gt")
        xbufs = Rot([P.alloc([128, D]) for _ in range(3)], "xt")
        ysb = P.alloc([128, D])
        junk = P.alloc([128, D])
        ss = P.alloc([128, 1])
        rstd = P.alloc([128, 1])
        tmp = P.alloc([128, D])
        xnb = Rot([P.alloc([128, D]) for _ in range(2)], "xnew")
        gsrc = gT_d.ap().rearrange("(j p) t -> p j t", p=128)
        ysbs = Rot([ysb, P.alloc([128, D])], "ysb")

        def f2_part1(tt):
            col = (CTX0 + tt * 128) if tt < 2 else (LAT0 + (tt - 2) * 128)
            gt, gk = gbufs.next()
            P.dma(gt, gsrc[:, :, col:col + 128], ["gT_d"], [gk])
            xt, xk = xbufs.next()
            P.dma(xt, xs.ap()[tt * 128:(tt + 1) * 128, :], ["xs%d" % tt], [xk])
            yv, yk = ysbs.next()
            for hh in range(2):
                ps, pk = psrot.next()
                for j in range(22):
                    P.mm(ps, gt[:, j, :], wd[:, j, hh * 512:(hh + 1) * 512], j == 0, j == 21, [gk, "wd"], [pk])
                P.act(yv[:, hh * 512:(hh + 1) * 512], ps, AF.Copy, [pk], [yk])
            return (xt, xk, yv, yk)

        tiles = list(range(2 if last else 0, NTT))
        pend = {tiles[0]: f2_part1(tiles[0])}
        for ti, tt in enumerate(tiles):
            if ti + 1 < len(tiles):
                pend[tiles[ti + 1]] = f2_part1(tiles[ti + 1])
            xt, xk, yv, yk = pend.pop(tt)
            v = 1 if tt < 2 else 0
            P.act(junk, yv, AF.Square, [yk], ["e_junk", "e_ss"], accum_out=ss)
            P.act(rstd, ss, AF.Sqrt, ["e_ss", "eps_t"], ["e_rstd"], bias=eps_t[:], scale=1.0 / D)
            P.v("dve", "reciprocal", ["e_rstd"], ["e_rstd"], rstd, rstd)
            P.v("dve", "scalar_tensor_tensor", [yk, "e_rstd", "grow"], ["e_tmp"], tmp, yv, rstd, grow[:, 1, v, :], ALU.mult, ALU.mult)
            xn, xnk = xnb.next()
            P.v("dve", "tensor_tensor", ["e_tmp", xk], [xnk], xn, tmp, xt, ALU.add)
            if last:
                P.dma(y_out.ap()[(tt - 2) * 128:(tt - 1) * 128, :], xn, [xnk], ["y"], q="pool")
            else:
                P.dma(xs.ap()[tt * 128:(tt + 1) * 128, :], xn, [xnk], ["xs%d" % tt], q="pool")
        P.barrier()
        P.release(m0)

    for l in layers:
        last = (l == DEPTH - 1)
        if "A" in stages:
            stage_A(l)
        mh = P.mark()
        hT = P.alloc([128, 8, T], BF16)
        if "B" in stages:
            stage_B(l, hT)
        if "C" in stages:
            stage_C(l, hT)
        P.barrier()
        P.release(mh)
        if "D" in stages:
            C = SimpleNamespace(B=B, P=P, nc=nc, psrot=psrot, pss=pss, pT_d=pT_d, hyp_d=hyp_d, rtok_d=rtok_d, oT_d=oT_d,
                                ident=ident, zero_sb=zero_sb, eps_t=eps_t, ones_f=ones_f, ones_b=ones_b, cfg=cfg)
            for mx in mixers:
                MIXERS[mx](C, l, last)
                P.barrier()
        mh = P.mark()
        h2T = P.alloc([128, 8, T], BF16)
        if "E" in stages:
            stage_E(l, h2T, last)
        if "F" in stages:
            stage_F(l, h2T, last)
        P.barrier()
        P.release(mh)
        if "F" in stages:
            stage_F2(l, last)
    P.emit()
    return B


MIXERS = {}
HOSTPREP = {}


def make_inputs(inp, cfg=None):
    sh = {}
    sh.update(host_consts())
    L = DEPTH
    sh["ada_w"] = np.ascontiguousarray(inp["ada_w"])
    sh["ada_bT"] = np.ascontiguousarray(inp["ada_b"].reshape(L, 48, 128).transpose(0, 2, 1))
    sh["ada_brow"] = np.ascontiguousarray(inp["ada_b"].reshape(L, 1, 6 * D))
    sh["ng_T"] = np.ascontiguousarray(inp["norm_g"].reshape(L, 4, 8, 128).transpose(0, 1, 3, 2))
    sh["ng_row"] = np.ascontiguousarray(inp["norm_g"].reshape(L, 4, 1, D))
    lw = [prep_layer_weights(inp, l) for l in range(L)]
    sh["w_fm"] = np.stack([w["w_fm"] for w in lw])
    sh["w_tm"] = np.stack([w["w_tm"] for w in lw])
    sh["w_out"] = np.ascontiguousarray(inp["w_out"])
    sh["w_up"] = np.ascontiguousarray(inp["ffn_w_up"])
    sh["ffn_cw"] = np.ascontiguousarray(inp["ffn_conv_w"].reshape(L, 3, 44, 128).transpose(0, 3, 2, 1))
    sh["ffn_cb"] = np.ascontiguousarray(inp["ffn_conv_b"].reshape(L, 44, 128).transpose(0, 2, 1))
    sh["w_down"] = np.ascontiguousarray(inp["ffn_w_down"])
    for fn in HOSTPREP.values():
        sh.update(fn(inp))
    per = []
    for core in range(8):
        b = core % NB
        d = {}
        d["x_in"] = np.ascontiguousarray(inp["x"][b])
        d["ctx_in"] = np.ascontiguousarray(inp["ctx"][b])
        cs = np.stack([inp["c"][b], inp["c_ctx"]], axis=-1)
        d["cs_in"] = np.ascontiguousarray(cs.reshape(8, 128, 2).transpose(1, 0, 2))
        per.append(d)
    return sh, per


_CACHE = {}
ACTIVE_CORES = [0, 1, 4, 5]


def kernel(**inputs):
    inp = {k: np.asarray(v) for k, v in inputs.items()}
    cfg = {}
    Bd = build_program(cfg)
    sh, per = make_inputs(inp)
    zero_keys = ("x_in", "ctx_in", "w_fm", "w_tm", "w_out", "w_up", "w_down", "ada_w")
    zeros = {k: np.zeros_like(sh[k] if k in sh else per[0][k]) for k in zero_keys}
    in_maps = []
    for core in range(8):
        m = dict(sh)
        if core in ACTIVE_CORES:
            m.update(per[ACTIVE_CORES.index(core)])
        else:
            m.update(per[0])
            m.update(zeros)
        in_maps.append(m)
    res = run_bass_kernel_spmd(Bd.nc, in_maps, core_ids=list(range(8)))
    out = np.stack([res.results[ACTIVE_CORES[b]]["y"] for b in range(NB)], axis=0)
    return out.astype(np.float32)


MLA_H = 4
QK = 96
PT_CQA, PT_CQB, PT_CKV, PT_KR, PT_KRS = 1536, 1664, 1792, 1920, 1952
OT_MLA = 768


def host_mla(inp):
    L = DEPTH
    o = {}
    wq = inp["mla_w_uq"].reshape(L, 192, MLA_H, QK)
    wqB = wq.copy()
    rope = wq[..., 64:96].reshape(L, 192, MLA_H, 2, 16)[..., ::-1, :].reshape(L, 192, MLA_H, 32)
    wqB[..., 64:96] = rope
    o["mla_wq"] = np.ascontiguousarray(np.stack([wq, wqB], axis=2).reshape(L, 192, 2 * MLA_H * QK))
    wkv = inp["mla_w_ukv"].reshape(L, 128, MLA_H, 128)
    o["mla_wkn"] = np.ascontiguousarray(wkv[..., 0:64].reshape(L, 128, 256))
    o["mla_wv"] = np.ascontiguousarray(wkv[..., 64:128].reshape(L, 128, 256))
    qg = np.zeros((L, 256), np.float32)
    qg[:, :192] = inp["mla_q_norm_g"]
    o["mla_qg"] = np.ascontiguousarray(qg.reshape(L, 2, 128).transpose(0, 2, 1))
    o["mla_kvg"] = np.ascontiguousarray(inp["mla_kv_norm_g"].reshape(L, 128, 1))
    n = SEQ
    rows = n // 64
    row = np.repeat(np.arange(rows), 64).astype(np.float32)
    col = np.tile(np.arange(64), rows).astype(np.float32)
    nf = 8
    inv = (np.float32(10000.0) ** (-np.arange(nf, dtype=np.float32) / nf)).astype(np.float32)
    ang = np.concatenate([row[:, None] * inv, col[:, None] * inv], axis=-1).astype(np.float32)
    cos, sin = np.cos(ang).astype(np.float32), np.sin(ang).astype(np.float32)
    Cf = np.ones((96, T), np.float32)
    Sf = np.zeros((96, T), np.float32)
    Cf[64:80, CTX:] = cos.T
    Cf[80:96, CTX:] = cos.T
    Sf[64:80, CTX:] = -sin.T
    Sf[80:96, CTX:] = sin.T
    o["mla_C"] = Cf
    o["mla_S"] = Sf
    return o


def mix_mla(C, l, last):
    P, B = C.P, C.B
    psrot = C.psrot
    if "mla_wq" not in B.din:
        B.inp("mla_wq", [DEPTH, 192, 768])
        B.inp("mla_wkn", [DEPTH, 128, 256])
        B.inp("mla_wv", [DEPTH, 128, 256])
        B.inp("mla_qg", [DEPTH, 128, 2])
        B.inp("mla_kvg", [DEPTH, 128, 1])
        B.inp("mla_C", [96, T])
        B.inp("mla_S", [96, T])
    din = B.din
    pT = C.pT_d.ap()
    m0 = P.mark()
    wq_st = P.alloc([128, 2, 768])
    wq = P.alloc([128, 2, 768], BF16)
    P.dma(wq_st[:, 0, :], din["mla_wq"].ap()[l, 0:128, :], [], ["mla_wq_st"])
    P.dma(wq_st[0:64, 1, :], din["mla_wq"].ap()[l, 128:192, :], [], ["mla_wq_st"])
    P.v("pool", "tensor_copy", ["mla_wq_st"], ["mla_wq"], wq[:, 0, :], wq_st[:, 0, :])
    P.v("pool", "tensor_copy", ["mla_wq_st"], ["mla_wq"], wq[0:64, 1, :], wq_st[0:64, 1, :])
    wk_st = P.alloc([128, 512])
    wkv = P.alloc([128, 512], BF16)
    P.dma(wk_st[:, 0:256], din["mla_wkn"].ap()[l], [], ["mla_wk_st"])
    P.dma(wk_st[:, 256:512], din["mla_wv"].ap()[l], [], ["mla_wk_st"])
    P.v("pool", "tensor_copy", ["mla_wk_st"], ["mla_wkv"], wkv, wk_st)
    qg = P.alloc([128, 2])
    kvg = P.alloc([128, 1])
    P.dma(qg, din["mla_qg"].ap()[l], [], ["mla_qg"])
    P.dma(kvg, din["mla_kvg"].ap()[l], [], ["mla_kvg"])
    qT = [P.alloc([96, T], BF16) for _ in range(MLA_H)]
    kT = [P.alloc([96, T], BF16) for _ in range(MLA_H)]
    vtok = P.alloc([128, NTT, MLA_H, 65], BF16)
    P.v("dve", "memset", [], ["m_vtok"], vtok, 1.0)
    m1 = P.mark()
    NB_ = 2
    cqa = Rot([P.alloc([128, 512]) for _ in range(NB_)], "m_cqa")
    cqb = Rot([P.alloc([64, 512]) for _ in range(NB_)], "m_cqb")
    ckv = Rot([P.alloc([128, 512]) for _ in range(NB_)], "m_ckv")
    krb = Rot([P.alloc([96, 2, 512]) for _ in range(NB_)], "m_kr")
    tabs = Rot([P.alloc([96, 2, 512]) for _ in range(NB_)], "m_tab")
    sq = P.alloc([128, 3, 512])
    rstd = P.alloc([128, 2, 512])
    cqn_a = P.alloc([128, 512], BF16)
    cqn_b = P.alloc([64, 512], BF16)
    ckvn = P.alloc([128, 512], BF16)
    t1 = P.alloc([96, 512])
    t2 = P.alloc([96, 512])
    for ch, (c0, cn) in enumerate(CH512):
        a, ak = cqa.next()
        b, bk = cqb.next()
        kv, kvk = ckv.next()
        kr, krk = krb.next()
        tb, tbk = tabs.next()
        P.dma(a[:, 0:cn], pT[PT_CQA:PT_CQA + 128, c0:c0 + cn], ["pT_d"], [ak])
        P.dma(b[:, 0:cn], pT[PT_CQB:PT_CQB + 64, c0:c0 + cn], ["pT_d"], [bk])
        P.dma(kv[:, 0:cn], pT[PT_CKV:PT_CKV + 128, c0:c0 + cn], ["pT_d"], [kvk])
        P.dma(kr[64:96, 0, 0:cn], pT[PT_KR:PT_KR + 32, c0:c0 + cn], ["pT_d"], [krk])
        P.dma(kr[64:96, 1, 0:cn], pT[PT_KRS:PT_KRS + 32, c0:c0 + cn], ["pT_d"], [krk])
        P.dma(tb[:, 0, 0:cn], din["mla_C"].ap()[:, c0:c0 + cn], [], [tbk])
        P.dma(tb[:, 1, 0:cn], din["mla_S"].ap()[:, c0:c0 + cn], [], [tbk])
        P.act(sq[:, 0, 0:cn], a[:, 0:cn], AF.Square, [ak], ["m_sq"])
        P.act(sq[0:64, 1, 0:cn], b[:, 0:cn], AF.Square, [bk], ["m_sq"])
        P.act(sq[:, 2, 0:cn], kv[:, 0:cn], AF.Square, [kvk], ["m_sq"])
        ps, pk = psrot.next()
        P.mm(ps[:, 0:cn], C.ones_f[:, :], sq[:, 0, 0:cn], True, False, ["m_sq", "ones_f"], [pk])
        P.mm(ps[:, 0:cn], C.ones_f[0:64, :], sq[0:64, 1, 0:cn], False, True, ["m_sq", "ones_f"], [pk])
        P.act(rstd[:, 0, 0:cn], ps[:, 0:cn], AF.Sqrt, [pk, "eps_t"], ["m_rstd"], bias=C.eps_t[:], scale=1.0 / 192)
        ps2, pk2 = psrot.next()
        P.mm(ps2[:, 0:cn], C.ones_f[:, :], sq[:, 2, 0:cn], True, True, ["m_sq", "ones_f"], [pk2])
        P.act(rstd[:, 1, 0:cn], ps2[:, 0:cn], AF.Sqrt, [pk2, "eps_t"], ["m_rstd"], bias=C.eps_t[:], scale=1.0 / 128)
        P.v("dve", "reciprocal", ["m_rstd"], ["m_rstd"], rstd[:, :, 0:cn], rstd[:, :, 0:cn])
        P.v("dve", "scalar_tensor_tensor", [ak, "mla_qg", "m_rstd"], ["m_cqn_a"], cqn_a[:, 0:cn], a[:, 0:cn], qg[:, 0:1],
            rstd[:, 0, 0:cn], ALU.mult, ALU.mult)
        P.v("dve", "scalar_tensor_tensor", [bk, "mla_qg", "m_rstd"], ["m_cqn_b"], cqn_b[:, 0:cn], b[:, 0:cn], qg[0:64, 1:2],
            rstd[0:64, 0, 0:cn], ALU.mult, ALU.mult)
        P.v("dve", "scalar_tensor_tensor", [kvk, "mla_kvg", "m_rstd"], ["m_ckvn"], ckvn[:, 0:cn], kv[:, 0:cn], kvg[:, 0:1],
            rstd[:, 1, 0:cn], ALU.mult, ALU.mult)
        for h in range(MLA_H):
            psA, pkA = psrot.next()
            psB, pkB = psrot.next()
            for (pp, ppk, ab) in ((psA, pkA, 0), (psB, pkB, 1)):
                w0 = (ab * MLA_H + h) * QK
                P.mm(pp[0:96, 0:cn], wq[:, 0, w0:w0 + QK], cqn_a[:, 0:cn], True, False, ["mla_wq", "m_cqn_a"], [ppk])
                P.mm(pp[0:96, 0:cn], wq[0:64, 1, w0:w0 + QK], cqn_b[:, 0:cn], False, True, ["mla_wq", "m_cqn_b"], [ppk])
            P.v("dve", "tensor_tensor", [pkA, tbk], ["m_t1"], t1[:, 0:cn], psA[0:96, 0:cn], tb[:, 0, 0:cn], ALU.mult)
            P.v("dve", "tensor_tensor", [pkB, tbk], ["m_t2"], t2[:, 0:cn], psB[0:96, 0:cn], tb[:, 1, 0:cn], ALU.mult)
            P.v("dve", "tensor_tensor", ["m_t1", "m_t2"], ["m_qT%d" % h], qT[h][:, c0:c0 + cn], t1[:, 0:cn], t2[:, 0:cn], ALU.add)
        for h in range(MLA_H):
            ps, pk = psrot.next()
            P.mm(ps[0:64, 0:cn], wkv[:, h * 64:(h + 1) * 64], ckvn[:, 0:cn], True, True, ["mla_wkv", "m_ckvn"], [pk])
            P.act(kT[h][0:64, c0:c0 + cn], ps[0:64, 0:cn], AF.Copy, [pk], ["m_kT%d" % h])
        P.v("dve", "tensor_tensor", [krk, tbk], ["m_t1"], t1[64:96, 0:cn], kr[64:96, 0, 0:cn], tb[64:96, 0, 0:cn], ALU.mult)
        P.v("dve", "tensor_tensor", [krk, tbk], ["m_t2"], t2[64:96, 0:cn], kr[64:96, 1, 0:cn], tb[64:96, 1, 0:cn], ALU.mult)
        P.v("dve", "tensor_tensor", ["m_t1", "m_t2"], ["m_t1"], t1[64:96, 0:cn], t1[64:96, 0:cn], t2[64:96, 0:cn], ALU.add)
        for h in range(MLA_H):
            if h % 2 == 0:
                P.act(kT[h][64:96, c0:c0 + cn], t1[64:96, 0:cn], AF.Copy, ["m_t1"], ["m_kT%d" % h])
            else:
                P.act(kT[h][64:96, c0:c0 + cn], t1[64:96, 0:cn], AF.Copy, ["m_t1"], ["m_kT%d" % h])
        for i in range(cn // 128):
            tt = c0 // 128 + i
            ps, pk = psrot.next()
            P.mm(ps[:, 0:256], ckvn[:, i * 128:(i + 1) * 128], wkv[:, 256:512], True, True, ["mla_wkv", "m_ckvn"], [pk])
            P.act(vtok[:, tt, :, 0:64], ps[:, 0:256].rearrange("p (h e) -> p h e", h=MLA_H), AF.Copy, [pk], ["m_vtok"])
    P.barrier()
    P.release(m1)
    srot = Rot([p[:] for p in C.pss[0:6]], "ps")
    ps_o, pk_o = C.pss[6][:], "ps6"
    ps_d, pk_d = C.pss[7][:], "ps7"
    pT_b = Rot([P.alloc([128, 512], BF16) for _ in range(6)], "m_pT")
    rden = P.alloc([64, 512])
    rrow = P.alloc([65, 512])
    ost = Rot([P.alloc([64, 512], BF16) for _ in range(2)], "m_ost")
    scale = 1.0 / math.sqrt(QK)
    blocks = [(CTX + i * 512, 512, 0, NTT) for i in range(SEQ // 512)]
    if not last:
        blocks = [(0, CTX, 0, 2)] + blocks
    steps = []
    for (q0, qn, kt0, kt1) in blocks:
        for h in range(MLA_H):
            for kt in range(kt0, kt1):
                steps.append((q0, qn, kt0, kt1, h, kt))
    LOOK = 2
    pend = {}

    def issue_scores(i):
        q0, qn, kt0, kt1, h, kt = steps[i]
        ps, pk = srot.next()
        P.mm(ps[:, 0:qn], kT[h][:, kt * 128:(kt + 1) * 128], qT[h][:, q0:q0 + qn], True, True,
             ["m_kT%d" % h, "m_qT%d" % h], [pk])
        pb, pbk = pT_b.next()
        P.act(pb[:, 0:qn], ps[:, 0:qn], AF.Exp, [pk], [pbk], scale=scale)
        pend[i] = (pb, pbk)

    for i in range(min(LOOK, len(steps))):
        issue_scores(i)
    for i in range(len(steps)):
        if i + LOOK < len(steps):
            issue_scores(i + LOOK)
        q0, qn, kt0, kt1, h, kt = steps[i]
        pb, pbk = pend.pop(i)
        P.mm(ps_o[0:65, 0:qn], vtok[:, kt, h, :], pb[:, 0:qn], kt == kt0, kt == kt1 - 1, [pbk, "m_vtok"], [pk_o])
        if kt == kt1 - 1:
            P.v("dve", "reciprocal", [pk_o], ["m_rrow"], rrow[64:65, 0:qn], ps_o[64:65, 0:qn])
            P.mm(ps_d[0:64, 0:qn], C.ones_f[64:65, 0:64], rrow[64:65, 0:qn], True, True, ["m_rrow", "ones_f"], [pk_d])
            P.act(rden[:, 0:qn], ps_d[0:64, 0:qn], AF.Copy, [pk_d], ["m_rden"])
            ob, obk = ost.next()
            P.v("dve", "tensor_tensor", [pk_o, "m_rden"], [obk], ob[:, 0:qn], ps_o[0:64, 0:qn], rden[:, 0:qn], ALU.mult)
            P.dma(C.oT_d.ap()[OT_MLA + h * 64:OT_MLA + (h + 1) * 64, q0:q0 + qn], ob[:, 0:qn], [obk], ["oT_d"], q="pool")
    P.barrier()
    P.release(m0)


MIXERS["mla"] = mix_mla
HOSTPREP["mla"] = host_mla


RH = 4
PT_RQ, PT_RK, PT_RG, PT_RQS, PT_RKS = 256, 512, 768, 1024, 1280
OT_RET = 512
LN2 = math.log(2.0)


def host_ret(inp):
    L = DEPTH
    o = {}
    f32 = np.float32
    theta = (f32(10000.0) ** (-np.linspace(0.0, 1.0, 32, dtype=f32))).astype(f32)
    ang = (np.arange(SEQ, dtype=f32)[:, None] * theta).astype(f32)
    cos, sin = np.cos(ang).astype(f32), np.sin(ang).astype(f32)
    Ct = np.ones((T, 64), f32)
    St = np.zeros((T, 64), f32)
    Ct[CTX:, 0:32] = cos
    Ct[CTX:, 32:64] = cos
    St[CTX:, 0:32] = -sin
    St[CTX:, 32:64] = sin
    o["ret_Ct"] = Ct
    o["ret_St"] = St
    o["ret_C"] = np.ascontiguousarray(np.tile(Ct.T, (2, 1)))
    o["ret_S"] = np.ascontiguousarray(np.tile(St.T, (2, 1)))
    j = np.arange(128, dtype=f32)[:, None]
    i = np.arange(128, dtype=f32)[None, :]
    cst = np.zeros((128, 6, 128), f32)
    cst[:, 0, :] = np.maximum(i - j, 0)
    cst[:, 1, :] = np.maximum(j - i, 0)
    cst[:, 2, :] = (i >= j)
    cst[:, 3, :] = (j >= i)
    cst[:, 4, :] = i + 1
    cst[:, 5, :] = 128 - i
    o["ret_cst"] = cst
    col = np.zeros((128, 2), f32)
    col[:, 0] = 127 - np.arange(128)
    col[:, 1] = np.arange(128)
    o["ret_col"] = col
    hm = np.zeros((128, 2), f32)
    hm[:64, 0] = 1.0
    hm[64:, 1] = 1.0
    o["ret_hm"] = hm
    o["ret_dexp"] = np.ascontiguousarray(inp["ret_decay_exp"].reshape(L, 1, 8))
    o["ret_gnT"] = np.ascontiguousarray(inp["ret_gn_g"].reshape(L, RH, 64).transpose(0, 2, 1))
    return o


def mix_ret(C, l, last):
    P, B = C.P, C.B
    psrot = C.psrot
    if "ret_C" not in B.din:
        B.inp("ret_Ct", [T, 64]); B.inp("ret_St", [T, 64])
        B.inp("ret_C", [128, T]); B.inp("ret_S", [128, T])
        B.inp("ret_cst", [128, 6, 128]); B.inp("ret_col", [128, 2]); B.inp("ret_hm", [128, 2])
        B.inp("ret_dexp", [DEPTH, 1, 8]); B.inp("ret_gnT", [DEPTH, 64, 4])
    din = B.din
    pT = C.pT_d.ap()
    m0 = P.mark()
    cst = P.alloc([128, 6, 128])
    P.dma(cst, din["ret_cst"].ap(), [], ["r_cst"])
    colc = P.alloc([128, 2])
    P.dma(colc, din["ret_col"].ap(), [], ["r_col"])
    gnT = P.alloc([64, 4])
    P.dma(gnT, din["ret_gnT"].ap()[l], [], ["r_gnT"])
    lg8 = P.alloc([128, 8])
    P.dma(lg8, din["ret_dexp"].ap()[l, 0:1, :].partition_broadcast(128)[:, 0, :], [], ["r_lg8"])
    P.act(lg8, lg8, AF.Exp, ["r_lg8"], ["r_lg8"], scale=-LN2)
    P.act(lg8, lg8, AF.Ln, ["r_lg8"], ["r_lg8"], scale=-1.0, bias=1.0)
    lgp = P.alloc([128, 2, 2])
    for d_ in range(2):
        for pr in range(2):
            P.v("dve", "tensor_copy", ["r_lg8"], ["r_lgp"], lgp[0:64, d_, pr:pr + 1], lg8[0:64, d_ * 4 + 2 * pr:d_ * 4 + 2 * pr + 1])
            P.v("dve", "tensor_copy", ["r_lg8"], ["r_lgp"], lgp[64:128, d_, pr:pr + 1], lg8[64:128, d_ * 4 + 2 * pr + 1:d_ * 4 + 2 * pr + 2])
    Mk = P.alloc([128, 4, 128])
    mt = P.alloc([128, 128])
    for h in range(RH):
        P.act(Mk[:, h, :], cst[:, 0, :], AF.Exp, ["r_cst", "r_lg8"], ["r_Mk"], scale=lg8[:, h:h + 1])
        P.v("dve", "tensor_tensor", ["r_Mk", "r_cst"], ["r_Mk"], Mk[:, h, :], Mk[:, h, :], cst[:, 2, :], ALU.mult)
        P.act(mt, cst[:, 1, :], AF.Exp, ["r_cst", "r_lg8"], ["r_mt"], scale=lg8[:, 4 + h:5 + h])
        P.v("dve", "tensor_tensor", ["r_mt", "r_cst"], ["r_mt"], mt, mt, cst[:, 3, :], ALU.mult)
        P.v("dve", "tensor_tensor", ["r_mt", "r_Mk"], ["r_Mk"], Mk[:, h, :], Mk[:, h, :], mt, ALU.add)
    XI = P.alloc([128, 2, 2, 128])
    gcolp = P.alloc([128, 2, 2])
    for d_ in range(2):
        for pr in range(2):
            P.act(XI[:, d_, pr, :], cst[:, 4 + d_, :], AF.Exp, ["r_cst", "r_lgp"], ["r_XI"], scale=lgp[:, d_, pr:pr + 1])
            P.act(gcolp[:, d_, pr:pr + 1], lgp[:, d_, pr:pr + 1], AF.Exp, ["r_lgp"], ["r_gcol"], scale=128.0)
    Z = P.alloc([128, 2, 4])
    for d_ in range(2):
        for h in range(RH):
            P.act(Z[:, d_, h:h + 1], colc[:, d_:d_ + 1], AF.Exp, ["r_col", "r_lg8"], ["r_Z"], scale=lg8[:, d_ * 4 + h:d_ * 4 + h + 1])
    hm = P.alloc([128, 2])
    P.dma(hm, din["ret_hm"].ap(), [], ["r_hm"])
    TAB = P.alloc([128, 3, 4, 128])
    for h in range(RH):
        pr, hp = h // 2, h % 2
        P.v("dve", "tensor_copy", ["r_hm"], ["r_TAB"], TAB[:, 0, h, :], hm[:, hp:hp + 1].to_broadcast([128, 128]))
        for d_ in range(2):
            P.v("dve", "tensor_scalar", ["r_XI", "r_hm"], ["r_TAB"], TAB[:, 1 + d_, h, :], XI[:, d_, pr, :], hm[:, hp:hp + 1], None, ALU.mult)
    if C.cfg.get("ret_stop", 9) <= 0:
        P.barrier(); P.release(m0); return
    qr = [P.alloc([128, T], BF16) for _ in range(2)]
    kr = [P.alloc([128, T], BF16) for _ in range(2)]
    vt = P.alloc([128, NTT, 256], BF16)
    Sall = [[P.alloc([128, NTT, 128], BF16) for _ in range(2)] for _ in range(2)]
    m1 = P.mark()
    kz = [P.alloc([128, NTT, 256], BF16) for _ in range(2)]
    m2 = P.mark()
    ld = Rot([P.alloc([128, 4, 512]) for _ in range(2)], "r_ld")
    tb = Rot([P.alloc([128, 2, 512]) for _ in range(2)], "r_tb")
    t1 = P.alloc([128, 512]); t2 = P.alloc([128, 512])
    for ch, (c0, cn) in enumerate(CH512):
        tbv, tbk = tb.next()
        P.dma(tbv[:, 0, 0:cn], din["ret_C"].ap()[:, c0:c0 + cn], [], [tbk])
        P.dma(tbv[:, 1, 0:cn], din["ret_S"].ap()[:, c0:c0 + cn], [], [tbk])
        nq = cn // 128
        for pr in range(2):
            lv, lk = ld.next()
            for ii, r0 in enumerate((PT_RQ, PT_RQS, PT_RK, PT_RKS)):
                P.dma(lv[:, ii, 0:cn], pT[r0 + pr * 128:r0 + (pr + 1) * 128, c0:c0 + cn], ["pT_d"], [lk])
            P.v("dve", "tensor_tensor", [lk, tbk], ["r_t1"], t1[:, 0:cn], lv[:, 0, 0:cn], tbv[:, 0, 0:cn], ALU.mult)
            P.v("pool", "tensor_tensor", [lk, tbk], ["r_t2"], t2[:, 0:cn], lv[:, 1, 0:cn], tbv[:, 1, 0:cn], ALU.mult)
            P.v("dve", "tensor_tensor", ["r_t1", "r_t2"], ["r_t1"], t1[:, 0:cn], t1[:, 0:cn], t2[:, 0:cn], ALU.add)
            P.act(qr[pr][:, c0:c0 + cn], t1[:, 0:cn], AF.Copy, ["r_t1"], ["r_qr%d" % pr], scale=0.125)
            P.v("dve", "tensor_tensor", [lk, tbk], ["r_t1"], t1[:, 0:cn], lv[:, 2, 0:cn], tbv[:, 0, 0:cn], ALU.mult)
            P.v("pool", "tensor_tensor", [lk, tbk], ["r_t2"], t2[:, 0:cn], lv[:, 3, 0:cn], tbv[:, 1, 0:cn], ALU.mult)
            P.v("pool", "tensor_tensor", ["r_t1", "r_t2"], ["r_kr%d" % pr], kr[pr][:, c0:c0 + cn], t1[:, 0:cn], t2[:, 0:cn], ALU.add)
    if C.cfg.get("ret_stop", 9) <= 0.5:
        P.barrier(); P.release(m0); return
    tl = Rot([P.alloc([128, 768]) for _ in range(3)], "r_tl")
    tt_tab = Rot([P.alloc([128, 2, 64]) for _ in range(3)], "r_ttab")
    kk1 = P.alloc([128, 4, 64]); kk2 = P.alloc([128, 4, 64])
    for tt in range(NTT):
        tv, tk = tl.next()
        P.dma(tv, C.rtok_d.ap()[tt * 128:(tt + 1) * 128, :], ["rtok_d"], [tk])
        tab, tabk = tt_tab.next()
        P.dma(tab[:, 0, :], din["ret_Ct"].ap()[tt * 128:(tt + 1) * 128, :], [], [tabk])
        P.dma(tab[:, 1, :], din["ret_St"].ap()[tt * 128:(tt + 1) * 128, :], [], [tabk])
        kview = tv[:, 0:256].rearrange("p (h d) -> p h d", h=4)
        ksview = tv[:, 256:512].rearrange("p (h d) -> p h d", h=4)
        P.v("dve", "tensor_tensor", [tk, tabk], ["r_kk1"], kk1, kview, tab[:, 0, :].unsqueeze(1).to_broadcast([128, 4, 64]), ALU.mult)
        P.v("pool", "tensor_tensor", [tk, tabk], ["r_kk2"], kk2, ksview, tab[:, 1, :].unsqueeze(1).to_broadcast([128, 4, 64]), ALU.mult)
        P.v("dve", "tensor_tensor", ["r_kk1", "r_kk2"], ["r_kk1"], kk1, kk1, kk2, ALU.add)
        for d_ in range(2):
            P.v("dve" if d_ == 0 else "pool", "tensor_tensor", ["r_kk1", "r_Z"], ["r_kz%d" % d_],
                kz[d_][:, tt, :].rearrange("p (h d) -> p h d", h=4), kk1,
                Z[:, d_, :].unsqueeze(2).to_broadcast([128, 4, 64]), ALU.mult)
        P.act(vt[:, tt, :], tv[:, 512:768], AF.Copy, [tk], ["r_vt"])
    if C.cfg.get("ret_stop", 9) <= 1:
        P.barrier(); P.release(m0); return
    Srun = [[[P.alloc([128, 128]) for _ in range(2)] for _ in range(2)] for _ in range(2)]
    order = [list(range(NTT)), [1, 0] + list(range(NTT - 1, 1, -1))]
    for d_ in range(2):
        for pr in range(2):
            P.v("dve", "memset", [], ["r_S%d%d" % (d_, pr)], Sall[d_][pr][:, order[d_][0], :], 0.0)
    for s in range(NTT - 1):
        for d_ in range(2):
            for pr in range(2):
                n = order[d_][s]
                nxt = order[d_][s + 1]
                ps, pk = psrot.next()
                cur = Srun[d_][pr][s % 2]
                new = Srun[d_][pr][(s + 1) % 2]
                ck = "r_Sr%d%d%d" % (d_, pr, s % 2)
                nk = "r_Sr%d%d%d" % (d_, pr, (s + 1) % 2)
                P.mm(ps[:, 0:128], kz[d_][:, n, pr * 128:(pr + 1) * 128], vt[:, n, pr * 128:(pr + 1) * 128], True, True,
                     ["r_kz%d" % d_, "r_vt"], [pk])
                if s > 0:
                    P.v("dve", "scalar_tensor_tensor", [pk, ck, "r_gcol"], [nk], new, cur, gcolp[:, d_, pr:pr + 1], ps[:, 0:128],
                        ALU.mult, ALU.add)
                else:
                    P.v("dve", "tensor_copy", [pk], [nk], new, ps[:, 0:128])
                P.act(Sall[d_][pr][:, nxt, :], new, AF.Copy, [nk], ["r_S%d%d" % (d_, pr)])
    P.barrier()
    P.release(m1)
    if C.cfg.get("ret_stop", 9) <= 2:
        P.barrier(); P.release(m0); return
    msk = Rot([P.alloc([128, 4, 128], BF16) for _ in range(2)], "r_msk")
    gld = Rot([P.alloc([64, 4, 128]) for _ in range(2)], "r_gld")
    sqo = P.alloc([64, 512])
    rstd = P.alloc([64, 512])
    on = P.alloc([64, 4, 128])
    ost = Rot([P.alloc([64, 4, 128], BF16) for _ in range(2)], "r_ost")
    qtr = Rot([P.alloc([128, 3, 2, 128], BF16) for _ in range(4)], "r_qt")
    gsrc = pT[PT_RG:PT_RG + 256, :].rearrange("(h e) t -> e h t", h=4)
    odst = C.oT_d.ap()[OT_RET:OT_RET + 256, :].rearrange("(h e) t -> e h t", h=4)
    for n in range(2 if last else 0, NTT):
        cs = slice(n * 128, (n + 1) * 128)
        gv, gk = gld.next()
        P.dma(gv, gsrc[:, :, cs], ["pT_d"], [gk])
        P.act(gv, gv, AF.Silu, [gk], [gk])
        QT = []
        for pr in range(2):
            qv, qk = qtr.next()
            P.v("dve", "tensor_tensor", ["r_qr%d" % pr, "r_TAB"], [qk], qv,
                qr[pr][:, cs].unsqueeze(1).unsqueeze(1).to_broadcast([128, 3, 2, 128]), TAB[:, :, 2 * pr:2 * pr + 2, :], ALU.mult)
            QT.append((qv, qk))
        ps_s, pk_s = psrot.next()
        for h in range(RH):
            pr, hp = h // 2, h % 2
            P.mm(ps_s[:, h * 128:(h + 1) * 128], kr[pr][:, cs], QT[pr][0][:, 0, hp, :], True, True,
                 ["r_kr%d" % pr, QT[pr][1]], [pk_s])
        mv, mk = msk.next()
        P.v("dve", "tensor_tensor", [pk_s, "r_Mk"], [mk], mv, ps_s.rearrange("p (h i) -> p h i", h=4), Mk, ALU.mult)
        ps_o, pk_o = psrot.next()
        for h in range(RH):
            pr, hp = h // 2, h % 2
            hs = slice(hp * 64, (hp + 1) * 64)
            oo = ps_o[0:64, h * 128:(h + 1) * 128]
            P.mm(oo, vt[:, n, h * 64:(h + 1) * 64], mv[:, h, :], True, False, ["r_vt", mk], [pk_o])
            P.mm(oo, Sall[0][pr][:, n, hs], QT[pr][0][:, 1, hp, :], False, False, ["r_S0%d" % pr, QT[pr][1]], [pk_o])
            P.mm(oo, Sall[1][pr][:, n, hs], QT[pr][0][:, 2, hp, :], False, True, ["r_S1%d" % pr, QT[pr][1]], [pk_o])
        P.act(sqo, ps_o[0:64, :], AF.Square, [pk_o], ["r_sqo"])
        ps_n, pk_n = psrot.next()
        P.mm(ps_n[0:64, :], C.ones_f[0:64, 0:64], sqo, True, True, ["r_sqo", "ones_f"], [pk_n])
        P.act(rstd, ps_n[0:64, :], AF.Sqrt, [pk_n, "eps_t"], ["r_rstd"], bias=C.eps_t[0:64, :], scale=1.0 / 64)
        P.v("dve", "reciprocal", ["r_rstd"], ["r_rstd"], rstd, rstd)
        P.v("dve", "tensor_tensor", [pk_o, "r_rstd"], ["r_on"], on, ps_o[0:64, :].rearrange("p (h i) -> p h i", h=4),
            rstd.rearrange("p (h i) -> p h i", h=4), ALU.mult)
        P.v("dve", "tensor_tensor", ["r_on", "r_gnT"], ["r_on"], on, on, gnT.unsqueeze(2).to_broadcast([64, 4, 128]), ALU.mult)
        ov, ok = ost.next()
        P.v("dve", "tensor_tensor", ["r_on", gk], [ok], ov, on, gv, ALU.mult)
        P.dma(odst[:, :, cs], ov, [ok], ["oT_d"], q="pool")
    P.barrier()
    P.release(m0)


MIXERS["ret"] = mix_ret
HOSTPREP["ret"] = host_ret


TWO_PI = 2.0 * math.pi
S5_SEGS = [(0, 256)] + [(256 + 1024 * k, 1024) for k in range(4)]


def host_s5(inp):
    L = DEPTH
    f32 = np.float32
    o = {}
    lam_re = inp["s5_lam_re"]; lam_im = inp["s5_lam_im"]
    ldt = np.repeat(inp["s5_log_dt"][..., None], 64, axis=-1)
    prow = np.stack([lam_re, lam_im, ldt], axis=1)
    o["s5_prow"] = np.ascontiguousarray(prow.reshape(L, 3, 1, 2048)).astype(f32)
    pT_ = prow.reshape(L, 3, 2, 8, 2, 64).transpose(0, 1, 4, 5, 2, 3)
    o["s5_pT"] = np.ascontiguousarray(pT_.reshape(L, 3, 128, 16)).astype(f32)
    Bexp = np.zeros((L, 2, 2, 128, 8, 128), f32)
    Cexp = np.zeros((L, 2, 2, 128, 8, 128), f32)
    for ri, (bk, ck) in enumerate((("s5_b_re", "s5_c_re"), ("s5_b_im", "s5_c_im"))):
        b = inp[bk]
        c = inp[ck]
        for m in range(8):
            for a in range(2):
                g = 2 * m + a
                q0 = 32 * (m % 4) + 16 * a
                Bexp[:, ri, :, q0:q0 + 16, m, 64 * a:64 * a + 64] = b[:, :, g].transpose(0, 1, 3, 2)
                Cexp[:, ri, :, 64 * a:64 * a + 64, m, q0:q0 + 16] = c[:, :, g].transpose(0, 1, 3, 2)
    o["s5_Bexp"] = Bexp
    o["s5_Cexp"] = Cexp
    o["s5_dT"] = np.ascontiguousarray(inp["s5_d"].reshape(L, 2, 128).transpose(0, 2, 1))
    o["s5_gbT"] = np.ascontiguousarray(inp["s5_glu_b"].reshape(L, 2, 128).transpose(0, 2, 1))
    o["s5_gw"] = np.ascontiguousarray(inp["s5_glu_w"])
    o["s5_iota"] = np.ascontiguousarray(np.broadcast_to(np.arange(1024, dtype=f32), (128, 1024)))
    return o


def rev_ap(ap):
    (ps, pn), (fs, fn) = ap.ap
    return bass.AP(ap.tensor, ap.offset + (fn - 1) * fs, [[ps, pn], [-fs, fn]])


def frac_round(P, eng_c, x, ki, kf, keys):
    xk, kik, kfk = keys
    P.v(eng_c, "tensor_copy", [xk], [kik], ki, x)
    P.v(eng_c, "tensor_copy", [kik], [kfk], kf, ki)
    P.v("dve", "tensor_tensor", [xk, kfk], [xk], x, x, kf, ALU.subtract)


def mix_s5(C, l, last):
    P, B = C.P, C.B
    psrot = C.psrot
    if "s5_prow" not in B.din:
        B.inp("s5_prow", [DEPTH, 3, 1, 2048]); B.inp("s5_pT", [DEPTH, 3, 128, 16])
        B.inp("s5_Bexp", [DEPTH, 2, 2, 128, 8, 128]); B.inp("s5_Cexp", [DEPTH, 2, 2, 128, 8, 128])
        B.inp("s5_dT", [DEPTH, 128, 2]); B.inp("s5_gbT", [DEPTH, 128, 2]); B.inp("s5_gw", [DEPTH, 256, 256])
        B.inp("s5_iota", [128, 1024])
    din = B.din
    pT = C.pT_d.ap()
    m0 = P.mark()
    NW = 2048
    LB = [[P.alloc([128, 2, 8, 128], BF16) for _ in range(2)]]
    LB = LB[0]
    LC = [P.alloc([128, 2, 8, 128], BF16) for _ in range(2)]
    rP = P.alloc([128, 16])
    thP = P.alloc([128, 16])
    m1 = P.mark()
    pr_ = [P.alloc([128, NW]) for _ in range(3)]
    for i in range(3):
        P.dma(pr_[i], din["s5_prow"].ap()[l, i, 0:1, :].partition_broadcast(128)[:, 0, :], [], ["s5_pr%d" % i])
    lre, lim, dt = pr_
    P.act(dt, dt, AF.Exp, ["s5_pr2"], ["s5_pr2"])
    re = P.alloc([128, NW]); im = P.alloc([128, NW])
    P.v("dve", "tensor_tensor", ["s5_pr0", "s5_pr2"], ["s5_re"], re, lre, dt, ALU.mult)
    P.v("dve", "tensor_tensor", ["s5_pr1", "s5_pr2"], ["s5_im"], im, lim, dt, ALU.mult)
    r_ = P.alloc([128, NW])
    P.act(r_, re, AF.Exp, ["s5_re"], ["s5_r"])
    ki = P.alloc([128, NW], I32); kf = P.alloc([128, NW])
    ph = P.alloc([128, NW]); ph2 = P.alloc([128, NW])
    P.v("dve", "tensor_scalar", ["s5_im"], ["s5_ph"], ph, im, 1.0 / TWO_PI, None, ALU.mult)
    P.v("dve", "tensor_scalar", ["s5_ph"], ["s5_ph2"], ph2, ph, 0.25, None, ALU.add)
    frac_round(P, "dve", ph, ki, kf, ("s5_ph", "s5_ki", "s5_kf"))
    frac_round(P, "dve", ph2, ki, kf, ("s5_ph2", "s5_ki", "s5_kf"))
    sn = ph; cs_ = ph2
    P.act(sn, ph, AF.Sin, ["s5_ph"], ["s5_ph"], scale=TWO_PI)
    P.act(cs_, ph2, AF.Sin, ["s5_ph2"], ["s5_ph2"], scale=TWO_PI)
    nre = re; nim = im
    P.v("dve", "tensor_tensor", ["s5_r", "s5_ph2"], ["s5_re"], nre, r_, cs_, ALU.mult)
    P.v("dve", "tensor_scalar", ["s5_re"], ["s5_re"], nre, nre, -1.0, None, ALU.add)
    P.v("dve", "tensor_tensor", ["s5_r", "s5_ph"], ["s5_im"], nim, r_, sn, ALU.mult)
    den = r_; tmp = kf
    P.v("dve", "tensor_tensor", ["s5_pr0", "s5_re", "s5_im"], ["s5_r"], den, lre, lre, ALU.mult)
    P.v("dve", "tensor_tensor", ["s5_pr1", "s5_kf"], ["s5_kf"], tmp, lim, lim, ALU.mult)
    P.v("dve", "tensor_tensor", ["s5_r", "s5_kf"], ["s5_r"], den, den, tmp, ALU.add)
    P.v("dve", "reciprocal", ["s5_r"], ["s5_r"], den, den)
    cre = ph; cim = ph2
    P.v("dve", "tensor_tensor", ["s5_re", "s5_pr0", "s5_ph"], ["s5_ph"], cre, nre, lre, ALU.mult)
    P.v("dve", "tensor_tensor", ["s5_im", "s5_pr1"], ["s5_kf"], tmp, nim, lim, ALU.mult)
    P.v("dve", "tensor_tensor", ["s5_ph", "s5_kf"], ["s5_ph"], cre, cre, tmp, ALU.add)
    P.v("dve", "tensor_tensor", ["s5_ph", "s5_r"], ["s5_ph"], cre, cre, den, ALU.mult)
    P.v("dve", "tensor_tensor", ["s5_im", "s5_pr0", "s5_ph2"], ["s5_ph2"], cim, nim, lre, ALU.mult)
    P.v("dve", "tensor_tensor", ["s5_re", "s5_pr1"], ["s5_kf"], tmp, nre, lim, ALU.mult)
    P.v("dve", "tensor_tensor", ["s5_ph2", "s5_kf"], ["s5_ph2"], cim, cim, tmp, ALU.subtract)
    P.v("dve", "tensor_tensor", ["s5_ph2", "s5_r"], ["s5_ph2"], cim, cim, den, ALU.mult)
    bre = pr_[0]; bim = pr_[1]; t_a = pr_[2]; t_b = re
    P.dma(bre.rearrange("p (d x) -> p d x", d=2), din["s5_Bexp"].ap()[l, 0].rearrange("d q m s -> q d (m s)"), ["s5_ph", "s5_ph2"], ["s5_pr0"], q="pool")
    P.dma(bim.rearrange("p (d x) -> p d x", d=2), din["s5_Bexp"].ap()[l, 1].rearrange("d q m s -> q d (m s)"), ["s5_ph", "s5_ph2"], ["s5_pr1"], q="pool")
    P.v("dve", "tensor_tensor", ["s5_ph", "s5_pr0"], ["s5_pr2"], t_a, cre, bre, ALU.mult)
    P.v("dve", "tensor_tensor", ["s5_ph2", "s5_pr1"], ["s5_re"], t_b, cim, bim, ALU.mult)
    P.v("dve", "tensor_tensor", ["s5_pr2", "s5_re"], ["s5_LB"], LB[0].rearrange("p d m s -> p (d m s)"), t_a, t_b, ALU.subtract)
    P.v("dve", "tensor_tensor", ["s5_ph", "s5_pr1"], ["s5_pr2"], t_a, cre, bim, ALU.mult)
    P.v("dve", "tensor_tensor", ["s5_ph2", "s5_pr0"], ["s5_re"], t_b, cim, bre, ALU.mult)
    P.v("dve", "tensor_tensor", ["s5_pr2", "s5_re"], ["s5_LB"], LB[1].rearrange("p d m s -> p (d m s)"), t_a, t_b, ALU.add)
    cst_ = im
    P.dma(cst_.rearrange("p (d x) -> p d x", d=2), din["s5_Cexp"].ap()[l, 0].rearrange("d q m s -> q d (m s)"), ["s5_im"], ["s5_im"], q="pool")
    P.act(LC[0].rearrange("p d m s -> p (d m s)"), cst_, AF.Copy, ["s5_im"], ["s5_LC"])
    cst2 = kf
    P.dma(cst2.rearrange("p (d x) -> p d x", d=2), din["s5_Cexp"].ap()[l, 1].rearrange("d q m s -> q d (m s)"), ["s5_kf"], ["s5_kf"], q="pool")
    P.act(LC[1].rearrange("p d m s -> p (d m s)"), cst2, AF.Copy, ["s5_kf"], ["s5_LC"], scale=-1.0)
    pP = P.alloc([128, 3, 16])
    P.dma(pP, din["s5_pT"].ap()[l].rearrange("i q n -> q i n"), [], ["s5_pP"])
    P.act(pP[:, 2, :], pP[:, 2, :], AF.Exp, ["s5_pP"], ["s5_pP"])
    P.v("dve", "tensor_tensor", ["s5_pP"], ["s5_rP"], rP, pP[:, 0, :], pP[:, 2, :], ALU.mult)
    P.act(rP, rP, AF.Exp, ["s5_rP"], ["s5_rP"])
    P.v("dve", "tensor_tensor", ["s5_pP"], ["s5_thP"], thP, pP[:, 1, :], pP[:, 2, :], ALU.mult)
    P.v("dve", "tensor_scalar", ["s5_thP"], ["s5_thP"], thP, thP, 1.0 / TWO_PI, None, ALU.mult)
    kiP = P.alloc([128, 16], I32); kfP = P.alloc([128, 16])
    frac_round(P, "dve", thP, kiP, kfP, ("s5_thP", "s5_kiP", "s5_kfP"))
    P.barrier()
    P.release(m1)
    ubf = P.alloc([128, 2, T], BF16)
    yacc = P.alloc([128, 2, T])
    iota = P.alloc([128, 1024])
    P.dma(iota, din["s5_iota"].ap(), [], ["s5_iota"])
    ust = Rot([P.alloc([128, 512]) for _ in range(2)], "s5_ust")
    for ut in range(2):
        for (c0, cn) in CH512:
            sv, sk = ust.next()
            P.dma(sv[:, 0:cn], pT[ut * 128:(ut + 1) * 128, c0:c0 + cn], ["pT_d"], [sk])
            P.act(ubf[:, ut, c0:c0 + cn], sv[:, 0:cn], AF.Copy, [sk], ["s5_ubf"])
    SEG = 1024
    NBUF = 2
    m_rot = P.mark()
    COS = Rot([P.alloc([128, SEG]) for _ in range(2)], "s5_cos")
    SIN = Rot([P.alloc([128, SEG]) for _ in range(2)], "s5_sin")
    phv = P.alloc([128, SEG]); kiv = P.alloc([128, SEG], I32); kfv = P.alloc([128, SEG])
    BUR = Rot([P.alloc([128, SEG]) for _ in range(NBUF)], "s5_bur")
    BUI = Rot([P.alloc([128, SEG]) for _ in range(NBUF)], "s5_bui")
    GR = Rot([P.alloc([128, SEG]) for _ in range(NBUF)], "s5_gr")
    GI = Rot([P.alloc([128, SEG]) for _ in range(NBUF)], "s5_gi")
    T1 = Rot([P.alloc([128, SEG]) for _ in range(NBUF)], "s5_t1")
    T1b = Rot([P.alloc([128, SEG]) for _ in range(1)], "s5_t1b")
    T2 = Rot([P.alloc([128, SEG]) for _ in range(NBUF)], "s5_t2")
    T2b = Rot([P.alloc([128, SEG]) for _ in range(1)], "s5_t2b")
    T1o = Rot([P.alloc([128, SEG]) for _ in range(1)], "s5_t1o")
    T2o = Rot([P.alloc([128, SEG]) for _ in range(1)], "s5_t2o")
    HR = Rot([P.alloc([128, SEG], BF16) for _ in range(NBUF)], "s5_hr")
    HI = Rot([P.alloc([128, SEG], BF16) for _ in range(NBUF)], "s5_hi")
    cn_ = P.alloc([128, 8])
    cni = P.alloc([128, 8], I32)
    cnf = P.alloc([128, 8])
    glast = P.alloc([128, 4])
    for d_ in range(2):
        seg_order = list(range(5)) if d_ == 0 else [0, 4, 3, 2, 1]
        for m in range(8):
            ut = m // 4
            col = d_ * 8 + m
            th = thP[:, col:col + 1]
            cosv, cosk = COS.next(); sinv, sink = SIN.next()
            P.act(phv, iota, AF.Identity, ["s5_iota", "s5_thP"], ["s5_phl"], scale=th)
            frac_round(P, "dve", phv, kiv, kfv, ("s5_phl", "s5_kil", "s5_kfl"))
            P.act(sinv, phv, AF.Sin, ["s5_phl"], [sink], scale=TWO_PI)
            P.act(phv, iota, AF.Identity, ["s5_iota", "s5_thP"], ["s5_phl"], scale=th, bias=0.25)
            frac_round(P, "dve", phv, kiv, kfv, ("s5_phl", "s5_kil", "s5_kfl"))
            P.act(cosv, phv, AF.Sin, ["s5_phl"], [cosk], scale=TWO_PI)
            P.v("dve", "tensor_scalar", ["s5_thP"], ["s5_cn"], cn_[:, 0:1], th, 256.0, None, ALU.mult)
            P.v("dve", "tensor_scalar", ["s5_thP"], ["s5_cn"], cn_[:, 2:3], th, 1024.0, None, ALU.mult)
            P.v("dve", "tensor_scalar", ["s5_cn"], ["s5_cn"], cn_[:, 1:2], cn_[:, 0:1], 0.25, None, ALU.add)
            P.v("dve", "tensor_scalar", ["s5_cn"], ["s5_cn"], cn_[:, 3:4], cn_[:, 2:3], 0.25, None, ALU.add)
            P.v("dve", "tensor_copy", ["s5_cn"], ["s5_cni"], cni[:, 0:4], cn_[:, 0:4])
            P.v("dve", "tensor_copy", ["s5_cni"], ["s5_cnf"], cnf[:, 0:4], cni[:, 0:4])
            P.v("dve", "tensor_tensor", ["s5_cn", "s5_cnf"], ["s5_cn"], cn_[:, 0:4], cn_[:, 0:4], cnf[:, 0:4], ALU.subtract)
            P.act(cn_[:, 4:8], cn_[:, 0:4], AF.Sin, ["s5_cn"], ["s5_cn"], scale=TWO_PI)
            def phaseA(si, sg, d_=d_, m=m, ut=ut, cosv=cosv, cosk=cosk, sinv=sinv, sink=sink):
                t0, n = S5_SEGS[sg]
                burv, burk = BUR.next(); buiv, buik = BUI.next()
                for c0 in range(0, n, 512):
                    cn = min(512, n - c0)
                    psr, pkr = psrot.next()
                    psi, pki = psrot.next()
                    P.mm(psr[:, 0:cn], LB[0][:, d_, m, :], ubf[:, ut, t0 + c0:t0 + c0 + cn], True, True, ["s5_LB", "s5_ubf"], [pkr])
                    P.mm(psi[:, 0:cn], LB[1][:, d_, m, :], ubf[:, ut, t0 + c0:t0 + c0 + cn], True, True, ["s5_LB", "s5_ubf"], [pki])
                    if d_ == 0:
                        j0 = c0
                        sr, si_ = psr[:, 0:cn], psi[:, 0:cn]
                    else:
                        j0 = n - c0 - cn
                        sr, si_ = rev_ap(psr[:, 0:cn]), rev_ap(psi[:, 0:cn])
                    P.act(burv[:, j0:j0 + cn], sr, AF.Copy, [pkr], [burk])
                    P.act(buiv[:, j0:j0 + cn], si_, AF.Copy, [pki], [buik])
                t1v, t1k = T1.next(); t1bv, t1bk = T1b.next(); t2v, t2k = T2.next(); t2bv, t2bk = T2b.next()
                ns = slice(0, n)
                P.v("dve", "tensor_tensor", [burk, cosk], [t1k], t1v[:, ns], burv[:, ns], cosv[:, ns], ALU.mult)
                P.v("dve", "tensor_tensor", [buik, sink], [t1bk], t1bv[:, ns], buiv[:, ns], sinv[:, ns], ALU.mult)
                P.v("dve", "tensor_tensor", [t1k, t1bk], [t1k], t1v[:, ns], t1v[:, ns], t1bv[:, ns], ALU.add)
                P.v("dve", "tensor_tensor", [buik, cosk], [t2k], t2v[:, ns], buiv[:, ns], cosv[:, ns], ALU.mult)
                P.v("dve", "tensor_tensor", [burk, sink], [t2bk], t2bv[:, ns], burv[:, ns], sinv[:, ns], ALU.mult)
                P.v("dve", "tensor_tensor", [t2k, t2bk], [t2k], t2v[:, ns], t2v[:, ns], t2bv[:, ns], ALU.subtract)
                return (t1v, t1k, t2v, t2k)

            def phaseB(si, sg, nprev, ares, d_=d_, m=m, ut=ut, col=col, cosv=cosv, cosk=cosk, sinv=sinv, sink=sink):
                t0, n = S5_SEGS[sg]
                ns = slice(0, n)
                a_t1v, a_t1k, a_t2v, a_t2k = ares
                if si == 0:
                    ini_r, ini_i = 0.0, 0.0
                else:
                    sc_, cc_ = (cn_[:, 4:5], cn_[:, 5:6]) if nprev == 256 else (cn_[:, 6:7], cn_[:, 7:8])
                    P.v("dve", "tensor_scalar", ["s5_glast", "s5_cn"], ["s5_glr"], glast[:, 2:3], glast[:, 1:2], sc_, None, ALU.mult)
                    P.v("dve", "tensor_scalar", ["s5_glast", "s5_cn"], ["s5_glr"], glast[:, 3:4], glast[:, 0:1], sc_, None, ALU.mult)
                    P.v("dve", "scalar_tensor_tensor", ["s5_glast", "s5_cn", "s5_glr"], ["s5_glr"], glast[:, 2:3], glast[:, 0:1], cc_, glast[:, 2:3],
                        ALU.mult, ALU.subtract)
                    P.v("dve", "scalar_tensor_tensor", ["s5_glast", "s5_cn", "s5_glr"], ["s5_glr"], glast[:, 3:4], glast[:, 1:2], cc_, glast[:, 3:4],
                        ALU.mult, ALU.add)
                    ini_r, ini_i = glast[:, 2:3], glast[:, 3:4]
                grv, grk = GR.next(); giv, gik = GI.next()
                rb = rP[:, col:col + 1].to_broadcast([128, n])
                P.v("dve", "tensor_tensor_scan", [a_t1k, "s5_rP", "s5_glr"], [grk], grv[:, ns], rb, a_t1v[:, ns], ini_r, ALU.mult, ALU.add)
                P.v("dve", "tensor_tensor_scan", [a_t2k, "s5_rP", "s5_glr"], [gik], giv[:, ns], rb, a_t2v[:, ns], ini_i, ALU.mult, ALU.add)
                P.v("dve", "tensor_copy", [grk], ["s5_glast"], glast[:, 0:1], grv[:, n - 1:n])
                P.v("dve", "tensor_copy", [gik], ["s5_glast"], glast[:, 1:2], giv[:, n - 1:n])
                hrv, hrk = HR.next(); hiv, hik = HI.next()
                t1v, t1k = T1o.next(); t1bv, t1bk = T1b.next(); t2v, t2k = T2o.next(); t2bv, t2bk = T2b.next()
                ho_r = hrv[:, ns] if d_ == 0 else rev_ap(hrv[:, ns])
                ho_i = hiv[:, ns] if d_ == 0 else rev_ap(hiv[:, ns])
                P.v("dve", "tensor_tensor", [grk, cosk], [t1k], t1v[:, ns], grv[:, ns], cosv[:, ns], ALU.mult)
                P.v("dve", "tensor_tensor", [gik, sink], [t1bk], t1bv[:, ns], giv[:, ns], sinv[:, ns], ALU.mult)
                P.v("dve", "tensor_tensor", [t1k, t1bk], [hrk], ho_r, t1v[:, ns], t1bv[:, ns], ALU.subtract)
                P.v("dve", "tensor_tensor", [grk, sink], [t2k], t2v[:, ns], grv[:, ns], sinv[:, ns], ALU.mult)
                P.v("dve", "tensor_tensor", [gik, cosk], [t2bk], t2bv[:, ns], giv[:, ns], cosv[:, ns], ALU.mult)
                P.v("dve", "tensor_tensor", [t2k, t2bk], [hik], ho_i, t2v[:, ns], t2bv[:, ns], ALU.add)
                for c0 in range(0, n, 512):
                    cn = min(512, n - c0)
                    ps, pk = psrot.next()
                    P.mm(ps[:, 0:cn], LC[0][:, d_, m, :], hrv[:, c0:c0 + cn], True, False, ["s5_LC", hrk], [pk])
                    P.mm(ps[:, 0:cn], LC[1][:, d_, m, :], hiv[:, c0:c0 + cn], False, True, ["s5_LC", hik], [pk])
                    ya = yacc[:, ut, t0 + c0:t0 + c0 + cn]
                    if d_ == 0 and m % 4 == 0:
                        P.act(ya, ps[:, 0:cn], AF.Copy, [pk], ["s5_yacc%d" % ut])
                    else:
                        P.v("dve", "tensor_tensor", [pk, "s5_yacc%d" % ut], ["s5_yacc%d" % ut], ya, ps[:, 0:cn], ya, ALU.add)

            ares = {0: phaseA(0, seg_order[0])}
            nprev = 0
            for si, sg in enumerate(seg_order):
                if si + 1 < len(seg_order):
                    ares[si + 1] = phaseA(si + 1, seg_order[si + 1])
                phaseB(si, sg, nprev, ares.pop(si))
                nprev = S5_SEGS[sg][1]
    P.barrier()
    P.release(m_rot)
    m2 = P.mark()
    gw_st = P.alloc([128, 2, 256]); gw = P.alloc([128, 2, 256], BF16)
    P.dma(gw_st, din["s5_gw"].ap()[l].rearrange("(k p) n -> p k n", p=128), [], ["s5_gwst"])
    P.v("pool", "tensor_copy", ["s5_gwst"], ["s5_gw"], gw, gw_st)
    dT = P.alloc([128, 2]); gbT = P.alloc([128, 2])
    P.dma(dT, din["s5_dT"].ap()[l], [], ["s5_dT"])
    P.dma(gbT, din["s5_gbT"].ap()[l], [], ["s5_gbT"])
    yg = Rot([P.alloc([128, 2, 512]) for _ in range(2)], "s5_yg")
    ygb = Rot([P.alloc([128, 2, 512], BF16) for _ in range(2)], "s5_ygb")
    sg_ = Rot([P.alloc([128, 512]) for _ in range(2)], "s5_sg")
    ob = Rot([P.alloc([128, 512], BF16) for _ in range(2)], "s5_ob")
    for (c0, cn) in CH512:
        if last and c0 + cn <= CTX:
            continue
        ygv, ygk = yg.next(); ybv, ybk = ygb.next()
        for ut in range(2):
            sv, sk = ust.next()
            P.dma(sv[:, 0:cn], pT[ut * 128:(ut + 1) * 128, c0:c0 + cn], ["pT_d"], [sk])
            P.v("dve", "scalar_tensor_tensor", [sk, "s5_dT", "s5_yacc%d" % ut], [ygk], ygv[:, ut, 0:cn], sv[:, 0:cn], dT[:, ut:ut + 1],
                yacc[:, ut, c0:c0 + cn], ALU.mult, ALU.add)
        P.act(ygv[:, :, 0:cn], ygv[:, :, 0:cn], AF.Gelu_apprx_tanh, [ygk], [ygk])
        P.v("pool", "tensor_copy", [ygk], [ybk], ybv[:, :, 0:cn], ygv[:, :, 0:cn])
        for uo in range(2):
            ps, pk = psrot.next()
            for k in range(2):
                P.mm(ps[:, 0:cn], gw[:, k, uo * 128:(uo + 1) * 128], ybv[:, k, 0:cn], k == 0, k == 1, ["s5_gw", ybk], [pk])
            sgv, sgk = sg_.next()
            P.act(sgv[:, 0:cn], ps[:, 0:cn], AF.Sigmoid, [pk, "s5_gbT"], [sgk], bias=gbT[:, uo:uo + 1])
            ov, ok = ob.next()
            P.v("dve", "tensor_tensor", [sgk, ygk], [ok], ov[:, 0:cn], sgv[:, 0:cn], ygv[:, uo, 0:cn], ALU.mult)
            P.dma(C.oT_d.ap()[uo * 128:(uo + 1) * 128, c0:c0 + cn], ov[:, 0:cn], [ok], ["oT_d"], q="pool")
    P.barrier()
    P.release(m0)


MIXERS["s5"] = mix_s5
HOSTPREP["s5"] = host_s5


TWO_PI = 2.0 * math.pi
NF = 8192
OT_HY = 256
K1B = [(i * 8, 8) for i in range(8)] + [(64, 1)]
CTX_K1 = [0, 16, 32, 48, 64]
K1LIST = [list(range(65)), CTX_K1]


def _hy_feat(pos, nt):
    f32 = np.float32
    pos = pos.astype(f32)
    t01 = pos / f32(nt - 1)
    bands = np.linspace(1e-4, 15, 16, dtype=f32)
    ang = (f32(2.0 * math.pi / nt) * pos[:, None] * bands[None, :]).astype(f32)
    z = np.concatenate([t01[:, None], np.cos(ang), -np.sin(ang)], axis=-1).astype(f32)
    return z, t01


def host_hy(inp):
    L = DEPTH
    f32 = np.float32
    bf = ml_dtypes.bfloat16
    o = {}
    n = np.arange(NF)
    zT = np.zeros((2, 33, NF), f32)
    nt01 = np.zeros((2, 128, 64), f32)
    msk = np.zeros((2, 128, 64), f32)
    for kind, nt in enumerate((SEQ, CTX)):
        pos = np.zeros(NF, np.int64)
        m = np.zeros(NF, f32)
        pos[:nt] = n[:nt]; m[:nt] = 1
        hi = n > NF - nt
        pos[hi] = NF - n[hi]; m[hi] = 1
        pos[NF // 2] = 0; m[NF // 2] = 1
        z, t01 = _hy_feat(pos, nt)
        zT[kind] = z.T
        nt01[kind] = (-t01).reshape(128, 64)
        msk[kind] = m.reshape(128, 64)
    o["hy_zT"] = zT
    o["hy_nt01"] = nt01
    o["hy_msk"] = msk
    n1 = np.arange(128)[:, None]; k1 = np.arange(65)[None, :]
    ang = 2 * np.pi * n1 * k1 / 128
    o["hy_D1"] = np.concatenate([np.cos(ang), -np.sin(ang)], axis=1).astype(bf)
    n2 = np.arange(64); k2 = np.arange(64)
    Tm = np.zeros((65, 128, 3, 128), np.float64)
    for kk in range(65):
        w = np.exp(-2j * np.pi * (n2[:, None] * kk / NF + n2[:, None] * k2[None, :] / 64))
        for c2 in range(2):
            Tm[kk, c2::2, 0, c2::2] = w.real
            Tm[kk, c2::2, 1, c2::2] = w.imag
            Tm[kk, c2::2, 2, c2::2] = -w.imag
    o["hy_T"] = Tm.astype(bf)
    t2 = np.arange(64)
    w = np.exp(2j * np.pi * k2[:, None] * t2[None, :] / 64)
    RA = np.zeros((128, 2, 64, 2)); RB = np.zeros((128, 2, 64, 2))
    for c2 in range(2):
        RA[c2::2, 0, :, c2] = w.real; RA[c2::2, 1, :, c2] = w.imag
        RB[c2::2, 0, :, c2] = -w.imag; RB[c2::2, 1, :, c2] = w.real
    o["hy_R"] = np.stack([RA.reshape(128, 256), RB.reshape(128, 256)], axis=1).astype(bf)
    k1v = np.arange(65); t1 = np.arange(64)
    wgt = np.full(65, 2.0); wgt[0] = 1; wgt[64] = 1
    V = np.zeros((65, 64, 2, 64))
    for tt in range(64):
        w = (wgt[:, None] / NF) * np.exp(2j * np.pi * (tt * k1v[:, None] / NF + t1[None, :] * k1v[:, None] / 128))
        V[:, tt, 0, :] = w.real; V[:, tt, 1, :] = -w.imag
    o["hy_V"] = V.astype(bf)
    o["hy_Vc"] = np.ascontiguousarray(V[CTX_K1][:, :, :, 0:4]).astype(bf)
    o["hy_w1"] = np.ascontiguousarray(inp["hy_w1"])
    o["hy_w2"] = np.ascontiguousarray(inp["hy_w2"])
    o["hy_cols"] = np.ascontiguousarray(np.stack([inp["hy_b1"], inp["hy_b2"], inp["hy_freq"]], axis=-1))
    w3 = inp["hy_w3"].reshape(L, 64, 2, 2, 256)
    o["hy_w3"] = np.ascontiguousarray(w3.transpose(0, 1, 3, 2, 4))
    dl = inp["hy_deltas"].reshape(L, 2, 2, 256)
    o["hy_dl"] = np.ascontiguousarray(dl.transpose(0, 2, 1, 3).reshape(L, 2, 1, 512))
    cw = np.concatenate([inp["hy_conv_w"], inp["hy_conv_b"][:, None, :]], axis=1)
    o["hy_cw"] = np.ascontiguousarray(cw.reshape(L, 1, 4 * 768))
    o["hy_bias"] = np.ascontiguousarray(inp["hy_bias"].reshape(L, 1, 512))
    return o


def mix_hy(C, l, last):
    P, B = C.P, C.B
    psrot = C.psrot
    if "hy_zT" not in B.din:
        B.inp("hy_zT", [2, 33, NF]); B.inp("hy_nt01", [2, 128, 64]); B.inp("hy_msk", [2, 128, 64])
        B.inp("hy_D1", [128, 130], BF16); B.inp("hy_T", [65, 128, 3, 128], BF16)
        B.inp("hy_R", [128, 2, 256], BF16); B.inp("hy_V", [65, 64, 2, 64], BF16); B.inp("hy_Vc", [5, 64, 2, 4], BF16)
        B.inp("hy_w1", [DEPTH, 33, 64]); B.inp("hy_w2", [DEPTH, 64, 64]); B.inp("hy_cols", [DEPTH, 64, 3])
        B.inp("hy_w3", [DEPTH, 64, 2, 2, 256]); B.inp("hy_dl", [DEPTH, 2, 1, 512])
        B.inp("hy_cw", [DEPTH, 1, 3072]); B.inp("hy_bias", [DEPTH, 1, 512])
        B.scr("hspec_d", [2, 2, 2, 2, 128, 65 * 64])
    din = B.din
    hspec = B.dscr["hspec_d"]
    kinds = [0] if last else [0, 1]
    m0 = P.mark()
    D1 = P.alloc([128, 130], BF16)
    P.dma(D1, din["hy_D1"].ap(), [], ["hy_D1"])
    Rm = P.alloc([128, 2, 256], BF16)
    P.dma(Rm, din["hy_R"].ap(), [], ["hy_R"])
    Vm = P.alloc([65, 64, 2, 64], BF16)
    P.dma(Vm, din["hy_V"].ap(), [], ["hy_V"])
    Vmc = P.alloc([5, 64, 2, 4], BF16)
    P.dma(Vmc, din["hy_Vc"].ap(), [], ["hy_V"])
    Trot = Rot([P.alloc([128, 3, 128], BF16) for _ in range(4)], "hy_T")
    B1 = P.alloc([128, 64, 130], BF16)
    mfwd = P.mark()

    def fwd(xv, Kk, xkey, consume, k1list):
        for q0 in range(0, 64, 3):
            qn = min(3, 64 - q0)
            ps, pk = psrot.next()
            for j in range(qn):
                q = q0 + j
                P.mm(ps[:, j * 130:(j + 1) * 130], xv[:, q, :, :].rearrange("p n c -> p (n c)"), D1[0:Kk, :], True, True, [xkey, "hy_D1"], [pk])
            src = ps[:, 0:qn * 130].rearrange("p (a b) -> p a b", a=qn)
            if (q0 // 3) % 2 == 0:
                P.act(B1[:, q0:q0 + qn, :], src, AF.Copy, [pk], ["hy_B1"])
            else:
                P.v("dve", "tensor_copy", [pk], ["hy_B1"], B1[:, q0:q0 + qn, :], src)
        for bi, k0 in enumerate(range(0, len(k1list), 8)):
            kn = min(8, len(k1list) - k0)
            psr, pkr = psrot.next()
            psi, pki = psrot.next()
            for j in range(kn):
                k1 = k1list[k0 + j]
                tv, tk = Trot.next()
                P.dma(tv, din["hy_T"].ap()[k1], [], [tk])
                br = B1[:, :, k1]
                bi_ = B1[:, :, 65 + k1]
                P.mm(psr[:, j * 64:(j + 1) * 64], tv[:, 0, :], br, True, False, [tk, "hy_B1"], [pkr])
                P.mm(psr[:, j * 64:(j + 1) * 64], tv[:, 2, :], bi_, False, True, [tk, "hy_B1"], [pkr])
                P.mm(psi[:, j * 64:(j + 1) * 64], tv[:, 1, :], br, True, False, [tk, "hy_B1"], [pki])
                P.mm(psi[:, j * 64:(j + 1) * 64], tv[:, 0, :], bi_, False, True, [tk, "hy_B1"], [pki])
            consume(bi, k0, kn, psr, pkr, psi, pki)

    w1 = P.alloc([33, 64]); w2 = P.alloc([64, 64]); cols = P.alloc([64, 3])
    P.dma(w1, din["hy_w1"].ap()[l], [], ["hy_w1"])
    P.dma(w2, din["hy_w2"].ap()[l], [], ["hy_w2"])
    P.dma(cols, din["hy_cols"].ap()[l], [], ["hy_cols"])
    sc = P.alloc([64, 4])
    P.v("dve", "tensor_scalar", ["hy_cols"], ["hy_sc"], sc[:, 0:1], cols[:, 2:3], 1.0 / TWO_PI, None, ALU.mult)
    P.v("dve", "tensor_tensor", ["hy_cols", "hy_sc"], ["hy_sc"], sc[:, 1:2], cols[:, 0:1], sc[:, 0:1], ALU.mult)
    P.v("dve", "tensor_tensor", ["hy_cols", "hy_sc"], ["hy_sc"], sc[:, 2:3], cols[:, 1:2], sc[:, 0:1], ALU.mult)
    w3 = P.alloc([64, 2, 2, 256])
    P.dma(w3, din["hy_w3"].ap()[l], [], ["hy_w3"])
    absd = P.alloc([128, 2, 256])
    for d_ in range(2):
        P.dma(absd[d_ * 64:(d_ + 1) * 64].rearrange("p o c -> p (o c)"),
              din["hy_dl"].ap()[l, d_, 0:1, :].partition_broadcast(64)[:, 0, :], [], ["hy_absd"])
    P.act(absd, absd, AF.Abs, ["hy_absd"], ["hy_absd"])
    mflt = P.mark()
    hid2 = P.alloc([64, NF])
    zrot = Rot([P.alloc([33, 512]) for _ in range(2)], "hy_z")
    phb = P.alloc([64, 512]); kib = P.alloc([64, 512], I32); kfb = P.alloc([64, 512]); h1b = P.alloc([64, 512])
    nt01 = P.alloc([128, 64]); msk = P.alloc([128, 64])
    krot = Rot([P.alloc([128, 256]) for _ in range(4)], "hy_kr")
    xk = P.alloc([128, 128, 64, 2], BF16)
    dec = Rot([P.alloc([128, 256]) for _ in range(3)], "hy_dec")
    sqr = Rot([P.alloc([128, 256]) for _ in range(4)], "hy_sq")
    ssa = P.alloc([128, 256]); rs = P.alloc([128, 256])
    yst = Rot([P.alloc([128, 2, 512]) for _ in range(2)], "hy_yst")

    def sin_layer(ps, pk, n, bcol, outv, okey):
        P.act(phb[:, 0:n], ps[0:64, 0:n], AF.Identity, [pk, "hy_sc"], ["hy_ph"], scale=sc[:, 0:1], bias=sc[:, bcol:bcol + 1])
        P.v("dve", "tensor_copy", ["hy_ph"], ["hy_ki"], kib[:, 0:n], phb[:, 0:n])
        P.v("dve", "tensor_copy", ["hy_ki"], ["hy_kf"], kfb[:, 0:n], kib[:, 0:n])
        P.v("dve", "tensor_tensor", ["hy_ph", "hy_kf"], ["hy_ph"], phb[:, 0:n], phb[:, 0:n], kfb[:, 0:n], ALU.subtract)
        P.act(outv, phb[:, 0:n], AF.Sin, ["hy_ph"], [okey], scale=TWO_PI)

    for kind in kinds:
        P.dma(nt01, din["hy_nt01"].ap()[kind], ["hy_nt01r"], ["hy_nt01"])
        P.dma(msk, din["hy_msk"].ap()[kind], ["hy_mskr"], ["hy_msk"])
        for ch in (range(NF // 512) if kind == 0 else (0, 8, 15)):
            zv, zk = zrot.next()
            P.dma(zv, din["hy_zT"].ap()[kind, :, ch * 512:(ch + 1) * 512], [], [zk])
            ps, pk = psrot.next()
            P.mm(ps[0:64, :], w1, zv, True, True, ["hy_w1", zk], [pk])
            sin_layer(ps, pk, 512, 1, h1b, "hy_h1")
            ps2, pk2 = psrot.next()
            P.mm(ps2[0:64, :], w2, h1b, True, True, ["hy_w2", "hy_h1"], [pk2])
            sin_layer(ps2, pk2, 512, 2, hid2[:, ch * 512:(ch + 1) * 512], "hy_hid2")
        for o_ in range(2):
            pss_, pks_ = C.pss[7][:], "ps7"
            prot7 = Rot([p[:] for p in C.pss[0:7]], "ps")
            LOOKF = 2
            pendf = {}

            def issue_k(n2, o_=o_, kind=kind):
                psf, pkf = prot7.next()
                psb, pkb = prot7.next()
                lh = hid2[:, n2:NF:64]
                P.mm(psf[:, 0:256], lh, w3[:, 0, o_, :], True, True, ["hy_hid2", "hy_w3"], [pkf])
                P.mm(psb[:, 0:256], lh, w3[:, 1, o_, :], True, True, ["hy_hid2", "hy_w3"], [pkb])
                dv, dk = dec.next()
                P.act(dv, absd[:, o_, :], AF.Exp, ["hy_absd", "hy_nt01"], [dk], scale=nt01[:, n2:n2 + 1])
                kv, kk_ = krot.next()
                P.v("dve", "scalar_tensor_tensor", [pkf, "hy_msk", dk], [kk_], kv[0:64, :], psf[0:64, 0:256],
                    msk[0:64, n2:n2 + 1], dv[0:64, :], ALU.mult, ALU.mult)
                P.v("dve", "scalar_tensor_tensor", [pkb, "hy_msk", dk], [kk_], kv[64:128, :], psb[64:128, 0:256],
                    msk[64:128, n2:n2 + 1], dv[64:128, :], ALU.mult, ALU.mult)
                sv, sk = sqr.next()
                P.act(sv[:, 0:256], kv, AF.Square, [kk_], [sk])
                P.act(xk[:, :, n2, :], kv.rearrange("p (q c) -> p q c", c=2), AF.Copy, [kk_], ["hy_xk"])
                pendf[n2] = (sv, sk)

            for n2 in range(LOOKF):
                issue_k(n2)
            for n2 in range(64):
                if n2 + LOOKF < 64:
                    issue_k(n2 + LOOKF)
                sv, sk = pendf.pop(n2)
                P.mm(pss_[:, 0:256], C.ones_f[:, :], sv[:, 0:256], n2 == 0, n2 == 63, [sk, "ones_f"], [pks_])
            P.act(rs, pss_[:, 0:256], AF.Sqrt, [pks_, "eps_t"], ["hy_rs"], bias=C.eps_t[:], scale=1.0)
            P.v("dve", "reciprocal", ["hy_rs"], ["hy_rs"], rs, rs)
            rsb = rs.rearrange("p (q c) -> p q c", c=2).unsqueeze(2).to_broadcast([128, 128, 16, 2])
            for g4 in range(4):
                xs_ = xk[:, :, g4 * 16:(g4 + 1) * 16, :]
                P.v("dve", "tensor_tensor", ["hy_xk", "hy_rs"], ["hy_xk"], xs_, xs_, rsb, ALU.mult)
            P.v("dve", "memset", [], ["hy_xk"], xk[64:65, :, 0, :], 0.0)
            if C.cfg.get("hy_dbg") and kind == 0 and o_ == 0:
                dbg1 = B.scr("hy_dbg_hid2", [64, NF])
                dbg2 = B.scr("hy_dbg_xk", [128, 128 * 64 * 2], BF16)
                dbg3 = B.scr("hy_dbg_rs", [128, 256])
                P.dma(dbg1.ap(), hid2, ["hy_hid2"], ["dbg1"], q="pool")
                P.dma(dbg2.ap(), xk.rearrange("p a b c -> p (a b c)"), ["hy_xk"], ["dbg2"], q="pool")
                P.dma(dbg3.ap(), rs, ["hy_rs"], ["dbg3"], q="pool")
            for hf in range(2):
                def consume(bi, k0, kn, psr, pkr, psi, pki, kind=kind, o_=o_, hf=hf):
                    yv, yk = yst.next()
                    fs = 1.0 if kind == 0 else 16.0
                    P.act(yv[:, 0, 0:kn * 64], psr[:, 0:kn * 64], AF.Copy, [pkr], [yk], scale=fs)
                    P.v("dve", "tensor_scalar", [pki], [yk], yv[:, 1, 0:kn * 64], psi[:, 0:kn * 64], fs, None, ALU.mult)
                    for ri in range(2):
                        P.dma(hspec.ap()[kind, o_, hf, ri, :, k0 * 64:(k0 + kn) * 64], yv[:, ri, 0:kn * 64], [yk], ["hspec_d"], q="pool")
                fwd(xk[:, hf * 64:(hf + 1) * 64, :, :], 128, "hy_xk", consume, K1LIST[kind])
    P.barrier()
    P.release(mflt)
    if C.cfg.get("hy_stop", 9) <= 1:
        P.release(m0); return
    cwr = P.alloc([64, 4, 768])
    P.dma(cwr.rearrange("p a c -> p (a c)"), din["hy_cw"].ap()[l, 0:1, :].partition_broadcast(64)[:, 0, :], [], ["hy_cw"])
    hbr = P.alloc([64, 2, 256])
    P.dma(hbr.rearrange("p a c -> p (a c)"), din["hy_bias"].ap()[l, 0:1, :].partition_broadcast(64)[:, 0, :], [], ["hy_hb"])
    xv = P.alloc([64, 64, 64, 2], BF16)
    z1 = P.alloc([64, 64, 64, 2], BF16)
    Zr = P.alloc([128, 65, 64], BF16); Zi = P.alloc([128, 65, 64], BF16)
    G = P.alloc([65, 2, 64, 128], BF16)
    oTs = P.alloc([128, SEQ], BF16)
    dwl = Rot([P.alloc([64, 3, 4, 128]) for _ in range(2)], "hy_dwl")
    dwa = Rot([P.alloc([64, 4, 128]) for _ in range(3)], "hy_dwa")
    dwt = P.alloc([64, 4, 128])
    hrot = Rot([P.alloc([128, 2, 512]) for _ in range(2)], "hy_h")
    tt1 = P.alloc([128, 512]); tt2 = P.alloc([128, 512])
    gt = P.alloc([64, 4, 128]); z2f = P.alloc([64, 4, 128])

    def blkv(x, M, b):
        return x[0:M, :, 4 * b:4 * b + 4, :].rearrange("p q n c -> p n q c")

    def dwconv_blk(kind, set_, hf, b):
        M = 64 if kind == 0 else 4
        base = LAT0 if kind == 0 else CTX0
        ntok = SEQ if kind == 0 else CTX
        c0 = set_ * 256 + hf * 128
        lv, lk = dwl.next()
        for s in range(3):
            src = C.hyp_d.ap()[base + s - 1:base + s - 1 + ntok, c0:c0 + 128].rearrange("(a b) c -> a b c", b=64)[:, 4 * b:4 * b + 4, :]
            P.dma(lv[0:M, s, :, :], src, ["hyp_d"], [lk])
        av, ak = dwa.next()
        wv = lambda tap: cwr[0:M, tap, c0:c0 + 128].unsqueeze(1).to_broadcast([M, 4, 128])
        P.v("dve", "tensor_tensor", [lk, "hy_cw"], [ak], av[0:M], lv[0:M, 1], wv(1), ALU.mult)
        P.v("dve", "tensor_tensor", [lk, "hy_cw"], ["hy_dwt"], dwt[0:M], lv[0:M, 0], wv(0), ALU.mult)
        P.v("dve", "tensor_tensor", [ak, "hy_dwt"], [ak], av[0:M], av[0:M], dwt[0:M], ALU.add)
        P.v("dve", "tensor_tensor", [lk, "hy_cw"], ["hy_dwt"], dwt[0:M], lv[0:M, 2], wv(2), ALU.mult)
        P.v("dve", "tensor_tensor", [ak, "hy_dwt"], [ak], av[0:M], av[0:M], dwt[0:M], ALU.add)
        P.v("dve", "tensor_tensor", [ak, "hy_cw"], [ak], av[0:M], av[0:M], wv(3), ALU.add)
        return av, ak

    def inverse(kind, evac):
        M = 64 if kind == 0 else 4
        nk = len(K1LIST[kind])
        Vsel = Vm if kind == 0 else Vmc
        for q0 in range(0, 64, 2):
            ps, pk = psrot.next()
            for j in range(2):
                q = q0 + j
                P.mm(ps[0:nk, j * 256:(j + 1) * 256], Zr[:, 0:nk, q], Rm[:, 0, :], True, False, ["hy_Z", "hy_R"], [pk])
                P.mm(ps[0:nk, j * 256:(j + 1) * 256], Zi[:, 0:nk, q], Rm[:, 1, :], False, True, ["hy_Z", "hy_R"], [pk])
            for j in range(2):
                q = q0 + j
                src = ps[0:nk, j * 256:(j + 1) * 256].rearrange("p (r t c) -> p r t c", r=2, t=64)
                if j == 0:
                    P.act(G[0:nk, :, :, 2 * q:2 * q + 2], src, AF.Copy, [pk], ["hy_G"])
                else:
                    P.v("dve", "tensor_copy", [pk], ["hy_G"], G[0:nk, :, :, 2 * q:2 * q + 2], src)
        for b in range(16):
            ps, pk = psrot.next()
            for j in range(4):
                t2 = 4 * b + j
                P.mm(ps[0:M, j * 128:(j + 1) * 128], Vsel[0:nk, t2, 0, 0:M], G[0:nk, 0, t2, :], True, False, ["hy_V", "hy_G"], [pk])
                P.mm(ps[0:M, j * 128:(j + 1) * 128], Vsel[0:nk, t2, 1, 0:M], G[0:nk, 1, t2, :], False, True, ["hy_V", "hy_G"], [pk])
            evac(b, ps, pk, M)

    for kind in kinds:
        M = 64 if kind == 0 else 4
        ntok = SEQ if kind == 0 else CTX
        tok0 = CTX if kind == 0 else 0
        for hf in range(2):
            if kind == 1:
                P.v("dve", "memset", [], ["hy_xv"], xv, 0.0)
                P.v("pool", "memset", [], ["hy_z1"], z1, 0.0)
            for b in range(16):
                av, ak = dwconv_blk(kind, 2, hf, b)
                P.act(blkv(xv, M, b), av[0:M].rearrange("p n (q c) -> p n q c", c=2), AF.Copy, [ak], ["hy_xv"])
            for o_ in range(2):
                src_x = xv if o_ == 0 else z1
                src_k = "hy_xv" if o_ == 0 else "hy_z1"

                def consume(bi, k0, kn, psr, pkr, psi, pki, kind=kind, o_=o_, hf=hf):
                    hv, hk = hrot.next()
                    for ri in range(2):
                        P.dma(hv[:, ri, 0:kn * 64], hspec.ap()[kind, o_, hf, ri, :, k0 * 64:(k0 + kn) * 64], ["hspec_d"], [hk])
                    w = kn * 64
                    zr = Zr[:, k0:k0 + kn, :].rearrange("p a b -> p (a b)")
                    zi = Zi[:, k0:k0 + kn, :].rearrange("p a b -> p (a b)")
                    P.v("dve", "tensor_tensor", [pkr, hk], ["hy_tt1"], tt1[:, 0:w], psr[:, 0:w], hv[:, 0, 0:w], ALU.mult)
                    P.v("dve", "tensor_tensor", [pki, hk], ["hy_tt2"], tt2[:, 0:w], psi[:, 0:w], hv[:, 1, 0:w], ALU.mult)
                    P.v("dve", "tensor_tensor", ["hy_tt1", "hy_tt2"], ["hy_Z"], zr, tt1[:, 0:w], tt2[:, 0:w], ALU.subtract)
                    P.v("dve", "tensor_tensor", [pkr, hk], ["hy_tt1"], tt1[:, 0:w], psr[:, 0:w], hv[:, 1, 0:w], ALU.mult)
                    P.v("dve", "tensor_tensor", [pki, hk], ["hy_tt2"], tt2[:, 0:w], psi[:, 0:w], hv[:, 0, 0:w], ALU.mult)
                    P.v("dve", "tensor_tensor", ["hy_tt1", "hy_tt2"], ["hy_Z"], zi, tt1[:, 0:w], tt2[:, 0:w], ALU.add)
                fwd(src_x, 64, src_k, consume, K1LIST[kind])

                def evac(b, ps, pk, M, kind=kind, o_=o_, hf=hf, src_x=src_x, src_k=src_k):
                    xg, xgk = dwconv_blk(kind, o_, hf, b)
                    yv = ps[0:M, :].rearrange("p (a c) -> p a c", a=4)
                    brow = hbr[0:M, o_, hf * 128:(hf + 1) * 128].unsqueeze(1).to_broadcast([M, 4, 128])
                    P.v("dve", "tensor_tensor", [src_k, "hy_hb"], ["hy_gt"], gt[0:M].rearrange("p n (q c) -> p n q c", c=2), blkv(src_x, M, b), brow.rearrange("p n (q c) -> p n q c", c=2), ALU.mult)
                    P.v("dve", "tensor_tensor", [pk, "hy_gt"], ["hy_gt"], gt[0:M], yv, gt[0:M], ALU.add)
                    if o_ == 0:
                        P.v("dve", "tensor_tensor", ["hy_gt", xgk], ["hy_z1"], blkv(z1, M, b), gt[0:M].rearrange("p n (q c) -> p n q c", c=2), xg[0:M].rearrange("p n (q c) -> p n q c", c=2), ALU.mult)
                    else:
                        P.v("dve", "tensor_tensor", ["hy_gt", xgk], ["hy_z2f"], z2f[0:M], gt[0:M], xg[0:M], ALU.mult)
                        pt, ptk = psrot.next()
                        for j in range(4):
                            P.tr(pt[:, j * 64:j * 64 + M], z2f[0:M, j, :], C.ident[0:M, 0:M], ["hy_z2f", "ident"], [ptk])
                        dst = oTs[:, 0:ntok].rearrange("p (a b) -> p b a", b=64)[:, 4 * b:4 * b + 4, :]
                        srcp = pt[:, 0:256].rearrange("p (j a) -> p j a", j=4)[:, :, 0:M]
                        P.act(dst, srcp, AF.Copy, [ptk], ["hy_oTs"])
                inverse(kind, evac)
            P.dma(C.oT_d.ap()[OT_HY + hf * 128:OT_HY + (hf + 1) * 128, tok0:tok0 + ntok], oTs[:, 0:ntok], ["hy_oTs"], ["oT_d"], q="pool")
    P.barrier()
    P.release(m0)


MIXERS["hy"] = mix_hy
HOSTPREP["hy"] = host_hy
```

```python
import math
import numpy as np
import ml_dtypes
import concourse.bass as bass
import concourse.mybir as mybir
from concourse.bass_utils import run_bass_kernel_spmd
from contextlib import ExitStack
from types import SimpleNamespace

F32 = mybir.dt.float32
BF16 = mybir.dt.bfloat16
I32 = mybir.dt.int32
ALU = mybir.AluOpType
AF = mybir.ActivationFunctionType

ENGS = ("pe", "act", "dve", "pool", "sp")
import os as _os
NO_SELF_SYNC = set(_os.environ.get("NO_SELF_SYNC", "").split(",")) - {""}
NSLOT = 6

D = 1024
NB = 4
SEQ = 4096
CTX = 256
T = SEQ + CTX
DEPTH = 2
EPS = 1e-6
WG = 256
DFF = 2816
NTT = T // 128
W_PAD = T + 4
CTX0 = 1
LAT0 = 259
NFM = 2048
NTM = 1536
CH512 = [(i * 512, min(512, T - i * 512)) for i in range((T + 511) // 512)]


class Prog:
    def __init__(self, nc):
        self.nc = nc
        self.ops = []
        self.es = ExitStack()
        self.arena = None
        self.aoff = 0

    def sb(self, name, shape, dtype=F32):
        return self.es.enter_context(self.nc.sbuf_tensor(name, list(shape), dtype))

    def ps(self, name, shape, dtype=F32):
        return self.es.enter_context(self.nc.psum_tensor(name, list(shape), dtype))

    def dram(self, name, shape, dtype=F32, kind="Internal"):
        return self.nc.dram_tensor(name, list(shape), dtype, kind=kind)

    def init_arena(self, words):
        self.arena = self.sb("arena", [128, words], F32)
        self.awords = words
        self.aoff = 0

    def mark(self):
        return self.aoff

    def release(self, m):
        self.aoff = m

    def alloc(self, shape, dtype=F32):
        npart = shape[0]
        n = int(np.prod(shape[1:]))
        if dtype == F32 or dtype == I32:
            words = n
        else:
            words = (n + 1) // 2
        words = (words + 7) // 8 * 8
        off = self.aoff
        self.aoff += words
        assert self.aoff <= self.awords, "arena overflow %d > %d" % (self.aoff, self.awords)
        v = self.arena[0:npart, off:off + words]
        if dtype != F32:
            v = v.bitcast(dtype)
        v = v[:, 0:n]
        if len(shape) == 3:
            v = v.rearrange("p (a b) -> p a b", a=shape[1])
        elif len(shape) == 4:
            v = v.rearrange("p (a b c) -> p a b c", a=shape[1], b=shape[2])
        return v

    def op(self, eng, fn, reads=(), writes=(), dma=False):
        self.ops.append(dict(eng=eng, fn=fn, reads=tuple(reads), writes=tuple(writes), dma=dma, bar=False))

    def barrier(self):
        self.ops.append(dict(bar=True))

    def dma(self, out, in_, reads, writes, q="sp", **kw):
        self.op(q, lambda e: e.dma_start(out=out, in_=in_, **kw), reads, writes, dma=True)

    def mm(self, out, lhsT, rhs, start, stop, reads, writes, **kw):
        self.op("pe", lambda e: e.matmul(out, lhsT, rhs, start=start, stop=stop, **kw), reads, writes)

    def tr(self, out, in_, ident, reads, writes):
        self.op("pe", lambda e: e.transpose(out, in_, ident), reads, writes)

    def act(self, out, in_, func, reads, writes, **kw):
        self.op("act", lambda e: e.activation(out=out, in_=in_, func=func, **kw), reads, writes)

    def v(self, eng, name, reads, writes, *a, **kw):
        self.op(eng, lambda e: getattr(e, name)(*a, **kw), reads, writes)

    def emit(self):
        nc = self.nc
        es = self.es
        csem = {e: es.enter_context(nc.semaphore("c_" + e)) for e in ("pe", "act", "dve", "pool")}
        dsem = {q: [es.enter_context(nc.semaphore("d_%s%d" % (q, i))) for i in range(NSLOT)]
                for q in ("sp", "act", "pool")}
        ccount = {e: 0 for e in csem}
        dcount = {q: 0 for q in dsem}
        slotuse = {q: [0] * NSLOT for q in dsem}
        semobj = {}
        for e in csem:
            semobj[("c", e)] = csem[e]
        for q in dsem:
            for i in range(NSLOT):
                semobj[("d", q, i)] = dsem[q][i]
        know = {e: {} for e in ENGS}
        last_w = {}
        readers = {}
        streams = {e: [] for e in ENGS}
        bar_know = {}
        bar_done = {e: True for e in ENGS}

        def cur_all():
            d = {}
            for q in dsem:
                for i in range(NSLOT):
                    if slotuse[q][i] > 0:
                        d[("d", q, i)] = slotuse[q][i] * 16
            for e in csem:
                if ccount[e] > 0:
                    d[("c", e)] = ccount[e]
            return d

        for op in self.ops:
            if op["bar"]:
                bar_know = cur_all()
                bar_done = {e: False for e in ENGS}
                continue
            e = op["eng"]
            deps = []
            for r in op["reads"]:
                if r in last_w:
                    deps.append(last_w[r])
            for w in op["writes"]:
                if w in last_w:
                    deps.append(last_w[w])
                for rd in readers.get(w, {}).values():
                    deps.append(rd)
            waits = {}
            kn = know[e]
            if not bar_done[e]:
                bar_done[e] = True
                for sk, val in bar_know.items():
                    if sk == ("c", "pe") and e == "pe":
                        continue
                    if kn.get(sk, 0) < val:
                        waits[sk] = val
                        kn[sk] = val
            if op["dma"]:
                slot = dcount[e] % NSLOT
                dcount[e] += 1
                sk = ("d", e, slot)
                prev = slotuse[e][slot] * 16
                if prev > 0 and kn.get(sk, 0) < prev:
                    waits[sk] = max(waits.get(sk, 0), prev)
                    kn[sk] = prev
                slotuse[e][slot] += 1
                tok = (sk, slotuse[e][slot] * 16)
            else:
                ccount[e] += 1
                tok = (("c", e), ccount[e])
            for (dtok, dknow) in deps:
                sk, val = dtok
                if sk == ("c", "pe") and e == "pe" and not op["dma"]:
                    continue
                if (not op["dma"]) and sk == ("c", e) and e in NO_SELF_SYNC:
                    continue
                if kn.get(sk, 0) >= val:
                    continue
                waits[sk] = max(waits.get(sk, 0), val)
                for k2, v2 in dknow.items():
                    if kn.get(k2, 0) < v2:
                        kn[k2] = v2
                kn[sk] = max(kn.get(sk, 0), val)
            myknow = dict(kn)
            myknow[tok[0]] = max(myknow.get(tok[0], 0), tok[1])
            if (not op["dma"]) and e == "pe":
                kn[tok[0]] = tok[1]
            streams[e].append((op, waits, tok))
            entry = (tok, myknow)
            for w in op["writes"]:
                last_w[w] = entry
                readers[w] = {}
            for r in op["reads"]:
                if r not in op["writes"]:
                    readers.setdefault(r, {})[tok[0]] = entry
        fin = cur_all()
        self.stats = dict(ccount=dict(ccount), dcount=dict(dcount))

        def run_stream(ename, eng):
            for (op, waits, tok) in streams[ename]:
                for sk, val in waits.items():
                    eng.wait_ge(semobj[sk], val)
                ins = op["fn"](eng)
                ins.then_inc(semobj[tok[0]], 16 if op["dma"] else 1)
            if ename == "sp":
                for sk, val in fin.items():
                    eng.wait_ge(semobj[sk], val)

        with nc.Block() as block:
            @block.sync
            def _(eng):
                run_stream("sp", eng)

            @block.tensor
            def _(eng):
                run_stream("pe", eng)

            @block.scalar
            def _(eng):
                run_stream("act", eng)

            @block.vector
            def _(eng):
                run_stream("dve", eng)

            @block.gpsimd
            def _(eng):
                run_stream("pool", eng)
        es.close()


class Rot:
    def __init__(self, views, name):
        self.views = views
        self.name = name
        self.i = 0

    def next(self):
        i = self.i % len(self.views)
        self.i += 1
        return self.views[i], "%s%d" % (self.name, i)


def _swap_halves_cols(w, width):
    n = w.shape[1]
    idx = np.arange(n).reshape(n // width, 2, width // 2)[:, ::-1, :].reshape(-1)
    return w[:, idx]


def host_consts():
    c = {}
    c["ident"] = np.eye(128, dtype=np.float32)
    return c


def arrange_in_cols(w_in):
    o = {}
    s5u = w_in[:, 0:256]
    hy = w_in[:, 256:1024]
    rq = w_in[:, 1024:1280]
    rk = w_in[:, 1280:1536]
    rv = w_in[:, 1536:1792]
    rg = w_in[:, 1792:2048]
    cq = w_in[:, 2048:2240]
    ckv = w_in[:, 2240:2368]
    kr = w_in[:, 2368:2400]
    z64 = np.zeros((w_in.shape[0], 64), np.float32)
    fm = np.concatenate([s5u, rq, rk, rg, _swap_halves_cols(rq, 64), _swap_halves_cols(rk, 64),
                         cq, z64, ckv, kr, _swap_halves_cols(kr, 32), z64], axis=1)
    assert fm.shape[1] == NFM
    tm = np.concatenate([hy, rk, _swap_halves_cols(rk, 64), rv], axis=1)
    assert tm.shape[1] == NTM
    o["w_fm"] = np.ascontiguousarray(fm)
    o["w_tm"] = np.ascontiguousarray(tm)
    return o


def prep_layer_weights(inp, l):
    return arrange_in_cols(inp["w_in"][l])


class Builder:
    def __init__(self, cfg):
        self.cfg = cfg
        self.nc = bass.Bass("TRN2", target_bir_lowering=False)
        self.P = Prog(self.nc)
        self.inputs = {}
        self.din = {}
        self.dscr = {}
        self.ext_out = []

    def inp(self, name, shape, dtype=F32):
        t = self.P.dram(name, shape, dtype, kind="ExternalInput")
        self.din[name] = t
        return t

    def scr(self, name, shape, dtype=F32):
        cfg = self.cfg
        if name in cfg.get("dbg_in", ()):
            kind = "ExternalInput"
        elif name in cfg.get("dbg_out", ()):
            kind = "ExternalOutput"
            self.ext_out.append(name)
        else:
            kind = "Internal"
        t = self.P.dram(name, shape, dtype, kind=kind)
        self.dscr[name] = t
        return t


def build_program(cfg):
    B = Builder(cfg)
    P = B.P
    nc = B.nc
    layers = cfg.get("layers", list(range(DEPTH)))
    stages = cfg.get("stages", "ABCDEF")
    mixers = cfg.get("mixers", ("s5", "hy", "ret", "mla"))

    x_in = B.inp("x_in", [SEQ, D])
    ctx_in = B.inp("ctx_in", [CTX, D])
    cs_in = B.inp("cs_in", [128, 8, 2])
    ident_in = B.inp("ident", [128, 128])
    ada_w = B.inp("ada_w", [DEPTH, D, 6 * D])
    ada_bT = B.inp("ada_bT", [DEPTH, 128, 48])
    ada_brow = B.inp("ada_brow", [DEPTH, 1, 6 * D])
    ng_T = B.inp("ng_T", [DEPTH, 4, 128, 8])
    ng_row = B.inp("ng_row", [DEPTH, 4, 1, D])
    w_fm = B.inp("w_fm", [DEPTH, D, NFM])
    w_tm = B.inp("w_tm", [DEPTH, D, NTM])
    w_out = B.inp("w_out", [DEPTH, D, D])
    w_up = B.inp("w_up", [DEPTH, D, 2 * DFF])
    ffn_cw = B.inp("ffn_cw", [DEPTH, 128, 44, 3])
    ffn_cb = B.inp("ffn_cb", [DEPTH, 128, 44])
    w_down = B.inp("w_down", [DEPTH, DFF, D])
    y_out = P.dram("y", [SEQ, D], F32, kind="ExternalOutput")

    xs = B.scr("xs", [T, D])
    pT_d = B.scr("pT_d", [NFM, T])
    hyp_d = B.scr("hyp_d", [W_PAD, 768])
    rtok_d = B.scr("rtok_d", [T, 768])
    oT_d = B.scr("oT_d", [D, T], BF16)
    gT_d = B.scr("gT_d", [DFF, W_PAD], BF16)

    ident = P.sb("ident_sb", [128, 128])
    zero_sb = P.sb("zero_sb", [128, 768])
    scs = P.sb("scs", [128, 8, 2])
    modT = P.sb("modT", [128, 48, 2])
    AB = P.sb("ABmod", [128, 4, 8, 2])
    grow = P.sb("grow", [128, 2, 2, D])
    eps_t = P.sb("eps_t", [128, 1])
    ones_f = P.sb("ones_f", [128, 128])
    ones_b = P.sb("ones_b", [128, 128], BF16)
    pss = [P.ps("ps%d" % i, [128, 512]) for i in range(8)]
    psrot = Rot([p[:] for p in pss], "ps")
    P.init_arena(cfg.get("arena_words", 47000))

    P.dma(ident[:], ident_in.ap(), [], ["ident"])
    P.v("dve", "memset", [], ["zero_sb"], zero_sb[:], 0.0)
    P.v("dve", "memset", [], ["eps_t"], eps_t[:], EPS)
    P.v("dve", "memset", [], ["ones_f"], ones_f[:], 1.0)
    P.v("dve", "memset", [], ["ones_b"], ones_b[:], 1.0)
    cs_raw = P.sb("cs_raw", [128, 8, 2])
    P.dma(cs_raw[:], cs_in.ap(), [], ["cs_raw"])
    P.act(scs[:], cs_raw[:], AF.Silu, ["cs_raw"], ["scs"])
    for r in (0, CTX0 + CTX, CTX0 + CTX + 1, W_PAD - 1):
        P.dma(hyp_d.ap()[r:r + 1, :], zero_sb[0:1, 0:768], ["zero_sb"], ["hyp_d"], q="pool")

    def xrows(layer, t0, n):
        if layer == 0:
            if t0 < CTX:
                return ctx_in.ap()[t0:t0 + n, :]
            return x_in.ap()[t0 - CTX:t0 - CTX + n, :]
        return xs.ap()[t0:t0 + n, :]

    def stage_A(l):
        m0 = P.mark()
        sc_rep = P.alloc([128, 8, 2, 128])
        P.v("dve", "tensor_copy", ["scs"], ["sc_rep"], sc_rep, scs[:].unsqueeze(3).to_broadcast([128, 8, 2, 128]))
        wbufs = Rot([P.alloc([128, 8, 512]) for _ in range(2)], "adaw")
        abT = P.alloc([128, 48])
        P.dma(abT, ada_bT.ap()[l], [], ["abT"])
        ngT = P.alloc([128, 4, 8])
        P.dma(ngT, ada_dummy_ngT(l), [], ["ngT"])
        brow = P.alloc([128, 2, D])
        ngrow = P.alloc([128, 2, D])
        for wi, c0 in enumerate((2 * D, 5 * D)):
            P.dma(brow[:, wi, :], ada_brow.ap()[l, 0:1, c0:c0 + D].partition_broadcast(128)[:, 0, :], [], ["brow"])
            P.dma(ngrow[:, wi, :], ng_row.ap()[l, 1 + 2 * wi, 0:1, :].partition_broadcast(128)[:, 0, :], [], ["ngrow"])
        for ci in range(12):
            wb, wk = wbufs.next()
            P.dma(wb, ada_w.ap()[l].rearrange("(k p) n -> p k n", p=128)[:, :, ci * 512:(ci + 1) * 512], [], [wk])
            for mi in range(4):
                m = ci * 4 + mi
                ps, pk = psrot.next()
                for k in range(8):
                    P.mm(ps[:, 0:2], wb[:, k, mi * 128:(mi + 1) * 128], scs[:, k, :], k == 0, k == 7,
                         [wk, "scs"], [pk])
                P.v("dve", "tensor_scalar", [pk, "abT"], ["modT"], modT[:, m, :], ps[:, 0:2], abT[:, m:m + 1], None, ALU.add)
            if ci in (4, 5, 10, 11):
                wi = 0 if ci < 6 else 1
                cc = (ci - 4) if ci < 6 else (ci - 10)
                for v in range(2):
                    ps, pk = psrot.next()
                    for k in range(8):
                        P.mm(ps, sc_rep[:, k, v, :], wb[:, k, :], k == 0, k == 7, [wk, "sc_rep"], [pk])
                    P.v("dve", "tensor_tensor", [pk, "brow"], ["grow"], grow[:, wi, v, cc * 512:(cc + 1) * 512], ps,
                        brow[:, wi, cc * 512:(cc + 1) * 512], ALU.add)
        for wi in range(2):
            for v in range(2):
                P.v("pool", "tensor_tensor", ["grow", "ngrow"], ["grow"], grow[:, wi, v, :], grow[:, wi, v, :], ngrow[:, wi, :], ALU.mult)
        for wh in range(2):
            shm = 0 if wh == 0 else 24
            scm = 8 if wh == 0 else 32
            for v in range(2):
                P.v("dve", "scalar_tensor_tensor", ["modT", "ngT"], ["AB"], AB[:, 2 * wh, :, v], modT[:, scm:scm + 8, v], 1.0,
                    ngT[:, 2 * wh, :], ALU.add, ALU.mult)
                P.v("dve", "tensor_copy", ["modT"], ["AB"], AB[:, 2 * wh + 1, :, v], modT[:, shm:shm + 8, v])
        P.barrier()
        P.release(m0)

    def ada_dummy_ngT(l):
        return ng_T.ap()[l].rearrange("f p k -> p f k")

    def norm_mod_transpose(xt, xkey, hT, hkey, tt, wh, scratch):
        v = 1 if tt < 2 else 0
        junk, ss, rstd, xn = scratch
        P.act(junk, xt, AF.Square, [xkey], ["nm_junk", "nm_ss"], accum_out=ss)
        P.act(rstd, ss, AF.Sqrt, ["nm_ss", "eps_t"], ["nm_rstd"], bias=eps_t[:], scale=1.0 / D)
        P.v("dve", "reciprocal", ["nm_rstd"], ["nm_rstd"], rstd, rstd)
        P.act(xn, xt, AF.Identity, [xkey, "nm_rstd"], ["nm_xn"], scale=rstd)
        for half in range(2):
            ps, pk = psrot.next()
            for kk in range(4):
                k = half * 4 + kk
                P.tr(ps[:, kk * 128:(kk + 1) * 128], xn[:, k * 128:(k + 1) * 128], ident[:], ["nm_xn", "ident"], [pk])
            for kk in range(4):
                k = half * 4 + kk
                eng = "dve" if kk % 2 == 0 else "pool"
                if eng == "pool":
                    P.act(hT[:, k, tt * 128:(tt + 1) * 128], ps[:, kk * 128:(kk + 1) * 128], AF.Identity,
                          [pk, "AB"], [hkey], bias=AB[:, 2 * wh + 1, k, v:v + 1], scale=AB[:, 2 * wh, k, v:v + 1])
                else:
                    P.v("dve", "tensor_scalar", [pk, "AB"], [hkey], hT[:, k, tt * 128:(tt + 1) * 128],
                        ps[:, kk * 128:(kk + 1) * 128], AB[:, 2 * wh, k, v:v + 1], AB[:, 2 * wh + 1, k, v:v + 1],
                        ALU.mult, ALU.add)

    def nm_scratch():
        return (P.alloc([128, D]), P.alloc([128, 1]), P.alloc([128, 1]), P.alloc([128, D]))

    def stage_B(l, hT):
        m0 = P.mark()
        xbufs = Rot([P.alloc([128, D]) for _ in range(3)], "xt")
        scratch = nm_scratch()
        for tt in range(NTT):
            xt, xk = xbufs.next()
            P.dma(xt, xrows(l, tt * 128, 128), ["xs%d" % tt], [xk])
            norm_mod_transpose(xt, xk, hT, "hT", tt, 0, scratch)

    def load_cast(dst_bf, src_ap, shape, stage_rot, key, eng="pool"):
        st, sk = stage_rot.next()
        P.dma(st, src_ap, [], [sk])
        if eng == "act":
            P.act(dst_bf, st, AF.Copy, [sk], [key])
        else:
            P.v(eng, "tensor_copy", [sk], [key], dst_bf, st)

    def stage_C(l, hT):
        m0 = P.mark()
        wst = Rot([P.alloc([128, 8, 512]) for _ in range(2)], "wst")
        wbf = Rot([P.alloc([128, 8, 512], BF16) for _ in range(2)], "wbf")
        outb = Rot([P.alloc([128, T]) for _ in range(2)], "pout")
        wsrc = w_fm.ap()[l].rearrange("(k p) n -> p k n", p=128)
        for cg in range(NFM // 512):
            wb, wk = wbf.next()
            load_cast(wb, wsrc[:, :, cg * 512:(cg + 1) * 512], None, wst, wk, eng="act")
            for ci in range(4):
                ct = cg * 4 + ci
                ob, ok = outb.next()
                for ch, (c0, cn) in enumerate(CH512):
                    ps, pk = psrot.next()
                    for k in range(8):
                        P.mm(ps[:, 0:cn], wb[:, k, ci * 128:(ci + 1) * 128], hT[:, k, c0:c0 + cn], k == 0, k == 7,
                             [wk, "hT"], [pk])
                    if ch % 2 == 0:
                        P.act(ob[:, c0:c0 + cn], ps[:, 0:cn], AF.Copy, [pk], [ok])
                    else:
                        P.v("dve", "tensor_copy", [pk], [ok], ob[:, c0:c0 + cn], ps[:, 0:cn])
                P.dma(pT_d.ap()[ct * 128:(ct + 1) * 128, :], ob, [ok], ["pT_d"], q="pool")
        tmo = Rot([P.alloc([128, 512]) for _ in range(3)], "tmo")
        wsrc = w_tm.ap()[l].rearrange("(k p) n -> p k n", p=128)
        for cg in range(NTM // 512):
            wb, wk = wbf.next()
            load_cast(wb, wsrc[:, :, cg * 512:(cg + 1) * 512], None, wst, wk, eng="act")
            for tt in range(NTT):
                ps, pk = psrot.next()
                for k in range(8):
                    P.mm(ps, hT[:, k, tt * 128:(tt + 1) * 128], wb[:, k, :], k == 0, k == 7, [wk, "hT"], [pk])
                ob, ok = tmo.next()
                if tt % 2 == 0:
                    P.act(ob, ps, AF.Copy, [pk], [ok])
                else:
                    P.v("dve", "tensor_copy", [pk], [ok], ob, ps)
                row_h = (CTX0 + tt * 128) if tt < 2 else (LAT0 + (tt - 2) * 128)
                c0 = cg * 512
                if c0 + 512 <= 768:
                    P.dma(hyp_d.ap()[row_h:row_h + 128, c0:c0 + 512], ob, [ok], ["hyp_d"], q="pool")
                elif c0 >= 768:
                    P.dma(rtok_d.ap()[tt * 128:(tt + 1) * 128, c0 - 768:c0 - 768 + 512], ob, [ok], ["rtok_d"], q="pool")
                else:
                    nh = 768 - c0
                    P.dma(hyp_d.ap()[row_h:row_h + 128, c0:768], ob[:, 0:nh], [ok], ["hyp_d"], q="pool")
                    P.dma(rtok_d.ap()[tt * 128:(tt + 1) * 128, 0:512 - nh], ob[:, nh:512], [ok], ["rtok_d"], q="pool")
        P.barrier()
        P.release(m0)

    def stage_E(l, h2T, last):
        m0 = P.mark()
        wst = Rot([P.alloc([128, 8, 512]) for _ in range(2)], "wst")
        wo = P.alloc([128, 8, D], BF16)
        wsrc = w_out.ap()[l].rearrange("(k p) n -> p k n", p=128)
        for hh in range(2):
            load_cast(wo[:, :, hh * 512:(hh + 1) * 512], wsrc[:, :, hh * 512:(hh + 1) * 512], None, wst, "wo", eng="pool")
        obufs = Rot([P.alloc([128, 8, 128], BF16) for _ in range(3)], "oT")
        xbufs = Rot([P.alloc([128, D]) for _ in range(3)], "xt")
        ysb = P.alloc([128, D])
        junk = P.alloc([128, D])
        ss = P.alloc([128, 1])
        rstd = P.alloc([128, 1])
        tmp = P.alloc([128, D])
        xnb = Rot([P.alloc([128, D]) for _ in range(2)], "xnew")
        scratch = nm_scratch()
        osrc = oT_d.ap().rearrange("(k p) t -> p k t", p=128)
        ysbs = Rot([ysb, P.alloc([128, D])], "ysb")

        def e_part1(tt):
            ob, ok = obufs.next()
            P.dma(ob, osrc[:, :, tt * 128:(tt + 1) * 128], ["oT_d"], [ok])
            xt, xk = xbufs.next()
            P.dma(xt, xrows(l, tt * 128, 128), ["xs%d" % tt], [xk])
            yv, yk = ysbs.next()
            for hh in range(2):
                ps, pk = psrot.next()
                for k in range(8):
                    P.mm(ps, ob[:, k, :], wo[:, k, hh * 512:(hh + 1) * 512], k == 0, k == 7, [ok, "wo"], [pk])
                P.act(yv[:, hh * 512:(hh + 1) * 512], ps, AF.Copy, [pk], [yk])
            return (xt, xk, yv, yk)

        tiles = list(range(2 if last else 0, NTT))
        pend = {tiles[0]: e_part1(tiles[0])}
        for ti, tt in enumerate(tiles):
            if ti + 1 < len(tiles):
                pend[tiles[ti + 1]] = e_part1(tiles[ti + 1])
            xt, xk, yv, yk = pend.pop(tt)
            v = 1 if tt < 2 else 0
            P.act(junk, yv, AF.Square, [yk], ["e_junk", "e_ss"], accum_out=ss)
            P.act(rstd, ss, AF.Sqrt, ["e_ss", "eps_t"], ["e_rstd"], bias=eps_t[:], scale=1.0 / D)
            P.v("dve", "reciprocal", ["e_rstd"], ["e_rstd"], rstd, rstd)
            P.v("dve", "scalar_tensor_tensor", [yk, "e_rstd", "grow"], ["e_tmp"], tmp, yv, rstd, grow[:, 0, v, :], ALU.mult, ALU.mult)
            xn, xnk = xnb.next()
            P.v("dve", "tensor_tensor", ["e_tmp", xk], [xnk], xn, tmp, xt, ALU.add)
            P.dma(xs.ap()[tt * 128:(tt + 1) * 128, :], xn, [xnk], ["xs%d" % tt], q="pool")
            norm_mod_transpose(xn, xnk, h2T, "h2T", tt, 1, scratch)
        P.barrier()
        P.release(m0)

    def stage_F(l, h2T, last):
        m0 = P.mark()
        wst = Rot([P.alloc([128, 8, 256]) for _ in range(2)], "fwst")
        wbf = Rot([P.alloc([128, 8, 256], BF16) for _ in range(2)], "fwbf")
        cw = P.alloc([128, 44, 3])
        cb = P.alloc([128, 44])
        P.dma(cw, ffn_cw.ap()[l], [], ["ffn_cw"])
        P.dma(cb, ffn_cb.ap()[l], [], ["ffn_cb"])
        uab = [P.alloc([128, W_PAD], BF16) for _ in range(2)]
        uvb = [P.alloc([128, W_PAD], BF16) for _ in range(2)]
        ca = P.alloc([128, W_PAD])
        cv = P.alloc([128, W_PAD])
        gb = Rot([P.alloc([128, W_PAD], BF16) for _ in range(1)], "gb")
        for i_ in range(2):
            P.v("dve", "memset", [], ["ua%d" % i_], uab[i_], 0.0)
            P.v("pool", "memset", [], ["uv%d" % i_], uvb[i_], 0.0)
        wsrc = w_up.ap()[l].rearrange("(k p) n -> p k n", p=128)
        Wp = W_PAD

        def f_mm(j):
            wb, wk = wbf.next()
            st, sk = wst.next()
            P.dma(st[:, :, 0:128], wsrc[:, :, j * 128:(j + 1) * 128], [], [sk])
            P.dma(st[:, :, 128:256], wsrc[:, :, DFF + j * 128:DFF + (j + 1) * 128], [], [sk])
            P.v("pool", "tensor_copy", [sk], [wk], wb, st)
            for half, (ub, ukey) in enumerate(((uab[j % 2], "ua%d" % (j % 2)), (uvb[j % 2], "uv%d" % (j % 2)))):
                for ch, (c0, cn) in enumerate(CH512):
                    ps, pk = psrot.next()
                    for k in range(8):
                        P.mm(ps[:, 0:cn], wb[:, k, half * 128:(half + 1) * 128], h2T[:, k, c0:c0 + cn], k == 0, k == 7,
                             [wk, "h2T"], [pk])
                    segs = []
                    if c0 < CTX:
                        segs.append((0, CTX, CTX0))
                        segs.append((CTX, cn - CTX, LAT0))
                    else:
                        segs.append((0, cn, LAT0 + c0 - CTX))
                    for (s0, sn, d0) in segs:
                        if (ch + half) % 2 == 0:
                            P.act(ub[:, d0:d0 + sn], ps[:, s0:s0 + sn], AF.Copy, [pk], [ukey])
                        else:
                            P.v("dve", "tensor_copy", [pk], [ukey], ub[:, d0:d0 + sn], ps[:, s0:s0 + sn])

        def f_conv(j):
            for half, (ub, ukey, cbuf, ckey) in enumerate(((uab[j % 2], "ua%d" % (j % 2), ca, "ca"), (uvb[j % 2], "uv%d" % (j % 2), cv, "cv"))):
                jj = j + 22 * half
                P.act(cbuf[:, 1:Wp - 1], ub[:, 1:Wp - 1], AF.Identity, [ukey, "ffn_cw", "ffn_cb"], [ckey],
                      bias=cb[:, jj:jj + 1], scale=cw[:, jj, 1:2])
                P.v("dve", "scalar_tensor_tensor", [ukey, ckey, "ffn_cw"], [ckey], cbuf[:, 1:Wp - 1], ub[:, 0:Wp - 2],
                    cw[:, jj, 0:1], cbuf[:, 1:Wp - 1], ALU.mult, ALU.add)
                P.v("dve", "scalar_tensor_tensor", [ukey, ckey, "ffn_cw"], [ckey], cbuf[:, 1:Wp - 1],
                    ub[:, 2:Wp], cw[:, jj, 2:3], cbuf[:, 1:Wp - 1], ALU.mult, ALU.add)
            P.act(ca[:, 1:Wp - 1], ca[:, 1:Wp - 1], AF.Silu, ["ca"], ["ca"])
            g, gk = gb.next()
            P.v("dve", "tensor_tensor", ["ca", "cv"], [gk], g[:, 1:Wp - 1], ca[:, 1:Wp - 1], cv[:, 1:Wp - 1], ALU.mult)
            P.dma(gT_d.ap()[j * 128:(j + 1) * 128, 1:Wp - 1], g[:, 1:Wp - 1], [gk], ["gT_d"], q="pool")

        for j in range(22):
            f_mm(j)
            if j >= 1:
                f_conv(j - 1)
        f_conv(21)
        P.barrier()
        P.release(m0)

    def stage_F2(l, last):
        m0 = P.mark()
        wst = Rot([P.alloc([128, 2, D]) for _ in range(2)], "dwst")
        wd = P.alloc([128, 22, D], BF16)
        wsrc = w_down.ap()[l].rearrange("(j p) n -> p j n", p=128)
        for j2 in range(11):
            load_cast(wd[:, 2 * j2:2 * j2 + 2, :], wsrc[:, 2 * j2:2 * j2 + 2, :], None, wst, "wd", eng="pool")
        gbufs = Rot([P.alloc([128, 22, 128], BF16) for _ in range(3)], "gt")
        xbufs = Rot([P.alloc([128, D]) for _ in range(3)], "xt")
        ysb = P.alloc([128, D])
        junk = P.alloc([128, D])
        ss = P.alloc([128, 1])
        rstd = P.alloc([128, 1])
        tmp = P.alloc([128, D])
        xnb = Rot([P.alloc([128, D]) for _ in range(2)], "xnew")
        gsrc = gT_d.ap().rearrange("(j p) t -> p j t", p=128)
        ysbs = Rot([ysb, P.alloc([128, D])], "ysb")

        def f2_part1(tt):
            col = (CTX0 + tt * 128) if tt < 2 else (LAT0 + (tt - 2) * 128)
            gt, gk = gbufs.next()
            P.dma(gt, gsrc[:, :, col:col + 128], ["gT_d"], [gk])
            xt, xk = xbufs.next()
            P.dma(xt, xs.ap()[tt * 128:(tt + 1) * 128, :], ["xs%d" % tt], [xk])
            yv, yk = ysbs.next()
            for hh in range(2):
                ps, pk = psrot.next()
                for j in range(22):
                    P.mm(ps, gt[:, j, :], wd[:, j, hh * 512:(hh + 1) * 512], j == 0, j == 21, [gk, "wd"], [pk])
                P.act(yv[:, hh * 512:(hh + 1) * 512], ps, AF.Copy, [pk], [yk])
            return (xt, xk, yv, yk)

        tiles = list(range(2 if last else 0, NTT))
        pend = {tiles[0]: f2_part1(tiles[0])}
        for ti, tt in enumerate(tiles):
            if ti + 1 < len(tiles):
                pend[tiles[ti + 1]] = f2_part1(tiles[ti + 1])
            xt, xk, yv, yk = pend.pop(tt)
            v = 1 if tt < 2 else 0
            P.act(junk, yv, AF.Square, [yk], ["e_junk", "e_ss"], accum_out=ss)
            P.act(rstd, ss, AF.Sqrt, ["e_ss", "eps_t"], ["e_rstd"], bias=eps_t[:], scale=1.0 / D)
            P.v("dve", "reciprocal", ["e_rstd"], ["e_rstd"], rstd, rstd)
            P.v("dve", "scalar_tensor_tensor", [yk, "e_rstd", "grow"], ["e_tmp"], tmp, yv, rstd, grow[:, 1, v, :], ALU.mult, ALU.mult)
            xn, xnk = xnb.next()
            P.v("dve", "tensor_tensor", ["e_tmp", xk], [xnk], xn, tmp, xt, ALU.add)
            if last:
                P.dma(y_out.ap()[(tt - 2) * 128:(tt - 1) * 128, :], xn, [xnk], ["y"], q="pool")
            else:
                P.dma(xs.ap()[tt * 128:(tt + 1) * 128, :], xn, [xnk], ["xs%d" % tt], q="pool")
        P.barrier()
        P.release(m0)

    for l in layers:
        last = (l == DEPTH - 1)
        if "A" in stages:
            stage_A(l)
        mh = P.mark()
        hT = P.alloc([128, 8, T], BF16)
        if "B" in stages:
            stage_B(l, hT)
        if "C" in stages:
            stage_C(l, hT)
        P.barrier()
        P.release(mh)
        if "D" in stages:
            C = SimpleNamespace(B=B, P=P, nc=nc, psrot=psrot, pss=pss, pT_d=pT_d, hyp_d=hyp_d, rtok_d=rtok_d, oT_d=oT_d,
                                ident=ident, zero_sb=zero_sb, eps_t=eps_t, ones_f=ones_f, ones_b=ones_b, cfg=cfg)
            for mx in mixers:
                MIXERS[mx](C, l, last)
                P.barrier()
        mh = P.mark()
        h2T = P.alloc([128, 8, T], BF16)
        if "E" in stages:
            stage_E(l, h2T, last)
        if "F" in stages:
            stage_F(l, h2T, last)
        P.barrier()
        P.release(mh)
        if "F" in stages:
            stage_F2(l, last)
    P.emit()
    return B


MIXERS = {}
HOSTPREP = {}


def make_inputs(inp, cfg=None):
    sh = {}
    sh.update(host_consts())
    L = DEPTH
    sh["ada_w"] = np.ascontiguousarray(inp["ada_w"])
    sh["ada_bT"] = np.ascontiguousarray(inp["ada_b"].reshape(L, 48, 128).transpose(0, 2, 1))
    sh["ada_brow"] = np.ascontiguousarray(inp["ada_b"].reshape(L, 1, 6 * D))
    sh["ng_T"] = np.ascontiguousarray(inp["norm_g"].reshape(L, 4, 8, 128).transpose(0, 1, 3, 2))
    sh["ng_row"] = np.ascontiguousarray(inp["norm_g"].reshape(L, 4, 1, D))
    lw = [prep_layer_weights(inp, l) for l in range(L)]
    sh["w_fm"] = np.stack([w["w_fm"] for w in lw])
    sh["w_tm"] = np.stack([w["w_tm"] for w in lw])
    sh["w_out"] = np.ascontiguousarray(inp["w_out"])
    sh["w_up"] = np.ascontiguousarray(inp["ffn_w_up"])
    sh["ffn_cw"] = np.ascontiguousarray(inp["ffn_conv_w"].reshape(L, 3, 44, 128).transpose(0, 3, 2, 1))
    sh["ffn_cb"] = np.ascontiguousarray(inp["ffn_conv_b"].reshape(L, 44, 128).transpose(0, 2, 1))
    sh["w_down"] = np.ascontiguousarray(inp["ffn_w_down"])
    for fn in HOSTPREP.values():
        sh.update(fn(inp))
    per = []
    for core in range(8):
        b = core % NB
        d = {}
        d["x_in"] = np.ascontiguousarray(inp["x"][b])
        d["ctx_in"] = np.ascontiguousarray(inp["ctx"][b])
        cs = np.stack([inp["c"][b], inp["c_ctx"]], axis=-1)
        d["cs_in"] = np.ascontiguousarray(cs.reshape(8, 128, 2).transpose(1, 0, 2))
        per.append(d)
    return sh, per


_CACHE = {}
ACTIVE_CORES = [0, 1, 4, 5]


def kernel(**inputs):
    inp = {k: np.asarray(v) for k, v in inputs.items()}
    cfg = {}
    Bd = build_program(cfg)
    sh, per = make_inputs(inp)
    zero_keys = ("x_in", "ctx_in", "w_fm", "w_tm", "w_out", "w_up", "w_down", "ada_w")
    zeros = {k: np.zeros_like(sh[k] if k in sh else per[0][k]) for k in zero_keys}
    in_maps = []
    for core in range(8):
        m = dict(sh)
        if core in ACTIVE_CORES:
            m.update(per[ACTIVE_CORES.index(core)])
        else:
            m.update(per[0])
            m.update(zeros)
        in_maps.append(m)
    res = run_bass_kernel_spmd(Bd.nc, in_maps, core_ids=list(range(8)))
    out = np.stack([res.results[ACTIVE_CORES[b]]["y"] for b in range(NB)], axis=0)
    return out.astype(np.float32)


MLA_H = 4
QK = 96
PT_CQA, PT_CQB, PT_CKV, PT_KR, PT_KRS = 1536, 1664, 1792, 1920, 1952
OT_MLA = 768


def host_mla(inp):
    L = DEPTH
    o = {}
    wq = inp["mla_w_uq"].reshape(L, 192, MLA_H, QK)
    wqB = wq.copy()
    rope = wq[..., 64:96].reshape(L, 192, MLA_H, 2, 16)[..., ::-1, :].reshape(L, 192, MLA_H, 32)
    wqB[..., 64:96] = rope
    o["mla_wq"] = np.ascontiguousarray(np.stack([wq, wqB], axis=2).reshape(L, 192, 2 * MLA_H * QK))
    wkv = inp["mla_w_ukv"].reshape(L, 128, MLA_H, 128)
    o["mla_wkn"] = np.ascontiguousarray(wkv[..., 0:64].reshape(L, 128, 256))
    o["mla_wv"] = np.ascontiguousarray(wkv[..., 64:128].reshape(L, 128, 256))
    qg = np.zeros((L, 256), np.float32)
    qg[:, :192] = inp["mla_q_norm_g"]
    o["mla_qg"] = np.ascontiguousarray(qg.reshape(L, 2, 128).transpose(0, 2, 1))
    o["mla_kvg"] = np.ascontiguousarray(inp["mla_kv_norm_g"].reshape(L, 128, 1))
    n = SEQ
    rows = n // 64
    row = np.repeat(np.arange(rows), 64).astype(np.float32)
    col = np.tile(np.arange(64), rows).astype(np.float32)
    nf = 8
    inv = (np.float32(10000.0) ** (-np.arange(nf, dtype=np.float32) / nf)).astype(np.float32)
    ang = np.concatenate([row[:, None] * inv, col[:, None] * inv], axis=-1).astype(np.float32)
    cos, sin = np.cos(ang).astype(np.float32), np.sin(ang).astype(np.float32)
    Cf = np.ones((96, T), np.float32)
    Sf = np.zeros((96, T), np.float32)
    Cf[64:80, CTX:] = cos.T
    Cf[80:96, CTX:] = cos.T
    Sf[64:80, CTX:] = -sin.T
    Sf[80:96, CTX:] = sin.T
    o["mla_C"] = Cf
    o["mla_S"] = Sf
    return o


def mix_mla(C, l, last):
    P, B = C.P, C.B
    psrot = C.psrot
    if "mla_wq" not in B.din:
        B.inp("mla_wq", [DEPTH, 192, 768])
        B.inp("mla_wkn", [DEPTH, 128, 256])
        B.inp("mla_wv", [DEPTH, 128, 256])
        B.inp("mla_qg", [DEPTH, 128, 2])
        B.inp("mla_kvg", [DEPTH, 128, 1])
        B.inp("mla_C", [96, T])
        B.inp("mla_S", [96, T])
    din = B.din
    pT = C.pT_d.ap()
    m0 = P.mark()
    wq_st = P.alloc([128, 2, 768])
    wq = P.alloc([128, 2, 768], BF16)
    P.dma(wq_st[:, 0, :], din["mla_wq"].ap()[l, 0:128, :], [], ["mla_wq_st"])
    P.dma(wq_st[0:64, 1, :], din["mla_wq"].ap()[l, 128:192, :], [], ["mla_wq_st"])
    P.v("pool", "tensor_copy", ["mla_wq_st"], ["mla_wq"], wq[:, 0, :], wq_st[:, 0, :])
    P.v("pool", "tensor_copy", ["mla_wq_st"], ["mla_wq"], wq[0:64, 1, :], wq_st[0:64, 1, :])
    wk_st = P.alloc([128, 512])
    wkv = P.alloc([128, 512], BF16)
    P.dma(wk_st[:, 0:256], din["mla_wkn"].ap()[l], [], ["mla_wk_st"])
    P.dma(wk_st[:, 256:512], din["mla_wv"].ap()[l], [], ["mla_wk_st"])
    P.v("pool", "tensor_copy", ["mla_wk_st"], ["mla_wkv"], wkv, wk_st)
    qg = P.alloc([128, 2])
    kvg = P.alloc([128, 1])
    P.dma(qg, din["mla_qg"].ap()[l], [], ["mla_qg"])
    P.dma(kvg, din["mla_kvg"].ap()[l], [], ["mla_kvg"])
    qT = [P.alloc([96, T], BF16) for _ in range(MLA_H)]
    kT = [P.alloc([96, T], BF16) for _ in range(MLA_H)]
    vtok = P.alloc([128, NTT, MLA_H, 65], BF16)
    P.v("dve", "memset", [], ["m_vtok"], vtok, 1.0)
    m1 = P.mark()
    NB_ = 2
    cqa = Rot([P.alloc([128, 512]) for _ in range(NB_)], "m_cqa")
    cqb = Rot([P.alloc([64, 512]) for _ in range(NB_)], "m_cqb")
    ckv = Rot([P.alloc([128, 512]) for _ in range(NB_)], "m_ckv")
    krb = Rot([P.alloc([96, 2, 512]) for _ in range(NB_)], "m_kr")
    tabs = Rot([P.alloc([96, 2, 512]) for _ in range(NB_)], "m_tab")
    sq = P.alloc([128, 3, 512])
    rstd = P.alloc([128, 2, 512])
    cqn_a = P.alloc([128, 512], BF16)
    cqn_b = P.alloc([64, 512], BF16)
    ckvn = P.alloc([128, 512], BF16)
    t1 = P.alloc([96, 512])
    t2 = P.alloc([96, 512])
    for ch, (c0, cn) in enumerate(CH512):
        a, ak = cqa.next()
        b, bk = cqb.next()
        kv, kvk = ckv.next()
        kr, krk = krb.next()
        tb, tbk = tabs.next()
        P.dma(a[:, 0:cn], pT[PT_CQA:PT_CQA + 128, c0:c0 + cn], ["pT_d"], [ak])
        P.dma(b[:, 0:cn], pT[PT_CQB:PT_CQB + 64, c0:c0 + cn], ["pT_d"], [bk])
        P.dma(kv[:, 0:cn], pT[PT_CKV:PT_CKV + 128, c0:c0 + cn], ["pT_d"], [kvk])
        P.dma(kr[64:96, 0, 0:cn], pT[PT_KR:PT_KR + 32, c0:c0 + cn], ["pT_d"], [krk])
        P.dma(kr[64:96, 1, 0:cn], pT[PT_KRS:PT_KRS + 32, c0:c0 + cn], ["pT_d"], [krk])
        P.dma(tb[:, 0, 0:cn], din["mla_C"].ap()[:, c0:c0 + cn], [], [tbk])
        P.dma(tb[:, 1, 0:cn], din["mla_S"].ap()[:, c0:c0 + cn], [], [tbk])
        P.act(sq[:, 0, 0:cn], a[:, 0:cn], AF.Square, [ak], ["m_sq"])
        P.act(sq[0:64, 1, 0:cn], b[:, 0:cn], AF.Square, [bk], ["m_sq"])
        P.act(sq[:, 2, 0:cn], kv[:, 0:cn], AF.Square, [kvk], ["m_sq"])
        ps, pk = psrot.next()
        P.mm(ps[:, 0:cn], C.ones_f[:, :], sq[:, 0, 0:cn], True, False, ["m_sq", "ones_f"], [pk])
        P.mm(ps[:, 0:cn], C.ones_f[0:64, :], sq[0:64, 1, 0:cn], False, True, ["m_sq", "ones_f"], [pk])
        P.act(rstd[:, 0, 0:cn], ps[:, 0:cn], AF.Sqrt, [pk, "eps_t"], ["m_rstd"], bias=C.eps_t[:], scale=1.0 / 192)
        ps2, pk2 = psrot.next()
        P.mm(ps2[:, 0:cn], C.ones_f[:, :], sq[:, 2, 0:cn], True, True, ["m_sq", "ones_f"], [pk2])
        P.act(rstd[:, 1, 0:cn], ps2[:, 0:cn], AF.Sqrt, [pk2, "eps_t"], ["m_rstd"], bias=C.eps_t[:], scale=1.0 / 128)
        P.v("dve", "reciprocal", ["m_rstd"], ["m_rstd"], rstd[:, :, 0:cn], rstd[:, :, 0:cn])
        P.v("dve", "scalar_tensor_tensor", [ak, "mla_qg", "m_rstd"], ["m_cqn_a"], cqn_a[:, 0:cn], a[:, 0:cn], qg[:, 0:1],
            rstd[:, 0, 0:cn], ALU.mult, ALU.mult)
        P.v("dve", "scalar_tensor_tensor", [bk, "mla_qg", "m_rstd"], ["m_cqn_b"], cqn_b[:, 0:cn], b[:, 0:cn], qg[0:64, 1:2],
            rstd[0:64, 0, 0:cn], ALU.mult, ALU.mult)
        P.v("dve", "scalar_tensor_tensor", [kvk, "mla_kvg", "m_rstd"], ["m_ckvn"], ckvn[:, 0:cn], kv[:, 0:cn], kvg[:, 0:1],
            rstd[:, 1, 0:cn], ALU.mult, ALU.mult)
        for h in range(MLA_H):
            psA, pkA = psrot.next()
            psB, pkB = psrot.next()
            for (pp, ppk, ab) in ((psA, pkA, 0), (psB, pkB, 1)):
                w0 = (ab * MLA_H + h) * QK
                P.mm(pp[0:96, 0:cn], wq[:, 0, w0:w0 + QK], cqn_a[:, 0:cn], True, False, ["mla_wq", "m_cqn_a"], [ppk])
                P.mm(pp[0:96, 0:cn], wq[0:64, 1, w0:w0 + QK], cqn_b[:, 0:cn], False, True, ["mla_wq", "m_cqn_b"], [ppk])
            P.v("dve", "tensor_tensor", [pkA, tbk], ["m_t1"], t1[:, 0:cn], psA[0:96, 0:cn], tb[:, 0, 0:cn], ALU.mult)
            P.v("dve", "tensor_tensor", [pkB, tbk], ["m_t2"], t2[:, 0:cn], psB[0:96, 0:cn], tb[:, 1, 0:cn], ALU.mult)
            P.v("dve", "tensor_tensor", ["m_t1", "m_t2"], ["m_qT%d" % h], qT[h][:, c0:c0 + cn], t1[:, 0:cn], t2[:, 0:cn], ALU.add)
        for h in range(MLA_H):
            ps, pk = psrot.next()
            P.mm(ps[0:64, 0:cn], wkv[:, h * 64:(h + 1) * 64], ckvn[:, 0:cn], True, True, ["mla_wkv", "m_ckvn"], [pk])
            P.act(kT[h][0:64, c0:c0 + cn], ps[0:64, 0:cn], AF.Copy, [pk], ["m_kT%d" % h])
        P.v("dve", "tensor_tensor", [krk, tbk], ["m_t1"], t1[64:96, 0:cn], kr[64:96, 0, 0:cn], tb[64:96, 0, 0:cn], ALU.mult)
        P.v("dve", "tensor_tensor", [krk, tbk], ["m_t2"], t2[64:96, 0:cn], kr[64:96, 1, 0:cn], tb[64:96, 1, 0:cn], ALU.mult)
        P.v("dve", "tensor_tensor", ["m_t1", "m_t2"], ["m_t1"], t1[64:96, 0:cn], t1[64:96, 0:cn], t2[64:96, 0:cn], ALU.add)
        for h in range(MLA_H):
            if h % 2 == 0:
                P.act(kT[h][64:96, c0:c0 + cn], t1[64:96, 0:cn], AF.Copy, ["m_t1"], ["m_kT%d" % h])
            else:
                P.act(kT[h][64:96, c0:c0 + cn], t1[64:96, 0:cn], AF.Copy, ["m_t1"], ["m_kT%d" % h])
        for i in range(cn // 128):
            tt = c0 // 128 + i
            ps, pk = psrot.next()
            P.mm(ps[:, 0:256], ckvn[:, i * 128:(i + 1) * 128], wkv[:, 256:512], True, True, ["mla_wkv", "m_ckvn"], [pk])
            P.act(vtok[:, tt, :, 0:64], ps[:, 0:256].rearrange("p (h e) -> p h e", h=MLA_H), AF.Copy, [pk], ["m_vtok"])
    P.barrier()
    P.release(m1)
    srot = Rot([p[:] for p in C.pss[0:6]], "ps")
    ps_o, pk_o = C.pss[6][:], "ps6"
    ps_d, pk_d = C.pss[7][:], "ps7"
    pT_b = Rot([P.alloc([128, 512], BF16) for _ in range(6)], "m_pT")
    rden = P.alloc([64, 512])
    rrow = P.alloc([65, 512])
    ost = Rot([P.alloc([64, 512], BF16) for _ in range(2)], "m_ost")
    scale = 1.0 / math.sqrt(QK)
    blocks = [(CTX + i * 512, 512, 0, NTT) for i in range(SEQ // 512)]
    if not last:
        blocks = [(0, CTX, 0, 2)] + blocks
    steps = []
    for (q0, qn, kt0, kt1) in blocks:
        for h in range(MLA_H):
            for kt in range(kt0, kt1):
                steps.append((q0, qn, kt0, kt1, h, kt))
    LOOK = 2
    pend = {}

    def issue_scores(i):
        q0, qn, kt0, kt1, h, kt = steps[i]
        ps, pk = srot.next()
        P.mm(ps[:, 0:qn], kT[h][:, kt * 128:(kt + 1) * 128], qT[h][:, q0:q0 + qn], True, True,
             ["m_kT%d" % h, "m_qT%d" % h], [pk])
        pb, pbk = pT_b.next()
        P.act(pb[:, 0:qn], ps[:, 0:qn], AF.Exp, [pk], [pbk], scale=scale)
        pend[i] = (pb, pbk)

    for i in range(min(LOOK, len(steps))):
        issue_scores(i)
    for i in range(len(steps)):
        if i + LOOK < len(steps):
            issue_scores(i + LOOK)
        q0, qn, kt0, kt1, h, kt = steps[i]
        pb, pbk = pend.pop(i)
        P.mm(ps_o[0:65, 0:qn], vtok[:, kt, h, :], pb[:, 0:qn], kt == kt0, kt == kt1 - 1, [pbk, "m_vtok"], [pk_o])
        if kt == kt1 - 1:
            P.v("dve", "reciprocal", [pk_o], ["m_rrow"], rrow[64:65, 0:qn], ps_o[64:65, 0:qn])
            P.mm(ps_d[0:64, 0:qn], C.ones_f[64:65, 0:64], rrow[64:65, 0:qn], True, True, ["m_rrow", "ones_f"], [pk_d])
            P.act(rden[:, 0:qn], ps_d[0:64, 0:qn], AF.Copy, [pk_d], ["m_rden"])
            ob, obk = ost.next()
            P.v("dve", "tensor_tensor", [pk_o, "m_rden"], [obk], ob[:, 0:qn], ps_o[0:64, 0:qn], rden[:, 0:qn], ALU.mult)
            P.dma(C.oT_d.ap()[OT_MLA + h * 64:OT_MLA + (h + 1) * 64, q0:q0 + qn], ob[:, 0:qn], [obk], ["oT_d"], q="pool")
    P.barrier()
    P.release(m0)


MIXERS["mla"] = mix_mla
HOSTPREP["mla"] = host_mla


RH = 4
PT_RQ, PT_RK, PT_RG, PT_RQS, PT_RKS = 256, 512, 768, 1024, 1280
OT_RET = 512
LN2 = math.log(2.0)


def host_ret(inp):
    L = DEPTH
    o = {}
    f32 = np.float32
    theta = (f32(10000.0) ** (-np.linspace(0.0, 1.0, 32, dtype=f32))).astype(f32)
    ang = (np.arange(SEQ, dtype=f32)[:, None] * theta).astype(f32)
    cos, sin = np.cos(ang).astype(f32), np.sin(ang).astype(f32)
    Ct = np.ones((T, 64), f32)
    St = np.zeros((T, 64), f32)
    Ct[CTX:, 0:32] = cos
    Ct[CTX:, 32:64] = cos
    St[CTX:, 0:32] = -sin
    St[CTX:, 32:64] = sin
    o["ret_Ct"] = Ct
    o["ret_St"] = St
    o["ret_C"] = np.ascontiguousarray(np.tile(Ct.T, (2, 1)))
    o["ret_S"] = np.ascontiguousarray(np.tile(St.T, (2, 1)))
    j = np.arange(128, dtype=f32)[:, None]
    i = np.arange(128, dtype=f32)[None, :]
    cst = np.zeros((128, 6, 128), f32)
    cst[:, 0, :] = np.maximum(i - j, 0)
    cst[:, 1, :] = np.maximum(j - i, 0)
    cst[:, 2, :] = (i >= j)
    cst[:, 3, :] = (j >= i)
    cst[:, 4, :] = i + 1
    cst[:, 5, :] = 128 - i
    o["ret_cst"] = cst
    col = np.zeros((128, 2), f32)
    col[:, 0] = 127 - np.arange(128)
    col[:, 1] = np.arange(128)
    o["ret_col"] = col
    hm = np.zeros((128, 2), f32)
    hm[:64, 0] = 1.0
    hm[64:, 1] = 1.0
    o["ret_hm"] = hm
    o["ret_dexp"] = np.ascontiguousarray(inp["ret_decay_exp"].reshape(L, 1, 8))
    o["ret_gnT"] = np.ascontiguousarray(inp["ret_gn_g"].reshape(L, RH, 64).transpose(0, 2, 1))
    return o


def mix_ret(C, l, last):
    P, B = C.P, C.B
    psrot = C.psrot
    if "ret_C" not in B.din:
        B.inp("ret_Ct", [T, 64]); B.inp("ret_St", [T, 64])
        B.inp("ret_C", [128, T]); B.inp("ret_S", [128, T])
        B.inp("ret_cst", [128, 6, 128]); B.inp("ret_col", [128, 2]); B.inp("ret_hm", [128, 2])
        B.inp("ret_dexp", [DEPTH, 1, 8]); B.inp("ret_gnT", [DEPTH, 64, 4])
    din = B.din
    pT = C.pT_d.ap()
    m0 = P.mark()
    cst = P.alloc([128, 6, 128])
    P.dma(cst, din["ret_cst"].ap(), [], ["r_cst"])
    colc = P.alloc([128, 2])
    P.dma(colc, din["ret_col"].ap(), [], ["r_col"])
    gnT = P.alloc([64, 4])
    P.dma(gnT, din["ret_gnT"].ap()[l], [], ["r_gnT"])
    lg8 = P.alloc([128, 8])
    P.dma(lg8, din["ret_dexp"].ap()[l, 0:1, :].partition_broadcast(128)[:, 0, :], [], ["r_lg8"])
    P.act(lg8, lg8, AF.Exp, ["r_lg8"], ["r_lg8"], scale=-LN2)
    P.act(lg8, lg8, AF.Ln, ["r_lg8"], ["r_lg8"], scale=-1.0, bias=1.0)
    lgp = P.alloc([128, 2, 2])
    for d_ in range(2):
        for pr in range(2):
            P.v("dve", "tensor_copy", ["r_lg8"], ["r_lgp"], lgp[0:64, d_, pr:pr + 1], lg8[0:64, d_ * 4 + 2 * pr:d_ * 4 + 2 * pr + 1])
            P.v("dve", "tensor_copy", ["r_lg8"], ["r_lgp"], lgp[64:128, d_, pr:pr + 1], lg8[64:128, d_ * 4 + 2 * pr + 1:d_ * 4 + 2 * pr + 2])
    Mk = P.alloc([128, 4, 128])
    mt = P.alloc([128, 128])
    for h in range(RH):
        P.act(Mk[:, h, :], cst[:, 0, :], AF.Exp, ["r_cst", "r_lg8"], ["r_Mk"], scale=lg8[:, h:h + 1])
        P.v("dve", "tensor_tensor", ["r_Mk", "r_cst"], ["r_Mk"], Mk[:, h, :], Mk[:, h, :], cst[:, 2, :], ALU.mult)
        P.act(mt, cst[:, 1, :], AF.Exp, ["r_cst", "r_lg8"], ["r_mt"], scale=lg8[:, 4 + h:5 + h])
        P.v("dve", "tensor_tensor", ["r_mt", "r_cst"], ["r_mt"], mt, mt, cst[:, 3, :], ALU.mult)
        P.v("dve", "tensor_tensor", ["r_mt", "r_Mk"], ["r_Mk"], Mk[:, h, :], Mk[:, h, :], mt, ALU.add)
    XI = P.alloc([128, 2, 2, 128])
    gcolp = P.alloc([128, 2, 2])
    for d_ in range(2):
        for pr in range(2):
            P.act(XI[:, d_, pr, :], cst[:, 4 + d_, :], AF.Exp, ["r_cst", "r_lgp"], ["r_XI"], scale=lgp[:, d_, pr:pr + 1])
            P.act(gcolp[:, d_, pr:pr + 1], lgp[:, d_, pr:pr + 1], AF.Exp, ["r_lgp"], ["r_gcol"], scale=128.0)
    Z = P.alloc([128, 2, 4])
    for d_ in range(2):
        for h in range(RH):
            P.act(Z[:, d_, h:h + 1], colc[:, d_:d_ + 1], AF.Exp, ["r_col", "r_lg8"], ["r_Z"], scale=lg8[:, d_ * 4 + h:d_ * 4 + h + 1])
    hm = P.alloc([128, 2])
    P.dma(hm, din["ret_hm"].ap(), [], ["r_hm"])
    TAB = P.alloc([128, 3, 4, 128])
    for h in range(RH):
        pr, hp = h // 2, h % 2
        P.v("dve", "tensor_copy", ["r_hm"], ["r_TAB"], TAB[:, 0, h, :], hm[:, hp:hp + 1].to_broadcast([128, 128]))
        for d_ in range(2):
            P.v("dve", "tensor_scalar", ["r_XI", "r_hm"], ["r_TAB"], TAB[:, 1 + d_, h, :], XI[:, d_, pr, :], hm[:, hp:hp + 1], None, ALU.mult)
    if C.cfg.get("ret_stop", 9) <= 0:
        P.barrier(); P.release(m0); return
    qr = [P.alloc([128, T], BF16) for _ in range(2)]
    kr = [P.alloc([128, T], BF16) for _ in range(2)]
    vt = P.alloc([128, NTT, 256], BF16)
    Sall = [[P.alloc([128, NTT, 128], BF16) for _ in range(2)] for _ in range(2)]
    m1 = P.mark()
    kz = [P.alloc([128, NTT, 256], BF16) for _ in range(2)]
    m2 = P.mark()
    ld = Rot([P.alloc([128, 4, 512]) for _ in range(2)], "r_ld")
    tb = Rot([P.alloc([128, 2, 512]) for _ in range(2)], "r_tb")
    t1 = P.alloc([128, 512]); t2 = P.alloc([128, 512])
    for ch, (c0, cn) in enumerate(CH512):
        tbv, tbk = tb.next()
        P.dma(tbv[:, 0, 0:cn], din["ret_C"].ap()[:, c0:c0 + cn], [], [tbk])
        P.dma(tbv[:, 1, 0:cn], din["ret_S"].ap()[:, c0:c0 + cn], [], [tbk])
        nq = cn // 128
        for pr in range(2):
            lv, lk = ld.next()
            for ii, r0 in enumerate((PT_RQ, PT_RQS, PT_RK, PT_RKS)):
                P.dma(lv[:, ii, 0:cn], pT[r0 + pr * 128:r0 + (pr + 1) * 128, c0:c0 + cn], ["pT_d"], [lk])
            P.v("dve", "tensor_tensor", [lk, tbk], ["r_t1"], t1[:, 0:cn], lv[:, 0, 0:cn], tbv[:, 0, 0:cn], ALU.mult)
            P.v("pool", "tensor_tensor", [lk, tbk], ["r_t2"], t2[:, 0:cn], lv[:, 1, 0:cn], tbv[:, 1, 0:cn], ALU.mult)
            P.v("dve", "tensor_tensor", ["r_t1", "r_t2"], ["r_t1"], t1[:, 0:cn], t1[:, 0:cn], t2[:, 0:cn], ALU.add)
            P.act(qr[pr][:, c0:c0 + cn], t1[:, 0:cn], AF.Copy, ["r_t1"], ["r_qr%d" % pr], scale=0.125)
            P.v("dve", "tensor_tensor", [lk, tbk], ["r_t1"], t1[:, 0:cn], lv[:, 2, 0:cn], tbv[:, 0, 0:cn], ALU.mult)
            P.v("pool", "tensor_tensor", [lk, tbk], ["r_t2"], t2[:, 0:cn], lv[:, 3, 0:cn], tbv[:, 1, 0:cn], ALU.mult)
            P.v("pool", "tensor_tensor", ["r_t1", "r_t2"], ["r_kr%d" % pr], kr[pr][:, c0:c0 + cn], t1[:, 0:cn], t2[:, 0:cn], ALU.add)
    if C.cfg.get("ret_stop", 9) <= 0.5:
        P.barrier(); P.release(m0); return
    tl = Rot([P.alloc([128, 768]) for _ in range(3)], "r_tl")
    tt_tab = Rot([P.alloc([128, 2, 64]) for _ in range(3)], "r_ttab")
    kk1 = P.alloc([128, 4, 64]); kk2 = P.alloc([128, 4, 64])
    for tt in range(NTT):
        tv, tk = tl.next()
        P.dma(tv, C.rtok_d.ap()[tt * 128:(tt + 1) * 128, :], ["rtok_d"], [tk])
        tab, tabk = tt_tab.next()
        P.dma(tab[:, 0, :], din["ret_Ct"].ap()[tt * 128:(tt + 1) * 128, :], [], [tabk])
        P.dma(tab[:, 1, :], din["ret_St"].ap()[tt * 128:(tt + 1) * 128, :], [], [tabk])
        kview = tv[:, 0:256].rearrange("p (h d) -> p h d", h=4)
        ksview = tv[:, 256:512].rearrange("p (h d) -> p h d", h=4)
        P.v("dve", "tensor_tensor", [tk, tabk], ["r_kk1"], kk1, kview, tab[:, 0, :].unsqueeze(1).to_broadcast([128, 4, 64]), ALU.mult)
        P.v("pool", "tensor_tensor", [tk, tabk], ["r_kk2"], kk2, ksview, tab[:, 1, :].unsqueeze(1).to_broadcast([128, 4, 64]), ALU.mult)
        P.v("dve", "tensor_tensor", ["r_kk1", "r_kk2"], ["r_kk1"], kk1, kk1, kk2, ALU.add)
        for d_ in range(2):
            P.v("dve" if d_ == 0 else "pool", "tensor_tensor", ["r_kk1", "r_Z"], ["r_kz%d" % d_],
                kz[d_][:, tt, :].rearrange("p (h d) -> p h d", h=4), kk1,
                Z[:, d_, :].unsqueeze(2).to_broadcast([128, 4, 64]), ALU.mult)
        P.act(vt[:, tt, :], tv[:, 512:768], AF.Copy, [tk], ["r_vt"])
    if C.cfg.get("ret_stop", 9) <= 1:
        P.barrier(); P.release(m0); return
    Srun = [[[P.alloc([128, 128]) for _ in range(2)] for _ in range(2)] for _ in range(2)]
    order = [list(range(NTT)), [1, 0] + list(range(NTT - 1, 1, -1))]
    for d_ in range(2):
        for pr in range(2):
            P.v("dve", "memset", [], ["r_S%d%d" % (d_, pr)], Sall[d_][pr][:, order[d_][0], :], 0.0)
    for s in range(NTT - 1):
        for d_ in range(2):
            for pr in range(2):
                n = order[d_][s]
                nxt = order[d_][s + 1]
                ps, pk = psrot.next()
                cur = Srun[d_][pr][s % 2]
                new = Srun[d_][pr][(s + 1) % 2]
                ck = "r_Sr%d%d%d" % (d_, pr, s % 2)
                nk = "r_Sr%d%d%d" % (d_, pr, (s + 1) % 2)
                P.mm(ps[:, 0:128], kz[d_][:, n, pr * 128:(pr + 1) * 128], vt[:, n, pr * 128:(pr + 1) * 128], True, True,
                     ["r_kz%d" % d_, "r_vt"], [pk])
                if s > 0:
                    P.v("dve", "scalar_tensor_tensor", [pk, ck, "r_gcol"], [nk], new, cur, gcolp[:, d_, pr:pr + 1], ps[:, 0:128],
                        ALU.mult, ALU.add)
                else:
                    P.v("dve", "tensor_copy", [pk], [nk], new, ps[:, 0:128])
                P.act(Sall[d_][pr][:, nxt, :], new, AF.Copy, [nk], ["r_S%d%d" % (d_, pr)])
    P.barrier()
    P.release(m1)
    if C.cfg.get("ret_stop", 9) <= 2:
        P.barrier(); P.release(m0); return
    msk = Rot([P.alloc([128, 4, 128], BF16) for _ in range(2)], "r_msk")
    gld = Rot([P.alloc([64, 4, 128]) for _ in range(2)], "r_gld")
    sqo = P.alloc([64, 512])
    rstd = P.alloc([64, 512])
    on = P.alloc([64, 4, 128])
    ost = Rot([P.alloc([64, 4, 128], BF16) for _ in range(2)], "r_ost")
    qtr = Rot([P.alloc([128, 3, 2, 128], BF16) for _ in range(4)], "r_qt")
    gsrc = pT[PT_RG:PT_RG + 256, :].rearrange("(h e) t -> e h t", h=4)
    odst = C.oT_d.ap()[OT_RET:OT_RET + 256, :].rearrange("(h e) t -> e h t", h=4)
    for n in range(2 if last else 0, NTT):
        cs = slice(n * 128, (n + 1) * 128)
        gv, gk = gld.next()
        P.dma(gv, gsrc[:, :, cs], ["pT_d"], [gk])
        P.act(gv, gv, AF.Silu, [gk], [gk])
        QT = []
        for pr in range(2):
            qv, qk = qtr.next()
            P.v("dve", "tensor_tensor", ["r_qr%d" % pr, "r_TAB"], [qk], qv,
                qr[pr][:, cs].unsqueeze(1).unsqueeze(1).to_broadcast([128, 3, 2, 128]), TAB[:, :, 2 * pr:2 * pr + 2, :], ALU.mult)
            QT.append((qv, qk))
        ps_s, pk_s = psrot.next()
        for h in range(RH):
            pr, hp = h // 2, h % 2
            P.mm(ps_s[:, h * 128:(h + 1) * 128], kr[pr][:, cs], QT[pr][0][:, 0, hp, :], True, True,
                 ["r_kr%d" % pr, QT[pr][1]], [pk_s])
        mv, mk = msk.next()
        P.v("dve", "tensor_tensor", [pk_s, "r_Mk"], [mk], mv, ps_s.rearrange("p (h i) -> p h i", h=4), Mk, ALU.mult)
        ps_o, pk_o = psrot.next()
        for h in range(RH):
            pr, hp = h // 2, h % 2
            hs = slice(hp * 64, (hp + 1) * 64)
            oo = ps_o[0:64, h * 128:(h + 1) * 128]
            P.mm(oo, vt[:, n, h * 64:(h + 1) * 64], mv[:, h, :], True, False, ["r_vt", mk], [pk_o])
            P.mm(oo, Sall[0][pr][:, n, hs], QT[pr][0][:, 1, hp, :], False, False, ["r_S0%d" % pr, QT[pr][1]], [pk_o])
            P.mm(oo, Sall[1][pr][:, n, hs], QT[pr][0][:, 2, hp, :], False, True, ["r_S1%d" % pr, QT[pr][1]], [pk_o])
        P.act(sqo, ps_o[0:64, :], AF.Square, [pk_o], ["r_sqo"])
        ps_n, pk_n = psrot.next()
        P.mm(ps_n[0:64, :], C.ones_f[0:64, 0:64], sqo, True, True, ["r_sqo", "ones_f"], [pk_n])
        P.act(rstd, ps_n[0:64, :], AF.Sqrt, [pk_n, "eps_t"], ["r_rstd"], bias=C.eps_t[0:64, :], scale=1.0 / 64)
        P.v("dve", "reciprocal", ["r_rstd"], ["r_rstd"], rstd, rstd)
        P.v("dve", "tensor_tensor", [pk_o, "r_rstd"], ["r_on"], on, ps_o[0:64, :].rearrange("p (h i) -> p h i", h=4),
            rstd.rearrange("p (h i) -> p h i", h=4), ALU.mult)
        P.v("dve", "tensor_tensor", ["r_on", "r_gnT"], ["r_on"], on, on, gnT.unsqueeze(2).to_broadcast([64, 4, 128]), ALU.mult)
        ov, ok = ost.next()
        P.v("dve", "tensor_tensor", ["r_on", gk], [ok], ov, on, gv, ALU.mult)
        P.dma(odst[:, :, cs], ov, [ok], ["oT_d"], q="pool")
    P.barrier()
    P.release(m0)


MIXERS["ret"] = mix_ret
HOSTPREP["ret"] = host_ret


TWO_PI = 2.0 * math.pi
S5_SEGS = [(0, 256)] + [(256 + 1024 * k, 1024) for k in range(4)]


def host_s5(inp):
    L = DEPTH
    f32 = np.float32
    o = {}
    lam_re = inp["s5_lam_re"]; lam_im = inp["s5_lam_im"]
    ldt = np.repeat(inp["s5_log_dt"][..., None], 64, axis=-1)
    prow = np.stack([lam_re, lam_im, ldt], axis=1)
    o["s5_prow"] = np.ascontiguousarray(prow.reshape(L, 3, 1, 2048)).astype(f32)
    pT_ = prow.reshape(L, 3, 2, 8, 2, 64).transpose(0, 1, 4, 5, 2, 3)
    o["s5_pT"] = np.ascontiguousarray(pT_.reshape(L, 3, 128, 16)).astype(f32)
    Bexp = np.zeros((L, 2, 2, 128, 8, 128), f32)
    Cexp = np.zeros((L, 2, 2, 128, 8, 128), f32)
    for ri, (bk, ck) in enumerate((("s5_b_re", "s5_c_re"), ("s5_b_im", "s5_c_im"))):
        b = inp[bk]
        c = inp[ck]
        for m in range(8):
            for a in range(2):
                g = 2 * m + a
                q0 = 32 * (m % 4) + 16 * a
                Bexp[:, ri, :, q0:q0 + 16, m, 64 * a:64 * a + 64] = b[:, :, g].transpose(0, 1, 3, 2)
                Cexp[:, ri, :, 64 * a:64 * a + 64, m, q0:q0 + 16] = c[:, :, g].transpose(0, 1, 3, 2)
    o["s5_Bexp"] = Bexp
    o["s5_Cexp"] = Cexp
    o["s5_dT"] = np.ascontiguousarray(inp["s5_d"].reshape(L, 2, 128).transpose(0, 2, 1))
    o["s5_gbT"] = np.ascontiguousarray(inp["s5_glu_b"].reshape(L, 2, 128).transpose(0, 2, 1))
    o["s5_gw"] = np.ascontiguousarray(inp["s5_glu_w"])
    o["s5_iota"] = np.ascontiguousarray(np.broadcast_to(np.arange(1024, dtype=f32), (128, 1024)))
    return o


def rev_ap(ap):
    (ps, pn), (fs, fn) = ap.ap
    return bass.AP(ap.tensor, ap.offset + (fn - 1) * fs, [[ps, pn], [-fs, fn]])


def frac_round(P, eng_c, x, ki, kf, keys):
    xk, kik, kfk = keys
    P.v(eng_c, "tensor_copy", [xk], [kik], ki, x)
    P.v(eng_c, "tensor_copy", [kik], [kfk], kf, ki)
    P.v("dve", "tensor_tensor", [xk, kfk], [xk], x, x, kf, ALU.subtract)


def mix_s5(C, l, last):
    P, B = C.P, C.B
    psrot = C.psrot
    if "s5_prow" not in B.din:
        B.inp("s5_prow", [DEPTH, 3, 1, 2048]); B.inp("s5_pT", [DEPTH, 3, 128, 16])
        B.inp("s5_Bexp", [DEPTH, 2, 2, 128, 8, 128]); B.inp("s5_Cexp", [DEPTH, 2, 2, 128, 8, 128])
        B.inp("s5_dT", [DEPTH, 128, 2]); B.inp("s5_gbT", [DEPTH, 128, 2]); B.inp("s5_gw", [DEPTH, 256, 256])
        B.inp("s5_iota", [128, 1024])
    din = B.din
    pT = C.pT_d.ap()
    m0 = P.mark()
    NW = 2048
    LB = [[P.alloc([128, 2, 8, 128], BF16) for _ in range(2)]]
    LB = LB[0]
    LC = [P.alloc([128, 2, 8, 128], BF16) for _ in range(2)]
    rP = P.alloc([128, 16])
    thP = P.alloc([128, 16])
    m1 = P.mark()
    pr_ = [P.alloc([128, NW]) for _ in range(3)]
    for i in range(3):
        P.dma(pr_[i], din["s5_prow"].ap()[l, i, 0:1, :].partition_broadcast(128)[:, 0, :], [], ["s5_pr%d" % i])
    lre, lim, dt = pr_
    P.act(dt, dt, AF.Exp, ["s5_pr2"], ["s5_pr2"])
    re = P.alloc([128, NW]); im = P.alloc([128, NW])
    P.v("dve", "tensor_tensor", ["s5_pr0", "s5_pr2"], ["s5_re"], re, lre, dt, ALU.mult)
    P.v("dve", "tensor_tensor", ["s5_pr1", "s5_pr2"], ["s5_im"], im, lim, dt, ALU.mult)
    r_ = P.alloc([128, NW])
    P.act(r_, re, AF.Exp, ["s5_re"], ["s5_r"])
    ki = P.alloc([128, NW], I32); kf = P.alloc([128, NW])
    ph = P.alloc([128, NW]); ph2 = P.alloc([128, NW])
    P.v("dve", "tensor_scalar", ["s5_im"], ["s5_ph"], ph, im, 1.0 / TWO_PI, None, ALU.mult)
    P.v("dve", "tensor_scalar", ["s5_ph"], ["s5_ph2"], ph2, ph, 0.25, None, ALU.add)
    frac_round(P, "dve", ph, ki, kf, ("s5_ph", "s5_ki", "s5_kf"))
    frac_round(P, "dve", ph2, ki, kf, ("s5_ph2", "s5_ki", "s5_kf"))
    sn = ph; cs_ = ph2
    P.act(sn, ph, AF.Sin, ["s5_ph"], ["s5_ph"], scale=TWO_PI)
    P.act(cs_, ph2, AF.Sin, ["s5_ph2"], ["s5_ph2"], scale=TWO_PI)
    nre = re; nim = im
    P.v("dve", "tensor_tensor", ["s5_r", "s5_ph2"], ["s5_re"], nre, r_, cs_, ALU.mult)
    P.v("dve", "tensor_scalar", ["s5_re"], ["s5_re"], nre, nre, -1.0, None, ALU.add)
    P.v("dve", "tensor_tensor", ["s5_r", "s5_ph"], ["s5_im"], nim, r_, sn, ALU.mult)
    den = r_; tmp = kf
    P.v("dve", "tensor_tensor", ["s5_pr0", "s5_re", "s5_im"], ["s5_r"], den, lre, lre, ALU.mult)
    P.v("dve", "tensor_tensor", ["s5_pr1", "s5_kf"], ["s5_kf"], tmp, lim, lim, ALU.mult)
    P.v("dve", "tensor_tensor", ["s5_r", "s5_kf"], ["s5_r"], den, den, tmp, ALU.add)
    P.v("dve", "reciprocal", ["s5_r"], ["s5_r"], den, den)
    cre = ph; cim = ph2
    P.v("dve", "tensor_tensor", ["s5_re", "s5_pr0", "s5_ph"], ["s5_ph"], cre, nre, lre, ALU.mult)
    P.v("dve", "tensor_tensor", ["s5_im", "s5_pr1"], ["s5_kf"], tmp, nim, lim, ALU.mult)
    P.v("dve", "tensor_tensor", ["s5_ph", "s5_kf"], ["s5_ph"], cre, cre, tmp, ALU.add)
    P.v("dve", "tensor_tensor", ["s5_ph", "s5_r"], ["s5_ph"], cre, cre, den, ALU.mult)
    P.v("dve", "tensor_tensor", ["s5_im", "s5_pr0", "s5_ph2"], ["s5_ph2"], cim, nim, lre, ALU.mult)
    P.v("dve", "tensor_tensor", ["s5_re", "s5_pr1"], ["s5_kf"], tmp, nre, lim, ALU.mult)
    P.v("dve", "tensor_tensor", ["s5_ph2", "s5_kf"], ["s5_ph2"], cim, cim, tmp, ALU.subtract)
    P.v("dve", "tensor_tensor", ["s5_ph2", "s5_r"], ["s5_ph2"], cim, cim, den, ALU.mult)
    bre = pr_[0]; bim = pr_[1]; t_a = pr_[2]; t_b = re
    P.dma(bre.rearrange("p (d x) -> p d x", d=2), din["s5_Bexp"].ap()[l, 0].rearrange("d q m s -> q d (m s)"), ["s5_ph", "s5_ph2"], ["s5_pr0"], q="pool")
    P.dma(bim.rearrange("p (d x) -> p d x", d=2), din["s5_Bexp"].ap()[l, 1].rearrange("d q m s -> q d (m s)"), ["s5_ph", "s5_ph2"], ["s5_pr1"], q="pool")
    P.v("dve", "tensor_tensor", ["s5_ph", "s5_pr0"], ["s5_pr2"], t_a, cre, bre, ALU.mult)
    P.v("dve", "tensor_tensor", ["s5_ph2", "s5_pr1"], ["s5_re"], t_b, cim, bim, ALU.mult)
    P.v("dve", "tensor_tensor", ["s5_pr2", "s5_re"], ["s5_LB"], LB[0].rearrange("p d m s -> p (d m s)"), t_a, t_b, ALU.subtract)
    P.v("dve", "tensor_tensor", ["s5_ph", "s5_pr1"], ["s5_pr2"], t_a, cre, bim, ALU.mult)
    P.v("dve", "tensor_tensor", ["s5_ph2", "s5_pr0"], ["s5_re"], t_b, cim, bre, ALU.mult)
    P.v("dve", "tensor_tensor", ["s5_pr2", "s5_re"], ["s5_LB"], LB[1].rearrange("p d m s -> p (d m s)"), t_a, t_b, ALU.add)
    cst_ = im
    P.dma(cst_.rearrange("p (d x) -> p d x", d=2), din["s5_Cexp"].ap()[l, 0].rearrange("d q m s -> q d (m s)"), ["s5_im"], ["s5_im"], q="pool")
    P.act(LC[0].rearrange("p d m s -> p (d m s)"), cst_, AF.Copy, ["s5_im"], ["s5_LC"])
    cst2 = kf
    P.dma(cst2.rearrange("p (d x) -> p d x", d=2), din["s5_Cexp"].ap()[l, 1].rearrange("d q m s -> q d (m s)"), ["s5_kf"], ["s5_kf"], q="pool")
    P.act(LC[1].rearrange("p d m s -> p (d m s)"), cst2, AF.Copy, ["s5_kf"], ["s5_LC"], scale=-1.0)
    pP = P.alloc([128, 3, 16])
    P.dma(pP, din["s5_pT"].ap()[l].rearrange("i q n -> q i n"), [], ["s5_pP"])
    P.act(pP[:, 2, :], pP[:, 2, :], AF.Exp, ["s5_pP"], ["s5_pP"])
    P.v("dve", "tensor_tensor", ["s5_pP"], ["s5_rP"], rP, pP[:, 0, :], pP[:, 2, :], ALU.mult)
    P.act(rP, rP, AF.Exp, ["s5_rP"], ["s5_rP"])
    P.v("dve", "tensor_tensor", ["s5_pP"], ["s5_thP"], thP, pP[:, 1, :], pP[:, 2, :], ALU.mult)
    P.v("dve", "tensor_scalar", ["s5_thP"], ["s5_thP"], thP, thP, 1.0 / TWO_PI, None, ALU.mult)
    kiP = P.alloc([128, 16], I32); kfP = P.alloc([128, 16])
    frac_round(P, "dve", thP, kiP, kfP, ("s5_thP", "s5_kiP", "s5_kfP"))
    P.barrier()
    P.release(m1)
    ubf = P.alloc([128, 2, T], BF16)
    yacc = P.alloc([128, 2, T])
    iota = P.alloc([128, 1024])
    P.dma(iota, din["s5_iota"].ap(), [], ["s5_iota"])
    ust = Rot([P.alloc([128, 512]) for _ in range(2)], "s5_ust")
    for ut in range(2):
        for (c0, cn) in CH512:
            sv, sk = ust.next()
            P.dma(sv[:, 0:cn], pT[ut * 128:(ut + 1) * 128, c0:c0 + cn], ["pT_d"], [sk])
            P.act(ubf[:, ut, c0:c0 + cn], sv[:, 0:cn], AF.Copy, [sk], ["s5_ubf"])
    SEG = 1024
    NBUF = 2
    m_rot = P.mark()
    COS = Rot([P.alloc([128, SEG], BF16) for _ in range(2)], "s5_cos")
    SIN = Rot([P.alloc([128, SEG], BF16) for _ in range(2)], "s5_sin")
    phv = P.alloc([128, SEG]); kiv = P.alloc([128, SEG], I32); kfv = P.alloc([128, SEG])
    BUR = Rot([P.alloc([128, SEG], BF16) for _ in range(NBUF)], "s5_bur")
    BUI = Rot([P.alloc([128, SEG], BF16) for _ in range(NBUF)], "s5_bui")
    GR = Rot([P.alloc([128, SEG], BF16) for _ in range(NBUF)], "s5_gr")
    GI = Rot([P.alloc([128, SEG], BF16) for _ in range(NBUF)], "s5_gi")
    T1 = Rot([P.alloc([128, SEG], BF16) for _ in range(NBUF)], "s5_t1")
    T1b = Rot([P.alloc([128, SEG], BF16) for _ in range(1)], "s5_t1b")
    T2 = Rot([P.alloc([128, SEG], BF16) for _ in range(NBUF)], "s5_t2")
    T2b = Rot([P.alloc([128, SEG], BF16) for _ in range(1)], "s5_t2b")
    T1o = Rot([P.alloc([128, SEG], BF16) for _ in range(1)], "s5_t1o")
    T2o = Rot([P.alloc([128, SEG], BF16) for _ in range(1)], "s5_t2o")
    HR = Rot([P.alloc([128, SEG], BF16) for _ in range(NBUF)], "s5_hr")
    HI = Rot([P.alloc([128, SEG], BF16) for _ in range(NBUF)], "s5_hi")
    cn_ = P.alloc([128, 8])
    cni = P.alloc([128, 8], I32)
    cnf = P.alloc([128, 8])
    glast = P.alloc([128, 4])
    for d_ in range(2):
        seg_order = list(range(5)) if d_ == 0 else [0, 4, 3, 2, 1]
        for m in range(8):
            ut = m // 4
            col = d_ * 8 + m
            th = thP[:, col:col + 1]
            cosv, cosk = COS.next(); sinv, sink = SIN.next()
            P.act(phv, iota, AF.Identity, ["s5_iota", "s5_thP"], ["s5_phl"], scale=th)
            frac_round(P, "dve", phv, kiv, kfv, ("s5_phl", "s5_kil", "s5_kfl"))
            P.act(sinv, phv, AF.Sin, ["s5_phl"], [sink], scale=TWO_PI)
            P.act(phv, iota, AF.Identity, ["s5_iota", "s5_thP"], ["s5_phl"], scale=th, bias=0.25)
            frac_round(P, "dve", phv, kiv, kfv, ("s5_phl", "s5_kil", "s5_kfl"))
            P.act(cosv, phv, AF.Sin, ["s5_phl"], [cosk], scale=TWO_PI)
            P.v("dve", "tensor_scalar", ["s5_thP"], ["s5_cn"], cn_[:, 0:1], th, 256.0, None, ALU.mult)
            P.v("dve", "tensor_scalar", ["s5_thP"], ["s5_cn"], cn_[:, 2:3], th, 1024.0, None, ALU.mult)
            P.v("dve", "tensor_scalar", ["s5_cn"], ["s5_cn"], cn_[:, 1:2], cn_[:, 0:1], 0.25, None, ALU.add)
            P.v("dve", "tensor_scalar", ["s5_cn"], ["s5_cn"], cn_[:, 3:4], cn_[:, 2:3], 0.25, None, ALU.add)
            P.v("dve", "tensor_copy", ["s5_cn"], ["s5_cni"], cni[:, 0:4], cn_[:, 0:4])
            P.v("dve", "tensor_copy", ["s5_cni"], ["s5_cnf"], cnf[:, 0:4], cni[:, 0:4])
            P.v("dve", "tensor_tensor", ["s5_cn", "s5_cnf"], ["s5_cn"], cn_[:, 0:4], cn_[:, 0:4], cnf[:, 0:4], ALU.subtract)
            P.act(cn_[:, 4:8], cn_[:, 0:4], AF.Sin, ["s5_cn"], ["s5_cn"], scale=TWO_PI)
            def phaseA(si, sg, d_=d_, m=m, ut=ut, cosv=cosv, cosk=cosk, sinv=sinv, sink=sink):
                t0, n = S5_SEGS[sg]
                burv, burk = BUR.next(); buiv, buik = BUI.next()
                for c0 in range(0, n, 512):
                    cn = min(512, n - c0)
                    psr, pkr = psrot.next()
                    psi, pki = psrot.next()
                    P.mm(psr[:, 0:cn], LB[0][:, d_, m, :], ubf[:, ut, t0 + c0:t0 + c0 + cn], True, True, ["s5_LB", "s5_ubf"], [pkr])
                    P.mm(psi[:, 0:cn], LB[1][:, d_, m, :], ubf[:, ut, t0 + c0:t0 + c0 + cn], True, True, ["s5_LB", "s5_ubf"], [pki])
                    if d_ == 0:
                        j0 = c0
                        sr, si_ = psr[:, 0:cn], psi[:, 0:cn]
                    else:
                        j0 = n - c0 - cn
                        sr, si_ = rev_ap(psr[:, 0:cn]), rev_ap(psi[:, 0:cn])
                    P.act(burv[:, j0:j0 + cn], sr, AF.Copy, [pkr], [burk])
                    P.act(buiv[:, j0:j0 + cn], si_, AF.Copy, [pki], [buik])
                t1v, t1k = T1.next(); t1bv, t1bk = T1b.next(); t2v, t2k = T2.next(); t2bv, t2bk = T2b.next()
                ns = slice(0, n)
                P.v("dve", "tensor_tensor", [burk, cosk], [t1k], t1v[:, ns], burv[:, ns], cosv[:, ns], ALU.mult)
                P.v("dve", "tensor_tensor", [buik, sink], [t1bk], t1bv[:, ns], buiv[:, ns], sinv[:, ns], ALU.mult)
                P.v("dve", "tensor_tensor", [t1k, t1bk], [t1k], t1v[:, ns], t1v[:, ns], t1bv[:, ns], ALU.add)
                P.v("dve", "tensor_tensor", [buik, cosk], [t2k], t2v[:, ns], buiv[:, ns], cosv[:, ns], ALU.mult)
                P.v("dve", "tensor_tensor", [burk, sink], [t2bk], t2bv[:, ns], burv[:, ns], sinv[:, ns], ALU.mult)
                P.v("dve", "tensor_tensor", [t2k, t2bk], [t2k], t2v[:, ns], t2v[:, ns], t2bv[:, ns], ALU.subtract)
                return (t1v, t1k, t2v, t2k)

            def phaseB(si, sg, nprev, ares, d_=d_, m=m, ut=ut, col=col, cosv=cosv, cosk=cosk, sinv=sinv, sink=sink):
                t0, n = S5_SEGS[sg]
                ns = slice(0, n)
                a_t1v, a_t1k, a_t2v, a_t2k = ares
                if si == 0:
                    ini_r, ini_i = 0.0, 0.0
                else:
                    sc_, cc_ = (cn_[:, 4:5], cn_[:, 5:6]) if nprev == 256 else (cn_[:, 6:7], cn_[:, 7:8])
                    P.v("dve", "tensor_scalar", ["s5_glast", "s5_cn"], ["s5_glr"], glast[:, 2:3], glast[:, 1:2], sc_, None, ALU.mult)
                    P.v("dve", "tensor_scalar", ["s5_glast", "s5_cn"], ["s5_glr"], glast[:, 3:4], glast[:, 0:1], sc_, None, ALU.mult)
                    P.v("dve", "scalar_tensor_tensor", ["s5_glast", "s5_cn", "s5_glr"], ["s5_glr"], glast[:, 2:3], glast[:, 0:1], cc_, glast[:, 2:3],
                        ALU.mult, ALU.subtract)
                    P.v("dve", "scalar_tensor_tensor", ["s5_glast", "s5_cn", "s5_glr"], ["s5_glr"], glast[:, 3:4], glast[:, 1:2], cc_, glast[:, 3:4],
                        ALU.mult, ALU.add)
                    ini_r, ini_i = glast[:, 2:3], glast[:, 3:4]
                grv, grk = GR.next(); giv, gik = GI.next()
                rb = rP[:, col:col + 1].to_broadcast([128, n])
                P.v("dve", "tensor_tensor_scan", [a_t1k, "s5_rP", "s5_glr"], [grk], grv[:, ns], rb, a_t1v[:, ns], ini_r, ALU.mult, ALU.add)
                P.v("dve", "tensor_tensor_scan", [a_t2k, "s5_rP", "s5_glr"], [gik], giv[:, ns], rb, a_t2v[:, ns], ini_i, ALU.mult, ALU.add)
                P.v("dve", "tensor_copy", [grk], ["s5_glast"], glast[:, 0:1], grv[:, n - 1:n])
                P.v("dve", "tensor_copy", [gik], ["s5_glast"], glast[:, 1:2], giv[:, n - 1:n])
                hrv, hrk = HR.next(); hiv, hik = HI.next()
                t1v, t1k = T1o.next(); t1bv, t1bk = T1b.next(); t2v, t2k = T2o.next(); t2bv, t2bk = T2b.next()
                ho_r = hrv[:, ns] if d_ == 0 else rev_ap(hrv[:, ns])
                ho_i = hiv[:, ns] if d_ == 0 else rev_ap(hiv[:, ns])
                P.v("dve", "tensor_tensor", [grk, cosk], [t1k], t1v[:, ns], grv[:, ns], cosv[:, ns], ALU.mult)
                P.v("dve", "tensor_tensor", [gik, sink], [t1bk], t1bv[:, ns], giv[:, ns], sinv[:, ns], ALU.mult)
                P.v("dve", "tensor_tensor", [t1k, t1bk], [hrk], ho_r, t1v[:, ns], t1bv[:, ns], ALU.subtract)
                P.v("dve", "tensor_tensor", [grk, sink], [t2k], t2v[:, ns], grv[:, ns], sinv[:, ns], ALU.mult)
                P.v("dve", "tensor_tensor", [gik, cosk], [t2bk], t2bv[:, ns], giv[:, ns], cosv[:, ns], ALU.mult)
                P.v("dve", "tensor_tensor", [t2k, t2bk], [hik], ho_i, t2v[:, ns], t2bv[:, ns], ALU.add)
                for c0 in range(0, n, 512):
                    cn = min(512, n - c0)
                    ps, pk = psrot.next()
                    P.mm(ps[:, 0:cn], LC[0][:, d_, m, :], hrv[:, c0:c0 + cn], True, False, ["s5_LC", hrk], [pk])
                    P.mm(ps[:, 0:cn], LC[1][:, d_, m, :], hiv[:, c0:c0 + cn], False, True, ["s5_LC", hik], [pk])
                    ya = yacc[:, ut, t0 + c0:t0 + c0 + cn]
                    if d_ == 0 and m % 4 == 0:
                        P.act(ya, ps[:, 0:cn], AF.Copy, [pk], ["s5_yacc%d" % ut])
                    else:
                        P.v("dve", "tensor_tensor", [pk, "s5_yacc%d" % ut], ["s5_yacc%d" % ut], ya, ps[:, 0:cn], ya, ALU.add)

            ares = {0: phaseA(0, seg_order[0])}
            nprev = 0
            for si, sg in enumerate(seg_order):
                if si + 1 < len(seg_order):
                    ares[si + 1] = phaseA(si + 1, seg_order[si + 1])
                phaseB(si, sg, nprev, ares.pop(si))
                nprev = S5_SEGS[sg][1]
    P.barrier()
    P.release(m_rot)
    m2 = P.mark()
    gw_st = P.alloc([128, 2, 256]); gw = P.alloc([128, 2, 256], BF16)
    P.dma(gw_st, din["s5_gw"].ap()[l].rearrange("(k p) n -> p k n", p=128), [], ["s5_gwst"])
    P.v("pool", "tensor_copy", ["s5_gwst"], ["s5_gw"], gw, gw_st)
    dT = P.alloc([128, 2]); gbT = P.alloc([128, 2])
    P.dma(dT, din["s5_dT"].ap()[l], [], ["s5_dT"])
    P.dma(gbT, din["s5_gbT"].ap()[l], [], ["s5_gbT"])
    yg = Rot([P.alloc([128, 2, 512]) for _ in range(2)], "s5_yg")
    ygb = Rot([P.alloc([128, 2, 512], BF16) for _ in range(2)], "s5_ygb")
    sg_ = Rot([P.alloc([128, 512]) for _ in range(2)], "s5_sg")
    ob = Rot([P.alloc([128, 512], BF16) for _ in range(2)], "s5_ob")
    for (c0, cn) in CH512:
        if last and c0 + cn <= CTX:
            continue
        ygv, ygk = yg.next(); ybv, ybk = ygb.next()
        for ut in range(2):
            sv, sk = ust.next()
            P.dma(sv[:, 0:cn], pT[ut * 128:(ut + 1) * 128, c0:c0 + cn], ["pT_d"], [sk])
            P.v("dve", "scalar_tensor_tensor", [sk, "s5_dT", "s5_yacc%d" % ut], [ygk], ygv[:, ut, 0:cn], sv[:, 0:cn], dT[:, ut:ut + 1],
                yacc[:, ut, c0:c0 + cn], ALU.mult, ALU.add)
        P.act(ygv[:, :, 0:cn], ygv[:, :, 0:cn], AF.Gelu_apprx_tanh, [ygk], [ygk])
        P.v("pool", "tensor_copy", [ygk], [ybk], ybv[:, :, 0:cn], ygv[:, :, 0:cn])
        for uo in range(2):
            ps, pk = psrot.next()
            for k in range(2):
                P.mm(ps[:, 0:cn], gw[:, k, uo * 128:(uo + 1) * 128], ybv[:, k, 0:cn], k == 0, k == 1, ["s5_gw", ybk], [pk])
            sgv, sgk = sg_.next()
            P.act(sgv[:, 0:cn], ps[:, 0:cn], AF.Sigmoid, [pk, "s5_gbT"], [sgk], bias=gbT[:, uo:uo + 1])
            ov, ok = ob.next()
            P.v("dve", "tensor_tensor", [sgk, ygk], [ok], ov[:, 0:cn], sgv[:, 0:cn], ygv[:, uo, 0:cn], ALU.mult)
            P.dma(C.oT_d.ap()[uo * 128:(uo + 1) * 128, c0:c0 + cn], ov[:, 0:cn], [ok], ["oT_d"], q="pool")
    P.barrier()
    P.release(m0)


MIXERS["s5"] = mix_s5
HOSTPREP["s5"] = host_s5


TWO_PI = 2.0 * math.pi
NF = 8192
OT_HY = 256
K1B = [(i * 8, 8) for i in range(8)] + [(64, 1)]
CTX_K1 = [0, 16, 32, 48, 64]
K1LIST = [list(range(65)), CTX_K1]


def _hy_feat(pos, nt):
    f32 = np.float32
    pos = pos.astype(f32)
    t01 = pos / f32(nt - 1)
    bands = np.linspace(1e-4, 15, 16, dtype=f32)
    ang = (f32(2.0 * math.pi / nt) * pos[:, None] * bands[None, :]).astype(f32)
    z = np.concatenate([t01[:, None], np.cos(ang), -np.sin(ang)], axis=-1).astype(f32)
    return z, t01


def host_hy(inp):
    L = DEPTH
    f32 = np.float32
    bf = ml_dtypes.bfloat16
    o = {}
    n = np.arange(NF)
    zT = np.zeros((2, 33, NF), f32)
    nt01 = np.zeros((2, 128, 64), f32)
    msk = np.zeros((2, 128, 64), f32)
    for kind, nt in enumerate((SEQ, CTX)):
        pos = np.zeros(NF, np.int64)
        m = np.zeros(NF, f32)
        pos[:nt] = n[:nt]; m[:nt] = 1
        hi = n > NF - nt
        pos[hi] = NF - n[hi]; m[hi] = 1
        pos[NF // 2] = 0; m[NF // 2] = 1
        z, t01 = _hy_feat(pos, nt)
        zT[kind] = z.T
        nt01[kind] = (-t01).reshape(128, 64)
        msk[kind] = m.reshape(128, 64)
    o["hy_zT"] = zT
    o["hy_nt01"] = nt01
    o["hy_msk"] = msk
    n1 = np.arange(128)[:, None]; k1 = np.arange(65)[None, :]
    ang = 2 * np.pi * n1 * k1 / 128
    o["hy_D1"] = np.concatenate([np.cos(ang), -np.sin(ang)], axis=1).astype(bf)
    n2 = np.arange(64); k2 = np.arange(64)
    Tm = np.zeros((65, 128, 3, 128), np.float64)
    for kk in range(65):
        w = np.exp(-2j * np.pi * (n2[:, None] * kk / NF + n2[:, None] * k2[None, :] / 64))
        for c2 in range(2):
            Tm[kk, c2::2, 0, c2::2] = w.real
            Tm[kk, c2::2, 1, c2::2] = w.imag
            Tm[kk, c2::2, 2, c2::2] = -w.imag
    o["hy_T"] = Tm.astype(bf)
    t2 = np.arange(64)
    w = np.exp(2j * np.pi * k2[:, None] * t2[None, :] / 64)
    RA = np.zeros((128, 2, 64, 2)); RB = np.zeros((128, 2, 64, 2))
    for c2 in range(2):
        RA[c2::2, 0, :, c2] = w.real; RA[c2::2, 1, :, c2] = w.imag
        RB[c2::2, 0, :, c2] = -w.imag; RB[c2::2, 1, :, c2] = w.real
    o["hy_R"] = np.stack([RA.reshape(128, 256), RB.reshape(128, 256)], axis=1).astype(bf)
    k1v = np.arange(65); t1 = np.arange(64)
    wgt = np.full(65, 2.0); wgt[0] = 1; wgt[64] = 1
    V = np.zeros((65, 64, 2, 64))
    for tt in range(64):
        w = (wgt[:, None] / NF) * np.exp(2j * np.pi * (tt * k1v[:, None] / NF + t1[None, :] * k1v[:, None] / 128))
        V[:, tt, 0, :] = w.real; V[:, tt, 1, :] = -w.imag
    o["hy_V"] = V.astype(bf)
    o["hy_Vc"] = np.ascontiguousarray(V[CTX_K1][:, :, :, 0:4]).astype(bf)
    o["hy_w1"] = np.ascontiguousarray(inp["hy_w1"])
    o["hy_w2"] = np.ascontiguousarray(inp["hy_w2"])
    o["hy_cols"] = np.ascontiguousarray(np.stack([inp["hy_b1"], inp["hy_b2"], inp["hy_freq"]], axis=-1))
    w3 = inp["hy_w3"].reshape(L, 64, 2, 2, 256)
    o["hy_w3"] = np.ascontiguousarray(w3.transpose(0, 1, 3, 2, 4))
    dl = inp["hy_deltas"].reshape(L, 2, 2, 256)
    o["hy_dl"] = np.ascontiguousarray(dl.transpose(0, 2, 1, 3).reshape(L, 2, 1, 512))
    cw = np.concatenate([inp["hy_conv_w"], inp["hy_conv_b"][:, None, :]], axis=1)
    o["hy_cw"] = np.ascontiguousarray(cw.reshape(L, 1, 4 * 768))
    o["hy_bias"] = np.ascontiguousarray(inp["hy_bias"].reshape(L, 1, 512))
    return o


def mix_hy(C, l, last):
    P, B = C.P, C.B
    psrot = C.psrot
    if "hy_zT" not in B.din:
        B.inp("hy_zT", [2, 33, NF]); B.inp("hy_nt01", [2, 128, 64]); B.inp("hy_msk", [2, 128, 64])
        B.inp("hy_D1", [128, 130], BF16); B.inp("hy_T", [65, 128, 3, 128], BF16)
        B.inp("hy_R", [128, 2, 256], BF16); B.inp("hy_V", [65, 64, 2, 64], BF16); B.inp("hy_Vc", [5, 64, 2, 4], BF16)
        B.inp("hy_w1", [DEPTH, 33, 64]); B.inp("hy_w2", [DEPTH, 64, 64]); B.inp("hy_cols", [DEPTH, 64, 3])
        B.inp("hy_w3", [DEPTH, 64, 2, 2, 256]); B.inp("hy_dl", [DEPTH, 2, 1, 512])
        B.inp("hy_cw", [DEPTH, 1, 3072]); B.inp("hy_bias", [DEPTH, 1, 512])
        B.scr("hspec_d", [2, 2, 2, 2, 128, 65 * 64])
    din = B.din
    hspec = B.dscr["hspec_d"]
    kinds = [0] if last else [0, 1]
    m0 = P.mark()
    D1 = P.alloc([128, 130], BF16)
    P.dma(D1, din["hy_D1"].ap(), [], ["hy_D1"])
    Rm = P.alloc([128, 2, 256], BF16)
    P.dma(Rm, din["hy_R"].ap(), [], ["hy_R"])
    Vm = P.alloc([65, 64, 2, 64], BF16)
    P.dma(Vm, din["hy_V"].ap(), [], ["hy_V"])
    Vmc = P.alloc([5, 64, 2, 4], BF16)
    P.dma(Vmc, din["hy_Vc"].ap(), [], ["hy_V"])
    Trot = Rot([P.alloc([128, 3, 128], BF16) for _ in range(5)], "hy_T")
    B1 = P.alloc([128, 64, 130], BF16)
    mfwd = P.mark()

    def fwd(xv, Kk, xkey, consume, k1list):
        for q0 in range(0, 64, 3):
            qn = min(3, 64 - q0)
            ps, pk = psrot.next()
            for j in range(qn):
                q = q0 + j
                P.mm(ps[:, j * 130:(j + 1) * 130], xv[:, q, :, :].rearrange("p n c -> p (n c)"), D1[0:Kk, :], True, True, [xkey, "hy_D1"], [pk])
            src = ps[:, 0:qn * 130].rearrange("p (a b) -> p a b", a=qn)
            if (q0 // 3) % 2 == 0:
                P.act(B1[:, q0:q0 + qn, :], src, AF.Copy, [pk], ["hy_B1"])
            else:
                P.v("dve", "tensor_copy", [pk], ["hy_B1"], B1[:, q0:q0 + qn, :], src)
        for bi, k0 in enumerate(range(0, len(k1list), 8)):
            kn = min(8, len(k1list) - k0)
            psr, pkr = psrot.next()
            psi, pki = psrot.next()
            for j in range(kn):
                k1 = k1list[k0 + j]
                tv, tk = Trot.next()
                P.dma(tv, din["hy_T"].ap()[k1], [], [tk])
                br = B1[:, :, k1]
                bi_ = B1[:, :, 65 + k1]
                P.mm(psr[:, j * 64:(j + 1) * 64], tv[:, 0, :], br, True, False, [tk, "hy_B1"], [pkr])
                P.mm(psr[:, j * 64:(j + 1) * 64], tv[:, 2, :], bi_, False, True, [tk, "hy_B1"], [pkr])
                P.mm(psi[:, j * 64:(j + 1) * 64], tv[:, 1, :], br, True, False, [tk, "hy_B1"], [pki])
                P.mm(psi[:, j * 64:(j + 1) * 64], tv[:, 0, :], bi_, False, True, [tk, "hy_B1"], [pki])
            consume(bi, k0, kn, psr, pkr, psi, pki)

    w1 = P.alloc([33, 64]); w2 = P.alloc([64, 64]); cols = P.alloc([64, 3])
    P.dma(w1, din["hy_w1"].ap()[l], [], ["hy_w1"])
    P.dma(w2, din["hy_w2"].ap()[l], [], ["hy_w2"])
    P.dma(cols, din["hy_cols"].ap()[l], [], ["hy_cols"])
    sc = P.alloc([64, 4])
    P.v("dve", "tensor_scalar", ["hy_cols"], ["hy_sc"], sc[:, 0:1], cols[:, 2:3], 1.0 / TWO_PI, None, ALU.mult)
    P.v("dve", "tensor_tensor", ["hy_cols", "hy_sc"], ["hy_sc"], sc[:, 1:2], cols[:, 0:1], sc[:, 0:1], ALU.mult)
    P.v("dve", "tensor_tensor", ["hy_cols", "hy_sc"], ["hy_sc"], sc[:, 2:3], cols[:, 1:2], sc[:, 0:1], ALU.mult)
    w3 = P.alloc([64, 2, 2, 256])
    P.dma(w3, din["hy_w3"].ap()[l], [], ["hy_w3"])
    absd = P.alloc([128, 2, 256])
    for d_ in range(2):
        P.dma(absd[d_ * 64:(d_ + 1) * 64].rearrange("p o c -> p (o c)"),
              din["hy_dl"].ap()[l, d_, 0:1, :].partition_broadcast(64)[:, 0, :], [], ["hy_absd"])
    P.act(absd, absd, AF.Abs, ["hy_absd"], ["hy_absd"])
    mflt = P.mark()
    hid2 = P.alloc([64, NF])
    zrot = Rot([P.alloc([33, 512]) for _ in range(2)], "hy_z")
    phb = P.alloc([64, 512]); kib = P.alloc([64, 512], I32); kfb = P.alloc([64, 512]); h1b = P.alloc([64, 512])
    nt01 = P.alloc([128, 64]); msk = P.alloc([128, 64])
    krot = Rot([P.alloc([128, 256]) for _ in range(4)], "hy_kr")
    xk = P.alloc([128, 128, 64, 2], BF16)
    dec = Rot([P.alloc([128, 256]) for _ in range(3)], "hy_dec")
    sqr = Rot([P.alloc([128, 256]) for _ in range(4)], "hy_sq")
    ssa = P.alloc([128, 256]); rs = P.alloc([128, 256])
    yst = Rot([P.alloc([128, 2, 512]) for _ in range(2)], "hy_yst")

    def sin_layer(ps, pk, n, bcol, outv, okey):
        P.act(phb[:, 0:n], ps[0:64, 0:n], AF.Identity, [pk, "hy_sc"], ["hy_ph"], scale=sc[:, 0:1], bias=sc[:, bcol:bcol + 1])
        P.v("dve", "tensor_copy", ["hy_ph"], ["hy_ki"], kib[:, 0:n], phb[:, 0:n])
        P.v("dve", "tensor_copy", ["hy_ki"], ["hy_kf"], kfb[:, 0:n], kib[:, 0:n])
        P.v("dve", "tensor_tensor", ["hy_ph", "hy_kf"], ["hy_ph"], phb[:, 0:n], phb[:, 0:n], kfb[:, 0:n], ALU.subtract)
        P.act(outv, phb[:, 0:n], AF.Sin, ["hy_ph"], [okey], scale=TWO_PI)

    for kind in kinds:
        P.dma(nt01, din["hy_nt01"].ap()[kind], ["hy_nt01r"], ["hy_nt01"])
        P.dma(msk, din["hy_msk"].ap()[kind], ["hy_mskr"], ["hy_msk"])
        for ch in (range(NF // 512) if kind == 0 else (0, 8, 15)):
            zv, zk = zrot.next()
            P.dma(zv, din["hy_zT"].ap()[kind, :, ch * 512:(ch + 1) * 512], [], [zk])
            ps, pk = psrot.next()
            P.mm(ps[0:64, :], w1, zv, True, True, ["hy_w1", zk], [pk])
            sin_layer(ps, pk, 512, 1, h1b, "hy_h1")
            ps2, pk2 = psrot.next()
            P.mm(ps2[0:64, :], w2, h1b, True, True, ["hy_w2", "hy_h1"], [pk2])
            sin_layer(ps2, pk2, 512, 2, hid2[:, ch * 512:(ch + 1) * 512], "hy_hid2")
        for o_ in range(2):
            pss_, pks_ = C.pss[7][:], "ps7"
            prot7 = Rot([p[:] for p in C.pss[0:7]], "ps")
            LOOKF = 2
            pendf = {}

            def issue_k(n2, o_=o_, kind=kind):
                psf, pkf = prot7.next()
                psb, pkb = prot7.next()
                lh = hid2[:, n2:NF:64]
                P.mm(psf[:, 0:256], lh, w3[:, 0, o_, :], True, True, ["hy_hid2", "hy_w3"], [pkf])
                P.mm(psb[:, 0:256], lh, w3[:, 1, o_, :], True, True, ["hy_hid2", "hy_w3"], [pkb])
                dv, dk = dec.next()
                P.act(dv, absd[:, o_, :], AF.Exp, ["hy_absd", "hy_nt01"], [dk], scale=nt01[:, n2:n2 + 1])
                kv, kk_ = krot.next()
                P.v("dve", "scalar_tensor_tensor", [pkf, "hy_msk", dk], [kk_], kv[0:64, :], psf[0:64, 0:256],
                    msk[0:64, n2:n2 + 1], dv[0:64, :], ALU.mult, ALU.mult)
                P.v("dve", "scalar_tensor_tensor", [pkb, "hy_msk", dk], [kk_], kv[64:128, :], psb[64:128, 0:256],
                    msk[64:128, n2:n2 + 1], dv[64:128, :], ALU.mult, ALU.mult)
                sv, sk = sqr.next()
                P.act(sv[:, 0:256], kv, AF.Square, [kk_], [sk])
                P.act(xk[:, :, n2, :], kv.rearrange("p (q c) -> p q c", c=2), AF.Copy, [kk_], ["hy_xk"])
                pendf[n2] = (sv, sk)

            for n2 in range(LOOKF):
                issue_k(n2)
            for n2 in range(64):
                if n2 + LOOKF < 64:
                    issue_k(n2 + LOOKF)
                sv, sk = pendf.pop(n2)
                P.mm(pss_[:, 0:256], C.ones_f[:, :], sv[:, 0:256], n2 == 0, n2 == 63, [sk, "ones_f"], [pks_])
            P.act(rs, pss_[:, 0:256], AF.Sqrt, [pks_, "eps_t"], ["hy_rs"], bias=C.eps_t[:], scale=1.0)
            P.v("dve", "reciprocal", ["hy_rs"], ["hy_rs"], rs, rs)
            rsb = rs.rearrange("p (q c) -> p q c", c=2).unsqueeze(2).to_broadcast([128, 128, 16, 2])
            for g4 in range(4):
                xs_ = xk[:, :, g4 * 16:(g4 + 1) * 16, :]
                P.v("dve", "tensor_tensor", ["hy_xk", "hy_rs"], ["hy_xk"], xs_, xs_, rsb, ALU.mult)
            P.v("dve", "memset", [], ["hy_xk"], xk[64:65, :, 0, :], 0.0)
            if C.cfg.get("hy_dbg") and kind == 0 and o_ == 0:
                dbg1 = B.scr("hy_dbg_hid2", [64, NF])
                dbg2 = B.scr("hy_dbg_xk", [128, 128 * 64 * 2], BF16)
                dbg3 = B.scr("hy_dbg_rs", [128, 256])
                P.dma(dbg1.ap(), hid2, ["hy_hid2"], ["dbg1"], q="pool")
                P.dma(dbg2.ap(), xk.rearrange("p a b c -> p (a b c)"), ["hy_xk"], ["dbg2"], q="pool")
                P.dma(dbg3.ap(), rs, ["hy_rs"], ["dbg3"], q="pool")
            for hf in range(2):
                def consume(bi, k0, kn, psr, pkr, psi, pki, kind=kind, o_=o_, hf=hf):
                    yv, yk = yst.next()
                    fs = 1.0 if kind == 0 else 16.0
                    P.act(yv[:, 0, 0:kn * 64], psr[:, 0:kn * 64], AF.Copy, [pkr], [yk], scale=fs)
                    P.v("dve", "tensor_scalar", [pki], [yk], yv[:, 1, 0:kn * 64], psi[:, 0:kn * 64], fs, None, ALU.mult)
                    for ri in range(2):
                        P.dma(hspec.ap()[kind, o_, hf, ri, :, k0 * 64:(k0 + kn) * 64], yv[:, ri, 0:kn * 64], [yk], ["hspec_d"], q="pool")
                fwd(xk[:, hf * 64:(hf + 1) * 64, :, :], 128, "hy_xk", consume, K1LIST[kind])
    P.barrier()
    P.release(mflt)
    if C.cfg.get("hy_stop", 9) <= 1:
        P.release(m0); return
    cwr = P.alloc([64, 4, 768])
    P.dma(cwr.rearrange("p a c -> p (a c)"), din["hy_cw"].ap()[l, 0:1, :].partition_broadcast(64)[:, 0, :], [], ["hy_cw"])
    hbr = P.alloc([64, 2, 256])
    P.dma(hbr.rearrange("p a c -> p (a c)"), din["hy_bias"].ap()[l, 0:1, :].partition_broadcast(64)[:, 0, :], [], ["hy_hb"])
    xv = P.alloc([64, 64, 64, 2], BF16)
    z1 = P.alloc([64, 64, 64, 2], BF16)
    Zr = P.alloc([128, 65, 64], BF16); Zi = P.alloc([128, 65, 64], BF16)
    G = P.alloc([65, 2, 64, 128], BF16)
    oTs = P.alloc([128, SEQ], BF16)
    dwl = Rot([P.alloc([64, 3, 4, 128]) for _ in range(2)], "hy_dwl")
    dwa = Rot([P.alloc([64, 4, 128]) for _ in range(3)], "hy_dwa")
    dwt = P.alloc([64, 4, 128])
    hrot = Rot([P.alloc([128, 2, 512]) for _ in range(2)], "hy_h")
    tt1 = P.alloc([128, 512]); tt2 = P.alloc([128, 512])
    gt = P.alloc([64, 4, 128]); z2f = P.alloc([64, 4, 128])

    def blkv(x, M, b):
        return x[0:M, :, 4 * b:4 * b + 4, :].rearrange("p q n c -> p n q c")

    def dwconv_blk(kind, set_, hf, b):
        M = 64 if kind == 0 else 4
        base = LAT0 if kind == 0 else CTX0
        ntok = SEQ if kind == 0 else CTX
        c0 = set_ * 256 + hf * 128
        lv, lk = dwl.next()
        for s in range(3):
            src = C.hyp_d.ap()[base + s - 1:base + s - 1 + ntok, c0:c0 + 128].rearrange("(a b) c -> a b c", b=64)[:, 4 * b:4 * b + 4, :]
            P.dma(lv[0:M, s, :, :], src, ["hyp_d"], [lk])
        av, ak = dwa.next()
        wv = lambda tap: cwr[0:M, tap, c0:c0 + 128].unsqueeze(1).to_broadcast([M, 4, 128])
        P.v("dve", "tensor_tensor", [lk, "hy_cw"], [ak], av[0:M], lv[0:M, 1], wv(1), ALU.mult)
        P.v("dve", "tensor_tensor", [lk, "hy_cw"], ["hy_dwt"], dwt[0:M], lv[0:M, 0], wv(0), ALU.mult)
        P.v("dve", "tensor_tensor", [ak, "hy_dwt"], [ak], av[0:M], av[0:M], dwt[0:M], ALU.add)
        P.v("dve", "tensor_tensor", [lk, "hy_cw"], ["hy_dwt"], dwt[0:M], lv[0:M, 2], wv(2), ALU.mult)
        P.v("dve", "tensor_tensor", [ak, "hy_dwt"], [ak], av[0:M], av[0:M], dwt[0:M], ALU.add)
        P.v("dve", "tensor_tensor", [ak, "hy_cw"], [ak], av[0:M], av[0:M], wv(3), ALU.add)
        return av, ak

    def inverse(kind, evac):
        M = 64 if kind == 0 else 4
        nk = len(K1LIST[kind])
        Vsel = Vm if kind == 0 else Vmc
        for q0 in range(0, 64, 2):
            ps, pk = psrot.next()
            for j in range(2):
                q = q0 + j
                P.mm(ps[0:nk, j * 256:(j + 1) * 256], Zr[:, 0:nk, q], Rm[:, 0, :], True, False, ["hy_Z", "hy_R"], [pk])
                P.mm(ps[0:nk, j * 256:(j + 1) * 256], Zi[:, 0:nk, q], Rm[:, 1, :], False, True, ["hy_Z", "hy_R"], [pk])
            for j in range(2):
                q = q0 + j
                src = ps[0:nk, j * 256:(j + 1) * 256].rearrange("p (r t c) -> p r t c", r=2, t=64)
                if j == 0:
                    P.act(G[0:nk, :, :, 2 * q:2 * q + 2], src, AF.Copy, [pk], ["hy_G"])
                else:
                    P.v("dve", "tensor_copy", [pk], ["hy_G"], G[0:nk, :, :, 2 * q:2 * q + 2], src)
        for b in range(16):
            ps, pk = psrot.next()
            for j in range(4):
                t2 = 4 * b + j
                P.mm(ps[0:M, j * 128:(j + 1) * 128], Vsel[0:nk, t2, 0, 0:M], G[0:nk, 0, t2, :], True, False, ["hy_V", "hy_G"], [pk])
                P.mm(ps[0:M, j * 128:(j + 1) * 128], Vsel[0:nk, t2, 1, 0:M], G[0:nk, 1, t2, :], False, True, ["hy_V", "hy_G"], [pk])
            evac(b, ps, pk, M)

    for kind in kinds:
        M = 64 if kind == 0 else 4
        ntok = SEQ if kind == 0 else CTX
        tok0 = CTX if kind == 0 else 0
        for hf in range(2):
            if kind == 1:
                P.v("dve", "memset", [], ["hy_xv"], xv, 0.0)
                P.v("pool", "memset", [], ["hy_z1"], z1, 0.0)
            for b in range(16):
                av, ak = dwconv_blk(kind, 2, hf, b)
                P.act(blkv(xv, M, b), av[0:M].rearrange("p n (q c) -> p n q c", c=2), AF.Copy, [ak], ["hy_xv"])
            for o_ in range(2):
                src_x = xv if o_ == 0 else z1
                src_k = "hy_xv" if o_ == 0 else "hy_z1"

                def consume(bi, k0, kn, psr, pkr, psi, pki, kind=kind, o_=o_, hf=hf):
                    hv, hk = hrot.next()
                    for ri in range(2):
                        P.dma(hv[:, ri, 0:kn * 64], hspec.ap()[kind, o_, hf, ri, :, k0 * 64:(k0 + kn) * 64], ["hspec_d"], [hk])
                    w = kn * 64
                    zr = Zr[:, k0:k0 + kn, :].rearrange("p a b -> p (a b)")
                    zi = Zi[:, k0:k0 + kn, :].rearrange("p a b -> p (a b)")
                    P.v("dve", "tensor_tensor", [pkr, hk], ["hy_tt1"], tt1[:, 0:w], psr[:, 0:w], hv[:, 0, 0:w], ALU.mult)
                    P.v("dve", "tensor_tensor", [pki, hk], ["hy_tt2"], tt2[:, 0:w], psi[:, 0:w], hv[:, 1, 0:w], ALU.mult)
                    P.v("dve", "tensor_tensor", ["hy_tt1", "hy_tt2"], ["hy_Z"], zr, tt1[:, 0:w], tt2[:, 0:w], ALU.subtract)
                    P.v("dve", "tensor_tensor", [pkr, hk], ["hy_tt1"], tt1[:, 0:w], psr[:, 0:w], hv[:, 1, 0:w], ALU.mult)
                    P.v("dve", "tensor_tensor", [pki, hk], ["hy_tt2"], tt2[:, 0:w], psi[:, 0:w], hv[:, 0, 0:w], ALU.mult)
                    P.v("dve", "tensor_tensor", ["hy_tt1", "hy_tt2"], ["hy_Z"], zi, tt1[:, 0:w], tt2[:, 0:w], ALU.add)
                fwd(src_x, 64, src_k, consume, K1LIST[kind])

                def evac(b, ps, pk, M, kind=kind, o_=o_, hf=hf, src_x=src_x, src_k=src_k):
                    xg, xgk = dwconv_blk(kind, o_, hf, b)
                    yv = ps[0:M, :].rearrange("p (a c) -> p a c", a=4)
                    brow = hbr[0:M, o_, hf * 128:(hf + 1) * 128].unsqueeze(1).to_broadcast([M, 4, 128])
                    P.v("dve", "tensor_tensor", [src_k, "hy_hb"], ["hy_gt"], gt[0:M].rearrange("p n (q c) -> p n q c", c=2), blkv(src_x, M, b), brow.rearrange("p n (q c) -> p n q c", c=2), ALU.mult)
                    P.v("dve", "tensor_tensor", [pk, "hy_gt"], ["hy_gt"], gt[0:M], yv, gt[0:M], ALU.add)
                    if o_ == 0:
                        P.v("dve", "tensor_tensor", ["hy_gt", xgk], ["hy_z1"], blkv(z1, M, b), gt[0:M].rearrange("p n (q c) -> p n q c", c=2), xg[0:M].rearrange("p n (q c) -> p n q c", c=2), ALU.mult)
                    else:
                        P.v("dve", "tensor_tensor", ["hy_gt", xgk], ["hy_z2f"], z2f[0:M], gt[0:M], xg[0:M], ALU.mult)
                        pt, ptk = psrot.next()
                        for j in range(4):
                            P.tr(pt[:, j * 64:j * 64 + M], z2f[0:M, j, :], C.ident[0:M, 0:M], ["hy_z2f", "ident"], [ptk])
                        dst = oTs[:, 0:ntok].rearrange("p (a b) -> p b a", b=64)[:, 4 * b:4 * b + 4, :]
                        srcp = pt[:, 0:256].rearrange("p (j a) -> p j a", j=4)[:, :, 0:M]
                        P.act(dst, srcp, AF.Copy, [ptk], ["hy_oTs"])
                inverse(kind, evac)
            P.dma(C.oT_d.ap()[OT_HY + hf * 128:OT_HY + (hf + 1) * 128, tok0:tok0 + ntok], oTs[:, 0:ntok], ["hy_oTs"], ["oT_d"], q="pool")
    P.barrier()
    P.release(m0)


MIXERS["hy"] = mix_hy
HOSTPREP["hy"] = host_hy
```
